# Optimizing a Trainium2 kernel written in Bass

```python
import jax, jax.numpy as jnp
from jax import lax
import numpy as np

D_MODEL = 1024
BATCH = 16
SEQ = 256
DEPTH = 1
DEC_BATCH = 8
DEC_SEQ = 1024
PAST_LEN = 256

GRID_W = 64
HEAD_DIM = 64
RET_HEADS = 8
RET_DK = 64
RET_DV = 64
RET_QK_WIDTH = RET_HEADS * RET_DK
RET_WIDTH = RET_HEADS * RET_DV
NA_HEADS = 8
NA_WIDTH = NA_HEADS * HEAD_DIM
MIX_WIDTH = RET_WIDTH + NA_WIDTH
IN_WIDTH = 2 * RET_QK_WIDTH + 2 * RET_WIDTH + 3 * NA_WIDTH
CHUNK = 128
WIN_H = 8
WIN_W = 16
QBLOCK = 128
ROPE_BASE = 10000.0
PEER_HEADS = 8
PEER_DKEY = 256
N_KEYS = 128
N_EXPERTS = N_KEYS * N_KEYS
PEER_TOPK = 16
TOKEN_BLOCK = 128
EPS = 1e-6
NEG_INF = -1e30

kernel_name = "hymba_retnet_natten_peer_prefix_dit"


def rms_norm(x, g):
    xf = x.astype(jnp.float32)
    y = xf * lax.rsqrt(jnp.mean(xf * xf, axis=-1, keepdims=True) + EPS)
    return (y * g.astype(jnp.float32)).astype(x.dtype)


def modulate(x, g, shift, scale):
    return rms_norm(x, g) * (1 + scale) + shift


def adaln(cvec, w_ada, b_ada):
    m = jax.nn.silu(cvec) @ w_ada + b_ada
    return jnp.split(m, 6, axis=-1)


def heads(x, n_heads):
    B, T, _ = x.shape
    return x.reshape(B, T, n_heads, -1)


def project_in(h, w_in):
    proj = h @ w_in
    cuts = np.cumsum([RET_QK_WIDTH, RET_QK_WIDTH, RET_WIDTH, RET_WIDTH, NA_WIDTH, NA_WIDTH]).tolist()
    return jnp.split(proj, cuts, axis=-1)


def rope_2d(x):
    T = x.shape[1]
    t = jnp.arange(T)
    rows = (t // GRID_W).astype(jnp.float32)
    cols = (t % GRID_W).astype(jnp.float32)
    half = RET_DK // 2
    n_freq = half // 2
    freqs = ROPE_BASE ** (-jnp.arange(n_freq, dtype=jnp.float32) / n_freq)

    def rot(xh, pos):
        ang = pos[:, None] * freqs[None, :]
        cos = jnp.cos(ang)[None, :, None, :]
        sin = jnp.sin(ang)[None, :, None, :]
        x1, x2 = xh[..., :n_freq], xh[..., n_freq:]
        return jnp.concatenate([x1 * cos - x2 * sin, x1 * sin + x2 * cos], axis=-1)

    out = jnp.concatenate([rot(x[..., :half], rows), rot(x[..., half:], cols)], axis=-1)
    return out.astype(x.dtype)


def retention_scan(q, k, v, decay_logit, s0, strict):
    B, T, H, dk = q.shape
    dv = v.shape[-1]
    n = T // CHUNK
    lg = jax.nn.log_sigmoid(decay_logit.astype(jnp.float32))
    pos = jnp.arange(CHUNK, dtype=jnp.float32)
    diff = pos[:, None] - pos[None, :]
    mask = (diff > 0) if strict else (diff >= 0)
    dmat = jnp.where(mask[None], jnp.exp(lg[:, None, None] * jnp.maximum(diff, 0.0)[None]), 0.0)
    xi = jnp.exp(lg[:, None] * (pos + 1.0)[None])
    zeta = jnp.exp(lg[:, None] * (CHUNK - 1.0 - pos)[None])
    cdec = jnp.exp(lg * CHUNK)

    def chunks(a):
        return a.astype(jnp.float32).reshape(B, n, CHUNK, H, a.shape[-1]).transpose(1, 0, 3, 2, 4)

    qs, ks, vs = chunks(q), chunks(k) * (dk ** -0.5), chunks(v)

    def step(S, inp):
        qc, kc, vc = inp
        sc = jnp.einsum('bhnd,bhmd->bhnm', qc, kc) * dmat[None]
        inner = jnp.einsum('bhnm,bhme->bhne', sc, vc)
        cross = jnp.einsum('bhnd,bhde->bhne', qc, S) * xi[None, :, :, None]
        S_new = S * cdec[None, :, None, None] + jnp.einsum('bhmd,bhme->bhde', kc * zeta[None, :, :, None], vc)
        return S_new, inner + cross

    s_fin, out = lax.scan(step, s0.astype(jnp.float32), (qs, ks, vs))
    out = out.transpose(1, 0, 3, 2, 4).reshape(B, T, H, dv)
    return out, s_fin


def bidir_retention(q, k, v, logit_f, logit_b, s0_f, s0_b):
    y_f, s_f = retention_scan(q, k, v, logit_f, s0_f, False)
    y_b, s_b = retention_scan(q[:, ::-1], k[:, ::-1], v[:, ::-1], logit_b, s0_b, True)
    return y_f + y_b[:, ::-1], s_f, s_b


def retention_output(y, g, gn_g):
    B, T = y.shape[:2]
    yn = y * lax.rsqrt(jnp.mean(y * y, axis=-1, keepdims=True) + EPS)
    yn = yn * gn_g.astype(jnp.float32).reshape(RET_HEADS, RET_DV)
    return (jax.nn.silu(g.astype(jnp.float32)) * yn.reshape(B, T, RET_WIDTH)).astype(g.dtype)


def dense_context_attention(q, k, v):
    B, T, H, d = q.shape
    qb = q.reshape(B, T // QBLOCK, QBLOCK, H, d).transpose(1, 0, 2, 3, 4)

    def blk(qi):
        s = jnp.einsum('bqhd,bkhd->bhqk', qi, k).astype(jnp.float32) * (d ** -0.5)
        p = jax.nn.softmax(s, axis=-1).astype(v.dtype)
        return jnp.einsum('bhqk,bkhd->bqhd', p, v)

    o = lax.map(blk, qb)
    return o.transpose(1, 0, 2, 3, 4).reshape(B, T, H, d)


def neighbourhood_attention(q, k, v, k_ctx, v_ctx, rpb):
    B, T, H, d = q.shape
    rows = T // GRID_W
    kh = min(WIN_H, rows)
    n_win = kh * GRID_W
    qg = q.reshape(B, rows, GRID_W, H, d)
    kg = k.reshape(B, rows, GRID_W, H, d)
    vg = v.reshape(B, rows, GRID_W, H, d)
    kcol = jnp.tile(jnp.arange(GRID_W), kh)
    krow_off = jnp.repeat(jnp.arange(kh), GRID_W)
    qcol = jnp.arange(GRID_W)
    cstart = jnp.clip(qcol - WIN_W // 2, 0, GRID_W - WIN_W)
    in_win = (kcol[None, :] >= cstart[:, None]) & (kcol[None, :] < cstart[:, None] + WIN_W)
    dc_idx = jnp.clip(kcol[None, :] - qcol[:, None] + WIN_W - 1, 0, 2 * WIN_W - 2)
    scale = d ** -0.5

    def row(r):
        rs = jnp.clip(r - kh // 2, 0, rows - kh)
        q_r = lax.dynamic_index_in_dim(qg, r, axis=1, keepdims=False)
        k_r = lax.dynamic_slice_in_dim(kg, rs, kh, axis=1).reshape(B, n_win, H, d)
        v_r = lax.dynamic_slice_in_dim(vg, rs, kh, axis=1).reshape(B, n_win, H, d)
        dr_idx = rs + krow_off - r + WIN_H - 1
        bias = rpb[:, dr_idx[None, :], dc_idx].astype(jnp.float32)
        s_win = jnp.einsum('bqhd,bkhd->bhqk', q_r, k_r).astype(jnp.float32) * scale + bias[None]
        s_win = jnp.where(in_win[None, None], s_win, NEG_INF)
        s_ctx = jnp.einsum('bqhd,bkhd->bhqk', q_r, k_ctx).astype(jnp.float32) * scale
        p = jax.nn.softmax(jnp.concatenate([s_win, s_ctx], axis=-1), axis=-1).astype(v.dtype)
        return (jnp.einsum('bhqk,bkhd->bqhd', p[..., :n_win], v_r)
                + jnp.einsum('bhqk,bkhd->bqhd', p[..., n_win:], v_ctx))

    o = lax.map(row, jnp.arange(rows))
    return o.transpose(1, 0, 2, 3, 4).reshape(B, T, H, d)


def peer(x, wq, keys, u, v):
    B, T, D = x.shape
    xt = x.reshape(-1, TOKEN_BLOCK, D)
    half = PEER_DKEY // 2

    def blk(xb):
        q = (xb @ wq).reshape(TOKEN_BLOCK, PEER_HEADS, PEER_DKEY)
        s1 = jnp.einsum('nhd,hkd->nhk', q[..., :half], keys[:, 0]).astype(jnp.float32)
        s2 = jnp.einsum('nhd,hkd->nhk', q[..., half:], keys[:, 1]).astype(jnp.float32)
        v1, i1 = lax.top_k(s1, PEER_TOPK)
        v2, i2 = lax.top_k(s2, PEER_TOPK)
        cand = (v1[..., :, None] + v2[..., None, :]).reshape(TOKEN_BLOCK, PEER_HEADS, PEER_TOPK * PEER_TOPK)
        cidx = (i1[..., :, None] * N_KEYS + i2[..., None, :]).reshape(TOKEN_BLOCK, PEER_HEADS, PEER_TOPK * PEER_TOPK)
        top, sel = lax.top_k(cand, PEER_TOPK)
        eidx = jnp.take_along_axis(cidx, sel, axis=-1)
        gate = jax.nn.softmax(top, axis=-1).astype(x.dtype)
        act = jax.nn.gelu(jnp.einsum('nd,nhkd->nhk', xb, u[eidx]))
        return jnp.einsum('nhk,nhkd->nd', gate * act, v[eidx])

    return lax.map(blk, xt).reshape(B, T, D)


def context_layer(x, c_ctx, w_ada, b_ada, norm1_g, norm2_g, w_in, ret_decay_f, ret_decay_b,
                  ret_gn_g, na_qn_g, na_kn_g, w_out, peer_wq, peer_keys, peer_u, peer_v):
    B, T, _ = x.shape
    sh1, sc1, g1, sh2, sc2, g2 = adaln(c_ctx, w_ada, b_ada)
    h = modulate(x, norm1_g, sh1, sc1)
    rq, rk, rv, rg, nq, nk, nv = project_in(h, w_in)
    zeros = jnp.zeros((B, RET_HEADS, RET_DK, RET_DV), jnp.float32)
    y_r, s_f, s_b = bidir_retention(heads(rq, RET_HEADS), heads(rk, RET_HEADS), heads(rv, RET_HEADS),
                                    ret_decay_f, ret_decay_b, zeros, zeros)
    o_r = retention_output(y_r, rg, ret_gn_g)
    qn = rms_norm(heads(nq, NA_HEADS), na_qn_g)
    kn = rms_norm(heads(nk, NA_HEADS), na_kn_g)
    vn = heads(nv, NA_HEADS)
    o_n = dense_context_attention(qn, kn, vn).reshape(B, T, NA_WIDTH)
    x = x + g1 * (jnp.concatenate([o_r, o_n], axis=-1) @ w_out)
    x = x + g2 * peer(modulate(x, norm2_g, sh2, sc2), peer_wq, peer_keys, peer_u, peer_v)
    state = jnp.stack([s_f, s_b], axis=1)
    return x, kn, vn, state


def latent_layer(x, c, k_ctx, v_ctx, state, w_ada, b_ada, norm1_g, norm2_g, w_in, ret_decay_f,
                 ret_decay_b, ret_gn_g, na_qn_g, na_kn_g, na_rpb, w_out, peer_wq, peer_keys, peer_u, peer_v):
    B, T, _ = x.shape
    sh1, sc1, g1, sh2, sc2, g2 = [m[:, None, :] for m in adaln(c, w_ada, b_ada)]
    h = modulate(x, norm1_g, sh1, sc1)
    rq, rk, rv, rg, nq, nk, nv = project_in(h, w_in)
    y_r, _, _ = bidir_retention(rope_2d(heads(rq, RET_HEADS)), rope_2d(heads(rk, RET_HEADS)),
                                heads(rv, RET_HEADS), ret_decay_f, ret_decay_b,
                                state[:, 0], state[:, 1])
    o_r = retention_output(y_r, rg, ret_gn_g)
    qn = rms_norm(heads(nq, NA_HEADS), na_qn_g)
    kn = rms_norm(heads(nk, NA_HEADS), na_kn_g)
    vn = heads(nv, NA_HEADS)
    o_n = neighbourhood_attention(qn, kn, vn, k_ctx, v_ctx, na_rpb).reshape(B, T, NA_WIDTH)
    x = x + g1 * (jnp.concatenate([o_r, o_n], axis=-1) @ w_out)
    x = x + g2 * peer(modulate(x, norm2_g, sh2, sc2), peer_wq, peer_keys, peer_u, peer_v)
    return x


def setup_inputs(seed: int = 0) -> dict:
    key = jax.random.key(seed)
    ks = jax.random.split(key, 24)
    f32 = jnp.float32
    nrm = lambda k, shape, s: jax.random.normal(k, shape, f32) * s
    base_logit = np.log(2.0 ** (5.0 + np.arange(RET_HEADS)) - 1.0).astype(np.float32)
    base_logit = jnp.asarray(base_logit)[None, :]
    return {
        "x_prompt": nrm(ks[0], (BATCH, SEQ, D_MODEL), 1.0),
        "x_sample": nrm(ks[1], (DEC_BATCH, DEC_SEQ, D_MODEL), 1.0),
        "cache_na_k": nrm(ks[2], (DEC_BATCH, DEPTH, PAST_LEN, NA_HEADS, HEAD_DIM), 1.0),
        "cache_na_v": nrm(ks[3], (DEC_BATCH, DEPTH, PAST_LEN, NA_HEADS, HEAD_DIM), 1.0),
        "state_ret": nrm(ks[4], (DEC_BATCH, DEPTH, 2, RET_HEADS, RET_DK, RET_DV), 0.5),
        "c": nrm(ks[5], (DEC_BATCH, D_MODEL), 1.0),
        "c_ctx": nrm(ks[6], (D_MODEL,), 1.0),
        "w_ada": nrm(ks[7], (DEPTH, D_MODEL, 6 * D_MODEL), 0.5 * D_MODEL ** -0.5),
        "b_ada": nrm(ks[8], (DEPTH, 6 * D_MODEL), 0.01),
        "norm1_g": 1.0 + nrm(ks[9], (DEPTH, D_MODEL), 0.02),
        "norm2_g": 1.0 + nrm(ks[10], (DEPTH, D_MODEL), 0.02),
        "w_in": nrm(ks[11], (DEPTH, D_MODEL, IN_WIDTH), D_MODEL ** -0.5),
        "ret_decay_f": base_logit + nrm(ks[12], (DEPTH, RET_HEADS), 0.1),
        "ret_decay_b": base_logit + nrm(ks[13], (DEPTH, RET_HEADS), 0.1),
        "ret_gn_g": 1.0 + nrm(ks[14], (DEPTH, RET_WIDTH), 0.02),
        "na_qn_g": 1.0 + nrm(ks[15], (DEPTH, HEAD_DIM), 0.02),
        "na_kn_g": 1.0 + nrm(ks[16], (DEPTH, HEAD_DIM), 0.02),
        "na_rpb": nrm(ks[17], (DEPTH, NA_HEADS, 2 * WIN_H - 1, 2 * WIN_W - 1), 0.1),
        "w_out": nrm(ks[18], (DEPTH, MIX_WIDTH, D_MODEL), MIX_WIDTH ** -0.5),
        "peer_wq": nrm(ks[19], (DEPTH, D_MODEL, PEER_HEADS * PEER_DKEY), D_MODEL ** -0.5),
        "peer_keys": nrm(ks[20], (DEPTH, PEER_HEADS, 2, N_KEYS, PEER_DKEY // 2), (PEER_DKEY // 2) ** -0.5),
        "peer_u": nrm(ks[21], (DEPTH, N_EXPERTS, D_MODEL), D_MODEL ** -0.5),
        "peer_v": nrm(ks[22], (DEPTH, N_EXPERTS, D_MODEL), (PEER_HEADS * PEER_TOPK) ** -0.5),
    }


def reference(x_prompt, x_sample, cache_na_k, cache_na_v, state_ret, c, c_ctx, w_ada, b_ada,
              norm1_g, norm2_g, w_in, ret_decay_f, ret_decay_b, ret_gn_g, na_qn_g, na_kn_g,
              na_rpb, w_out, peer_wq, peer_keys, peer_u, peer_v):
    y_p = x_prompt
    y_s = x_sample
    new_k, new_v, new_s = [], [], []
    for l in range(DEPTH):
        y_p, k_l, v_l, s_l = context_layer(
            y_p, c_ctx, w_ada[l], b_ada[l], norm1_g[l], norm2_g[l], w_in[l], ret_decay_f[l],
            ret_decay_b[l], ret_gn_g[l], na_qn_g[l], na_kn_g[l], w_out[l], peer_wq[l],
            peer_keys[l], peer_u[l], peer_v[l])
        new_k.append(k_l)
        new_v.append(v_l)
        new_s.append(s_l)
        y_s = latent_layer(
            y_s, c, cache_na_k[:, l], cache_na_v[:, l], state_ret[:, l], w_ada[l], b_ada[l],
            norm1_g[l], norm2_g[l], w_in[l], ret_decay_f[l], ret_decay_b[l], ret_gn_g[l],
            na_qn_g[l], na_kn_g[l], na_rpb[l], w_out[l], peer_wq[l], peer_keys[l], peer_u[l], peer_v[l])
    new_na_k = jnp.stack(new_k, axis=1)
    new_na_v = jnp.stack(new_v, axis=1)
    new_state_ret = jnp.stack(new_s, axis=1)
    return (y_p, y_s, new_na_k, new_na_v, new_state_ret)
```

```python
import math
import os
from contextlib import ExitStack

import numpy as np
import concourse.bass as bass
import concourse.mybir as mybir
from concourse.bass_utils import run_bass_kernel_spmd

F32 = mybir.dt.float32
BF16 = mybir.dt.bfloat16
I32 = mybir.dt.int32
U32 = mybir.dt.uint32
ALU = mybir.AluOpType
AF = mybir.ActivationFunctionType
AX = mybir.AxisListType

NCORES = 8
D = 1024
EPS = 1e-6
NEG = -1e30


class Buf:
    __slots__ = ("name", "w", "r")

    def __init__(self, name):
        self.name = name
        self.w = None
        self.r = []


class Eng:
    def __init__(self, name, sem, same_sync):
        self.name = name
        self.sem = sem
        self.count = 0
        self.waited = {}
        self.same_sync = same_sync
        self.prog = []


class Sched:
    def __init__(self, nc, stack, n_dma_slots=int(os.environ.get("NDMA", "24"))):
        self.nc = nc
        self.sems = {}
        self.engs = {}
        for name, same in (("pe", False), ("act", True), ("dve", True), ("pool", True), ("sp", True)):
            sem = stack.enter_context(nc.semaphore("s_" + name))
            self.sems[name] = sem
            self.engs[name] = Eng(name, sem, same)
        self.dma_slots = []
        for i in range(n_dma_slots):
            key = "dma%d" % i
            self.sems[key] = stack.enter_context(nc.semaphore("s_" + key))
            self.dma_slots.append([key, 0])
        self.dma_i = 0
        self.bg_slots = []
        for i in range(16):
            key = "bg%d" % i
            self.sems[key] = stack.enter_context(nc.semaphore("s_" + key))
            self.bg_slots.append([key, 0])
        self.bg_i = 0
        self.n_inst = 0

    def _wait(self, e, tok):
        if tok is None:
            return
        key, val = tok
        if key == e.name and not e.same_sync:
            return
        if e.waited.get(key, 0) >= val:
            return
        e.waited[key] = val
        sem = self.sems[key]
        e.prog.append(lambda q, sem=sem, val=val: q.wait_ge(sem, val))

    def _deps(self, e, reads, writes):
        for b in reads:
            self._wait(e, b.w)
            if b.name.startswith("ps"):
                for t in b.r:
                    if t[0] != e.name:
                        self._wait(e, t)
        for b in writes:
            self._wait(e, b.w)
            for t in b.r:
                self._wait(e, t)

    @staticmethod
    def _mark(tok, reads, writes):
        for b in reads:
            b.r.append(tok)
            if len(b.r) > 64:
                b.r = b.r[-64:] if False else b.r
        for b in writes:
            b.w = tok
            b.r = []

    def op(self, eng, fn, reads=(), writes=()):
        e = self.engs[eng]
        self._deps(e, reads, writes)
        e.count += 1
        tok = (e.name, e.count)
        sem = e.sem
        e.prog.append(lambda q, fn=fn, sem=sem: fn(q).then_inc(sem, 1))
        self._mark(tok, reads, writes)
        self.n_inst += 1
        return tok

    def dma(self, eng, out, in_, reads=(), writes=(), bg=False, **kw):
        e = self.engs[eng]
        self._deps(e, reads, writes)
        if bg:
            slot = self.bg_slots[self.bg_i % len(self.bg_slots)]
            self.bg_i += 1
        else:
            slot = self.dma_slots[self.dma_i % len(self.dma_slots)]
            self.dma_i += 1
        key = slot[0]
        if slot[1] > 0:
            self._wait(e, (key, slot[1]))
        slot[1] += 16
        tok = (key, slot[1])
        sem = self.sems[key]
        e.prog.append(lambda q, out=out, in_=in_, sem=sem, kw=kw:
                      q.dma_start(out=out, in_=in_, **kw).then_inc(sem, 16))
        self._mark(tok, reads, writes)
        self.n_inst += 1
        return tok

    def barrier(self):
        for e in self.engs.values():
            for key, val in self.dma_slots + self.bg_slots:
                if val > 0:
                    self._wait(e, (key, val))
            for o in self.engs.values():
                if o is not e and o.count > 0:
                    self._wait(e, (o.name, o.count))

    def emit(self):
        nc = self.nc
        progs = {k: v.prog for k, v in self.engs.items()}
        with nc.Block() as block:
            @block.tensor
            def _(q):
                for f in progs["pe"]:
                    f(q)

            @block.scalar
            def _(q):
                for f in progs["act"]:
                    f(q)

            @block.vector
            def _(q):
                for f in progs["dve"]:
                    f(q)

            @block.gpsimd
            def _(q):
                for f in progs["pool"]:
                    f(q)

            @block.sync
            def _(q):
                for f in progs["sp"]:
                    f(q)


def _rope_tables():
    T = 1024
    n = np.arange(T)
    rows = (n // 64).astype(np.float32)
    cols = (n % 64).astype(np.float32)
    freqs = (np.float32(10000.0) ** (-np.arange(16, dtype=np.float32) / np.float32(16))).astype(np.float32)
    C = np.zeros((128, T), np.float32)
    Sg = np.zeros((128, T), np.float32)
    for p in range(128):
        d = p % 64
        pos = rows if d < 32 else cols
        dd = d % 32
        f = dd % 16
        ang = (pos * freqs[f]).astype(np.float32)
        C[p] = np.cos(ang)
        Sg[p] = -np.sin(ang) if dd < 16 else np.sin(ang)
    return C, Sg


def _rope_partner_perm():
    perm = np.zeros(512, np.int64)
    for h in range(8):
        for d in range(64):
            dd = d % 32
            partner = d + 16 if dd < 16 else d - 16
            perm[h * 64 + d] = h * 64 + partner
    return perm


def _na_windows():
    rows, kh = 16, 8
    q_of_k = {}
    for kr in range(rows):
        qs = [qr for qr in range(rows) if min(max(qr - kh // 2, 0), rows - kh) <= kr < min(max(qr - kh // 2, 0), rows - kh) + kh]
        assert qs == list(range(qs[0], qs[-1] + 1))
        q_of_k[kr] = (qs[0], qs[-1])
    return q_of_k


def _host_constants():
    c = {}
    c["ident"] = np.eye(128, dtype=np.float32)
    ob = np.zeros((128, 128), np.float32)
    ob[:64, :64] = 1.0 / 64
    ob[64:, 64:] = 1.0 / 64
    c["onesbd"] = ob
    op = np.zeros((128, 2, 128), np.float32)
    op[:, 0, :64] = 1.0
    op[:, 1, 64:] = 1.0
    c["onespad"] = op
    p = np.arange(128, dtype=np.float32)[:, None]
    j = np.arange(1920, dtype=np.float32)[None, :]
    c["expo"] = (j - 896.0 - p).astype(np.float32)
    c["iota_n1"] = np.broadcast_to(np.arange(1, 1025, dtype=np.float32)[None], (128, 1024)).copy()
    c["iota_rev"] = np.broadcast_to((1024 - np.arange(1024, dtype=np.float32))[None], (128, 1024)).copy()
    tq = np.zeros((128, 2, 2), np.float32)
    for t in range(2):
        tq[:, 0, t] = 255 - (t * 128 + np.arange(128))
        tq[:, 1, t] = t * 128 + np.arange(128)
    c["tq"] = tq
    C, Sg = _rope_tables()
    c["rope_c"] = C
    c["rope_s"] = Sg
    qc = np.arange(64)
    cstart = np.clip(qc - 8, 0, 48)
    kc = np.arange(64)
    inwin = (kc[:, None] >= cstart[None, :]) & (kc[:, None] < cstart[None, :] + 16)
    cm = np.where(inwin, 0.0, NEG).astype(np.float32)
    c["cmask"] = np.concatenate([cm, cm], axis=0)
    c["iota_row"] = np.broadcast_to(np.arange(128, dtype=np.float32)[None], (128, 128)).copy()
    c["iota16"] = np.broadcast_to(np.arange(16, dtype=np.float32)[None], (128, 16)).copy()
    return c


CONST_SHAPES = {
    "ident": [128, 128], "onesbd": [128, 128], "onespad": [128, 2, 128], "expo": [128, 1920],
    "iota_n1": [128, 1024], "iota_rev": [128, 1024], "tq": [128, 2, 2], "rope_c": [128, 1024],
    "rope_s": [128, 1024], "cmask": [128, 64], "iota_row": [128, 128], "iota16": [128, 16],
}

IN_SHAPES = {
    "x": [1536, 1024], "cT": [128, 8, 2], "w_ada": [1024, 6144], "b_ada": [1, 6144],
    "n1g": [1, 1024], "n2g": [1, 1024], "w_in": [1024, 3584], "w_sw": [1024, 1024], "dec": [1, 16],
    "gn_g": [128, 4], "qkn_g": [128, 2], "kn_row": [1, 64], "rpbT": [128, 8, 15, 64],
    "w_out": [1024, 1024], "wq": [1024, 2048], "keysT": [128, 16, 128],
    "Ut": [128, 128, 8, 128], "Vp": [128, 128, 1024],
    "kctxT": [512, 256], "vctx": [256, 512], "s0": [2, 8, 64, 64],
}

SEQS = [(0, 8, 1, True, [(0, 8)]), (8, 4, 0, False, [(0, 2), (2, 2)])]


class Builder:
    def __init__(self, stage=3, debug=False):
        self.stage = stage
        self.debug = debug
        self.nc = bass.Bass("TRN2", target_bir_lowering=False)
        nc = self.nc
        self.din = {}
        for name, shp in list(IN_SHAPES.items()) + list(CONST_SHAPES.items()):
            self.din[name] = nc.dram_tensor(name, shp, F32, kind="ExternalInput").ap()
        self.y = nc.dram_tensor("y", [1536, 1024], F32, kind="ExternalOutput").ap()
        self.nk = nc.dram_tensor("nk", [512, 512], F32, kind="ExternalOutput").ap()
        self.nv = nc.dram_tensor("nv", [512, 512], F32, kind="ExternalOutput").ap()
        self.st = nc.dram_tensor("st", [2, 2, 8, 64, 64], F32, kind="ExternalOutput").ap()
        self.bufs = {}

    def bg_issue(self, n, dep=None):
        for _ in range(n):
            if not self.bg_todo:
                return
            out, in_, name = self.bg_todo.pop(0)
            self.dma(out, in_, reads=[self.B(dep)] if dep else [], writes=[self.B(name)], eng="pool", bg=True)

    def B(self, name):
        b = self.bufs.get(name)
        if b is None:
            b = self.bufs[name] = Buf(name)
        return b

    def sb(self, st, name, shape, dt=F32):
        return st.enter_context(self.nc.sbuf_tensor("sb_" + name, shape, dt))

    def mm(self, out, lhsT, rhs, start, stop, reads, writes, skip=False):
        if skip:
            self.S.op("pe", lambda q: q.matmul(out, lhsT=lhsT, rhs=rhs, start=start, stop=stop, skip_group_check=True), reads, writes)
        else:
            self.S.op("pe", lambda q: q.matmul(out, lhsT=lhsT, rhs=rhs, start=start, stop=stop), reads, writes)

    def tr(self, out, in_, ident, reads, writes):
        self.S.op("pe", lambda q: q.transpose(out=out, in_=in_, identity=ident), reads, writes)

    def act(self, out, in_, func, reads, writes, **kw):
        self.S.op("act", lambda q: q.activation(out=out, in_=in_, func=func, **kw), reads, writes)

    def tt(self, eng, out, in0, in1, op, reads, writes):
        self.S.op(eng, lambda q: q.tensor_tensor(out=out, in0=in0, in1=in1, op=op), reads, writes)

    def ts(self, eng, out, in0, s1, s2, op0, op1, reads, writes):
        if s2 is None:
            self.S.op(eng, lambda q: q.tensor_scalar(out=out, in0=in0, scalar1=s1, scalar2=None, op0=op0), reads, writes)
        else:
            self.S.op(eng, lambda q: q.tensor_scalar(out=out, in0=in0, scalar1=s1, scalar2=s2, op0=op0, op1=op1), reads, writes)

    def stt(self, eng, out, in0, scalar, in1, op0, op1, reads, writes):
        self.S.op(eng, lambda q: q.scalar_tensor_tensor(out=out, in0=in0, scalar=scalar, in1=in1, op0=op0, op1=op1), reads, writes)

    def cp(self, eng, out, in_, reads, writes):
        if eng == "act":
            self.S.op("act", lambda q: q.copy(out=out, in_=in_), reads, writes)
        else:
            self.S.op(eng, lambda q: q.tensor_copy(out=out, in_=in_), reads, writes)

    def memset(self, eng, ap, val, writes):
        self.S.op(eng, lambda q: q.memset(ap, val), (), writes)

    def recip(self, out, in_, reads, writes):
        self.S.op("dve", lambda q: q.reciprocal(out=out, in_=in_), reads, writes)

    def dma(self, out, in_, reads=(), writes=(), eng="sp", bg=False):
        self.S.dma(eng, out, in_, reads, writes, bg=bg)

    def build(self):
        nc = self.nc
        with ExitStack() as top:
            self.S = Sched(nc, top)
            self.psA = [top.enter_context(nc.psum_tensor("psA%d" % i, [128, 512], F32)) for i in range(2)]
            self.psB = top.enter_context(nc.psum_tensor("psB", [128, 1024], F32))
            self.psC = top.enter_context(nc.psum_tensor("psC", [128, 1024], F32))
            self.psD = top.enter_context(nc.psum_tensor("psD", [128, 1024], F32))
            self.scrU = nc.dram_tensor("scrU", [128, 128, 1024], BF16).ap()
            self.scrV = nc.dram_tensor("scrV", [128, 128, 1024], BF16).ap()
            self.scrQ = nc.dram_tensor("scrQ", [16, 128, 1024], BF16).ap()
            self.bg_todo = []
            if self.stage >= 3:
                Ut2 = self.din["Ut"].rearrange("j p c i -> j p (c i)")
                wq4 = self.din["wq"].rearrange("(kc p) (c n) -> c p kc n", p=128, n=128)
                for c in range(16):
                    self.bg_todo.append((self.scrQ[c].rearrange("p (kc n) -> p kc n", n=128), wq4[c], "scrQ%d" % c))
                for j in range(128):
                    self.bg_todo.append((self.scrU[j], Ut2[j], "scrU%d" % j))
                    self.bg_todo.append((self.scrV[j], self.din["Vp"][j], "scrV%d" % j))
            self.mod = self.sb(top, "mod", [128, 2, 6144], BF16)
            self.ident_b = self.sb(top, "ident_b", [128, 128], BF16)
            self.small = self.sb(top, "small", [128, 64], F32)
            with ExitStack() as ph1:
                self.phase1_alloc(ph1)
                self.setup(ph1)
                with ExitStack() as ws:
                    self.phase1_work(ws)
                    if self.stage >= 1:
                        for si, seq in enumerate(SEQS):
                            self.attn_seq(si, *seq)
                    self.S.barrier()
            if self.stage >= 3:
                with ExitStack() as ph2:
                    self.peer(ph2)
                    self.S.barrier()
            self.S.barrier()
            self.S.emit()
        return nc

    def phase1_alloc(self, st):
        sb = lambda n, s, d=F32: self.sb(st, n, s, d)
        self.ones_bd = sb("ones_bd", [128, 128], BF16)
        self.ones_pad = sb("ones_pad", [128, 2, 128], BF16)
        self.strips = sb("strips", [128, 8, 1920], BF16)
        self.lg = sb("lg", [128, 16])
        self.nlgb = sb("nlgb", [128, 8])
        self.lgcol = sb("lgcol", [128, 2, 4])
        self.wst = sb("wst", [128, 2, 2, 8])
        self.gn_g = sb("gn_g", [128, 4])
        self.qkn_g = sb("qkn_g", [128, 2])
        self.kn_bc = sb("kn_bc", [128, 64])
        self.wo = sb("wo", [128, 8, 1024], BF16)
        self.iota_n1 = sb("iota_n1", [128, 1024])
        self.iota_rev = sb("iota_rev", [128, 1024])
        self.rope_c = sb("rope_c", [128, 1024])
        self.rope_s = sb("rope_s", [128, 1024])
        self.trb = sb("trb", [128, 8, 15, 64], BF16)

    def phase1_work(self, st):
        sb = lambda n, s, d=F32: self.sb(st, n, s, d)
        B = self.B
        self.hT = sb("hT", [128, 8, 1024], BF16)
        self.oT = sb("oT", [128, 8, 1024], BF16)
        self.xb = [sb("xb%d" % i, [128, 1024]) for i in range(2)]
        self.tmpf = [sb("tmpf%d" % i, [128, 1024]) for i in range(2)]
        self.hb = sb("hb", [128, 1024], BF16)
        self.wstage = [sb("wstage%d" % i, [128, 8, 128]) for i in range(3)]
        self.wbf = [sb("wbf%d" % i, [128, 8, 128], BF16) for i in range(3)]
        self.qT = sb("qT", [128, 1024], BF16)
        self.kTm = sb("kTm", [128, 2, 1024], BF16)
        self.sg = sb("sg", [128, 1024], BF16)
        self.vpad = sb("vpad", [128, 8, 2, 128], BF16)
        self.kcm = sb("kcm", [128, 2, 256], BF16)
        self.vcp = sb("vcp", [128, 2, 2, 128], BF16)
        self.qf = sb("qf", [128, 2, 1024], BF16)
        self.s0bd = sb("s0bd", [128, 2, 128], BF16)
        self.s0st = sb("s0st", [128, 2, 128])
        self.sc = [sb("sc%d" % i, [128, 768], BF16) for i in range(3)]
        self.sbias = [sb("sbias%d" % i, [128, 768]) for i in range(2)]
        self.ctxst = sb("ctxst", [128, 256])
        self.kw = sb("kw", [128, 2, 128], BF16)
        self.nko = sb("nko", [128, 128])
        self.nvo = sb("nvo", [128, 128])
        self.sto = sb("sto", [64, 2, 2, 2, 64])
        self.memset("pool", self.kTm[:], 0.0, [B("kTm")])
        self.memset("pool", self.vpad[:], 0.0, [B("vpad")])
        self.memset("pool", self.kcm[:], 0.0, [B("kcm")])
        self.memset("pool", self.vcp[:], 0.0, [B("vcp")])
        self.memset("pool", self.s0st[:], 0.0, [B("s0st")])

    def setup(self, st_outer):
        B, din = self.B, self.din
        with ExitStack() as st:
            sb = lambda n, s, d=F32: self.sb(st, n, s, d)
            ada = self.adaln_gen(st)
            next(ada)
            self.bg_issue(32)
            identf = sb("identf", [128, 128])
            self.dma(identf[:], din["ident"], writes=[B("identf")])
            self.cp("dve", self.ident_b[:], identf[:], [B("identf")], [B("ident_b")])
            tmp128 = sb("tmp128", [128, 2, 128])
            self.dma(tmp128[:, 0, :], din["onesbd"], writes=[B("tmp128")])
            self.cp("dve", self.ones_bd[:], tmp128[:, 0, :], [B("tmp128")], [B("ones_bd")])
            self.dma(tmp128[:], din["onespad"], reads=[], writes=[B("tmp128")])
            self.cp("dve", self.ones_pad[:], tmp128[:], [B("tmp128")], [B("ones_pad")])
            for nm, t in (("iota_n1", self.iota_n1), ("iota_rev", self.iota_rev), ("rope_c", self.rope_c),
                          ("rope_s", self.rope_s), ("gn_g", self.gn_g), ("qkn_g", self.qkn_g)):
                self.dma(t[:], din[nm], writes=[B(nm)])
            self.dma(self.kn_bc[:], din["kn_row"][0].partition_broadcast(128), writes=[B("kn_bc")])
            dec = sb("dec", [128, 16])
            self.dma(dec[:], din["dec"][0].partition_broadcast(128), writes=[B("dec")])
            self.act(dec[:], dec[:], AF.Exp, [B("dec")], [B("dec")], scale=-1.0)
            self.act(dec[:], dec[:], AF.Ln, [B("dec")], [B("dec")], bias=1.0)
            self.ts("dve", self.lg[:], dec[:], -1.0, None, ALU.mult, None, [B("dec")], [B("lg")])
            self.ts("dve", self.nlgb[:], self.lg[:, 8:16], -1.0, None, ALU.mult, None, [B("lg")], [B("nlgb")])
            for r in range(2):
                self.cp("dve", self.lgcol[0:64, r, :], self.lg[0:64, r * 8:r * 8 + 8:2], [B("lg")], [B("lgcol")])
                self.cp("dve", self.lgcol[64:128, r, :], self.lg[64:128, r * 8 + 1:r * 8 + 8:2], [B("lg")], [B("lgcol")])
            tq = sb("tq", [128, 2, 2])
            self.dma(tq[:], din["tq"], writes=[B("tq")])
            for r in range(2):
                self.tt("dve", self.wst[:, r], tq[:, r, :].unsqueeze(2).to_broadcast([128, 2, 8]),
                        self.lg[:, r * 8:(r + 1) * 8].unsqueeze(1).to_broadcast([128, 2, 8]), ALU.mult,
                        [B("tq"), B("lg")], [B("wst")])
            self.act(self.wst[:], self.wst[:], AF.Exp, [B("wst")], [B("wst")])
            self.ts("dve", self.wst[:], self.wst[:], 0.125, None, ALU.mult, None, [B("wst")], [B("wst")])
            expo = sb("expo", [128, 1920])
            t1 = sb("t1", [128, 1920])
            t2 = sb("t2", [128, 1920])
            self.dma(expo[:], din["expo"], writes=[B("expo")])
            for h in range(8):
                self.ts("dve", t1[:], expo[:], self.lg[:, h:h + 1], None, ALU.mult, None, [B("expo"), B("lg")], [B("t1")])
                self.stt("dve", t2[:], expo[:], self.nlgb[:, h:h + 1], t1[:], ALU.mult, ALU.min,
                         [B("expo"), B("nlgb"), B("t1")], [B("t2")])
                self.act(t2[:], t2[:], AF.Exp, [B("t2")], [B("t2")])
                self.ts("dve", self.strips[:, h, :], t2[:], 0.125, None, ALU.mult, None, [B("t2")], [B("strips")])
                next(ada, None)
            cmask = sb("cmask", [128, 64])
            self.dma(cmask[:], din["cmask"], writes=[B("cmask")])
            for h in range(8):
                trs = t1[:, 0:960].rearrange("p (x q) -> p x q", q=64)
                self.dma(trs, din["rpbT"][:, h], writes=[B("t1")])
                self.tt("dve", self.trb[:, h], trs, cmask[:].unsqueeze(1).to_broadcast([128, 15, 64]), ALU.add,
                        [B("t1"), B("cmask")], [B("trb")])
            for kc in range(8):
                wsl = t2[:, 0:1024]
                self.dma(wsl, din["w_out"][kc * 128:(kc + 1) * 128, :], writes=[B("t2")])
                self.cp("act", self.wo[:, kc, :], wsl, [B("t2")], [B("wo")])
            for _ in ada:
                pass
            self.bg_issue(32)
        self.S.barrier()

    def adaln_gen(self, st):
        B, din = self.B, self.din
        if True:
            sb = lambda n, s, d=F32: self.sb(st, n, s, d)
            cT = sb("cT", [128, 8, 2])
            rep = sb("rep", [128, 8, 2, 128], BF16)
            wst = [sb("awst%d" % i, [128, 8, 512]) for i in range(2)]
            wbf = [sb("awbf%d" % i, [128, 8, 512], BF16) for i in range(2)]
            bbc = [sb("bbc%d" % i, [128, 512]) for i in range(2)]
            ngb = sb("ngb", [128, 2, 1024])
            self.dma(cT[:], din["cT"], writes=[B("cT")])
            self.act(cT[:], cT[:], AF.Silu, [B("cT")], [B("cT")])
            self.cp("dve", rep[:], cT[:].unsqueeze(3).to_broadcast([128, 8, 2, 128]), [B("cT")], [B("rep")])
            self.dma(ngb[:, 0, :], din["n1g"][0].partition_broadcast(128), writes=[B("ngb")])
            self.dma(ngb[:, 1, :], din["n2g"][0].partition_broadcast(128), writes=[B("ngb")])
            w3 = din["w_ada"].rearrange("(kc p) n -> p kc n", p=128)
            yield
            for blk in range(12):
                i = blk % 2
                cs = slice(blk * 512, (blk + 1) * 512)
                self.dma(wst[i][:], w3[:, :, cs], writes=[B("awst%d" % i)])
                self.dma(bbc[i][:], din["b_ada"][0, cs].partition_broadcast(128), writes=[B("bbc%d" % i)])
                self.cp("pool" if blk % 2 == 0 else "act", wbf[i][:], wst[i][:], [B("awst%d" % i)], [B("awbf%d" % i)])
                for v in range(2):
                    ps = self.psA[v]
                    for kc in range(8):
                        self.mm(ps[:], rep[:, kc, v, :], wbf[i][:, kc, :], kc == 0, kc == 7,
                                [B("rep"), B("awbf%d" % i)], [B("psA%d" % v)])
                    self.tt("dve", self.mod[:, v, cs], ps[:], bbc[i][:], ALU.add,
                            [B("psA%d" % v), B("bbc%d" % i)], [B("mod")])
                yield
            for v in range(2):
                for j, ch in ((0, 1), (1, 4)):
                    sl = self.mod[:, v, ch * 1024:(ch + 1) * 1024]
                    self.stt("dve", sl, sl, 1.0, ngb[:, j, :], ALU.add, ALU.mult, [B("mod"), B("ngb")], [B("mod")])

    def modsl(self, v, ch):
        return self.mod[:, v, ch * 1024:(ch + 1) * 1024]

    def load_w(self, dram_cols, k):
        B = self.B
        i = self.wcount % 3
        j = self.wcount % 3
        self.wcount += 1
        self.bg_issue(3, "wbf%d" % ((j + 2) % 3))
        self.dma(self.wstage[i][:], dram_cols.rearrange("(kc p) n -> p kc n", p=128), writes=[B("wstage%d" % i)])
        self.cp("act", self.wbf[j][:], self.wstage[i][:], [B("wstage%d" % i)], [B("wbf%d" % j)])
        return self.wbf[j], B("wbf%d" % j)

    def proj_fm(self, w, wB, T, ps, psB_, b0, bn):
        for kc in range(8):
            self.mm(ps[:, 0:bn], w[:, kc, :], self.hT[:, kc, b0:b0 + bn], kc == 0, kc == 7,
                    [wB, self.B("hT")], [psB_])

    def proj_tm(self, w, wB, t, ps, psB_):
        for kc in range(8):
            self.mm(ps[:, 0:128], self.hT[:, kc, t * 128:(t + 1) * 128], w[:, kc, :], kc == 0, kc == 7,
                    [wB, self.B("hT")], [psB_])

    def attn_seq(self, si, tile0, NT, v, is_sample, segs):
        B, din = self.B, self.din
        T = NT * 128
        blocks = [(b0, min(512, T - b0)) for b0 in range(0, T, 512)]
        x_rows = lambda t: slice((tile0 + t) * 128, (tile0 + t + 1) * 128)
        w_in = din["w_in"]
        self.wcount = getattr(self, "wcount", 0)
        psT = self.psA[0][:].bitcast(BF16).rearrange("p (c n) -> p c n", n=128)
        for t in range(NT):
            xb = self.xb[t % 2]
            xB = B("xb%d" % (t % 2))
            self.dma(xb[:], din["x"][x_rows(t), :], writes=[xB])
            tf = self.tmpf[0]
            self.memset("dve", self.small[:, 0:1], 0.0, [B("ss")])
            self.act(tf[:], xb[:], AF.Square, [xB], [B("tmpf0"), B("ss")], accum_out=self.small[:, 0:1])
            self.act(self.small[:, 1:2], self.small[:, 0:1], AF.Sqrt, [B("ss")], [B("sd")], scale=1.0 / D, bias=EPS)
            self.recip(self.small[:, 2:3], self.small[:, 1:2], [B("sd")], [B("rstd")])
            self.stt("dve", tf[:], xb[:], self.small[:, 2:3], self.modsl(v, 1), ALU.mult, ALU.mult,
                     [xB, B("rstd"), B("mod")], [B("tmpf0")])
            self.tt("dve", self.hb[:], tf[:], self.modsl(v, 0), ALU.add, [B("tmpf0"), B("mod")], [B("hb")])
            for kc in range(8):
                self.tr(psT[:, kc, :], self.hb[:, kc * 128:(kc + 1) * 128], self.ident_b[:],
                        [B("hb"), B("ident_b")], [B("psA0")])
            self.cp("act", self.hT[:, :, t * 128:(t + 1) * 128], psT, [B("psA0")], [B("hT")])

        ablocks = []
        for (s0, sn) in segs:
            for q0 in range(s0 * 128, (s0 + sn) * 128, 512):
                ablocks.append((q0, min(512, (s0 + sn) * 128 - q0), list(range(s0, s0 + sn))))
        self.ablocks = ablocks
        for a in range(4):
            self.bg_issue(0)
            col = lambda base: w_in[:, base + a * 128: base + (a + 1) * 128]
            wq_, wqB = self.load_w(col(0), 0)
            if is_sample:
                wqs, wqsB = self.load_w(din["w_sw"][:, a * 128:(a + 1) * 128], 0)
            for bi, (b0, bn) in enumerate(blocks):
                self.proj_fm(wq_, wqB, T, self.psA[0], B("psA0"), b0, bn)
                if is_sample:
                    self.proj_fm(wqs, wqsB, T, self.psA[1], B("psA1"), b0, bn)
                    tf = self.tmpf[0]
                    self.tt("dve", tf[:, 0:bn], self.psA[0][:, 0:bn], self.rope_c[:, b0:b0 + bn], ALU.mult,
                            [B("psA0"), B("rope_c")], [B("tmpf0")])
                    tg = self.tmpf[1]
                    self.tt("dve", tg[:, 0:bn], self.psA[1][:, 0:bn], self.rope_s[:, b0:b0 + bn], ALU.mult,
                            [B("psA1"), B("rope_s")], [B("tmpf1")])
                    self.tt("dve", self.qT[:, b0:b0 + bn], tf[:, 0:bn], tg[:, 0:bn], ALU.add,
                            [B("tmpf0"), B("tmpf1")], [B("qT")])
                else:
                    self.cp("act", self.qT[:, b0:b0 + bn], self.psA[0][:, 0:bn], [B("psA0")], [B("qT")])
            wk_, wkB = self.load_w(col(512), 0)
            if is_sample:
                wks, wksB = self.load_w(din["w_sw"][:, 512 + a * 128:512 + (a + 1) * 128], 0)
            for bi, (b0, bn) in enumerate(blocks):
                self.proj_fm(wk_, wkB, T, self.psA[0], B("psA0"), b0, bn)
                if is_sample:
                    self.proj_fm(wks, wksB, T, self.psA[1], B("psA1"), b0, bn)
                    tf = self.tmpf[0]
                    self.tt("dve", tf[:, 0:bn], self.psA[0][:, 0:bn], self.rope_c[:, b0:b0 + bn], ALU.mult,
                            [B("psA0"), B("rope_c")], [B("tmpf0")])
                    tg = self.tmpf[1]
                    self.tt("dve", tg[:, 0:bn], self.psA[1][:, 0:bn], self.rope_s[:, b0:b0 + bn], ALU.mult,
                            [B("psA1"), B("rope_s")], [B("tmpf1")])
                    for hh in range(2):
                        ps_ = slice(hh * 64, (hh + 1) * 64)
                        self.tt("dve", self.kTm[ps_, hh, b0:b0 + bn], tf[ps_, 0:bn], tg[ps_, 0:bn], ALU.add,
                                [B("tmpf0"), B("tmpf1")], [B("kTm")])
                else:
                    for hh in range(2):
                        ps_ = slice(hh * 64, (hh + 1) * 64)
                        self.cp("act", self.kTm[ps_, hh, b0:b0 + bn], self.psA[0][ps_, 0:bn], [B("psA0")], [B("kTm")])
            wg_, wgB = self.load_w(col(1536), 0)
            for bi, (b0, bn) in enumerate(blocks):
                ps, pB = self.psA[bi % 2], B("psA%d" % (bi % 2))
                self.proj_fm(wg_, wgB, T, ps, pB, b0, bn)
                self.act(self.sg[:, b0:b0 + bn], ps[:, 0:bn], AF.Silu, [pB], [B("sg")])
            wv_, wvB = self.load_w(col(1024), 0)
            for t in range(NT):
                ps, pB = self.psA[t % 2], B("psA%d" % (t % 2))
                self.proj_tm(wv_, wvB, t, ps, pB)
                for hh in range(2):
                    cs = slice(hh * 64, (hh + 1) * 64)
                    self.cp("act", self.vpad[:, t, hh, cs], ps[:, cs], [pB], [B("vpad")])
            if not is_sample:
                for si_, (s0, sn) in enumerate(segs):
                    pS = self.psD[0:64, si_ * 512:si_ * 512 + 256].rearrange("p (r h e) -> p r h e", r=2, h=2)
                    for tr in range(sn):
                        t = s0 + tr
                        ps, pB = self.psA[t % 2], B("psA%d" % (t % 2))
                        self.proj_tm(wk_, wkB, t, ps, pB)
                        for r in range(2):
                            self.tt("dve", self.kw[:, r, :].rearrange("p (h d) -> p h d", d=64),
                                    ps[:, 0:128].rearrange("p (h d) -> p h d", d=64),
                                    self.wst[:, r, tr, 2 * a:2 * a + 2].unsqueeze(2).to_broadcast([128, 2, 64]), ALU.mult,
                                    [pB, B("wst")], [B("kw")])
                        for r in range(2):
                            for hh in range(2):
                                cs = slice(hh * 64, (hh + 1) * 64)
                                self.mm(pS[:, r, hh, :], self.kw[:, r, cs], self.vpad[:, t, hh, cs],
                                        (tr == 0 and r == 0 and hh == 0), tr == sn - 1, [B("kw"), B("vpad")], [B("psD")], skip=True)
                    self.cp("dve", self.sto[:, si_], pS, [B("psD")], [B("sto")])
                    for r in range(2):
                        self.dma(self.st[si_][r, 2 * a:2 * a + 2].rearrange("h d e -> d h e"), self.sto[:, si_, r], reads=[B("sto")])
            if is_sample:
                for r in range(2):
                    self.dma(self.s0st[0:64, r, 0:64], din["s0"][r, 2 * a], writes=[B("s0st")])
                    self.dma(self.s0st[64:128, r, 64:128], din["s0"][r, 2 * a + 1], writes=[B("s0st")])
                self.cp("dve", self.s0bd[:], self.s0st[:], [B("s0st")], [B("s0bd")])
                for r in range(2):
                    tf = self.tmpf[r]
                    self.act(tf[:], (self.iota_n1 if r == 0 else self.iota_rev)[:], AF.Exp,
                             [B("iota_n1"), B("iota_rev"), B("lgcol")], [B("tmpf%d" % r)], scale=self.lgcol[:, r, a:a + 1])
                    self.tt("dve", self.qf[:, r, :], self.qT[:], tf[:], ALU.mult, [B("qT"), B("tmpf%d" % r)], [B("qf")])
            for bi, (b0, bn, ktiles) in enumerate(ablocks):
                first = True
                if is_sample:
                    for r in range(2):
                        self.mm(self.psC[:, b0:b0 + bn], self.s0bd[:, r, :], self.qf[:, r, b0:b0 + bn], first, False,
                                [B("s0bd"), B("qf")], [B("psC")])
                        first = False
                items = [(hh, mc) for hh in range(2) for mc in ktiles]

                def r_score(k, b0=b0, bn=bn):
                    hh, mc = items[k]
                    h = 2 * a + hh
                    half = k % 2
                    pb = self.psB[:, half * 512: half * 512 + bn]
                    pbB = B("psB%d" % half)
                    self.mm(pb, self.kTm[:, hh, mc * 128:(mc + 1) * 128], self.qT[:, b0:b0 + bn], True, True,
                            [B("kTm"), B("qT")], [pbB])
                    off = b0 - mc * 128 + 896
                    self.tt("dve", self.sc[k % 3][:, 0:bn], pb, self.strips[:, h, off:off + bn], ALU.mult,
                            [pbB, B("strips")], [B("sc%d" % (k % 3))])

                def r_pv(k, first, b0=b0, bn=bn):
                    hh, mc = items[k]
                    self.mm(self.psC[:, b0:b0 + bn], self.vpad[:, mc, hh, :], self.sc[k % 3][:, 0:bn], first, k == len(items) - 1,
                            [B("vpad"), B("sc%d" % (k % 3))], [B("psC")])

                r_score(0)
                for k in range(len(items)):
                    if k + 1 < len(items):
                        r_score(k + 1)
                    r_pv(k, first)
                    first = False
                sq = self.sc[0]
                self.act(sq[:, 0:bn], self.psC[:, b0:b0 + bn], AF.Square, [B("psC")], [B("sc0")])
                msp = self.psA[0]
                self.mm(msp[:, 0:bn], self.ones_bd[:], sq[:, 0:bn], True, True, [B("ones_bd"), B("sc0")], [B("psA0")])
                tf = self.tmpf[0]
                self.act(tf[:, 0:bn], msp[:, 0:bn], AF.Sqrt, [B("psA0")], [B("tmpf0")], bias=EPS)
                self.recip(tf[:, 0:bn], tf[:, 0:bn], [B("tmpf0")], [B("tmpf0")])
                tg = self.tmpf[1]
                self.tt("dve", tg[:, 0:bn], self.psC[:, b0:b0 + bn], tf[:, 0:bn], ALU.mult, [B("psC"), B("tmpf0")], [B("tmpf1")])
                self.stt("dve", self.oT[:, a, b0:b0 + bn], tg[:, 0:bn], self.gn_g[:, a:a + 1], self.sg[:, b0:b0 + bn],
                         ALU.mult, ALU.mult, [B("tmpf1"), B("gn_g"), B("sg")], [B("oT")])

        opts = os.environ.get("KOPT", "")
        if self.stage >= 2:
            if "nona" not in opts and not ("nonas" in opts and is_sample) and not ("nonap" in opts and not is_sample):
                self.na_seq(si, tile0, NT, v, is_sample, segs)
            if "noout" not in opts:
                self.out_proj(si, tile0, NT, v, is_sample)

    def qk_norm(self, ps, pB, bn, gcol, outs):
        B = self.B
        sq = self.sc[0]
        self.act(sq[:, 0:bn], ps[:, 0:bn], AF.Square, [pB], [B("sc0")])
        msp = self.psB[:, 0:bn]
        self.mm(msp, self.ones_bd[:], sq[:, 0:bn], True, True, [B("ones_bd"), B("sc0")], [B("psB0")])
        tf = self.tmpf[0]
        self.act(tf[:, 0:bn], msp, AF.Sqrt, [B("psB0")], [B("tmpf0")], bias=EPS)
        self.recip(tf[:, 0:bn], tf[:, 0:bn], [B("tmpf0")], [B("tmpf0")])
        for psl, out, oB in outs:
            self.stt("dve", out, ps[psl, 0:bn], self.qkn_g[psl, gcol:gcol + 1], tf[psl, 0:bn], ALU.mult, ALU.mult,
                     [pB, B("qkn_g"), B("tmpf0")], [oB])

    def na_seq(self, si, tile0, NT, v, is_sample, segs):
        B, din = self.B, self.din
        T = NT * 128
        blocks = [(b0, min(512, T - b0)) for b0 in range(0, T, 512)]
        w_in = din["w_in"]
        q_of_k = _na_windows()
        ablocks = self.ablocks
        npairs = int(os.environ.get("NAS" if is_sample else "NAP", "4"))
        for a in range(npairs):
            self.bg_issue(0)
            col = lambda base: w_in[:, base + a * 128: base + (a + 1) * 128]
            wq_, wqB = self.load_w(col(2048), 0)
            for bi, (b0, bn) in enumerate(blocks):
                ps, pB = self.psA[bi % 2], B("psA%d" % (bi % 2))
                self.proj_fm(wq_, wqB, T, ps, pB, b0, bn)
                self.qk_norm(ps, pB, bn, 0, [(slice(0, 128), self.qT[:, b0:b0 + bn], B("qT"))])
            wk_, wkB = self.load_w(col(2560), 0)
            for bi, (b0, bn) in enumerate(blocks):
                ps, pB = self.psA[bi % 2], B("psA%d" % (bi % 2))
                self.proj_fm(wk_, wkB, T, ps, pB, b0, bn)
                self.qk_norm(ps, pB, bn, 1, [(slice(hh * 64, (hh + 1) * 64), self.kTm[hh * 64:(hh + 1) * 64, hh, b0:b0 + bn], B("kTm"))
                                             for hh in range(2)])
            wv_, wvB = self.load_w(col(3072), 0)
            for t in range(NT):
                ps, pB = self.psA[t % 2], B("psA%d" % (t % 2))
                self.proj_tm(wv_, wvB, t, ps, pB)
                for hh in range(2):
                    cs = slice(hh * 64, (hh + 1) * 64)
                    self.cp("act", self.vpad[:, t, hh, cs], ps[:, cs], [pB], [B("vpad")])
                if not is_sample and "nonvout" not in os.environ.get("KOPT", ""):
                    rows = slice(t * 128, (t + 1) * 128)
                    if "nvnocopy" not in os.environ.get("KOPT", ""):
                        self.cp("dve", self.nvo[:], ps[:, 0:128], [pB], [B("nvo")])
                    if "nvnodma" not in os.environ.get("KOPT", ""):
                        self.dma(self.nv[rows, a * 128:(a + 1) * 128], self.nvo[:], reads=[B("nvo")])
            if not is_sample and "nonk" not in os.environ.get("KOPT", ""):
                for t in range(NT):
                    ps, pB = self.psA[t % 2], B("psA%d" % (t % 2))
                    self.proj_tm(wk_, wkB, t, ps, pB)
                    tf = self.tmpf[0]
                    p3 = ps[:, 0:128].rearrange("p (h d) -> p h d", d=64)
                    t3 = tf[:, 0:128].rearrange("p (h d) -> p h d", d=64)
                    self.act(tf[:, 0:128], ps[:, 0:128], AF.Square, [pB], [B("tmpf0")])
                    self.S.op("dve", lambda q, t3=t3: q.tensor_reduce(out=self.small[:, 8:10], in_=t3, axis=AX.X, op=ALU.add),
                              [B("tmpf0")], [B("nkss")])
                    self.act(self.small[:, 10:12], self.small[:, 8:10], AF.Sqrt, [B("nkss")], [B("nksd")], scale=1.0 / 64, bias=EPS)
                    self.recip(self.small[:, 12:14], self.small[:, 10:12], [B("nksd")], [B("nkrs")])
                    self.tt("dve", t3, p3, self.small[:, 12:14].unsqueeze(2).to_broadcast([128, 2, 64]), ALU.mult,
                            [pB, B("nkrs")], [B("tmpf0")])
                    self.tt("dve", self.nko[:].rearrange("p (h d) -> p h d", d=64), t3,
                            self.kn_bc[:].unsqueeze(1).to_broadcast([128, 2, 64]), ALU.mult, [B("tmpf0"), B("kn_bc")], [B("nko")])
                    rows = slice(t * 128, (t + 1) * 128)
                    self.dma(self.nk[rows, a * 128:(a + 1) * 128], self.nko[:], reads=[B("nko")])
            if is_sample:
                self.dma(self.ctxst[:], din["kctxT"][a * 128:(a + 1) * 128, :], writes=[B("ctxst")])
                for hh in range(2):
                    psl = slice(hh * 64, (hh + 1) * 64)
                    self.cp("dve", self.kcm[psl, hh, :], self.ctxst[psl, :], [B("ctxst")], [B("kcm")])
                for kc in range(2):
                    self.dma(self.ctxst[:, 0:128], din["vctx"][kc * 128:(kc + 1) * 128, a * 128:(a + 1) * 128],
                             reads=[], writes=[B("ctxst")])
                    for hh in range(2):
                        cs = slice(hh * 64, (hh + 1) * 64)
                        self.cp("dve", self.vcp[:, kc, hh, cs], self.ctxst[:, cs], [B("ctxst")], [B("vcp")])
            cnt = 0

            def pv(lv, lvB, p, pB_, q0, qn, first, last):
                c0 = q0
                while c0 < q0 + qn:
                    c1 = min((c0 // 512 + 1) * 512, q0 + qn)
                    self.mm(self.psC[:, c0:c1], lv, p[:, c0 - q0:c1 - q0], first, last, [lvB, pB_], [B("psC")])
                    self.mm(self.psD[:, c0:c1], self.ones_pad[:, lv_hh[0], :], p[:, c0 - q0:c1 - q0], first, last,
                            [B("ones_pad"), pB_], [B("psD")])
                    c0 = c1

            lv_hh = [0]

            jobs = []

            def add_dense(hh, kc, first, last, qblocks=None):
                for bi, (b0, bn) in enumerate(qblocks if qblocks is not None else blocks):
                    k = len(jobs)
                    half = k % 2
                    pb = self.psB[:, half * 512: half * 512 + bn]
                    pbB = B("psB%d" % half)
                    sc = self.sc[k % 3]
                    scB = B("sc%d" % (k % 3))

                    def score(hh=hh, kc=kc, b0=b0, bn=bn, pb=pb, pbB=pbB, sc=sc, scB=scB):
                        lk = self.kTm[:, hh, kc * 128:(kc + 1) * 128] if not is_sample else self.kcm[:, hh, kc * 128:(kc + 1) * 128]
                        self.mm(pb, lk, self.qT[:, b0:b0 + bn], True, True, [B("kTm"), B("kcm"), B("qT")], [pbB])
                        self.act(sc[:, 0:bn], pb, AF.Exp, [pbB], [scB], scale=0.125)

                    def pvj(hh=hh, kc=kc, b0=b0, bn=bn, sc=sc, scB=scB, first=first, last=last):
                        lv_hh[0] = hh
                        lv = self.vpad[:, kc, hh, :] if not is_sample else self.vcp[:, kc, hh, :]
                        pv(lv, B("vpad") if not is_sample else B("vcp"), sc, scB, b0, bn, first, last)

                    jobs.append((score, pvj))

            def add_window(hh, c):
                k = len(jobs)
                h = 2 * a + hh
                r0 = [q_of_k[2 * c], q_of_k[2 * c + 1]]
                qlo = min(r0[0][0], r0[1][0])
                qhi = max(r0[0][1], r0[1][1])
                q0, qn = qlo * 64, (qhi - qlo + 1) * 64
                sbt = self.sbias[k % 2]
                sbB = B("sbias%d" % (k % 2))
                sc = self.sc[k % 3]
                scB = B("sc%d" % (k % 3))

                def score():
                    self.mm(self.psB[:, 0:min(qn, 512)], self.kTm[:, hh, c * 128:(c + 1) * 128], self.qT[:, q0:q0 + min(qn, 512)],
                            True, True, [B("kTm"), B("qT")], [B("psB0"), B("psB1")])
                    if qn > 512:
                        self.mm(self.psB[:, 512:qn], self.kTm[:, hh, c * 128:(c + 1) * 128], self.qT[:, q0 + 512:q0 + qn],
                                True, True, [B("kTm"), B("qT")], [B("psB0"), B("psB1")])
                    for krl in range(2):
                        kr = 2 * c + krl
                        psl = slice(krl * 64, (krl + 1) * 64)
                        a0, a1 = r0[krl]
                        lo, hi = (a0 - qlo) * 64, (a1 - qlo + 1) * 64
                        x0 = a0 - kr + 7
                        bias = self.trb[psl, h, x0:x0 + (a1 - a0 + 1), :]
                        self.stt("dve", sbt[psl, lo:hi].rearrange("p (x q) -> p x q", q=64),
                                 self.psB[psl, lo:hi].rearrange("p (x q) -> p x q", q=64), 0.125, bias,
                                 ALU.mult, ALU.add, [B("psB0"), B("psB1"), B("trb")], [sbB])
                        if lo > 0:
                            self.memset("dve", sbt[psl, 0:lo], NEG, [sbB])
                        if hi < qn:
                            self.memset("dve", sbt[psl, hi:qn], NEG, [sbB])
                    self.act(sc[:, 0:qn], sbt[:, 0:qn], AF.Exp, [sbB], [scB])

                def pvj():
                    lv_hh[0] = hh
                    pv(self.vpad[:, c, hh, :], B("vpad"), sc, scB, q0, qn, False, False)

                jobs.append((score, pvj))

            for hh in range(2):
                if "noatt" in os.environ.get("KOPT", ""):
                    continue
                if not is_sample:
                    continue
                else:
                    add_dense(hh, 0, hh == 0, False)
                    for c in range(8):
                        add_window(hh, c)
                    add_dense(hh, 1, False, hh == 1)
            if not is_sample and "noatt" not in os.environ.get("KOPT", ""):
                for (b0, bn, ktiles) in ablocks:
                    for hh in range(2):
                        for kc in ktiles:
                            add_dense(hh, kc, hh == 0 and kc == ktiles[0], hh == 1 and kc == ktiles[-1], [(b0, bn)])
            if jobs:
                jobs[0][0]()
            for k in range(len(jobs)):
                if k + 1 < len(jobs):
                    jobs[k + 1][0]()
                jobs[k][1]()
            for bi, (b0, bn, _kt) in enumerate(ablocks):
                tf = self.tmpf[bi % 2]
                tB = B("tmpf%d" % (bi % 2))
                self.recip(tf[:, 0:bn], self.psD[:, b0:b0 + bn], [B("psD")], [tB])
                self.tt("dve", self.oT[:, 4 + a, b0:b0 + bn], self.psC[:, b0:b0 + bn], tf[:, 0:bn], ALU.mult,
                        [B("psC"), tB], [B("oT")])

    def out_proj(self, si, tile0, NT, v, is_sample):
        B, din = self.B, self.din
        for t in range(NT):
            ps, pB = (self.psC, B("psC")) if t % 2 == 0 else (self.psD, B("psD"))
            for half in range(2):
                cs = slice(half * 512, (half + 1) * 512)
                for c in range(8):
                    self.mm(ps[:, cs], self.oT[:, c, t * 128:(t + 1) * 128], self.wo[:, c, cs], c == 0, c == 7,
                            [B("oT"), B("wo")], [pB])
            xb = self.xb[t % 2]
            xB = B("xb%d" % (t % 2))
            rows = slice((tile0 + t) * 128, (tile0 + t + 1) * 128)
            self.dma(xb[:], din["x"][rows, :], writes=[xB])
            tf = self.tmpf[t % 2]
            tB = B("tmpf%d" % (t % 2))
            self.tt("dve", tf[:], ps[:], self.modsl(v, 2), ALU.mult, [pB, B("mod")], [tB])
            self.tt("dve", xb[:], tf[:], xb[:], ALU.add, [tB, xB], [xB])
            self.dma(self.y[rows, :], xb[:], reads=[xB])

    def peer(self, st):
        B, din = self.B, self.din
        sb = lambda n, s, d=F32: self.sb(st, n, s, d)
        TG = 256
        self.bg_issue(10000)
        scrU, scrV, scrQ = self.scrU, self.scrV, self.scrQ
        G3 = [sb("G3_%d" % i, [128, 128, 128], BF16) for i in range(3)]
        XT = [sb("XT%d" % i, [128, 8, TG], BF16) for i in range(2)]
        x1 = sb("x1_0", [128, 1024])
        xm = sb("xm", [128, 1024], BF16)
        ptmp = sb("ptmp0", [128, 1024])
        keysT = sb("keysT", [128, 16, 128], BF16)
        iota_row = sb("iota_rowp", [128, 128])
        iota16 = sb("iota16p", [128, 16])
        iota_rb = sb("iota_rb", [128, 128], BF16)
        qTc = [sb("qTc%d" % i, [128, TG], BF16) for i in range(2)]
        SscR = [[sb("Ssc%d_%d" % (t, k), [128, 128]) for k in range(3)] for t in range(2)]
        v16 = [sb("v16_%d" % i, [128, 16, 16]) for i in range(2)]
        i16u = [sb("i16u_%d" % i, [128, 16, 16], U32) for i in range(2)]
        i16f = sb("i16f", [128, 16, 16])
        cand = sb("cand", [128, 8, 256])
        oh = cand[:].rearrange("p h (a b) -> p h a b", b=16)
        tops = [sb("top%d" % i, [128, 8, 16]) for i in range(2)]
        pu = sb("pu", [128, 8, 16], U32)
        abu = sb("abu", [128, 2, 8, 16], U32)
        abf = sb("abf", [128, 2, 8, 16])
        wsels = [sb("wsel%d" % i, [128, 3, 128]) for i in range(2)]
        wselb = sb("wselb", [128, 3, 128], BF16)
        zs = sb("zs", [128, 8, 2])
        sT = [sb("sT%d" % i, [128, 3, TG], BF16) for i in range(2)]
        wbf = [sb("pwbf%d" % i, [128, 8, 128], BF16) for i in range(2)]
        ubf = [sb("ubf%d" % i, [128, 1024], BF16) for i in range(3)]
        vbf = [sb("vbf%d" % i, [128, 1024], BF16) for i in range(4)]
        hg = [sb("hg%d" % i, [128, TG], BF16) for i in range(3)]
        AT = [sb("AT%d" % i, [128, TG], BF16) for i in range(3)]
        P1 = [sb("P1_%d" % i, [128, 4, 128], BF16) for i in range(4)]
        P2 = [sb("P2_%d" % i, [128, 4, 128], BF16) for i in range(4)]

        self.dma(iota_row[:], din["iota_row"], writes=[B("iota_rowp")])
        self.dma(iota16[:], din["iota16"], writes=[B("iota16p")])
        self.cp("dve", iota_rb[:], iota_row[:], [B("iota_rowp")], [B("iota_rb")])
        for c4 in range(4):
            kst = ptmp[:, 0:512].rearrange("p (c k) -> p c k", k=128)
            self.dma(kst, din["keysT"][:, c4 * 4:(c4 + 1) * 4, :], writes=[B("ptmp0")])
            self.cp("dve", keysT[:, c4 * 4:(c4 + 1) * 4, :], kst, [B("ptmp0")], [B("keysT")])

        psT = self.psA[0][:].bitcast(BF16).rearrange("p (c n) -> p c n", n=128)
        psG = self.psA[0][:].rearrange("p (t j) -> p t j", j=128)
        psH = [self.psB[:, 0:TG], self.psB[:, 512:512 + TG], self.psA[1][:, 0:TG]]
        psHB = [B("psB0"), B("psB1"), B("psA1")]
        psO = [self.psC, self.psD]
        psOB = [B("psC"), B("psD")]
        ngroups = int(os.environ.get("PGROUPS", "6"))
        nj = int(os.environ.get("PNJ", "128"))

        def prep_a(g):
            v = 1 if g < 4 else 0
            XTg, XB = XT[g % 2], B("XT%d" % (g % 2))
            for tt in range(2):
                rows = slice((2 * g + tt) * 128, (2 * g + tt + 1) * 128)
                xB = B("x1_0")
                self.dma(x1[:], self.y[rows, :], writes=[xB])
                tf = ptmp
                self.memset("dve", self.small[:, 0:1], 0.0, [B("ss")])
                self.act(tf[:], x1[:], AF.Square, [xB], [B("ptmp0"), B("ss")], accum_out=self.small[:, 0:1])
                self.act(self.small[:, 1:2], self.small[:, 0:1], AF.Sqrt, [B("ss")], [B("sd")], scale=1.0 / D, bias=EPS)
                self.recip(self.small[:, 2:3], self.small[:, 1:2], [B("sd")], [B("rstd")])
                self.stt("dve", tf[:], x1[:], self.small[:, 2:3], self.modsl(v, 4), ALU.mult, ALU.mult,
                         [xB, B("rstd"), B("mod")], [B("ptmp0")])
                self.tt("dve", xm[:], tf[:], self.modsl(v, 3), ALU.add, [B("ptmp0"), B("mod")], [B("xm")])
                for kc in range(8):
                    self.tr(psT[:, kc, :], xm[:, kc * 128:(kc + 1) * 128], self.ident_b[:], [B("xm"), B("ident_b")], [B("psA0")])
                self.cp("act", XTg[:, :, tt * 128:(tt + 1) * 128], psT, [B("psA0")], [XB])

        def prep(g):
            XTg, XB = XT[g % 2], B("XT%d" % (g % 2))

            def wload(c):
                i = c % 2
                self.dma(wbf[i][:], scrQ[c].rearrange("p (kc n) -> p kc n", n=128), reads=[B("scrQ%d" % c)], writes=[B("pwbf%d" % i)])

            def scores_mm(c):
                for tt in range(2):
                    self.mm(self.psA[0][:, 256 + tt * 128:256 + (tt + 1) * 128], qTc[c % 2][:, tt * 128:(tt + 1) * 128], keysT[:, c, :], True, True,
                            [B("qTc%d" % (c % 2)), B("keysT")], [B("psA0")])

            def scores_cp(c):
                for tt in range(2):
                    self.cp("dve", SscR[tt][c % 3][:], self.psA[0][:, 256 + tt * 128:256 + (tt + 1) * 128], [B("psA0")], [B("Ssc%d_%d" % (tt, c % 3))])

            def level1(c):
                for tt in range(2):
                    S = SscR[tt][c % 3]
                    SB = B("Ssc%d_%d" % (tt, c % 3))
                    vB, iB = B("v16_%d" % tt), B("i16u_%d" % tt)
                    for half8 in range(2):
                        vs = v16[tt][:, c, half8 * 8:(half8 + 1) * 8]
                        iu = i16u[tt][:, c, half8 * 8:(half8 + 1) * 8]
                        self.S.op("dve", lambda q, vs=vs, S=S: q.max(out=vs, in_=S[:]), [SB], [vB])
                        self.S.op("dve", lambda q, vs=vs, S=S, iu=iu: q.max_index(out=iu, in_max=vs, in_values=S[:]), [SB, vB], [iB])
                        if half8 == 0:
                            self.S.op("dve", lambda q, vs=vs, S=S: q.match_replace(out=S[:], in_to_replace=vs, in_values=S[:], imm_value=NEG),
                                      [SB, vB], [SB])

            wload(0)
            for c in range(17):
                if c + 1 < 16:
                    wload(c + 1)
                if c < 16:
                    i = c % 2
                    for kc in range(8):
                        self.mm(self.psA[0][:, 0:TG], wbf[i][:, kc, :], XTg[:, kc, :], kc == 0, kc == 7, [B("pwbf%d" % i), XB], [B("psA0")])
                if c > 0:
                    scores_mm(c - 1)
                if c < 16:
                    self.cp("dve", qTc[c % 2][:], self.psA[0][:, 0:TG], [B("psA0")], [B("qTc%d" % (c % 2))])
                if c > 0:
                    scores_cp(c - 1)
                    level1(c - 1)
                yield 4.2 if c > 0 else 0.5
            for tt in range(2):
                top, topB = tops[tt], B("top%d" % tt)
                wsel, wselB = wsels[tt], B("wsel%d" % tt)
                vB, iB = B("v16_%d" % tt), B("i16u_%d" % tt)
                self.cp("dve", i16f[:], i16u[tt][:], [iB], [B("i16f")])
                v16r = v16[tt][:].rearrange("p (h f) k -> p h f k", f=2)
                i16r = i16f[:].rearrange("p (h f) k -> p h f k", f=2)
                cand4 = cand[:].rearrange("p h (a b) -> p h a b", b=16)
                self.tt("dve", cand4, v16r[:, :, 0, :].unsqueeze(3).to_broadcast([128, 8, 16, 16]),
                        v16r[:, :, 1, :].unsqueeze(2).to_broadcast([128, 8, 16, 16]), ALU.add, [vB], [B("cand")])
                yield 2.6
                for h in range(8):
                    for half8 in range(2):
                        vs = top[:, h, half8 * 8:(half8 + 1) * 8]
                        self.S.op("dve", lambda q, vs=vs, h=h: q.max(out=vs, in_=cand[:, h, :]), [B("cand")], [topB])
                        self.S.op("dve", lambda q, vs=vs, h=h, half8=half8: q.max_index(out=pu[:, h, half8 * 8:(half8 + 1) * 8], in_max=vs, in_values=cand[:, h, :]),
                                  [B("cand"), topB], [B("pu")])
                        if half8 == 0:
                            self.S.op("dve", lambda q, vs=vs, h=h: q.match_replace(out=cand[:, h, :], in_to_replace=vs, in_values=cand[:, h, :], imm_value=NEG),
                                      [B("cand"), topB], [B("cand")])
                    yield 2.4
                self.S.op("dve", lambda q: q.tensor_single_scalar(out=abu[:, 0], in_=pu[:], scalar=4, op=ALU.logical_shift_right), [B("pu")], [B("abu")])
                self.S.op("dve", lambda q: q.tensor_single_scalar(out=abu[:, 1], in_=pu[:], scalar=15, op=ALU.bitwise_and), [B("pu")], [B("abu")])
                self.cp("dve", abf[:], abu[:], [B("abu")], [B("abf")])
                yield 1.0
                wsel4 = wsel[:].rearrange("p w (h k) -> p w h k", k=16)
                for f in range(2):
                    self.tt("dve", oh[:], abf[:, f].unsqueeze(3).to_broadcast([128, 8, 16, 16]),
                            iota16[:].unsqueeze(1).unsqueeze(1).to_broadcast([128, 8, 16, 16]), ALU.is_equal,
                            [B("abf"), B("iota16p")], [B("cand")])
                    self.tt("dve", oh[:], oh[:], i16r[:, :, f, :].unsqueeze(2).to_broadcast([128, 8, 16, 16]), ALU.mult,
                            [B("cand"), B("i16f")], [B("cand")])
                    self.S.op("dve", lambda q, f=f, wsel4=wsel4: q.tensor_reduce(out=wsel4[:, f], in_=oh[:], axis=AX.X, op=ALU.add), [B("cand")], [wselB])
                    yield 6.6
            yield ("wait", 2.0)
            prep_tail(g)
            yield 3.0

        def prep_tail(g):
            sTg, sTB = sT[g % 2], B("sT%d" % (g % 2))
            for tt in range(2):
                top, topB = tops[tt], B("top%d" % tt)
                wsel, wselB = wsels[tt], B("wsel%d" % tt)
                wsel4 = wsel[:].rearrange("p w (h k) -> p w h k", k=16)
                self.tt("dve", top[:], top[:], top[:, :, 0:1].to_broadcast([128, 8, 16]), ALU.subtract, [topB], [topB])
                self.act(top[:], top[:], AF.Exp, [topB], [topB])
                self.S.op("dve", lambda q, top=top: q.tensor_reduce(out=zs[:, :, 0], in_=top[:], axis=AX.X, op=ALU.add), [topB], [B("zs")])
                self.recip(zs[:, :, 1], zs[:, :, 0], [B("zs")], [B("zs")])
                self.tt("dve", wsel4[:, 2], top[:], zs[:, :, 1:2].to_broadcast([128, 8, 16]), ALU.mult, [topB, B("zs")], [wselB])
                self.cp("dve", wselb[:], wsel[:], [wselB], [B("wselb")])
                for w in range(3):
                    self.tr(psT[:, w, :], wselb[:, w, :], self.ident_b[:], [B("wselb"), B("ident_b")], [B("psA0")])
                self.cp("act", sTg[:, :, tt * 128:(tt + 1) * 128], psT[:, 0:3, :], [B("psA0")], [sTB])

        def gconstruct(g, tt, dst):
            sTg, sTB = sT[g % 2], B("sT%d" % (g % 2))
            Gd, GB = G3[dst], B("G3_%d" % dst)
            io4 = iota_rb[:].unsqueeze(1).to_broadcast([128, 4, 128])
            nb = 32

            def dve_part(bi):
                r = bi % 4
                n0 = tt * 128 + bi * 4
                bc = lambda w: sTg[:, w, n0:n0 + 4].unsqueeze(2).to_broadcast([128, 4, 128])
                self.tt("dve", P1[r][:], io4, bc(0), ALU.is_equal, [B("iota_rb"), sTB], [B("P1_%d" % r)])
                self.tt("dve", P2[r][:], io4, bc(1), ALU.is_equal, [B("iota_rb"), sTB], [B("P2_%d" % r)])
                self.tt("dve", P1[r][:], P1[r][:], bc(2), ALU.mult, [B("P1_%d" % r), sTB], [B("P1_%d" % r)])

            def pe_part(bi):
                r = bi % 4
                for k in range(4):
                    self.mm(psG[:, k, :], P1[r][:, k, :], P2[r][:, k, :], True, True, [B("P1_%d" % r), B("P2_%d" % r)], [B("psA0")])
                self.cp("dve", Gd[:, bi * 4:bi * 4 + 4, :], psG, [B("psA0")], [GB])

            for u in range(nb // 2 + 1):
                if u > 0:
                    pe_part(2 * u - 2)
                    pe_part(2 * u - 1)
                if u < nb // 2:
                    dve_part(2 * u)
                    dve_part(2 * u + 1)
                yield 4.9 if u < nb // 2 else 1.2

        def run_all(gen):
            for _ in gen:
                pass

        def main(g, Ta, Tb, inter):
            XTg, XB = XT[g % 2], B("XT%d" % (g % 2))
            Gt = [G3[Ta], G3[Tb]]
            GtB = [B("G3_%d" % Ta), B("G3_%d" % Tb)]

            def load(j):
                self.dma(ubf[j % 3][:], scrU[j], reads=[B("scrU%d" % j)], writes=[B("ubf%d" % (j % 3))])
                self.dma(vbf[j % 4][:], scrV[j], reads=[B("scrV%d" % j)], writes=[B("vbf%d" % (j % 4))])

            def Hm(j):
                ib = j % 3
                u3 = ubf[ib][:].rearrange("p (c i) -> p c i", i=128)
                for kc in range(8):
                    self.mm(psH[j % 3], u3[:, kc, :], XTg[:, kc, :], kc == 0, kc == 7, [B("ubf%d" % ib), XB], [psHB[j % 3]])

            def post(j):
                i3 = j % 3
                self.act(hg[i3][:], psH[i3], AF.Gelu_apprx_tanh, [psHB[i3]], [B("hg%d" % i3)])
                for tt in range(2):
                    cs = slice(tt * 128, (tt + 1) * 128)
                    self.tt("pool", AT[i3][:, cs], hg[i3][:, cs], Gt[tt][:, :, j], ALU.mult, [B("hg%d" % i3), GtB[tt]], [B("AT%d" % i3)])

            def outm(j):
                i3, ib = j % 3, j % 4
                for tt in range(2):
                    for half in range(2):
                        cs = slice(half * 512, (half + 1) * 512)
                        self.mm(psO[tt][:, cs], AT[i3][:, tt * 128:(tt + 1) * 128], vbf[ib][:, cs], j == 0, j == nj - 1,
                                [B("AT%d" % i3), B("vbf%d" % ib)], [psOB[tt]])

            W = 0.0
            waiting = None
            for j0 in range(min(3, nj)):
                load(j0)
            Hm(0)
            if nj > 1:
                Hm(1)
            for j in range(nj):
                if j + 3 < nj:
                    load(j + 3)
                if j + 2 < nj:
                    Hm(j + 2)
                post(j)
                outm(j)
                if inter is None:
                    continue
                budget = (j + 1) * 1.8 * 0.9
                emitted = 0
                while inter is not None and emitted < 1:
                    if waiting is not None:
                        if W + waiting > budget:
                            break
                        waiting = None
                    if W > budget:
                        break
                    c = next(inter, "done")
                    if c == "done":
                        inter = None
                    elif isinstance(c, tuple):
                        waiting = c[1]
                    else:
                        W += c
                        emitted += 1
            if inter is not None:
                run_all(inter)

        def epilogue(g):
            v = 1 if g < 4 else 0
            for tt in range(2):
                rows = slice((2 * g + tt) * 128, (2 * g + tt + 1) * 128)
                tf, tB = ptmp, B("ptmp0")
                self.dma(x1[:], self.y[rows, :], writes=[B("x1_0")])
                self.tt("dve", tf[:], psO[tt][:], self.modsl(v, 5), ALU.mult, [psOB[tt], B("mod")], [tB])
                self.tt("dve", x1[:], tf[:], x1[:], ALU.add, [tB, B("x1_0")], [B("x1_0")])
                self.dma(self.y[rows, :], x1[:], reads=[B("x1_0")])

        def chain(*gens):
            for gn in gens:
                for item in gn:
                    yield item

        prep_a(0)
        run_all(prep(0))
        run_all(gconstruct(0, 0, 0))
        run_all(gconstruct(0, 1, 1))
        Ta, Tb, Fr = 0, 1, 2
        for g in range(ngroups):
            if g + 1 < ngroups:
                prep_a(g + 1)
                main(g, Ta, Tb, chain(prep(g + 1), gconstruct(g + 1, 0, Fr)))
                epilogue(g)
                run_all(gconstruct(g + 1, 1, Ta))
                Ta, Tb, Fr = Fr, Ta, Tb
            else:
                main(g, Ta, Tb, None)
                epilogue(g)


def _prep_inputs(inp):
    f = lambda a: np.ascontiguousarray(np.asarray(a, dtype=np.float32))
    consts = _host_constants()
    shared = dict(consts)
    w_in = f(inp["w_in"][0])
    shared["w_ada"] = f(inp["w_ada"][0])
    shared["b_ada"] = f(inp["b_ada"][0]).reshape(1, 6144)
    shared["n1g"] = f(inp["norm1_g"][0]).reshape(1, 1024)
    shared["n2g"] = f(inp["norm2_g"][0]).reshape(1, 1024)
    shared["w_in"] = w_in
    perm = _rope_partner_perm()
    shared["w_sw"] = f(np.concatenate([w_in[:, 0:512][:, perm], w_in[:, 512:1024][:, perm]], axis=1))
    shared["dec"] = f(np.concatenate([inp["ret_decay_f"][0], inp["ret_decay_b"][0]])).reshape(1, 16)
    shared["gn_g"] = f(np.asarray(inp["ret_gn_g"][0]).reshape(4, 128).T)
    qg = np.tile(np.asarray(inp["na_qn_g"][0]), 2)
    kg = np.tile(np.asarray(inp["na_kn_g"][0]), 2)
    shared["qkn_g"] = f(np.stack([qg, kg], axis=1))
    shared["kn_row"] = f(inp["na_kn_g"][0]).reshape(1, 64)
    rpb = np.asarray(inp["na_rpb"][0], dtype=np.float32)
    kc = np.arange(64)[:, None]
    qc = np.arange(64)[None, :]
    dc = np.clip(kc - qc + 15, 0, 30)
    x = np.arange(15)
    tr = rpb[:, (14 - x)[:, None, None], dc[None, :, :]]
    tr = np.transpose(tr, (2, 0, 1, 3))
    shared["rpbT"] = f(np.concatenate([tr, tr], axis=0))
    shared["w_out"] = f(inp["w_out"][0])
    shared["wq"] = f(inp["peer_wq"][0])
    keys = np.asarray(inp["peer_keys"][0], dtype=np.float32)
    shared["keysT"] = f(np.transpose(keys.reshape(16, 128, 128), (2, 0, 1)))
    U = np.asarray(inp["peer_u"][0], dtype=np.float32)
    shared["Ut"] = f(np.transpose(U.reshape(128, 128, 8, 128), (1, 3, 2, 0)))
    V = np.asarray(inp["peer_v"][0], dtype=np.float32)
    shared["Vp"] = f(np.transpose(V.reshape(128, 128, 1024), (1, 0, 2)))
    xp = np.asarray(inp["x_prompt"], dtype=np.float32)
    xs = np.asarray(inp["x_sample"], dtype=np.float32)
    cc = np.asarray(inp["c"], dtype=np.float32)
    cctx = np.asarray(inp["c_ctx"], dtype=np.float32)
    maps = []
    for c in range(NCORES):
        m = dict(shared)
        m["x"] = f(np.concatenate([xs[c], xp[2 * c], xp[2 * c + 1]], axis=0))
        cv = np.stack([cctx, cc[c]], axis=0)
        m["cT"] = f(np.transpose(cv.reshape(2, 8, 128), (2, 1, 0)))
        m["kctxT"] = f(np.asarray(inp["cache_na_k"][c, 0], dtype=np.float32).reshape(256, 512).T)
        m["vctx"] = f(np.asarray(inp["cache_na_v"][c, 0], dtype=np.float32).reshape(256, 512))
        m["s0"] = f(inp["state_ret"][c, 0])
        maps.append(m)
    return maps


_NC_CACHE = {}


def _get_nc(stage=3):
    if stage not in _NC_CACHE:
        _NC_CACHE[stage] = Builder(stage=stage).build()
    return _NC_CACHE[stage]


def kernel(**inputs):
    maps = _prep_inputs(inputs)
    nc = _get_nc()
    res = run_bass_kernel_spmd(nc, maps, core_ids=list(range(NCORES)))
    outs = res.results
    y_p = np.zeros((16, 256, 1024), np.float32)
    y_s = np.zeros((8, 1024, 1024), np.float32)
    nk = np.zeros((16, 1, 256, 8, 64), np.float32)
    nv = np.zeros((16, 1, 256, 8, 64), np.float32)
    st = np.zeros((16, 1, 2, 8, 64, 64), np.float32)
    for c in range(NCORES):
        r = outs[c]
        y = r["y"]
        y_s[c] = y[0:1024]
        y_p[2 * c] = y[1024:1280]
        y_p[2 * c + 1] = y[1280:1536]
        nk[2 * c:2 * c + 2, 0] = r["nk"].reshape(2, 256, 8, 64)
        nv[2 * c:2 * c + 2, 0] = r["nv"].reshape(2, 256, 8, 64)
        st[2 * c:2 * c + 2, 0] = r["st"]
    return (y_p, y_s, nk, nv, st)
```

```python
import math
import os
from contextlib import ExitStack

import numpy as np
import concourse.bass as bass
import concourse.mybir as mybir
from concourse.bass_utils import run_bass_kernel_spmd

F32 = mybir.dt.float32
BF16 = mybir.dt.bfloat16
I32 = mybir.dt.int32
U32 = mybir.dt.uint32
ALU = mybir.AluOpType
AF = mybir.ActivationFunctionType
AX = mybir.AxisListType

NCORES = 8
D = 1024
EPS = 1e-6
NEG = -1e30


class Buf:
    __slots__ = ("name", "w", "r")

    def __init__(self, name):
        self.name = name
        self.w = None
        self.r = []


class Eng:
    def __init__(self, name, sem, same_sync):
        self.name = name
        self.sem = sem
        self.count = 0
        self.waited = {}
        self.same_sync = same_sync
        self.prog = []


class Sched:
    def __init__(self, nc, stack, n_dma_slots=int(os.environ.get("NDMA", "24"))):
        self.nc = nc
        self.sems = {}
        self.engs = {}
        for name, same in (("pe", False), ("act", True), ("dve", True), ("pool", True), ("sp", True)):
            sem = stack.enter_context(nc.semaphore("s_" + name))
            self.sems[name] = sem
            self.engs[name] = Eng(name, sem, same)
        self.dma_slots = []
        for i in range(n_dma_slots):
            key = "dma%d" % i
            self.sems[key] = stack.enter_context(nc.semaphore("s_" + key))
            self.dma_slots.append([key, 0])
        self.dma_i = 0
        self.bg_slots = []
        for i in range(16):
            key = "bg%d" % i
            self.sems[key] = stack.enter_context(nc.semaphore("s_" + key))
            self.bg_slots.append([key, 0])
        self.bg_i = 0
        self.n_inst = 0

    def _wait(self, e, tok):
        if tok is None:
            return
        key, val = tok
        if key == e.name and not e.same_sync:
            return
        if e.waited.get(key, 0) >= val:
            return
        e.waited[key] = val
        sem = self.sems[key]
        e.prog.append(lambda q, sem=sem, val=val: q.wait_ge(sem, val))

    def _deps(self, e, reads, writes):
        for b in reads:
            self._wait(e, b.w)
            if b.name.startswith("ps"):
                for t in b.r:
                    if t[0] != e.name:
                        self._wait(e, t)
        for b in writes:
            self._wait(e, b.w)
            for t in b.r:
                self._wait(e, t)

    @staticmethod
    def _mark(tok, reads, writes):
        for b in reads:
            b.r.append(tok)
            if len(b.r) > 64:
                b.r = b.r[-64:] if False else b.r
        for b in writes:
            b.w = tok
            b.r = []

    def op(self, eng, fn, reads=(), writes=()):
        e = self.engs[eng]
        self._deps(e, reads, writes)
        e.count += 1
        tok = (e.name, e.count)
        sem = e.sem
        e.prog.append(lambda q, fn=fn, sem=sem: fn(q).then_inc(sem, 1))
        self._mark(tok, reads, writes)
        self.n_inst += 1
        return tok

    def dma(self, eng, out, in_, reads=(), writes=(), bg=False, **kw):
        e = self.engs[eng]
        self._deps(e, reads, writes)
        if bg:
            slot = self.bg_slots[self.bg_i % len(self.bg_slots)]
            self.bg_i += 1
        else:
            slot = self.dma_slots[self.dma_i % len(self.dma_slots)]
            self.dma_i += 1
        key = slot[0]
        if slot[1] > 0:
            self._wait(e, (key, slot[1]))
        slot[1] += 16
        tok = (key, slot[1])
        sem = self.sems[key]
        e.prog.append(lambda q, out=out, in_=in_, sem=sem, kw=kw:
                      q.dma_start(out=out, in_=in_, **kw).then_inc(sem, 16))
        self._mark(tok, reads, writes)
        self.n_inst += 1
        return tok

    def barrier(self):
        for e in self.engs.values():
            for key, val in self.dma_slots + self.bg_slots:
                if val > 0:
                    self._wait(e, (key, val))
            for o in self.engs.values():
                if o is not e and o.count > 0:
                    self._wait(e, (o.name, o.count))

    def emit(self):
        nc = self.nc
        progs = {k: v.prog for k, v in self.engs.items()}
        with nc.Block() as block:
            @block.tensor
            def _(q):
                for f in progs["pe"]:
                    f(q)

            @block.scalar
            def _(q):
                for f in progs["act"]:
                    f(q)

            @block.vector
            def _(q):
                for f in progs["dve"]:
                    f(q)

            @block.gpsimd
            def _(q):
                for f in progs["pool"]:
                    f(q)

            @block.sync
            def _(q):
                for f in progs["sp"]:
                    f(q)


def _rope_tables():
    T = 1024
    n = np.arange(T)
    rows = (n // 64).astype(np.float32)
    cols = (n % 64).astype(np.float32)
    freqs = (np.float32(10000.0) ** (-np.arange(16, dtype=np.float32) / np.float32(16))).astype(np.float32)
    C = np.zeros((128, T), np.float32)
    Sg = np.zeros((128, T), np.float32)
    for p in range(128):
        d = p % 64
        pos = rows if d < 32 else cols
        dd = d % 32
        f = dd % 16
        ang = (pos * freqs[f]).astype(np.float32)
        C[p] = np.cos(ang)
        Sg[p] = -np.sin(ang) if dd < 16 else np.sin(ang)
    return C, Sg


def _rope_partner_perm():
    perm = np.zeros(512, np.int64)
    for h in range(8):
        for d in range(64):
            dd = d % 32
            partner = d + 16 if dd < 16 else d - 16
            perm[h * 64 + d] = h * 64 + partner
    return perm


def _na_windows():
    rows, kh = 16, 8
    q_of_k = {}
    for kr in range(rows):
        qs = [qr for qr in range(rows) if min(max(qr - kh // 2, 0), rows - kh) <= kr < min(max(qr - kh // 2, 0), rows - kh) + kh]
        assert qs == list(range(qs[0], qs[-1] + 1))
        q_of_k[kr] = (qs[0], qs[-1])
    return q_of_k


def _host_constants():
    c = {}
    c["ident"] = np.eye(128, dtype=np.float32)
    ob = np.zeros((128, 128), np.float32)
    ob[:64, :64] = 1.0 / 64
    ob[64:, 64:] = 1.0 / 64
    c["onesbd"] = ob
    op = np.zeros((128, 2, 128), np.float32)
    op[:, 0, :64] = 1.0
    op[:, 1, 64:] = 1.0
    c["onespad"] = op
    p = np.arange(128, dtype=np.float32)[:, None]
    j = np.arange(1920, dtype=np.float32)[None, :]
    c["expo"] = (j - 896.0 - p).astype(np.float32)
    c["iota_n1"] = np.broadcast_to(np.arange(1, 1025, dtype=np.float32)[None], (128, 1024)).copy()
    c["iota_rev"] = np.broadcast_to((1024 - np.arange(1024, dtype=np.float32))[None], (128, 1024)).copy()
    tq = np.zeros((128, 2, 2), np.float32)
    for t in range(2):
        tq[:, 0, t] = 255 - (t * 128 + np.arange(128))
        tq[:, 1, t] = t * 128 + np.arange(128)
    c["tq"] = tq
    C, Sg = _rope_tables()
    c["rope_c"] = C
    c["rope_s"] = Sg
    qc = np.arange(64)
    cstart = np.clip(qc - 8, 0, 48)
    kc = np.arange(64)
    inwin = (kc[:, None] >= cstart[None, :]) & (kc[:, None] < cstart[None, :] + 16)
    cm = np.where(inwin, 0.0, NEG).astype(np.float32)
    c["cmask"] = np.concatenate([cm, cm], axis=0)
    c["iota_row"] = np.broadcast_to(np.arange(128, dtype=np.float32)[None], (128, 128)).copy()
    c["iota16"] = np.broadcast_to(np.arange(16, dtype=np.float32)[None], (128, 16)).copy()
    return c


CONST_SHAPES = {
    "ident": [128, 128], "onesbd": [128, 128], "onespad": [128, 2, 128], "expo": [128, 1920],
    "iota_n1": [128, 1024], "iota_rev": [128, 1024], "tq": [128, 2, 2], "rope_c": [128, 1024],
    "rope_s": [128, 1024], "cmask": [128, 64], "iota_row": [128, 128], "iota16": [128, 16],
}

IN_SHAPES = {
    "x": [1536, 1024], "cT": [128, 8, 2], "w_ada": [1024, 6144], "b_ada": [1, 6144],
    "n1g": [1, 1024], "n2g": [1, 1024], "w_in": [1024, 3584], "w_sw": [1024, 1024], "dec": [1, 16],
    "gn_g": [128, 4], "qkn_g": [128, 2], "kn_row": [1, 64], "rpbT": [128, 8, 15, 64],
    "w_out": [1024, 1024], "wq": [1024, 2048], "keysT": [128, 16, 128],
    "Ut": [128, 128, 8, 128], "Vp": [128, 128, 1024],
    "kctxT": [512, 256], "vctx": [256, 512], "s0": [2, 8, 64, 64],
}

SEQS = [(0, 8, 1, True, [(0, 8)]), (8, 4, 0, False, [(0, 2), (2, 2)])]


class Builder:
    def __init__(self, stage=3, debug=False):
        self.stage = stage
        self.debug = debug
        self.nc = bass.Bass("TRN2", target_bir_lowering=False)
        nc = self.nc
        self.din = {}
        for name, shp in list(IN_SHAPES.items()) + list(CONST_SHAPES.items()):
            self.din[name] = nc.dram_tensor(name, shp, F32, kind="ExternalInput").ap()
        self.y = nc.dram_tensor("y", [1536, 1024], F32, kind="ExternalOutput").ap()
        self.nk = nc.dram_tensor("nk", [512, 512], F32, kind="ExternalOutput").ap()
        self.nv = nc.dram_tensor("nv", [512, 512], F32, kind="ExternalOutput").ap()
        self.st = nc.dram_tensor("st", [2, 2, 8, 64, 64], F32, kind="ExternalOutput").ap()
        self.bufs = {}

    def bg_issue(self, n, dep=None):
        for _ in range(n):
            if not self.bg_todo:
                return
            out, in_, name = self.bg_todo.pop(0)
            self.dma(out, in_, reads=[self.B(dep)] if dep else [], writes=[self.B(name)], eng="pool", bg=True)

    def B(self, name):
        b = self.bufs.get(name)
        if b is None:
            b = self.bufs[name] = Buf(name)
        return b

    def sb(self, st, name, shape, dt=F32):
        return st.enter_context(self.nc.sbuf_tensor("sb_" + name, shape, dt))

    def mm(self, out, lhsT, rhs, start, stop, reads, writes, skip=False):
        if skip:
            self.S.op("pe", lambda q: q.matmul(out, lhsT=lhsT, rhs=rhs, start=start, stop=stop, skip_group_check=True), reads, writes)
        else:
            self.S.op("pe", lambda q: q.matmul(out, lhsT=lhsT, rhs=rhs, start=start, stop=stop), reads, writes)

    def tr(self, out, in_, ident, reads, writes):
        self.S.op("pe", lambda q: q.transpose(out=out, in_=in_, identity=ident), reads, writes)

    def act(self, out, in_, func, reads, writes, **kw):
        self.S.op("act", lambda q: q.activation(out=out, in_=in_, func=func, **kw), reads, writes)

    def tt(self, eng, out, in0, in1, op, reads, writes):
        self.S.op(eng, lambda q: q.tensor_tensor(out=out, in0=in0, in1=in1, op=op), reads, writes)

    def ts(self, eng, out, in0, s1, s2, op0, op1, reads, writes):
        if s2 is None:
            self.S.op(eng, lambda q: q.tensor_scalar(out=out, in0=in0, scalar1=s1, scalar2=None, op0=op0), reads, writes)
        else:
            self.S.op(eng, lambda q: q.tensor_scalar(out=out, in0=in0, scalar1=s1, scalar2=s2, op0=op0, op1=op1), reads, writes)

    def stt(self, eng, out, in0, scalar, in1, op0, op1, reads, writes):
        self.S.op(eng, lambda q: q.scalar_tensor_tensor(out=out, in0=in0, scalar=scalar, in1=in1, op0=op0, op1=op1), reads, writes)

    def cp(self, eng, out, in_, reads, writes):
        if eng == "act":
            self.S.op("act", lambda q: q.copy(out=out, in_=in_), reads, writes)
        else:
            self.S.op(eng, lambda q: q.tensor_copy(out=out, in_=in_), reads, writes)

    def memset(self, eng, ap, val, writes):
        self.S.op(eng, lambda q: q.memset(ap, val), (), writes)

    def recip(self, out, in_, reads, writes):
        self.S.op("dve", lambda q: q.reciprocal(out=out, in_=in_), reads, writes)

    def dma(self, out, in_, reads=(), writes=(), eng="sp", bg=False):
        self.S.dma(eng, out, in_, reads, writes, bg=bg)

    def build(self):
        nc = self.nc
        with ExitStack() as top:
            self.S = Sched(nc, top)
            self.psA = [top.enter_context(nc.psum_tensor("psA%d" % i, [128, 512], F32)) for i in range(2)]
            self.psB = top.enter_context(nc.psum_tensor("psB", [128, 1024], F32))
            self.psC = top.enter_context(nc.psum_tensor("psC", [128, 1024], F32))
            self.psD = top.enter_context(nc.psum_tensor("psD", [128, 1024], F32))
            self.scrU = nc.dram_tensor("scrU", [128, 128, 1024], BF16).ap()
            self.scrV = nc.dram_tensor("scrV", [128, 128, 1024], BF16).ap()
            self.scrQ = nc.dram_tensor("scrQ", [16, 128, 1024], BF16).ap()
            self.bg_todo = []
            if self.stage >= 3:
                Ut2 = self.din["Ut"].rearrange("j p c i -> j p (c i)")
                wq4 = self.din["wq"].rearrange("(kc p) (c n) -> c p kc n", p=128, n=128)
                for c in range(16):
                    self.bg_todo.append((self.scrQ[c].rearrange("p (kc n) -> p kc n", n=128), wq4[c], "scrQ%d" % c))
                for j in range(128):
                    self.bg_todo.append((self.scrU[j], Ut2[j], "scrU%d" % j))
                    self.bg_todo.append((self.scrV[j], self.din["Vp"][j], "scrV%d" % j))
            self.mod = self.sb(top, "mod", [128, 2, 6144], BF16)
            self.ident_b = self.sb(top, "ident_b", [128, 128], BF16)
            self.small = self.sb(top, "small", [128, 64], F32)
            with ExitStack() as ph1:
                self.phase1_alloc(ph1)
                self.setup(ph1)
                with ExitStack() as ws:
                    self.phase1_work(ws)
                    if self.stage >= 1:
                        for si, seq in enumerate(SEQS):
                            self.attn_seq(si, *seq)
                    self.S.barrier()
            if self.stage >= 3:
                with ExitStack() as ph2:
                    self.peer(ph2)
                    self.S.barrier()
            self.S.barrier()
            self.S.emit()
        return nc

    def phase1_alloc(self, st):
        sb = lambda n, s, d=F32: self.sb(st, n, s, d)
        self.ones_bd = sb("ones_bd", [128, 128], BF16)
        self.ones_pad = sb("ones_pad", [128, 2, 128], BF16)
        self.strips = sb("strips", [128, 8, 1920], BF16)
        self.lg = sb("lg", [128, 16])
        self.nlgb = sb("nlgb", [128, 8])
        self.lgcol = sb("lgcol", [128, 2, 4])
        self.wst = sb("wst", [128, 2, 2, 8])
        self.gn_g = sb("gn_g", [128, 4])
        self.qkn_g = sb("qkn_g", [128, 2])
        self.kn_bc = sb("kn_bc", [128, 64])
        self.wo = sb("wo", [128, 8, 1024], BF16)
        self.iota_n1 = sb("iota_n1", [128, 1024])
        self.iota_rev = sb("iota_rev", [128, 1024])
        self.rope_c = sb("rope_c", [128, 1024])
        self.rope_s = sb("rope_s", [128, 1024])
        self.trb = sb("trb", [128, 8, 15, 64], BF16)

    def phase1_work(self, st):
        sb = lambda n, s, d=F32: self.sb(st, n, s, d)
        B = self.B
        self.hT = sb("hT", [128, 8, 1024], BF16)
        self.oT = sb("oT", [128, 8, 1024], BF16)
        self.xb = [sb("xb%d" % i, [128, 1024]) for i in range(2)]
        self.tmpf = [sb("tmpf%d" % i, [128, 1024]) for i in range(2)]
        self.hb = sb("hb", [128, 1024], BF16)
        self.wstage = [sb("wstage%d" % i, [128, 8, 128]) for i in range(3)]
        self.wbf = [sb("wbf%d" % i, [128, 8, 128], BF16) for i in range(3)]
        self.qT = sb("qT", [128, 1024], BF16)
        self.kTm = sb("kTm", [128, 2, 1024], BF16)
        self.sg = sb("sg", [128, 1024], BF16)
        self.vpad = sb("vpad", [128, 8, 2, 128], BF16)
        self.kcm = sb("kcm", [128, 2, 256], BF16)
        self.vcp = sb("vcp", [128, 2, 2, 128], BF16)
        self.qf = sb("qf", [128, 2, 1024], BF16)
        self.s0bd = sb("s0bd", [128, 2, 128], BF16)
        self.s0st = sb("s0st", [128, 2, 128])
        self.sc = [sb("sc%d" % i, [128, 768], BF16) for i in range(3)]
        self.sbias = [sb("sbias%d" % i, [128, 768]) for i in range(2)]
        self.ctxst = sb("ctxst", [128, 256])
        self.kw = sb("kw", [128, 2, 128], BF16)
        self.nko = sb("nko", [128, 128])
        self.nvo = sb("nvo", [128, 128])
        self.sto = sb("sto", [64, 2, 2, 2, 64])
        self.memset("pool", self.kTm[:], 0.0, [B("kTm")])
        self.memset("pool", self.vpad[:], 0.0, [B("vpad")])
        self.memset("pool", self.kcm[:], 0.0, [B("kcm")])
        self.memset("pool", self.vcp[:], 0.0, [B("vcp")])
        self.memset("pool", self.s0st[:], 0.0, [B("s0st")])

    def setup(self, st_outer):
        B, din = self.B, self.din
        with ExitStack() as st:
            sb = lambda n, s, d=F32: self.sb(st, n, s, d)
            identf = sb("identf", [128, 128])
            self.dma(identf[:], din["ident"], writes=[B("identf")])
            self.cp("dve", self.ident_b[:], identf[:], [B("identf")], [B("ident_b")])
            tmp128 = sb("tmp128", [128, 2, 128])
            self.dma(tmp128[:, 0, :], din["onesbd"], writes=[B("tmp128")])
            self.cp("dve", self.ones_bd[:], tmp128[:, 0, :], [B("tmp128")], [B("ones_bd")])
            self.dma(tmp128[:], din["onespad"], reads=[], writes=[B("tmp128")])
            self.cp("dve", self.ones_pad[:], tmp128[:], [B("tmp128")], [B("ones_pad")])
            for nm, t in (("iota_n1", self.iota_n1), ("iota_rev", self.iota_rev), ("rope_c", self.rope_c),
                          ("rope_s", self.rope_s), ("gn_g", self.gn_g), ("qkn_g", self.qkn_g)):
                self.dma(t[:], din[nm], writes=[B(nm)])
            self.dma(self.kn_bc[:], din["kn_row"][0].partition_broadcast(128), writes=[B("kn_bc")])
            dec = sb("dec", [128, 16])
            self.dma(dec[:], din["dec"][0].partition_broadcast(128), writes=[B("dec")])
            self.act(dec[:], dec[:], AF.Exp, [B("dec")], [B("dec")], scale=-1.0)
            self.act(dec[:], dec[:], AF.Ln, [B("dec")], [B("dec")], bias=1.0)
            self.ts("dve", self.lg[:], dec[:], -1.0, None, ALU.mult, None, [B("dec")], [B("lg")])
            self.ts("dve", self.nlgb[:], self.lg[:, 8:16], -1.0, None, ALU.mult, None, [B("lg")], [B("nlgb")])
            for r in range(2):
                self.cp("dve", self.lgcol[0:64, r, :], self.lg[0:64, r * 8:r * 8 + 8:2], [B("lg")], [B("lgcol")])
                self.cp("dve", self.lgcol[64:128, r, :], self.lg[64:128, r * 8 + 1:r * 8 + 8:2], [B("lg")], [B("lgcol")])
            tq = sb("tq", [128, 2, 2])
            self.dma(tq[:], din["tq"], writes=[B("tq")])
            for r in range(2):
                self.tt("dve", self.wst[:, r], tq[:, r, :].unsqueeze(2).to_broadcast([128, 2, 8]),
                        self.lg[:, r * 8:(r + 1) * 8].unsqueeze(1).to_broadcast([128, 2, 8]), ALU.mult,
                        [B("tq"), B("lg")], [B("wst")])
            self.act(self.wst[:], self.wst[:], AF.Exp, [B("wst")], [B("wst")])
            self.ts("dve", self.wst[:], self.wst[:], 0.125, None, ALU.mult, None, [B("wst")], [B("wst")])
            expo = sb("expo", [128, 1920])
            t1 = sb("t1", [128, 1920])
            t2 = sb("t2", [128, 1920])
            self.dma(expo[:], din["expo"], writes=[B("expo")])
            for h in range(8):
                self.ts("dve", t1[:], expo[:], self.lg[:, h:h + 1], None, ALU.mult, None, [B("expo"), B("lg")], [B("t1")])
                self.stt("dve", t2[:], expo[:], self.nlgb[:, h:h + 1], t1[:], ALU.mult, ALU.min,
                         [B("expo"), B("nlgb"), B("t1")], [B("t2")])
                self.act(t2[:], t2[:], AF.Exp, [B("t2")], [B("t2")])
                self.ts("dve", self.strips[:, h, :], t2[:], 0.125, None, ALU.mult, None, [B("t2")], [B("strips")])
            cmask = sb("cmask", [128, 64])
            self.dma(cmask[:], din["cmask"], writes=[B("cmask")])
            for h in range(8):
                trs = t1[:, 0:960].rearrange("p (x q) -> p x q", q=64)
                self.dma(trs, din["rpbT"][:, h], writes=[B("t1")])
                self.tt("dve", self.trb[:, h], trs, cmask[:].unsqueeze(1).to_broadcast([128, 15, 64]), ALU.add,
                        [B("t1"), B("cmask")], [B("trb")])
            for kc in range(8):
                wsl = t2[:, 0:1024]
                self.dma(wsl, din["w_out"][kc * 128:(kc + 1) * 128, :], writes=[B("t2")])
                self.cp("act", self.wo[:, kc, :], wsl, [B("t2")], [B("wo")])
        self.S.barrier()
        self.bg_issue(64)
        self.adaln()
        self.S.barrier()

    def adaln(self):
        B, din = self.B, self.din
        with ExitStack() as st:
            sb = lambda n, s, d=F32: self.sb(st, n, s, d)
            cT = sb("cT", [128, 8, 2])
            rep = sb("rep", [128, 8, 2, 128], BF16)
            wst = [sb("awst%d" % i, [128, 8, 512]) for i in range(2)]
            wbf = [sb("awbf%d" % i, [128, 8, 512], BF16) for i in range(2)]
            bbc = [sb("bbc%d" % i, [128, 512]) for i in range(2)]
            ngb = sb("ngb", [128, 2, 1024])
            self.dma(cT[:], din["cT"], writes=[B("cT")])
            self.act(cT[:], cT[:], AF.Silu, [B("cT")], [B("cT")])
            self.cp("dve", rep[:], cT[:].unsqueeze(3).to_broadcast([128, 8, 2, 128]), [B("cT")], [B("rep")])
            self.dma(ngb[:, 0, :], din["n1g"][0].partition_broadcast(128), writes=[B("ngb")])
            self.dma(ngb[:, 1, :], din["n2g"][0].partition_broadcast(128), writes=[B("ngb")])
            w3 = din["w_ada"].rearrange("(kc p) n -> p kc n", p=128)
            for blk in range(12):
                i = blk % 2
                cs = slice(blk * 512, (blk + 1) * 512)
                self.dma(wst[i][:], w3[:, :, cs], writes=[B("awst%d" % i)])
                self.dma(bbc[i][:], din["b_ada"][0, cs].partition_broadcast(128), writes=[B("bbc%d" % i)])
                self.cp("dve" if blk % 2 == 0 else "act", wbf[i][:], wst[i][:], [B("awst%d" % i)], [B("awbf%d" % i)])
                for v in range(2):
                    ps = self.psA[v]
                    for kc in range(8):
                        self.mm(ps[:], rep[:, kc, v, :], wbf[i][:, kc, :], kc == 0, kc == 7,
                                [B("rep"), B("awbf%d" % i)], [B("psA%d" % v)])
                    self.tt("dve", self.mod[:, v, cs], ps[:], bbc[i][:], ALU.add,
                            [B("psA%d" % v), B("bbc%d" % i)], [B("mod")])
            for v in range(2):
                for j, ch in ((0, 1), (1, 4)):
                    sl = self.mod[:, v, ch * 1024:(ch + 1) * 1024]
                    self.stt("dve", sl, sl, 1.0, ngb[:, j, :], ALU.add, ALU.mult, [B("mod"), B("ngb")], [B("mod")])

    def modsl(self, v, ch):
        return self.mod[:, v, ch * 1024:(ch + 1) * 1024]

    def load_w(self, dram_cols, k):
        B = self.B
        i = self.wcount % 3
        j = self.wcount % 3
        self.wcount += 1
        self.bg_issue(3, "wbf%d" % ((j + 2) % 3))
        self.dma(self.wstage[i][:], dram_cols.rearrange("(kc p) n -> p kc n", p=128), writes=[B("wstage%d" % i)])
        self.cp("act", self.wbf[j][:], self.wstage[i][:], [B("wstage%d" % i)], [B("wbf%d" % j)])
        return self.wbf[j], B("wbf%d" % j)

    def proj_fm(self, w, wB, T, ps, psB_, b0, bn):
        for kc in range(8):
            self.mm(ps[:, 0:bn], w[:, kc, :], self.hT[:, kc, b0:b0 + bn], kc == 0, kc == 7,
                    [wB, self.B("hT")], [psB_])

    def proj_tm(self, w, wB, t, ps, psB_):
        for kc in range(8):
            self.mm(ps[:, 0:128], self.hT[:, kc, t * 128:(t + 1) * 128], w[:, kc, :], kc == 0, kc == 7,
                    [wB, self.B("hT")], [psB_])

    def attn_seq(self, si, tile0, NT, v, is_sample, segs):
        B, din = self.B, self.din
        T = NT * 128
        blocks = [(b0, min(512, T - b0)) for b0 in range(0, T, 512)]
        x_rows = lambda t: slice((tile0 + t) * 128, (tile0 + t + 1) * 128)
        w_in = din["w_in"]
        self.wcount = getattr(self, "wcount", 0)
        psT = self.psA[0][:].bitcast(BF16).rearrange("p (c n) -> p c n", n=128)
        for t in range(NT):
            xb = self.xb[t % 2]
            xB = B("xb%d" % (t % 2))
            self.dma(xb[:], din["x"][x_rows(t), :], writes=[xB])
            tf = self.tmpf[0]
            self.memset("dve", self.small[:, 0:1], 0.0, [B("ss")])
            self.act(tf[:], xb[:], AF.Square, [xB], [B("tmpf0"), B("ss")], accum_out=self.small[:, 0:1])
            self.act(self.small[:, 1:2], self.small[:, 0:1], AF.Sqrt, [B("ss")], [B("sd")], scale=1.0 / D, bias=EPS)
            self.recip(self.small[:, 2:3], self.small[:, 1:2], [B("sd")], [B("rstd")])
            self.stt("dve", tf[:], xb[:], self.small[:, 2:3], self.modsl(v, 1), ALU.mult, ALU.mult,
                     [xB, B("rstd"), B("mod")], [B("tmpf0")])
            self.tt("dve", self.hb[:], tf[:], self.modsl(v, 0), ALU.add, [B("tmpf0"), B("mod")], [B("hb")])
            for kc in range(8):
                self.tr(psT[:, kc, :], self.hb[:, kc * 128:(kc + 1) * 128], self.ident_b[:],
                        [B("hb"), B("ident_b")], [B("psA0")])
            self.cp("act", self.hT[:, :, t * 128:(t + 1) * 128], psT, [B("psA0")], [B("hT")])

        ablocks = []
        for (s0, sn) in segs:
            for q0 in range(s0 * 128, (s0 + sn) * 128, 512):
                ablocks.append((q0, min(512, (s0 + sn) * 128 - q0), list(range(s0, s0 + sn))))
        self.ablocks = ablocks
        for a in range(4):
            self.bg_issue(0)
            col = lambda base: w_in[:, base + a * 128: base + (a + 1) * 128]
            wq_, wqB = self.load_w(col(0), 0)
            if is_sample:
                wqs, wqsB = self.load_w(din["w_sw"][:, a * 128:(a + 1) * 128], 0)
            for bi, (b0, bn) in enumerate(blocks):
                self.proj_fm(wq_, wqB, T, self.psA[0], B("psA0"), b0, bn)
                if is_sample:
                    self.proj_fm(wqs, wqsB, T, self.psA[1], B("psA1"), b0, bn)
                    tf = self.tmpf[0]
                    self.tt("dve", tf[:, 0:bn], self.psA[0][:, 0:bn], self.rope_c[:, b0:b0 + bn], ALU.mult,
                            [B("psA0"), B("rope_c")], [B("tmpf0")])
                    tg = self.tmpf[1]
                    self.tt("dve", tg[:, 0:bn], self.psA[1][:, 0:bn], self.rope_s[:, b0:b0 + bn], ALU.mult,
                            [B("psA1"), B("rope_s")], [B("tmpf1")])
                    self.tt("dve", self.qT[:, b0:b0 + bn], tf[:, 0:bn], tg[:, 0:bn], ALU.add,
                            [B("tmpf0"), B("tmpf1")], [B("qT")])
                else:
                    self.cp("act", self.qT[:, b0:b0 + bn], self.psA[0][:, 0:bn], [B("psA0")], [B("qT")])
            wk_, wkB = self.load_w(col(512), 0)
            if is_sample:
                wks, wksB = self.load_w(din["w_sw"][:, 512 + a * 128:512 + (a + 1) * 128], 0)
            for bi, (b0, bn) in enumerate(blocks):
                self.proj_fm(wk_, wkB, T, self.psA[0], B("psA0"), b0, bn)
                if is_sample:
                    self.proj_fm(wks, wksB, T, self.psA[1], B("psA1"), b0, bn)
                    tf = self.tmpf[0]
                    self.tt("dve", tf[:, 0:bn], self.psA[0][:, 0:bn], self.rope_c[:, b0:b0 + bn], ALU.mult,
                            [B("psA0"), B("rope_c")], [B("tmpf0")])
                    tg = self.tmpf[1]
                    self.tt("dve", tg[:, 0:bn], self.psA[1][:, 0:bn], self.rope_s[:, b0:b0 + bn], ALU.mult,
                            [B("psA1"), B("rope_s")], [B("tmpf1")])
                    for hh in range(2):
                        ps_ = slice(hh * 64, (hh + 1) * 64)
                        self.tt("dve", self.kTm[ps_, hh, b0:b0 + bn], tf[ps_, 0:bn], tg[ps_, 0:bn], ALU.add,
                                [B("tmpf0"), B("tmpf1")], [B("kTm")])
                else:
                    for hh in range(2):
                        ps_ = slice(hh * 64, (hh + 1) * 64)
                        self.cp("act", self.kTm[ps_, hh, b0:b0 + bn], self.psA[0][ps_, 0:bn], [B("psA0")], [B("kTm")])
            wg_, wgB = self.load_w(col(1536), 0)
            for bi, (b0, bn) in enumerate(blocks):
                ps, pB = self.psA[bi % 2], B("psA%d" % (bi % 2))
                self.proj_fm(wg_, wgB, T, ps, pB, b0, bn)
                self.act(self.sg[:, b0:b0 + bn], ps[:, 0:bn], AF.Silu, [pB], [B("sg")])
            wv_, wvB = self.load_w(col(1024), 0)
            for t in range(NT):
                ps, pB = self.psA[t % 2], B("psA%d" % (t % 2))
                self.proj_tm(wv_, wvB, t, ps, pB)
                for hh in range(2):
                    cs = slice(hh * 64, (hh + 1) * 64)
                    self.cp("act", self.vpad[:, t, hh, cs], ps[:, cs], [pB], [B("vpad")])
            if not is_sample:
                for si_, (s0, sn) in enumerate(segs):
                    pS = self.psD[0:64, si_ * 512:si_ * 512 + 256].rearrange("p (r h e) -> p r h e", r=2, h=2)
                    for tr in range(sn):
                        t = s0 + tr
                        ps, pB = self.psA[t % 2], B("psA%d" % (t % 2))
                        self.proj_tm(wk_, wkB, t, ps, pB)
                        for r in range(2):
                            self.tt("dve", self.kw[:, r, :].rearrange("p (h d) -> p h d", d=64),
                                    ps[:, 0:128].rearrange("p (h d) -> p h d", d=64),
                                    self.wst[:, r, tr, 2 * a:2 * a + 2].unsqueeze(2).to_broadcast([128, 2, 64]), ALU.mult,
                                    [pB, B("wst")], [B("kw")])
                        for r in range(2):
                            for hh in range(2):
                                cs = slice(hh * 64, (hh + 1) * 64)
                                self.mm(pS[:, r, hh, :], self.kw[:, r, cs], self.vpad[:, t, hh, cs],
                                        (tr == 0 and r == 0 and hh == 0), tr == sn - 1, [B("kw"), B("vpad")], [B("psD")], skip=True)
                    self.cp("dve", self.sto[:, si_], pS, [B("psD")], [B("sto")])
                    for r in range(2):
                        self.dma(self.st[si_][r, 2 * a:2 * a + 2].rearrange("h d e -> d h e"), self.sto[:, si_, r], reads=[B("sto")])
            if is_sample:
                for r in range(2):
                    self.dma(self.s0st[0:64, r, 0:64], din["s0"][r, 2 * a], writes=[B("s0st")])
                    self.dma(self.s0st[64:128, r, 64:128], din["s0"][r, 2 * a + 1], writes=[B("s0st")])
                self.cp("dve", self.s0bd[:], self.s0st[:], [B("s0st")], [B("s0bd")])
                for r in range(2):
                    tf = self.tmpf[r]
                    self.act(tf[:], (self.iota_n1 if r == 0 else self.iota_rev)[:], AF.Exp,
                             [B("iota_n1"), B("iota_rev"), B("lgcol")], [B("tmpf%d" % r)], scale=self.lgcol[:, r, a:a + 1])
                    self.tt("dve", self.qf[:, r, :], self.qT[:], tf[:], ALU.mult, [B("qT"), B("tmpf%d" % r)], [B("qf")])
            for bi, (b0, bn, ktiles) in enumerate(ablocks):
                first = True
                if is_sample:
                    for r in range(2):
                        self.mm(self.psC[:, b0:b0 + bn], self.s0bd[:, r, :], self.qf[:, r, b0:b0 + bn], first, False,
                                [B("s0bd"), B("qf")], [B("psC")])
                        first = False
                items = [(hh, mc) for hh in range(2) for mc in ktiles]

                def r_score(k, b0=b0, bn=bn):
                    hh, mc = items[k]
                    h = 2 * a + hh
                    half = k % 2
                    pb = self.psB[:, half * 512: half * 512 + bn]
                    pbB = B("psB%d" % half)
                    self.mm(pb, self.kTm[:, hh, mc * 128:(mc + 1) * 128], self.qT[:, b0:b0 + bn], True, True,
                            [B("kTm"), B("qT")], [pbB])
                    off = b0 - mc * 128 + 896
                    self.tt("dve", self.sc[k % 3][:, 0:bn], pb, self.strips[:, h, off:off + bn], ALU.mult,
                            [pbB, B("strips")], [B("sc%d" % (k % 3))])

                def r_pv(k, first, b0=b0, bn=bn):
                    hh, mc = items[k]
                    self.mm(self.psC[:, b0:b0 + bn], self.vpad[:, mc, hh, :], self.sc[k % 3][:, 0:bn], first, k == len(items) - 1,
                            [B("vpad"), B("sc%d" % (k % 3))], [B("psC")])

                r_score(0)
                for k in range(len(items)):
                    if k + 1 < len(items):
                        r_score(k + 1)
                    r_pv(k, first)
                    first = False
                sq = self.sc[0]
                self.act(sq[:, 0:bn], self.psC[:, b0:b0 + bn], AF.Square, [B("psC")], [B("sc0")])
                msp = self.psA[0]
                self.mm(msp[:, 0:bn], self.ones_bd[:], sq[:, 0:bn], True, True, [B("ones_bd"), B("sc0")], [B("psA0")])
                tf = self.tmpf[0]
                self.act(tf[:, 0:bn], msp[:, 0:bn], AF.Sqrt, [B("psA0")], [B("tmpf0")], bias=EPS)
                self.recip(tf[:, 0:bn], tf[:, 0:bn], [B("tmpf0")], [B("tmpf0")])
                tg = self.tmpf[1]
                self.tt("dve", tg[:, 0:bn], self.psC[:, b0:b0 + bn], tf[:, 0:bn], ALU.mult, [B("psC"), B("tmpf0")], [B("tmpf1")])
                self.stt("dve", self.oT[:, a, b0:b0 + bn], tg[:, 0:bn], self.gn_g[:, a:a + 1], self.sg[:, b0:b0 + bn],
                         ALU.mult, ALU.mult, [B("tmpf1"), B("gn_g"), B("sg")], [B("oT")])

        opts = os.environ.get("KOPT", "")
        if self.stage >= 2:
            if "nona" not in opts and not ("nonas" in opts and is_sample) and not ("nonap" in opts and not is_sample):
                self.na_seq(si, tile0, NT, v, is_sample, segs)
            if "noout" not in opts:
                self.out_proj(si, tile0, NT, v, is_sample)

    def qk_norm(self, ps, pB, bn, gcol, outs):
        B = self.B
        sq = self.sc[0]
        self.act(sq[:, 0:bn], ps[:, 0:bn], AF.Square, [pB], [B("sc0")])
        msp = self.psB[:, 0:bn]
        self.mm(msp, self.ones_bd[:], sq[:, 0:bn], True, True, [B("ones_bd"), B("sc0")], [B("psB0")])
        tf = self.tmpf[0]
        self.act(tf[:, 0:bn], msp, AF.Sqrt, [B("psB0")], [B("tmpf0")], bias=EPS)
        self.recip(tf[:, 0:bn], tf[:, 0:bn], [B("tmpf0")], [B("tmpf0")])
        for psl, out, oB in outs:
            self.stt("dve", out, ps[psl, 0:bn], self.qkn_g[psl, gcol:gcol + 1], tf[psl, 0:bn], ALU.mult, ALU.mult,
                     [pB, B("qkn_g"), B("tmpf0")], [oB])

    def na_seq(self, si, tile0, NT, v, is_sample, segs):
        B, din = self.B, self.din
        T = NT * 128
        blocks = [(b0, min(512, T - b0)) for b0 in range(0, T, 512)]
        w_in = din["w_in"]
        q_of_k = _na_windows()
        ablocks = self.ablocks
        npairs = int(os.environ.get("NAS" if is_sample else "NAP", "4"))
        for a in range(npairs):
            self.bg_issue(0)
            col = lambda base: w_in[:, base + a * 128: base + (a + 1) * 128]
            wq_, wqB = self.load_w(col(2048), 0)
            for bi, (b0, bn) in enumerate(blocks):
                ps, pB = self.psA[bi % 2], B("psA%d" % (bi % 2))
                self.proj_fm(wq_, wqB, T, ps, pB, b0, bn)
                self.qk_norm(ps, pB, bn, 0, [(slice(0, 128), self.qT[:, b0:b0 + bn], B("qT"))])
            wk_, wkB = self.load_w(col(2560), 0)
            for bi, (b0, bn) in enumerate(blocks):
                ps, pB = self.psA[bi % 2], B("psA%d" % (bi % 2))
                self.proj_fm(wk_, wkB, T, ps, pB, b0, bn)
                self.qk_norm(ps, pB, bn, 1, [(slice(hh * 64, (hh + 1) * 64), self.kTm[hh * 64:(hh + 1) * 64, hh, b0:b0 + bn], B("kTm"))
                                             for hh in range(2)])
            wv_, wvB = self.load_w(col(3072), 0)
            for t in range(NT):
                ps, pB = self.psA[t % 2], B("psA%d" % (t % 2))
                self.proj_tm(wv_, wvB, t, ps, pB)
                for hh in range(2):
                    cs = slice(hh * 64, (hh + 1) * 64)
                    self.cp("act", self.vpad[:, t, hh, cs], ps[:, cs], [pB], [B("vpad")])
                if not is_sample and "nonvout" not in os.environ.get("KOPT", ""):
                    rows = slice(t * 128, (t + 1) * 128)
                    if "nvnocopy" not in os.environ.get("KOPT", ""):
                        self.cp("dve", self.nvo[:], ps[:, 0:128], [pB], [B("nvo")])
                    if "nvnodma" not in os.environ.get("KOPT", ""):
                        self.dma(self.nv[rows, a * 128:(a + 1) * 128], self.nvo[:], reads=[B("nvo")])
            if not is_sample and "nonk" not in os.environ.get("KOPT", ""):
                for t in range(NT):
                    ps, pB = self.psA[t % 2], B("psA%d" % (t % 2))
                    self.proj_tm(wk_, wkB, t, ps, pB)
                    tf = self.tmpf[0]
                    p3 = ps[:, 0:128].rearrange("p (h d) -> p h d", d=64)
                    t3 = tf[:, 0:128].rearrange("p (h d) -> p h d", d=64)
                    self.act(tf[:, 0:128], ps[:, 0:128], AF.Square, [pB], [B("tmpf0")])
                    self.S.op("dve", lambda q, t3=t3: q.tensor_reduce(out=self.small[:, 8:10], in_=t3, axis=AX.X, op=ALU.add),
                              [B("tmpf0")], [B("nkss")])
                    self.act(self.small[:, 10:12], self.small[:, 8:10], AF.Sqrt, [B("nkss")], [B("nksd")], scale=1.0 / 64, bias=EPS)
                    self.recip(self.small[:, 12:14], self.small[:, 10:12], [B("nksd")], [B("nkrs")])
                    self.tt("dve", t3, p3, self.small[:, 12:14].unsqueeze(2).to_broadcast([128, 2, 64]), ALU.mult,
                            [pB, B("nkrs")], [B("tmpf0")])
                    self.tt("dve", self.nko[:].rearrange("p (h d) -> p h d", d=64), t3,
                            self.kn_bc[:].unsqueeze(1).to_broadcast([128, 2, 64]), ALU.mult, [B("tmpf0"), B("kn_bc")], [B("nko")])
                    rows = slice(t * 128, (t + 1) * 128)
                    self.dma(self.nk[rows, a * 128:(a + 1) * 128], self.nko[:], reads=[B("nko")])
            if is_sample:
                self.dma(self.ctxst[:], din["kctxT"][a * 128:(a + 1) * 128, :], writes=[B("ctxst")])
                for hh in range(2):
                    psl = slice(hh * 64, (hh + 1) * 64)
                    self.cp("dve", self.kcm[psl, hh, :], self.ctxst[psl, :], [B("ctxst")], [B("kcm")])
                for kc in range(2):
                    self.dma(self.ctxst[:, 0:128], din["vctx"][kc * 128:(kc + 1) * 128, a * 128:(a + 1) * 128],
                             reads=[], writes=[B("ctxst")])
                    for hh in range(2):
                        cs = slice(hh * 64, (hh + 1) * 64)
                        self.cp("dve", self.vcp[:, kc, hh, cs], self.ctxst[:, cs], [B("ctxst")], [B("vcp")])
            cnt = 0

            def pv(lv, lvB, p, pB_, q0, qn, first, last):
                c0 = q0
                while c0 < q0 + qn:
                    c1 = min((c0 // 512 + 1) * 512, q0 + qn)
                    self.mm(self.psC[:, c0:c1], lv, p[:, c0 - q0:c1 - q0], first, last, [lvB, pB_], [B("psC")])
                    self.mm(self.psD[:, c0:c1], self.ones_pad[:, lv_hh[0], :], p[:, c0 - q0:c1 - q0], first, last,
                            [B("ones_pad"), pB_], [B("psD")])
                    c0 = c1

            lv_hh = [0]

            jobs = []

            def add_dense(hh, kc, first, last, qblocks=None):
                for bi, (b0, bn) in enumerate(qblocks if qblocks is not None else blocks):
                    k = len(jobs)
                    half = k % 2
                    pb = self.psB[:, half * 512: half * 512 + bn]
                    pbB = B("psB%d" % half)
                    sc = self.sc[k % 3]
                    scB = B("sc%d" % (k % 3))

                    def score(hh=hh, kc=kc, b0=b0, bn=bn, pb=pb, pbB=pbB, sc=sc, scB=scB):
                        lk = self.kTm[:, hh, kc * 128:(kc + 1) * 128] if not is_sample else self.kcm[:, hh, kc * 128:(kc + 1) * 128]
                        self.mm(pb, lk, self.qT[:, b0:b0 + bn], True, True, [B("kTm"), B("kcm"), B("qT")], [pbB])
                        self.act(sc[:, 0:bn], pb, AF.Exp, [pbB], [scB], scale=0.125)

                    def pvj(hh=hh, kc=kc, b0=b0, bn=bn, sc=sc, scB=scB, first=first, last=last):
                        lv_hh[0] = hh
                        lv = self.vpad[:, kc, hh, :] if not is_sample else self.vcp[:, kc, hh, :]
                        pv(lv, B("vpad") if not is_sample else B("vcp"), sc, scB, b0, bn, first, last)

                    jobs.append((score, pvj))

            def add_window(hh, c):
                k = len(jobs)
                h = 2 * a + hh
                r0 = [q_of_k[2 * c], q_of_k[2 * c + 1]]
                qlo = min(r0[0][0], r0[1][0])
                qhi = max(r0[0][1], r0[1][1])
                q0, qn = qlo * 64, (qhi - qlo + 1) * 64
                sbt = self.sbias[k % 2]
                sbB = B("sbias%d" % (k % 2))
                sc = self.sc[k % 3]
                scB = B("sc%d" % (k % 3))

                def score():
                    self.mm(self.psB[:, 0:min(qn, 512)], self.kTm[:, hh, c * 128:(c + 1) * 128], self.qT[:, q0:q0 + min(qn, 512)],
                            True, True, [B("kTm"), B("qT")], [B("psB0"), B("psB1")])
                    if qn > 512:
                        self.mm(self.psB[:, 512:qn], self.kTm[:, hh, c * 128:(c + 1) * 128], self.qT[:, q0 + 512:q0 + qn],
                                True, True, [B("kTm"), B("qT")], [B("psB0"), B("psB1")])
                    for krl in range(2):
                        kr = 2 * c + krl
                        psl = slice(krl * 64, (krl + 1) * 64)
                        a0, a1 = r0[krl]
                        lo, hi = (a0 - qlo) * 64, (a1 - qlo + 1) * 64
                        x0 = a0 - kr + 7
                        bias = self.trb[psl, h, x0:x0 + (a1 - a0 + 1), :]
                        self.stt("dve", sbt[psl, lo:hi].rearrange("p (x q) -> p x q", q=64),
                                 self.psB[psl, lo:hi].rearrange("p (x q) -> p x q", q=64), 0.125, bias,
                                 ALU.mult, ALU.add, [B("psB0"), B("psB1"), B("trb")], [sbB])
                        if lo > 0:
                            self.memset("dve", sbt[psl, 0:lo], NEG, [sbB])
                        if hi < qn:
                            self.memset("dve", sbt[psl, hi:qn], NEG, [sbB])
                    self.act(sc[:, 0:qn], sbt[:, 0:qn], AF.Exp, [sbB], [scB])

                def pvj():
                    lv_hh[0] = hh
                    pv(self.vpad[:, c, hh, :], B("vpad"), sc, scB, q0, qn, False, False)

                jobs.append((score, pvj))

            for hh in range(2):
                if "noatt" in os.environ.get("KOPT", ""):
                    continue
                if not is_sample:
                    continue
                else:
                    add_dense(hh, 0, hh == 0, False)
                    for c in range(8):
                        add_window(hh, c)
                    add_dense(hh, 1, False, hh == 1)
            if not is_sample and "noatt" not in os.environ.get("KOPT", ""):
                for (b0, bn, ktiles) in ablocks:
                    for hh in range(2):
                        for kc in ktiles:
                            add_dense(hh, kc, hh == 0 and kc == ktiles[0], hh == 1 and kc == ktiles[-1], [(b0, bn)])
            if jobs:
                jobs[0][0]()
            for k in range(len(jobs)):
                if k + 1 < len(jobs):
                    jobs[k + 1][0]()
                jobs[k][1]()
            for bi, (b0, bn, _kt) in enumerate(ablocks):
                tf = self.tmpf[bi % 2]
                tB = B("tmpf%d" % (bi % 2))
                self.recip(tf[:, 0:bn], self.psD[:, b0:b0 + bn], [B("psD")], [tB])
                self.tt("dve", self.oT[:, 4 + a, b0:b0 + bn], self.psC[:, b0:b0 + bn], tf[:, 0:bn], ALU.mult,
                        [B("psC"), tB], [B("oT")])

    def out_proj(self, si, tile0, NT, v, is_sample):
        B, din = self.B, self.din
        for t in range(NT):
            ps, pB = (self.psC, B("psC")) if t % 2 == 0 else (self.psD, B("psD"))
            for half in range(2):
                cs = slice(half * 512, (half + 1) * 512)
                for c in range(8):
                    self.mm(ps[:, cs], self.oT[:, c, t * 128:(t + 1) * 128], self.wo[:, c, cs], c == 0, c == 7,
                            [B("oT"), B("wo")], [pB])
            xb = self.xb[t % 2]
            xB = B("xb%d" % (t % 2))
            rows = slice((tile0 + t) * 128, (tile0 + t + 1) * 128)
            self.dma(xb[:], din["x"][rows, :], writes=[xB])
            tf = self.tmpf[t % 2]
            tB = B("tmpf%d" % (t % 2))
            self.tt("dve", tf[:], ps[:], self.modsl(v, 2), ALU.mult, [pB, B("mod")], [tB])
            self.tt("dve", xb[:], tf[:], xb[:], ALU.add, [tB, xB], [xB])
            self.dma(self.y[rows, :], xb[:], reads=[xB])

    def peer(self, st):
        B, din = self.B, self.din
        sb = lambda n, s, d=F32: self.sb(st, n, s, d)
        TG = 256
        self.bg_issue(10000)
        scrU, scrV, scrQ = self.scrU, self.scrV, self.scrQ
        G3 = [sb("G3_%d" % i, [128, 128, 128], BF16) for i in range(3)]
        XT = [sb("XT%d" % i, [128, 8, TG], BF16) for i in range(2)]
        x1 = sb("x1_0", [128, 1024])
        xm = sb("xm", [128, 1024], BF16)
        ptmp = sb("ptmp0", [128, 1024])
        keysT = sb("keysT", [128, 16, 128], BF16)
        iota_row = sb("iota_rowp", [128, 128])
        iota16 = sb("iota16p", [128, 16])
        iota_rb = sb("iota_rb", [128, 128], BF16)
        qTc = [sb("qTc%d" % i, [128, TG], BF16) for i in range(2)]
        SscR = [[sb("Ssc%d_%d" % (t, k), [128, 128]) for k in range(3)] for t in range(2)]
        v16 = [sb("v16_%d" % i, [128, 16, 16]) for i in range(2)]
        i16u = [sb("i16u_%d" % i, [128, 16, 16], U32) for i in range(2)]
        i16f = sb("i16f", [128, 16, 16])
        cand = sb("cand", [128, 8, 256])
        oh = cand[:].rearrange("p h (a b) -> p h a b", b=16)
        tops = [sb("top%d" % i, [128, 8, 16]) for i in range(2)]
        pu = sb("pu", [128, 8, 16], U32)
        abu = sb("abu", [128, 2, 8, 16], U32)
        abf = sb("abf", [128, 2, 8, 16])
        wsels = [sb("wsel%d" % i, [128, 3, 128]) for i in range(2)]
        wselb = sb("wselb", [128, 3, 128], BF16)
        zs = sb("zs", [128, 8, 2])
        sT = [sb("sT%d" % i, [128, 3, TG], BF16) for i in range(2)]
        wbf = [sb("pwbf%d" % i, [128, 8, 128], BF16) for i in range(2)]
        ubf = [sb("ubf%d" % i, [128, 1024], BF16) for i in range(3)]
        vbf = [sb("vbf%d" % i, [128, 1024], BF16) for i in range(4)]
        hg = [sb("hg%d" % i, [128, TG], BF16) for i in range(3)]
        AT = [sb("AT%d" % i, [128, TG], BF16) for i in range(3)]
        P1 = [sb("P1_%d" % i, [128, 4, 128], BF16) for i in range(4)]
        P2 = [sb("P2_%d" % i, [128, 4, 128], BF16) for i in range(4)]

        self.dma(iota_row[:], din["iota_row"], writes=[B("iota_rowp")])
        self.dma(iota16[:], din["iota16"], writes=[B("iota16p")])
        self.cp("dve", iota_rb[:], iota_row[:], [B("iota_rowp")], [B("iota_rb")])
        for c4 in range(4):
            kst = ptmp[:, 0:512].rearrange("p (c k) -> p c k", k=128)
            self.dma(kst, din["keysT"][:, c4 * 4:(c4 + 1) * 4, :], writes=[B("ptmp0")])
            self.cp("dve", keysT[:, c4 * 4:(c4 + 1) * 4, :], kst, [B("ptmp0")], [B("keysT")])

        psT = self.psA[0][:].bitcast(BF16).rearrange("p (c n) -> p c n", n=128)
        psG = self.psA[0][:].rearrange("p (t j) -> p t j", j=128)
        psH = [self.psB[:, 0:TG], self.psB[:, 512:512 + TG], self.psA[1][:, 0:TG]]
        psHB = [B("psB0"), B("psB1"), B("psA1")]
        psO = [self.psC, self.psD]
        psOB = [B("psC"), B("psD")]
        ngroups = int(os.environ.get("PGROUPS", "6"))
        nj = int(os.environ.get("PNJ", "128"))

        def prep_a(g):
            v = 1 if g < 4 else 0
            XTg, XB = XT[g % 2], B("XT%d" % (g % 2))
            for tt in range(2):
                rows = slice((2 * g + tt) * 128, (2 * g + tt + 1) * 128)
                xB = B("x1_0")
                self.dma(x1[:], self.y[rows, :], writes=[xB])
                tf = ptmp
                self.memset("dve", self.small[:, 0:1], 0.0, [B("ss")])
                self.act(tf[:], x1[:], AF.Square, [xB], [B("ptmp0"), B("ss")], accum_out=self.small[:, 0:1])
                self.act(self.small[:, 1:2], self.small[:, 0:1], AF.Sqrt, [B("ss")], [B("sd")], scale=1.0 / D, bias=EPS)
                self.recip(self.small[:, 2:3], self.small[:, 1:2], [B("sd")], [B("rstd")])
                self.stt("dve", tf[:], x1[:], self.small[:, 2:3], self.modsl(v, 4), ALU.mult, ALU.mult,
                         [xB, B("rstd"), B("mod")], [B("ptmp0")])
                self.tt("dve", xm[:], tf[:], self.modsl(v, 3), ALU.add, [B("ptmp0"), B("mod")], [B("xm")])
                for kc in range(8):
                    self.tr(psT[:, kc, :], xm[:, kc * 128:(kc + 1) * 128], self.ident_b[:], [B("xm"), B("ident_b")], [B("psA0")])
                self.cp("act", XTg[:, :, tt * 128:(tt + 1) * 128], psT, [B("psA0")], [XB])

        def prep(g):
            XTg, XB = XT[g % 2], B("XT%d" % (g % 2))

            def wload(c):
                i = c % 2
                self.dma(wbf[i][:], scrQ[c].rearrange("p (kc n) -> p kc n", n=128), reads=[B("scrQ%d" % c)], writes=[B("pwbf%d" % i)])

            def scores_mm(c):
                for tt in range(2):
                    self.mm(self.psA[0][:, 256 + tt * 128:256 + (tt + 1) * 128], qTc[c % 2][:, tt * 128:(tt + 1) * 128], keysT[:, c, :], True, True,
                            [B("qTc%d" % (c % 2)), B("keysT")], [B("psA0")])

            def scores_cp(c):
                for tt in range(2):
                    self.cp("dve", SscR[tt][c % 3][:], self.psA[0][:, 256 + tt * 128:256 + (tt + 1) * 128], [B("psA0")], [B("Ssc%d_%d" % (tt, c % 3))])

            def level1(c):
                for tt in range(2):
                    S = SscR[tt][c % 3]
                    SB = B("Ssc%d_%d" % (tt, c % 3))
                    vB, iB = B("v16_%d" % tt), B("i16u_%d" % tt)
                    for half8 in range(2):
                        vs = v16[tt][:, c, half8 * 8:(half8 + 1) * 8]
                        iu = i16u[tt][:, c, half8 * 8:(half8 + 1) * 8]
                        self.S.op("dve", lambda q, vs=vs, S=S: q.max(out=vs, in_=S[:]), [SB], [vB])
                        self.S.op("dve", lambda q, vs=vs, S=S, iu=iu: q.max_index(out=iu, in_max=vs, in_values=S[:]), [SB, vB], [iB])
                        if half8 == 0:
                            self.S.op("dve", lambda q, vs=vs, S=S: q.match_replace(out=S[:], in_to_replace=vs, in_values=S[:], imm_value=NEG),
                                      [SB, vB], [SB])

            wload(0)
            for c in range(17):
                if c + 1 < 16:
                    wload(c + 1)
                if c < 16:
                    i = c % 2
                    for kc in range(8):
                        self.mm(self.psA[0][:, 0:TG], wbf[i][:, kc, :], XTg[:, kc, :], kc == 0, kc == 7, [B("pwbf%d" % i), XB], [B("psA0")])
                if c > 0:
                    scores_mm(c - 1)
                if c < 16:
                    self.cp("dve", qTc[c % 2][:], self.psA[0][:, 0:TG], [B("psA0")], [B("qTc%d" % (c % 2))])
                if c > 0:
                    scores_cp(c - 1)
                    level1(c - 1)
                yield 4.2 if c > 0 else 0.5
            for tt in range(2):
                top, topB = tops[tt], B("top%d" % tt)
                wsel, wselB = wsels[tt], B("wsel%d" % tt)
                vB, iB = B("v16_%d" % tt), B("i16u_%d" % tt)
                self.cp("dve", i16f[:], i16u[tt][:], [iB], [B("i16f")])
                v16r = v16[tt][:].rearrange("p (h f) k -> p h f k", f=2)
                i16r = i16f[:].rearrange("p (h f) k -> p h f k", f=2)
                cand4 = cand[:].rearrange("p h (a b) -> p h a b", b=16)
                self.tt("dve", cand4, v16r[:, :, 0, :].unsqueeze(3).to_broadcast([128, 8, 16, 16]),
                        v16r[:, :, 1, :].unsqueeze(2).to_broadcast([128, 8, 16, 16]), ALU.add, [vB], [B("cand")])
                yield 2.6
                for h in range(8):
                    for half8 in range(2):
                        vs = top[:, h, half8 * 8:(half8 + 1) * 8]
                        self.S.op("dve", lambda q, vs=vs, h=h: q.max(out=vs, in_=cand[:, h, :]), [B("cand")], [topB])
                        self.S.op("dve", lambda q, vs=vs, h=h, half8=half8: q.max_index(out=pu[:, h, half8 * 8:(half8 + 1) * 8], in_max=vs, in_values=cand[:, h, :]),
                                  [B("cand"), topB], [B("pu")])
                        if half8 == 0:
                            self.S.op("dve", lambda q, vs=vs, h=h: q.match_replace(out=cand[:, h, :], in_to_replace=vs, in_values=cand[:, h, :], imm_value=NEG),
                                      [B("cand"), topB], [B("cand")])
                    yield 2.4
                self.S.op("dve", lambda q: q.tensor_single_scalar(out=abu[:, 0], in_=pu[:], scalar=4, op=ALU.logical_shift_right), [B("pu")], [B("abu")])
                self.S.op("dve", lambda q: q.tensor_single_scalar(out=abu[:, 1], in_=pu[:], scalar=15, op=ALU.bitwise_and), [B("pu")], [B("abu")])
                self.cp("dve", abf[:], abu[:], [B("abu")], [B("abf")])
                yield 1.0
                wsel4 = wsel[:].rearrange("p w (h k) -> p w h k", k=16)
                for f in range(2):
                    self.tt("dve", oh[:], abf[:, f].unsqueeze(3).to_broadcast([128, 8, 16, 16]),
                            iota16[:].unsqueeze(1).unsqueeze(1).to_broadcast([128, 8, 16, 16]), ALU.is_equal,
                            [B("abf"), B("iota16p")], [B("cand")])
                    self.tt("dve", oh[:], oh[:], i16r[:, :, f, :].unsqueeze(2).to_broadcast([128, 8, 16, 16]), ALU.mult,
                            [B("cand"), B("i16f")], [B("cand")])
                    self.S.op("dve", lambda q, f=f, wsel4=wsel4: q.tensor_reduce(out=wsel4[:, f], in_=oh[:], axis=AX.X, op=ALU.add), [B("cand")], [wselB])
                    yield 6.6
            yield ("wait", 2.0)
            prep_tail(g)
            yield 3.0

        def prep_tail(g):
            sTg, sTB = sT[g % 2], B("sT%d" % (g % 2))
            for tt in range(2):
                top, topB = tops[tt], B("top%d" % tt)
                wsel, wselB = wsels[tt], B("wsel%d" % tt)
                wsel4 = wsel[:].rearrange("p w (h k) -> p w h k", k=16)
                self.cp("dve", zs[:, :, 0:1], top[:, :, 0:1], [topB], [B("zs")])
                self.tt("dve", top[:], top[:], zs[:, :, 0:1].to_broadcast([128, 8, 16]), ALU.subtract, [topB, B("zs")], [topB])
                self.act(top[:], top[:], AF.Exp, [topB], [topB])
                self.S.op("dve", lambda q, top=top: q.tensor_reduce(out=zs[:, :, 0], in_=top[:], axis=AX.X, op=ALU.add), [topB], [B("zs")])
                self.recip(zs[:, :, 1], zs[:, :, 0], [B("zs")], [B("zs")])
                self.tt("dve", wsel4[:, 2], top[:], zs[:, :, 1:2].to_broadcast([128, 8, 16]), ALU.mult, [topB, B("zs")], [wselB])
                self.cp("dve", wselb[:], wsel[:], [wselB], [B("wselb")])
                for w in range(3):
                    self.tr(psT[:, w, :], wselb[:, w, :], self.ident_b[:], [B("wselb"), B("ident_b")], [B("psA0")])
                self.cp("act", sTg[:, :, tt * 128:(tt + 1) * 128], psT[:, 0:3, :], [B("psA0")], [sTB])

        def gconstruct(g, tt, dst, ev="dve"):
            sTg, sTB = sT[g % 2], B("sT%d" % (g % 2))
            Gd, GB = G3[dst], B("G3_%d" % dst)
            io4 = iota_rb[:].unsqueeze(1).to_broadcast([128, 4, 128])
            nb = 32

            def dve_part(bi):
                r = bi % 4
                n0 = tt * 128 + bi * 4
                bc = lambda w: sTg[:, w, n0:n0 + 4].unsqueeze(2).to_broadcast([128, 4, 128])
                self.tt("dve", P1[r][:], io4, bc(0), ALU.is_equal, [B("iota_rb"), sTB], [B("P1_%d" % r)])
                self.tt("dve", P2[r][:], io4, bc(1), ALU.is_equal, [B("iota_rb"), sTB], [B("P2_%d" % r)])
                self.tt("dve", P1[r][:], P1[r][:], bc(2), ALU.mult, [B("P1_%d" % r), sTB], [B("P1_%d" % r)])

            def pe_part(bi):
                r = bi % 4
                for k in range(4):
                    self.mm(psG[:, k, :], P1[r][:, k, :], P2[r][:, k, :], True, True, [B("P1_%d" % r), B("P2_%d" % r)], [B("psA0")])
                self.cp(ev, Gd[:, bi * 4:bi * 4 + 4, :], psG, [B("psA0")], [GB])

            for u in range(nb // 2 + 1):
                if u > 0:
                    pe_part(2 * u - 2)
                    pe_part(2 * u - 1)
                if u < nb // 2:
                    dve_part(2 * u)
                    dve_part(2 * u + 1)
                yield 4.9 if u < nb // 2 else 1.2

        def run_all(gen):
            for _ in gen:
                pass

        def main(g, Ta, Tb, inter):
            XTg, XB = XT[g % 2], B("XT%d" % (g % 2))
            Gt = [G3[Ta], G3[Tb]]
            GtB = [B("G3_%d" % Ta), B("G3_%d" % Tb)]

            def load(j):
                self.dma(ubf[j % 3][:], scrU[j], reads=[B("scrU%d" % j)], writes=[B("ubf%d" % (j % 3))])
                self.dma(vbf[j % 4][:], scrV[j], reads=[B("scrV%d" % j)], writes=[B("vbf%d" % (j % 4))])

            def Hm(j):
                ib = j % 3
                u3 = ubf[ib][:].rearrange("p (c i) -> p c i", i=128)
                for kc in range(8):
                    self.mm(psH[j % 3], u3[:, kc, :], XTg[:, kc, :], kc == 0, kc == 7, [B("ubf%d" % ib), XB], [psHB[j % 3]])

            def post(j):
                i3 = j % 3
                self.act(hg[i3][:], psH[i3], AF.Gelu_apprx_tanh, [psHB[i3]], [B("hg%d" % i3)])
                for tt in range(2):
                    cs = slice(tt * 128, (tt + 1) * 128)
                    self.tt("pool", AT[i3][:, cs], hg[i3][:, cs], Gt[tt][:, :, j], ALU.mult, [B("hg%d" % i3), GtB[tt]], [B("AT%d" % i3)])

            def outm(j):
                i3, ib = j % 3, j % 4
                for tt in range(2):
                    for half in range(2):
                        cs = slice(half * 512, (half + 1) * 512)
                        self.mm(psO[tt][:, cs], AT[i3][:, tt * 128:(tt + 1) * 128], vbf[ib][:, cs], j == 0, j == nj - 1,
                                [B("AT%d" % i3), B("vbf%d" % ib)], [psOB[tt]])

            W = 0.0
            waiting = None
            for j0 in range(min(3, nj)):
                load(j0)
            Hm(0)
            if nj > 1:
                Hm(1)
            for j in range(nj):
                if j + 3 < nj:
                    load(j + 3)
                if j + 2 < nj:
                    Hm(j + 2)
                post(j)
                outm(j)
                if inter is None:
                    continue
                budget = (j + 1) * 1.8 * 0.9
                emitted = 0
                while inter is not None and emitted < 1:
                    if waiting is not None:
                        if W + waiting > budget:
                            break
                        waiting = None
                    if W > budget:
                        break
                    c = next(inter, "done")
                    if c == "done":
                        inter = None
                    elif isinstance(c, tuple):
                        waiting = c[1]
                    else:
                        W += c
                        emitted += 1
            if inter is not None:
                run_all(inter)

        def epilogue(g):
            v = 1 if g < 4 else 0
            for tt in range(2):
                rows = slice((2 * g + tt) * 128, (2 * g + tt + 1) * 128)
                tf, tB = ptmp, B("ptmp0")
                self.dma(x1[:], self.y[rows, :], writes=[B("x1_0")])
                self.tt("dve", tf[:], psO[tt][:], self.modsl(v, 5), ALU.mult, [psOB[tt], B("mod")], [tB])
                self.tt("dve", x1[:], tf[:], x1[:], ALU.add, [tB, B("x1_0")], [B("x1_0")])
                self.dma(self.y[rows, :], x1[:], reads=[B("x1_0")])

        def chain(*gens):
            for gn in gens:
                for item in gn:
                    yield item

        prep_a(0)
        run_all(prep(0))
        run_all(gconstruct(0, 0, 0, "act"))
        run_all(gconstruct(0, 1, 1, "act"))
        Ta, Tb, Fr = 0, 1, 2
        for g in range(ngroups):
            if g + 1 < ngroups:
                prep_a(g + 1)
                main(g, Ta, Tb, chain(prep(g + 1), gconstruct(g + 1, 0, Fr)))
                epilogue(g)
                run_all(gconstruct(g + 1, 1, Ta, "act"))
                Ta, Tb, Fr = Fr, Ta, Tb
            else:
                main(g, Ta, Tb, None)
                epilogue(g)


def _prep_inputs(inp):
    f = lambda a: np.ascontiguousarray(np.asarray(a, dtype=np.float32))
    consts = _host_constants()
    shared = dict(consts)
    w_in = f(inp["w_in"][0])
    shared["w_ada"] = f(inp["w_ada"][0])
    shared["b_ada"] = f(inp["b_ada"][0]).reshape(1, 6144)
    shared["n1g"] = f(inp["norm1_g"][0]).reshape(1, 1024)
    shared["n2g"] = f(inp["norm2_g"][0]).reshape(1, 1024)
    shared["w_in"] = w_in
    perm = _rope_partner_perm()
    shared["w_sw"] = f(np.concatenate([w_in[:, 0:512][:, perm], w_in[:, 512:1024][:, perm]], axis=1))
    shared["dec"] = f(np.concatenate([inp["ret_decay_f"][0], inp["ret_decay_b"][0]])).reshape(1, 16)
    shared["gn_g"] = f(np.asarray(inp["ret_gn_g"][0]).reshape(4, 128).T)
    qg = np.tile(np.asarray(inp["na_qn_g"][0]), 2)
    kg = np.tile(np.asarray(inp["na_kn_g"][0]), 2)
    shared["qkn_g"] = f(np.stack([qg, kg], axis=1))
    shared["kn_row"] = f(inp["na_kn_g"][0]).reshape(1, 64)
    rpb = np.asarray(inp["na_rpb"][0], dtype=np.float32)
    kc = np.arange(64)[:, None]
    qc = np.arange(64)[None, :]
    dc = np.clip(kc - qc + 15, 0, 30)
    x = np.arange(15)
    tr = rpb[:, (14 - x)[:, None, None], dc[None, :, :]]
    tr = np.transpose(tr, (2, 0, 1, 3))
    shared["rpbT"] = f(np.concatenate([tr, tr], axis=0))
    shared["w_out"] = f(inp["w_out"][0])
    shared["wq"] = f(inp["peer_wq"][0])
    keys = np.asarray(inp["peer_keys"][0], dtype=np.float32)
    shared["keysT"] = f(np.transpose(keys.reshape(16, 128, 128), (2, 0, 1)))
    U = np.asarray(inp["peer_u"][0], dtype=np.float32)
    shared["Ut"] = f(np.transpose(U.reshape(128, 128, 8, 128), (1, 3, 2, 0)))
    V = np.asarray(inp["peer_v"][0], dtype=np.float32)
    shared["Vp"] = f(np.transpose(V.reshape(128, 128, 1024), (1, 0, 2)))
    xp = np.asarray(inp["x_prompt"], dtype=np.float32)
    xs = np.asarray(inp["x_sample"], dtype=np.float32)
    cc = np.asarray(inp["c"], dtype=np.float32)
    cctx = np.asarray(inp["c_ctx"], dtype=np.float32)
    maps = []
    for c in range(NCORES):
        m = dict(shared)
        m["x"] = f(np.concatenate([xs[c], xp[2 * c], xp[2 * c + 1]], axis=0))
        cv = np.stack([cctx, cc[c]], axis=0)
        m["cT"] = f(np.transpose(cv.reshape(2, 8, 128), (2, 1, 0)))
        m["kctxT"] = f(np.asarray(inp["cache_na_k"][c, 0], dtype=np.float32).reshape(256, 512).T)
        m["vctx"] = f(np.asarray(inp["cache_na_v"][c, 0], dtype=np.float32).reshape(256, 512))
        m["s0"] = f(inp["state_ret"][c, 0])
        maps.append(m)
    return maps


_NC_CACHE = {}


def _get_nc(stage=3):
    if stage not in _NC_CACHE:
        _NC_CACHE[stage] = Builder(stage=stage).build()
    return _NC_CACHE[stage]


def kernel(**inputs):
    maps = _prep_inputs(inputs)
    nc = _get_nc()
    res = run_bass_kernel_spmd(nc, maps, core_ids=list(range(NCORES)))
    outs = res.results
    y_p = np.zeros((16, 256, 1024), np.float32)
    y_s = np.zeros((8, 1024, 1024), np.float32)
    nk = np.zeros((16, 1, 256, 8, 64), np.float32)
    nv = np.zeros((16, 1, 256, 8, 64), np.float32)
    st = np.zeros((16, 1, 2, 8, 64, 64), np.float32)
    for c in range(NCORES):
        r = outs[c]
        y = r["y"]
        y_s[c] = y[0:1024]
        y_p[2 * c] = y[1024:1280]
        y_p[2 * c + 1] = y[1280:1536]
        nk[2 * c:2 * c + 2, 0] = r["nk"].reshape(2, 256, 8, 64)
        nv[2 * c:2 * c + 2, 0] = r["nv"].reshape(2, 256, 8, 64)
        st[2 * c:2 * c + 2, 0] = r["st"]
    return (y_p, y_s, nk, nv, st)
```

```python
import math
import os
from contextlib import ExitStack

import numpy as np
import concourse.bass as bass
import concourse.mybir as mybir
from concourse.bass_utils import run_bass_kernel_spmd

F32 = mybir.dt.float32
BF16 = mybir.dt.bfloat16
I32 = mybir.dt.int32
U32 = mybir.dt.uint32
ALU = mybir.AluOpType
AF = mybir.ActivationFunctionType
AX = mybir.AxisListType

NCORES = 8
D = 1024
EPS = 1e-6
NEG = -1e30


class Buf:
    __slots__ = ("name", "w", "r")

    def __init__(self, name):
        self.name = name
        self.w = None
        self.r = []


class Eng:
    def __init__(self, name, sem, same_sync):
        self.name = name
        self.sem = sem
        self.count = 0
        self.waited = {}
        self.same_sync = same_sync
        self.prog = []


class Sched:
    def __init__(self, nc, stack, n_dma_slots=int(os.environ.get("NDMA", "24"))):
        self.nc = nc
        self.sems = {}
        self.engs = {}
        for name, same in (("pe", False), ("act", True), ("dve", True), ("pool", True), ("sp", True)):
            sem = stack.enter_context(nc.semaphore("s_" + name))
            self.sems[name] = sem
            self.engs[name] = Eng(name, sem, same)
        self.dma_slots = []
        for i in range(n_dma_slots):
            key = "dma%d" % i
            self.sems[key] = stack.enter_context(nc.semaphore("s_" + key))
            self.dma_slots.append([key, 0])
        self.dma_i = 0
        self.bg_slots = []
        for i in range(16):
            key = "bg%d" % i
            self.sems[key] = stack.enter_context(nc.semaphore("s_" + key))
            self.bg_slots.append([key, 0])
        self.bg_i = 0
        self.n_inst = 0

    def _wait(self, e, tok):
        if tok is None:
            return
        key, val = tok
        if key == e.name and not e.same_sync:
            return
        if e.waited.get(key, 0) >= val:
            return
        e.waited[key] = val
        sem = self.sems[key]
        e.prog.append(lambda q, sem=sem, val=val: q.wait_ge(sem, val))

    def _deps(self, e, reads, writes):
        for b in reads:
            self._wait(e, b.w)
            if b.name.startswith("ps"):
                for t in b.r:
                    if t[0] != e.name:
                        self._wait(e, t)
        for b in writes:
            self._wait(e, b.w)
            for t in b.r:
                self._wait(e, t)

    @staticmethod
    def _mark(tok, reads, writes):
        for b in reads:
            b.r.append(tok)
            if len(b.r) > 64:
                b.r = b.r[-64:] if False else b.r
        for b in writes:
            b.w = tok
            b.r = []

    def op(self, eng, fn, reads=(), writes=()):
        e = self.engs[eng]
        self._deps(e, reads, writes)
        e.count += 1
        tok = (e.name, e.count)
        sem = e.sem
        e.prog.append(lambda q, fn=fn, sem=sem: fn(q).then_inc(sem, 1))
        self._mark(tok, reads, writes)
        self.n_inst += 1
        return tok

    def dma(self, eng, out, in_, reads=(), writes=(), bg=False, **kw):
        e = self.engs[eng]
        self._deps(e, reads, writes)
        if bg:
            slot = self.bg_slots[self.bg_i % len(self.bg_slots)]
            self.bg_i += 1
        else:
            slot = self.dma_slots[self.dma_i % len(self.dma_slots)]
            self.dma_i += 1
        key = slot[0]
        if slot[1] > 0:
            self._wait(e, (key, slot[1]))
        slot[1] += 16
        tok = (key, slot[1])
        sem = self.sems[key]
        e.prog.append(lambda q, out=out, in_=in_, sem=sem, kw=kw:
                      q.dma_start(out=out, in_=in_, **kw).then_inc(sem, 16))
        self._mark(tok, reads, writes)
        self.n_inst += 1
        return tok

    def barrier(self):
        for e in self.engs.values():
            for key, val in self.dma_slots + self.bg_slots:
                if val > 0:
                    self._wait(e, (key, val))
            for o in self.engs.values():
                if o is not e and o.count > 0:
                    self._wait(e, (o.name, o.count))

    def emit(self):
        nc = self.nc
        progs = {k: v.prog for k, v in self.engs.items()}
        with nc.Block() as block:
            @block.tensor
            def _(q):
                for f in progs["pe"]:
                    f(q)

            @block.scalar
            def _(q):
                for f in progs["act"]:
                    f(q)

            @block.vector
            def _(q):
                for f in progs["dve"]:
                    f(q)

            @block.gpsimd
            def _(q):
                for f in progs["pool"]:
                    f(q)

            @block.sync
            def _(q):
                for f in progs["sp"]:
                    f(q)


def _rope_tables():
    T = 1024
    n = np.arange(T)
    rows = (n // 64).astype(np.float32)
    cols = (n % 64).astype(np.float32)
    freqs = (np.float32(10000.0) ** (-np.arange(16, dtype=np.float32) / np.float32(16))).astype(np.float32)
    C = np.zeros((128, T), np.float32)
    Sg = np.zeros((128, T), np.float32)
    for p in range(128):
        d = p % 64
        pos = rows if d < 32 else cols
        dd = d % 32
        f = dd % 16
        ang = (pos * freqs[f]).astype(np.float32)
        C[p] = np.cos(ang)
        Sg[p] = -np.sin(ang) if dd < 16 else np.sin(ang)
    return C, Sg


def _rope_partner_perm():
    perm = np.zeros(512, np.int64)
    for h in range(8):
        for d in range(64):
            dd = d % 32
            partner = d + 16 if dd < 16 else d - 16
            perm[h * 64 + d] = h * 64 + partner
    return perm


def _na_windows():
    rows, kh = 16, 8
    q_of_k = {}
    for kr in range(rows):
        qs = [qr for qr in range(rows) if min(max(qr - kh // 2, 0), rows - kh) <= kr < min(max(qr - kh // 2, 0), rows - kh) + kh]
        assert qs == list(range(qs[0], qs[-1] + 1))
        q_of_k[kr] = (qs[0], qs[-1])
    return q_of_k


def _host_constants():
    c = {}
    c["ident"] = np.eye(128, dtype=np.float32)
    ob = np.zeros((128, 128), np.float32)
    ob[:64, :64] = 1.0 / 64
    ob[64:, 64:] = 1.0 / 64
    c["onesbd"] = ob
    op = np.zeros((128, 2, 128), np.float32)
    op[:, 0, :64] = 1.0
    op[:, 1, 64:] = 1.0
    c["onespad"] = op
    p = np.arange(128, dtype=np.float32)[:, None]
    j = np.arange(1920, dtype=np.float32)[None, :]
    c["expo"] = (j - 896.0 - p).astype(np.float32)
    c["iota_n1"] = np.broadcast_to(np.arange(1, 1025, dtype=np.float32)[None], (128, 1024)).copy()
    c["iota_rev"] = np.broadcast_to((1024 - np.arange(1024, dtype=np.float32))[None], (128, 1024)).copy()
    tq = np.zeros((128, 2, 2), np.float32)
    for t in range(2):
        tq[:, 0, t] = 255 - (t * 128 + np.arange(128))
        tq[:, 1, t] = t * 128 + np.arange(128)
    c["tq"] = tq
    C, Sg = _rope_tables()
    c["rope_c"] = C
    c["rope_s"] = Sg
    qc = np.arange(64)
    cstart = np.clip(qc - 8, 0, 48)
    kc = np.arange(64)
    inwin = (kc[:, None] >= cstart[None, :]) & (kc[:, None] < cstart[None, :] + 16)
    cm = np.where(inwin, 0.0, NEG).astype(np.float32)
    c["cmask"] = np.concatenate([cm, cm], axis=0)
    c["iota_row"] = np.broadcast_to(np.arange(128, dtype=np.float32)[None], (128, 128)).copy()
    c["iota16"] = np.broadcast_to(np.arange(16, dtype=np.float32)[None], (128, 16)).copy()
    return c


CONST_SHAPES = {
    "ident": [128, 128], "onesbd": [128, 128], "onespad": [128, 2, 128], "expo": [128, 1920],
    "iota_n1": [128, 1024], "iota_rev": [128, 1024], "tq": [128, 2, 2], "rope_c": [128, 1024],
    "rope_s": [128, 1024], "cmask": [128, 64], "iota_row": [128, 128], "iota16": [128, 16],
}

IN_SHAPES = {
    "x": [1536, 1024], "cT": [128, 8, 2], "w_ada": [1024, 6144], "b_ada": [1, 6144],
    "n1g": [1, 1024], "n2g": [1, 1024], "w_in": [1024, 3584], "w_sw": [1024, 1024], "dec": [1, 16],
    "gn_g": [128, 4], "qkn_g": [128, 2], "kn_row": [1, 64], "rpbT": [128, 8, 15, 64],
    "w_out": [1024, 1024], "wq": [1024, 2048], "keysT": [128, 16, 128],
    "Ut": [128, 128, 8, 128], "Vp": [128, 128, 1024],
    "kctxT": [512, 256], "vctx": [256, 512], "s0": [2, 8, 64, 64],
}

SEQS = [(0, 8, 1, True, [(0, 8)]), (8, 4, 0, False, [(0, 2), (2, 2)])]


class Builder:
    def __init__(self, stage=3, debug=False):
        self.stage = stage
        self.debug = debug
        self.nc = bass.Bass("TRN2", target_bir_lowering=False)
        nc = self.nc
        self.din = {}
        for name, shp in list(IN_SHAPES.items()) + list(CONST_SHAPES.items()):
            self.din[name] = nc.dram_tensor(name, shp, F32, kind="ExternalInput").ap()
        self.y = nc.dram_tensor("y", [1536, 1024], F32, kind="ExternalOutput").ap()
        self.nk = nc.dram_tensor("nk", [512, 512], F32, kind="ExternalOutput").ap()
        self.nv = nc.dram_tensor("nv", [512, 512], F32, kind="ExternalOutput").ap()
        self.st = nc.dram_tensor("st", [2, 2, 8, 64, 64], F32, kind="ExternalOutput").ap()
        self.bufs = {}

    def bg_issue(self, n, dep=None):
        for _ in range(n):
            if not self.bg_todo:
                return
            out, in_, name = self.bg_todo.pop(0)
            self.dma(out, in_, reads=[self.B(dep)] if dep else [], writes=[self.B(name)], eng="pool", bg=True)

    def B(self, name):
        b = self.bufs.get(name)
        if b is None:
            b = self.bufs[name] = Buf(name)
        return b

    def sb(self, st, name, shape, dt=F32):
        return st.enter_context(self.nc.sbuf_tensor("sb_" + name, shape, dt))

    def mm(self, out, lhsT, rhs, start, stop, reads, writes, skip=False):
        if skip:
            self.S.op("pe", lambda q: q.matmul(out, lhsT=lhsT, rhs=rhs, start=start, stop=stop, skip_group_check=True), reads, writes)
        else:
            self.S.op("pe", lambda q: q.matmul(out, lhsT=lhsT, rhs=rhs, start=start, stop=stop), reads, writes)

    def tr(self, out, in_, ident, reads, writes):
        self.S.op("pe", lambda q: q.transpose(out=out, in_=in_, identity=ident), reads, writes)

    def act(self, out, in_, func, reads, writes, **kw):
        self.S.op("act", lambda q: q.activation(out=out, in_=in_, func=func, **kw), reads, writes)

    def tt(self, eng, out, in0, in1, op, reads, writes):
        self.S.op(eng, lambda q: q.tensor_tensor(out=out, in0=in0, in1=in1, op=op), reads, writes)

    def ts(self, eng, out, in0, s1, s2, op0, op1, reads, writes):
        if s2 is None:
            self.S.op(eng, lambda q: q.tensor_scalar(out=out, in0=in0, scalar1=s1, scalar2=None, op0=op0), reads, writes)
        else:
            self.S.op(eng, lambda q: q.tensor_scalar(out=out, in0=in0, scalar1=s1, scalar2=s2, op0=op0, op1=op1), reads, writes)

    def stt(self, eng, out, in0, scalar, in1, op0, op1, reads, writes):
        self.S.op(eng, lambda q: q.scalar_tensor_tensor(out=out, in0=in0, scalar=scalar, in1=in1, op0=op0, op1=op1), reads, writes)

    def cp(self, eng, out, in_, reads, writes):
        if eng == "act":
            self.S.op("act", lambda q: q.copy(out=out, in_=in_), reads, writes)
        else:
            self.S.op(eng, lambda q: q.tensor_copy(out=out, in_=in_), reads, writes)

    def memset(self, eng, ap, val, writes):
        self.S.op(eng, lambda q: q.memset(ap, val), (), writes)

    def recip(self, out, in_, reads, writes):
        self.S.op("dve", lambda q: q.reciprocal(out=out, in_=in_), reads, writes)

    def dma(self, out, in_, reads=(), writes=(), eng="sp", bg=False):
        self.S.dma(eng, out, in_, reads, writes, bg=bg)

    def build(self):
        nc = self.nc
        with ExitStack() as top:
            self.S = Sched(nc, top)
            self.psA = [top.enter_context(nc.psum_tensor("psA%d" % i, [128, 512], F32)) for i in range(2)]
            self.psB = top.enter_context(nc.psum_tensor("psB", [128, 1024], F32))
            self.psC = top.enter_context(nc.psum_tensor("psC", [128, 1024], F32))
            self.psD = top.enter_context(nc.psum_tensor("psD", [128, 1024], F32))
            self.scrU = nc.dram_tensor("scrU", [128, 128, 1024], BF16).ap()
            self.scrV = nc.dram_tensor("scrV", [128, 128, 1024], BF16).ap()
            self.scrQ = nc.dram_tensor("scrQ", [16, 128, 1024], BF16).ap()
            self.bg_todo = []
            if self.stage >= 3:
                Ut2 = self.din["Ut"].rearrange("j p c i -> j p (c i)")
                wq4 = self.din["wq"].rearrange("(kc p) (c n) -> c p kc n", p=128, n=128)
                for c in range(16):
                    self.bg_todo.append((self.scrQ[c].rearrange("p (kc n) -> p kc n", n=128), wq4[c], "scrQ%d" % c))
                for j in range(128):
                    self.bg_todo.append((self.scrU[j], Ut2[j], "scrU%d" % j))
                    self.bg_todo.append((self.scrV[j], self.din["Vp"][j], "scrV%d" % j))
            self.mod = self.sb(top, "mod", [128, 2, 6144], BF16)
            self.ident_b = self.sb(top, "ident_b", [128, 128], BF16)
            self.small = self.sb(top, "small", [128, 64], F32)
            with ExitStack() as ph1:
                self.phase1_alloc(ph1)
                self.setup(ph1)
                with ExitStack() as ws:
                    self.phase1_work(ws)
                    if self.stage >= 1:
                        for si, seq in enumerate(SEQS):
                            self.attn_seq(si, *seq)
                    self.S.barrier()
            if self.stage >= 3:
                with ExitStack() as ph2:
                    self.peer(ph2)
                    self.S.barrier()
            self.S.barrier()
            self.S.emit()
        return nc

    def phase1_alloc(self, st):
        sb = lambda n, s, d=F32: self.sb(st, n, s, d)
        self.ones_bd = sb("ones_bd", [128, 128], BF16)
        self.ones_pad = sb("ones_pad", [128, 2, 128], BF16)
        self.strips = sb("strips", [128, 8, 1920], BF16)
        self.lg = sb("lg", [128, 16])
        self.nlgb = sb("nlgb", [128, 8])
        self.lgcol = sb("lgcol", [128, 2, 4])
        self.wst = sb("wst", [128, 2, 2, 8])
        self.gn_g = sb("gn_g", [128, 4])
        self.qkn_g = sb("qkn_g", [128, 2])
        self.kn_bc = sb("kn_bc", [128, 64])
        self.wo = sb("wo", [128, 8, 1024], BF16)
        self.iota_n1 = sb("iota_n1", [128, 1024])
        self.iota_rev = sb("iota_rev", [128, 1024])
        self.rope_c = sb("rope_c", [128, 1024])
        self.rope_s = sb("rope_s", [128, 1024])
        self.trb = sb("trb", [128, 8, 15, 64], BF16)

    def phase1_work(self, st):
        sb = lambda n, s, d=F32: self.sb(st, n, s, d)
        B = self.B
        self.hT = sb("hT", [128, 8, 1024], BF16)
        self.oT = sb("oT", [128, 8, 1024], BF16)
        self.xb = [sb("xb%d" % i, [128, 1024]) for i in range(2)]
        self.tmpf = [sb("tmpf%d" % i, [128, 1024]) for i in range(2)]
        self.hb = sb("hb", [128, 1024], BF16)
        self.wstage = [sb("wstage%d" % i, [128, 8, 128]) for i in range(3)]
        self.wbf = [sb("wbf%d" % i, [128, 8, 128], BF16) for i in range(3)]
        self.qT = sb("qT", [128, 1024], BF16)
        self.kTm = sb("kTm", [128, 2, 1024], BF16)
        self.sg = sb("sg", [128, 1024], BF16)
        self.vpad = sb("vpad", [128, 8, 2, 128], BF16)
        self.kcm = sb("kcm", [128, 2, 256], BF16)
        self.vcp = sb("vcp", [128, 2, 2, 128], BF16)
        self.qf = sb("qf", [128, 2, 1024], BF16)
        self.s0bd = sb("s0bd", [128, 2, 128], BF16)
        self.s0st = sb("s0st", [128, 2, 128])
        self.sc = [sb("sc%d" % i, [128, 768], BF16) for i in range(3)]
        self.sbias = [sb("sbias%d" % i, [128, 768]) for i in range(2)]
        self.ctxst = sb("ctxst", [128, 256])
        self.kw = sb("kw", [128, 2, 128], BF16)
        self.nko = sb("nko", [128, 128])
        self.nvo = sb("nvo", [128, 128])
        self.sto = sb("sto", [64, 2, 2, 2, 64])
        self.memset("pool", self.kTm[:], 0.0, [B("kTm")])
        self.memset("pool", self.vpad[:], 0.0, [B("vpad")])
        self.memset("pool", self.kcm[:], 0.0, [B("kcm")])
        self.memset("pool", self.vcp[:], 0.0, [B("vcp")])
        self.memset("pool", self.s0st[:], 0.0, [B("s0st")])

    def setup(self, st_outer):
        B, din = self.B, self.din
        with ExitStack() as st:
            sb = lambda n, s, d=F32: self.sb(st, n, s, d)
            identf = sb("identf", [128, 128])
            self.dma(identf[:], din["ident"], writes=[B("identf")])
            self.cp("dve", self.ident_b[:], identf[:], [B("identf")], [B("ident_b")])
            tmp128 = sb("tmp128", [128, 2, 128])
            self.dma(tmp128[:, 0, :], din["onesbd"], writes=[B("tmp128")])
            self.cp("dve", self.ones_bd[:], tmp128[:, 0, :], [B("tmp128")], [B("ones_bd")])
            self.dma(tmp128[:], din["onespad"], reads=[], writes=[B("tmp128")])
            self.cp("dve", self.ones_pad[:], tmp128[:], [B("tmp128")], [B("ones_pad")])
            for nm, t in (("iota_n1", self.iota_n1), ("iota_rev", self.iota_rev), ("rope_c", self.rope_c),
                          ("rope_s", self.rope_s), ("gn_g", self.gn_g), ("qkn_g", self.qkn_g)):
                self.dma(t[:], din[nm], writes=[B(nm)])
            self.dma(self.kn_bc[:], din["kn_row"][0].partition_broadcast(128), writes=[B("kn_bc")])
            dec = sb("dec", [128, 16])
            self.dma(dec[:], din["dec"][0].partition_broadcast(128), writes=[B("dec")])
            self.act(dec[:], dec[:], AF.Exp, [B("dec")], [B("dec")], scale=-1.0)
            self.act(dec[:], dec[:], AF.Ln, [B("dec")], [B("dec")], bias=1.0)
            self.ts("dve", self.lg[:], dec[:], -1.0, None, ALU.mult, None, [B("dec")], [B("lg")])
            self.ts("dve", self.nlgb[:], self.lg[:, 8:16], -1.0, None, ALU.mult, None, [B("lg")], [B("nlgb")])
            for r in range(2):
                self.cp("dve", self.lgcol[0:64, r, :], self.lg[0:64, r * 8:r * 8 + 8:2], [B("lg")], [B("lgcol")])
                self.cp("dve", self.lgcol[64:128, r, :], self.lg[64:128, r * 8 + 1:r * 8 + 8:2], [B("lg")], [B("lgcol")])
            tq = sb("tq", [128, 2, 2])
            self.dma(tq[:], din["tq"], writes=[B("tq")])
            for r in range(2):
                self.tt("dve", self.wst[:, r], tq[:, r, :].unsqueeze(2).to_broadcast([128, 2, 8]),
                        self.lg[:, r * 8:(r + 1) * 8].unsqueeze(1).to_broadcast([128, 2, 8]), ALU.mult,
                        [B("tq"), B("lg")], [B("wst")])
            self.act(self.wst[:], self.wst[:], AF.Exp, [B("wst")], [B("wst")])
            self.ts("dve", self.wst[:], self.wst[:], 0.125, None, ALU.mult, None, [B("wst")], [B("wst")])
            expo = sb("expo", [128, 1920])
            t1 = sb("t1", [128, 1920])
            t2 = sb("t2", [128, 1920])
            self.dma(expo[:], din["expo"], writes=[B("expo")])
            for h in range(8):
                self.ts("dve", t1[:], expo[:], self.lg[:, h:h + 1], None, ALU.mult, None, [B("expo"), B("lg")], [B("t1")])
                self.stt("dve", t2[:], expo[:], self.nlgb[:, h:h + 1], t1[:], ALU.mult, ALU.min,
                         [B("expo"), B("nlgb"), B("t1")], [B("t2")])
                self.act(t2[:], t2[:], AF.Exp, [B("t2")], [B("t2")])
                self.ts("dve", self.strips[:, h, :], t2[:], 0.125, None, ALU.mult, None, [B("t2")], [B("strips")])
            cmask = sb("cmask", [128, 64])
            self.dma(cmask[:], din["cmask"], writes=[B("cmask")])
            for h in range(8):
                trs = t1[:, 0:960].rearrange("p (x q) -> p x q", q=64)
                self.dma(trs, din["rpbT"][:, h], writes=[B("t1")])
                self.tt("dve", self.trb[:, h], trs, cmask[:].unsqueeze(1).to_broadcast([128, 15, 64]), ALU.add,
                        [B("t1"), B("cmask")], [B("trb")])
            for kc in range(8):
                wsl = t2[:, 0:1024]
                self.dma(wsl, din["w_out"][kc * 128:(kc + 1) * 128, :], writes=[B("t2")])
                self.cp("act", self.wo[:, kc, :], wsl, [B("t2")], [B("wo")])
        self.S.barrier()
        self.bg_issue(64)
        self.adaln()
        self.S.barrier()

    def adaln(self):
        B, din = self.B, self.din
        with ExitStack() as st:
            sb = lambda n, s, d=F32: self.sb(st, n, s, d)
            cT = sb("cT", [128, 8, 2])
            rep = sb("rep", [128, 8, 2, 128], BF16)
            wst = [sb("awst%d" % i, [128, 8, 512]) for i in range(2)]
            wbf = [sb("awbf%d" % i, [128, 8, 512], BF16) for i in range(2)]
            bbc = [sb("bbc%d" % i, [128, 512]) for i in range(2)]
            ngb = sb("ngb", [128, 2, 1024])
            self.dma(cT[:], din["cT"], writes=[B("cT")])
            self.act(cT[:], cT[:], AF.Silu, [B("cT")], [B("cT")])
            self.cp("dve", rep[:], cT[:].unsqueeze(3).to_broadcast([128, 8, 2, 128]), [B("cT")], [B("rep")])
            self.dma(ngb[:, 0, :], din["n1g"][0].partition_broadcast(128), writes=[B("ngb")])
            self.dma(ngb[:, 1, :], din["n2g"][0].partition_broadcast(128), writes=[B("ngb")])
            w3 = din["w_ada"].rearrange("(kc p) n -> p kc n", p=128)
            for blk in range(12):
                i = blk % 2
                cs = slice(blk * 512, (blk + 1) * 512)
                self.dma(wst[i][:], w3[:, :, cs], writes=[B("awst%d" % i)])
                self.dma(bbc[i][:], din["b_ada"][0, cs].partition_broadcast(128), writes=[B("bbc%d" % i)])
                self.cp("dve" if blk % 2 == 0 else "act", wbf[i][:], wst[i][:], [B("awst%d" % i)], [B("awbf%d" % i)])
                for v in range(2):
                    ps = self.psA[v]
                    for kc in range(8):
                        self.mm(ps[:], rep[:, kc, v, :], wbf[i][:, kc, :], kc == 0, kc == 7,
                                [B("rep"), B("awbf%d" % i)], [B("psA%d" % v)])
                    self.tt("dve", self.mod[:, v, cs], ps[:], bbc[i][:], ALU.add,
                            [B("psA%d" % v), B("bbc%d" % i)], [B("mod")])
            for v in range(2):
                for j, ch in ((0, 1), (1, 4)):
                    sl = self.mod[:, v, ch * 1024:(ch + 1) * 1024]
                    self.stt("dve", sl, sl, 1.0, ngb[:, j, :], ALU.add, ALU.mult, [B("mod"), B("ngb")], [B("mod")])

    def modsl(self, v, ch):
        return self.mod[:, v, ch * 1024:(ch + 1) * 1024]

    def load_w(self, dram_cols, k):
        B = self.B
        i = self.wcount % 3
        j = self.wcount % 3
        self.wcount += 1
        self.bg_issue(3, "wbf%d" % ((j + 2) % 3))
        self.dma(self.wstage[i][:], dram_cols.rearrange("(kc p) n -> p kc n", p=128), writes=[B("wstage%d" % i)])
        self.cp("act", self.wbf[j][:], self.wstage[i][:], [B("wstage%d" % i)], [B("wbf%d" % j)])
        return self.wbf[j], B("wbf%d" % j)

    def proj_fm(self, w, wB, T, ps, psB_, b0, bn):
        for kc in range(8):
            self.mm(ps[:, 0:bn], w[:, kc, :], self.hT[:, kc, b0:b0 + bn], kc == 0, kc == 7,
                    [wB, self.B("hT")], [psB_])

    def proj_tm(self, w, wB, t, ps, psB_):
        for kc in range(8):
            self.mm(ps[:, 0:128], self.hT[:, kc, t * 128:(t + 1) * 128], w[:, kc, :], kc == 0, kc == 7,
                    [wB, self.B("hT")], [psB_])

    def attn_seq(self, si, tile0, NT, v, is_sample, segs):
        B, din = self.B, self.din
        T = NT * 128
        blocks = [(b0, min(512, T - b0)) for b0 in range(0, T, 512)]
        x_rows = lambda t: slice((tile0 + t) * 128, (tile0 + t + 1) * 128)
        w_in = din["w_in"]
        self.wcount = getattr(self, "wcount", 0)
        psT = self.psA[0][:].bitcast(BF16).rearrange("p (c n) -> p c n", n=128)
        for t in range(NT):
            xb = self.xb[t % 2]
            xB = B("xb%d" % (t % 2))
            self.dma(xb[:], din["x"][x_rows(t), :], writes=[xB])
            tf = self.tmpf[0]
            self.memset("dve", self.small[:, 0:1], 0.0, [B("ss")])
            self.act(tf[:], xb[:], AF.Square, [xB], [B("tmpf0"), B("ss")], accum_out=self.small[:, 0:1])
            self.act(self.small[:, 1:2], self.small[:, 0:1], AF.Sqrt, [B("ss")], [B("sd")], scale=1.0 / D, bias=EPS)
            self.recip(self.small[:, 2:3], self.small[:, 1:2], [B("sd")], [B("rstd")])
            self.stt("dve", tf[:], xb[:], self.small[:, 2:3], self.modsl(v, 1), ALU.mult, ALU.mult,
                     [xB, B("rstd"), B("mod")], [B("tmpf0")])
            self.tt("dve", self.hb[:], tf[:], self.modsl(v, 0), ALU.add, [B("tmpf0"), B("mod")], [B("hb")])
            for kc in range(8):
                self.tr(psT[:, kc, :], self.hb[:, kc * 128:(kc + 1) * 128], self.ident_b[:],
                        [B("hb"), B("ident_b")], [B("psA0")])
            self.cp("act", self.hT[:, :, t * 128:(t + 1) * 128], psT, [B("psA0")], [B("hT")])

        ablocks = []
        for (s0, sn) in segs:
            for q0 in range(s0 * 128, (s0 + sn) * 128, 512):
                ablocks.append((q0, min(512, (s0 + sn) * 128 - q0), list(range(s0, s0 + sn))))
        self.ablocks = ablocks
        for a in range(4):
            self.bg_issue(0)
            col = lambda base: w_in[:, base + a * 128: base + (a + 1) * 128]
            wq_, wqB = self.load_w(col(0), 0)
            if is_sample:
                wqs, wqsB = self.load_w(din["w_sw"][:, a * 128:(a + 1) * 128], 0)
            for bi, (b0, bn) in enumerate(blocks):
                self.proj_fm(wq_, wqB, T, self.psA[0], B("psA0"), b0, bn)
                if is_sample:
                    self.proj_fm(wqs, wqsB, T, self.psA[1], B("psA1"), b0, bn)
                    tf = self.tmpf[0]
                    self.tt("dve", tf[:, 0:bn], self.psA[0][:, 0:bn], self.rope_c[:, b0:b0 + bn], ALU.mult,
                            [B("psA0"), B("rope_c")], [B("tmpf0")])
                    tg = self.tmpf[1]
                    self.tt("dve", tg[:, 0:bn], self.psA[1][:, 0:bn], self.rope_s[:, b0:b0 + bn], ALU.mult,
                            [B("psA1"), B("rope_s")], [B("tmpf1")])
                    self.tt("dve", self.qT[:, b0:b0 + bn], tf[:, 0:bn], tg[:, 0:bn], ALU.add,
                            [B("tmpf0"), B("tmpf1")], [B("qT")])
                else:
                    self.cp("act", self.qT[:, b0:b0 + bn], self.psA[0][:, 0:bn], [B("psA0")], [B("qT")])
            wk_, wkB = self.load_w(col(512), 0)
            if is_sample:
                wks, wksB = self.load_w(din["w_sw"][:, 512 + a * 128:512 + (a + 1) * 128], 0)
            for bi, (b0, bn) in enumerate(blocks):
                self.proj_fm(wk_, wkB, T, self.psA[0], B("psA0"), b0, bn)
                if is_sample:
                    self.proj_fm(wks, wksB, T, self.psA[1], B("psA1"), b0, bn)
                    tf = self.tmpf[0]
                    self.tt("dve", tf[:, 0:bn], self.psA[0][:, 0:bn], self.rope_c[:, b0:b0 + bn], ALU.mult,
                            [B("psA0"), B("rope_c")], [B("tmpf0")])
                    tg = self.tmpf[1]
                    self.tt("dve", tg[:, 0:bn], self.psA[1][:, 0:bn], self.rope_s[:, b0:b0 + bn], ALU.mult,
                            [B("psA1"), B("rope_s")], [B("tmpf1")])
                    for hh in range(2):
                        ps_ = slice(hh * 64, (hh + 1) * 64)
                        self.tt("dve", self.kTm[ps_, hh, b0:b0 + bn], tf[ps_, 0:bn], tg[ps_, 0:bn], ALU.add,
                                [B("tmpf0"), B("tmpf1")], [B("kTm")])
                else:
                    for hh in range(2):
                        ps_ = slice(hh * 64, (hh + 1) * 64)
                        self.cp("act", self.kTm[ps_, hh, b0:b0 + bn], self.psA[0][ps_, 0:bn], [B("psA0")], [B("kTm")])
            wg_, wgB = self.load_w(col(1536), 0)
            for bi, (b0, bn) in enumerate(blocks):
                ps, pB = self.psA[bi % 2], B("psA%d" % (bi % 2))
                self.proj_fm(wg_, wgB, T, ps, pB, b0, bn)
                self.act(self.sg[:, b0:b0 + bn], ps[:, 0:bn], AF.Silu, [pB], [B("sg")])
            wv_, wvB = self.load_w(col(1024), 0)
            for t in range(NT):
                ps, pB = self.psA[t % 2], B("psA%d" % (t % 2))
                self.proj_tm(wv_, wvB, t, ps, pB)
                for hh in range(2):
                    cs = slice(hh * 64, (hh + 1) * 64)
                    self.cp("act", self.vpad[:, t, hh, cs], ps[:, cs], [pB], [B("vpad")])
            if not is_sample:
                for si_, (s0, sn) in enumerate(segs):
                    pS = self.psD[0:64, si_ * 512:si_ * 512 + 256].rearrange("p (r h e) -> p r h e", r=2, h=2)
                    for tr in range(sn):
                        t = s0 + tr
                        ps, pB = self.psA[t % 2], B("psA%d" % (t % 2))
                        self.proj_tm(wk_, wkB, t, ps, pB)
                        for r in range(2):
                            self.tt("dve", self.kw[:, r, :].rearrange("p (h d) -> p h d", d=64),
                                    ps[:, 0:128].rearrange("p (h d) -> p h d", d=64),
                                    self.wst[:, r, tr, 2 * a:2 * a + 2].unsqueeze(2).to_broadcast([128, 2, 64]), ALU.mult,
                                    [pB, B("wst")], [B("kw")])
                        for r in range(2):
                            for hh in range(2):
                                cs = slice(hh * 64, (hh + 1) * 64)
                                self.mm(pS[:, r, hh, :], self.kw[:, r, cs], self.vpad[:, t, hh, cs],
                                        (tr == 0 and r == 0 and hh == 0), tr == sn - 1, [B("kw"), B("vpad")], [B("psD")], skip=True)
                    self.cp("dve", self.sto[:, si_], pS, [B("psD")], [B("sto")])
                    for r in range(2):
                        self.dma(self.st[si_][r, 2 * a:2 * a + 2].rearrange("h d e -> d h e"), self.sto[:, si_, r], reads=[B("sto")])
            if is_sample:
                for r in range(2):
                    self.dma(self.s0st[0:64, r, 0:64], din["s0"][r, 2 * a], writes=[B("s0st")])
                    self.dma(self.s0st[64:128, r, 64:128], din["s0"][r, 2 * a + 1], writes=[B("s0st")])
                self.cp("dve", self.s0bd[:], self.s0st[:], [B("s0st")], [B("s0bd")])
                for r in range(2):
                    tf = self.tmpf[r]
                    self.act(tf[:], (self.iota_n1 if r == 0 else self.iota_rev)[:], AF.Exp,
                             [B("iota_n1"), B("iota_rev"), B("lgcol")], [B("tmpf%d" % r)], scale=self.lgcol[:, r, a:a + 1])
                    self.tt("dve", self.qf[:, r, :], self.qT[:], tf[:], ALU.mult, [B("qT"), B("tmpf%d" % r)], [B("qf")])
            for bi, (b0, bn, ktiles) in enumerate(ablocks):
                first = True
                if is_sample:
                    for r in range(2):
                        self.mm(self.psC[:, b0:b0 + bn], self.s0bd[:, r, :], self.qf[:, r, b0:b0 + bn], first, False,
                                [B("s0bd"), B("qf")], [B("psC")])
                        first = False
                items = [(hh, mc) for hh in range(2) for mc in ktiles]

                def r_score(k, b0=b0, bn=bn):
                    hh, mc = items[k]
                    h = 2 * a + hh
                    half = k % 2
                    pb = self.psB[:, half * 512: half * 512 + bn]
                    pbB = B("psB%d" % half)
                    self.mm(pb, self.kTm[:, hh, mc * 128:(mc + 1) * 128], self.qT[:, b0:b0 + bn], True, True,
                            [B("kTm"), B("qT")], [pbB])
                    off = b0 - mc * 128 + 896
                    self.tt("dve", self.sc[k % 3][:, 0:bn], pb, self.strips[:, h, off:off + bn], ALU.mult,
                            [pbB, B("strips")], [B("sc%d" % (k % 3))])

                def r_pv(k, first, b0=b0, bn=bn):
                    hh, mc = items[k]
                    self.mm(self.psC[:, b0:b0 + bn], self.vpad[:, mc, hh, :], self.sc[k % 3][:, 0:bn], first, k == len(items) - 1,
                            [B("vpad"), B("sc%d" % (k % 3))], [B("psC")])

                r_score(0)
                for k in range(len(items)):
                    if k + 1 < len(items):
                        r_score(k + 1)
                    r_pv(k, first)
                    first = False
                sq = self.sc[0]
                self.act(sq[:, 0:bn], self.psC[:, b0:b0 + bn], AF.Square, [B("psC")], [B("sc0")])
                msp = self.psA[0]
                self.mm(msp[:, 0:bn], self.ones_bd[:], sq[:, 0:bn], True, True, [B("ones_bd"), B("sc0")], [B("psA0")])
                tf = self.tmpf[0]
                self.act(tf[:, 0:bn], msp[:, 0:bn], AF.Sqrt, [B("psA0")], [B("tmpf0")], bias=EPS)
                self.recip(tf[:, 0:bn], tf[:, 0:bn], [B("tmpf0")], [B("tmpf0")])
                tg = self.tmpf[1]
                self.tt("dve", tg[:, 0:bn], self.psC[:, b0:b0 + bn], tf[:, 0:bn], ALU.mult, [B("psC"), B("tmpf0")], [B("tmpf1")])
                self.stt("dve", self.oT[:, a, b0:b0 + bn], tg[:, 0:bn], self.gn_g[:, a:a + 1], self.sg[:, b0:b0 + bn],
                         ALU.mult, ALU.mult, [B("tmpf1"), B("gn_g"), B("sg")], [B("oT")])

        opts = os.environ.get("KOPT", "")
        if self.stage >= 2:
            if "nona" not in opts and not ("nonas" in opts and is_sample) and not ("nonap" in opts and not is_sample):
                self.na_seq(si, tile0, NT, v, is_sample, segs)
            if "noout" not in opts:
                self.out_proj(si, tile0, NT, v, is_sample)

    def qk_norm(self, ps, pB, bn, gcol, outs):
        B = self.B
        sq = self.sc[0]
        self.act(sq[:, 0:bn], ps[:, 0:bn], AF.Square, [pB], [B("sc0")])
        msp = self.psB[:, 0:bn]
        self.mm(msp, self.ones_bd[:], sq[:, 0:bn], True, True, [B("ones_bd"), B("sc0")], [B("psB0")])
        tf = self.tmpf[0]
        self.act(tf[:, 0:bn], msp, AF.Sqrt, [B("psB0")], [B("tmpf0")], bias=EPS)
        self.recip(tf[:, 0:bn], tf[:, 0:bn], [B("tmpf0")], [B("tmpf0")])
        for psl, out, oB in outs:
            self.stt("dve", out, ps[psl, 0:bn], self.qkn_g[psl, gcol:gcol + 1], tf[psl, 0:bn], ALU.mult, ALU.mult,
                     [pB, B("qkn_g"), B("tmpf0")], [oB])

    def na_seq(self, si, tile0, NT, v, is_sample, segs):
        B, din = self.B, self.din
        T = NT * 128
        blocks = [(b0, min(512, T - b0)) for b0 in range(0, T, 512)]
        w_in = din["w_in"]
        q_of_k = _na_windows()
        ablocks = self.ablocks
        npairs = int(os.environ.get("NAS" if is_sample else "NAP", "4"))
        for a in range(npairs):
            self.bg_issue(0)
            col = lambda base: w_in[:, base + a * 128: base + (a + 1) * 128]
            wq_, wqB = self.load_w(col(2048), 0)
            for bi, (b0, bn) in enumerate(blocks):
                ps, pB = self.psA[bi % 2], B("psA%d" % (bi % 2))
                self.proj_fm(wq_, wqB, T, ps, pB, b0, bn)
                self.qk_norm(ps, pB, bn, 0, [(slice(0, 128), self.qT[:, b0:b0 + bn], B("qT"))])
            wk_, wkB = self.load_w(col(2560), 0)
            for bi, (b0, bn) in enumerate(blocks):
                ps, pB = self.psA[bi % 2], B("psA%d" % (bi % 2))
                self.proj_fm(wk_, wkB, T, ps, pB, b0, bn)
                self.qk_norm(ps, pB, bn, 1, [(slice(hh * 64, (hh + 1) * 64), self.kTm[hh * 64:(hh + 1) * 64, hh, b0:b0 + bn], B("kTm"))
                                             for hh in range(2)])
            wv_, wvB = self.load_w(col(3072), 0)
            for t in range(NT):
                ps, pB = self.psA[t % 2], B("psA%d" % (t % 2))
                self.proj_tm(wv_, wvB, t, ps, pB)
                for hh in range(2):
                    cs = slice(hh * 64, (hh + 1) * 64)
                    self.cp("act", self.vpad[:, t, hh, cs], ps[:, cs], [pB], [B("vpad")])
                if not is_sample and "nonvout" not in os.environ.get("KOPT", ""):
                    rows = slice(t * 128, (t + 1) * 128)
                    if "nvnocopy" not in os.environ.get("KOPT", ""):
                        self.cp("dve", self.nvo[:], ps[:, 0:128], [pB], [B("nvo")])
                    if "nvnodma" not in os.environ.get("KOPT", ""):
                        self.dma(self.nv[rows, a * 128:(a + 1) * 128], self.nvo[:], reads=[B("nvo")])
            if not is_sample and "nonk" not in os.environ.get("KOPT", ""):
                for t in range(NT):
                    ps, pB = self.psA[t % 2], B("psA%d" % (t % 2))
                    self.proj_tm(wk_, wkB, t, ps, pB)
                    tf = self.tmpf[0]
                    p3 = ps[:, 0:128].rearrange("p (h d) -> p h d", d=64)
                    t3 = tf[:, 0:128].rearrange("p (h d) -> p h d", d=64)
                    self.act(tf[:, 0:128], ps[:, 0:128], AF.Square, [pB], [B("tmpf0")])
                    self.S.op("dve", lambda q, t3=t3: q.tensor_reduce(out=self.small[:, 8:10], in_=t3, axis=AX.X, op=ALU.add),
                              [B("tmpf0")], [B("nkss")])
                    self.act(self.small[:, 10:12], self.small[:, 8:10], AF.Sqrt, [B("nkss")], [B("nksd")], scale=1.0 / 64, bias=EPS)
                    self.recip(self.small[:, 12:14], self.small[:, 10:12], [B("nksd")], [B("nkrs")])
                    self.tt("dve", t3, p3, self.small[:, 12:14].unsqueeze(2).to_broadcast([128, 2, 64]), ALU.mult,
                            [pB, B("nkrs")], [B("tmpf0")])
                    self.tt("dve", self.nko[:].rearrange("p (h d) -> p h d", d=64), t3,
                            self.kn_bc[:].unsqueeze(1).to_broadcast([128, 2, 64]), ALU.mult, [B("tmpf0"), B("kn_bc")], [B("nko")])
                    rows = slice(t * 128, (t + 1) * 128)
                    self.dma(self.nk[rows, a * 128:(a + 1) * 128], self.nko[:], reads=[B("nko")])
            if is_sample:
                self.dma(self.ctxst[:], din["kctxT"][a * 128:(a + 1) * 128, :], writes=[B("ctxst")])
                for hh in range(2):
                    psl = slice(hh * 64, (hh + 1) * 64)
                    self.cp("dve", self.kcm[psl, hh, :], self.ctxst[psl, :], [B("ctxst")], [B("kcm")])
                for kc in range(2):
                    self.dma(self.ctxst[:, 0:128], din["vctx"][kc * 128:(kc + 1) * 128, a * 128:(a + 1) * 128],
                             reads=[], writes=[B("ctxst")])
                    for hh in range(2):
                        cs = slice(hh * 64, (hh + 1) * 64)
                        self.cp("dve", self.vcp[:, kc, hh, cs], self.ctxst[:, cs], [B("ctxst")], [B("vcp")])
            cnt = 0

            def pv(lv, lvB, p, pB_, q0, qn, first, last):
                c0 = q0
                while c0 < q0 + qn:
                    c1 = min((c0 // 512 + 1) * 512, q0 + qn)
                    self.mm(self.psC[:, c0:c1], lv, p[:, c0 - q0:c1 - q0], first, last, [lvB, pB_], [B("psC")])
                    self.mm(self.psD[:, c0:c1], self.ones_pad[:, lv_hh[0], :], p[:, c0 - q0:c1 - q0], first, last,
                            [B("ones_pad"), pB_], [B("psD")])
                    c0 = c1

            lv_hh = [0]

            jobs = []

            def add_dense(hh, kc, first, last, qblocks=None):
                for bi, (b0, bn) in enumerate(qblocks if qblocks is not None else blocks):
                    k = len(jobs)
                    half = k % 2
                    pb = self.psB[:, half * 512: half * 512 + bn]
                    pbB = B("psB%d" % half)
                    sc = self.sc[k % 3]
                    scB = B("sc%d" % (k % 3))

                    def score(hh=hh, kc=kc, b0=b0, bn=bn, pb=pb, pbB=pbB, sc=sc, scB=scB):
                        lk = self.kTm[:, hh, kc * 128:(kc + 1) * 128] if not is_sample else self.kcm[:, hh, kc * 128:(kc + 1) * 128]
                        self.mm(pb, lk, self.qT[:, b0:b0 + bn], True, True, [B("kTm"), B("kcm"), B("qT")], [pbB])
                        self.act(sc[:, 0:bn], pb, AF.Exp, [pbB], [scB], scale=0.125)

                    def pvj(hh=hh, kc=kc, b0=b0, bn=bn, sc=sc, scB=scB, first=first, last=last):
                        lv_hh[0] = hh
                        lv = self.vpad[:, kc, hh, :] if not is_sample else self.vcp[:, kc, hh, :]
                        pv(lv, B("vpad") if not is_sample else B("vcp"), sc, scB, b0, bn, first, last)

                    jobs.append((score, pvj))

            def add_window(hh, c):
                k = len(jobs)
                h = 2 * a + hh
                r0 = [q_of_k[2 * c], q_of_k[2 * c + 1]]
                qlo = min(r0[0][0], r0[1][0])
                qhi = max(r0[0][1], r0[1][1])
                q0, qn = qlo * 64, (qhi - qlo + 1) * 64
                sbt = self.sbias[k % 2]
                sbB = B("sbias%d" % (k % 2))
                sc = self.sc[k % 3]
                scB = B("sc%d" % (k % 3))

                def score():
                    self.mm(self.psB[:, 0:min(qn, 512)], self.kTm[:, hh, c * 128:(c + 1) * 128], self.qT[:, q0:q0 + min(qn, 512)],
                            True, True, [B("kTm"), B("qT")], [B("psB0"), B("psB1")])
                    if qn > 512:
                        self.mm(self.psB[:, 512:qn], self.kTm[:, hh, c * 128:(c + 1) * 128], self.qT[:, q0 + 512:q0 + qn],
                                True, True, [B("kTm"), B("qT")], [B("psB0"), B("psB1")])
                    for krl in range(2):
                        kr = 2 * c + krl
                        psl = slice(krl * 64, (krl + 1) * 64)
                        a0, a1 = r0[krl]
                        lo, hi = (a0 - qlo) * 64, (a1 - qlo + 1) * 64
                        x0 = a0 - kr + 7
                        bias = self.trb[psl, h, x0:x0 + (a1 - a0 + 1), :]
                        self.stt("dve", sbt[psl, lo:hi].rearrange("p (x q) -> p x q", q=64),
                                 self.psB[psl, lo:hi].rearrange("p (x q) -> p x q", q=64), 0.125, bias,
                                 ALU.mult, ALU.add, [B("psB0"), B("psB1"), B("trb")], [sbB])
                        if lo > 0:
                            self.memset("dve", sbt[psl, 0:lo], NEG, [sbB])
                        if hi < qn:
                            self.memset("dve", sbt[psl, hi:qn], NEG, [sbB])
                    self.act(sc[:, 0:qn], sbt[:, 0:qn], AF.Exp, [sbB], [scB])

                def pvj():
                    lv_hh[0] = hh
                    pv(self.vpad[:, c, hh, :], B("vpad"), sc, scB, q0, qn, False, False)

                jobs.append((score, pvj))

            for hh in range(2):
                if "noatt" in os.environ.get("KOPT", ""):
                    continue
                if not is_sample:
                    continue
                else:
                    add_dense(hh, 0, hh == 0, False)
                    for c in range(8):
                        add_window(hh, c)
                    add_dense(hh, 1, False, hh == 1)
            if not is_sample and "noatt" not in os.environ.get("KOPT", ""):
                for (b0, bn, ktiles) in ablocks:
                    for hh in range(2):
                        for kc in ktiles:
                            add_dense(hh, kc, hh == 0 and kc == ktiles[0], hh == 1 and kc == ktiles[-1], [(b0, bn)])
            if jobs:
                jobs[0][0]()
            for k in range(len(jobs)):
                if k + 1 < len(jobs):
                    jobs[k + 1][0]()
                jobs[k][1]()
            for bi, (b0, bn, _kt) in enumerate(ablocks):
                tf = self.tmpf[bi % 2]
                tB = B("tmpf%d" % (bi % 2))
                self.recip(tf[:, 0:bn], self.psD[:, b0:b0 + bn], [B("psD")], [tB])
                self.tt("dve", self.oT[:, 4 + a, b0:b0 + bn], self.psC[:, b0:b0 + bn], tf[:, 0:bn], ALU.mult,
                        [B("psC"), tB], [B("oT")])

    def out_proj(self, si, tile0, NT, v, is_sample):
        B, din = self.B, self.din
        for t in range(NT):
            ps, pB = (self.psC, B("psC")) if t % 2 == 0 else (self.psD, B("psD"))
            for half in range(2):
                cs = slice(half * 512, (half + 1) * 512)
                for c in range(8):
                    self.mm(ps[:, cs], self.oT[:, c, t * 128:(t + 1) * 128], self.wo[:, c, cs], c == 0, c == 7,
                            [B("oT"), B("wo")], [pB])
            xb = self.xb[t % 2]
            xB = B("xb%d" % (t % 2))
            rows = slice((tile0 + t) * 128, (tile0 + t + 1) * 128)
            self.dma(xb[:], din["x"][rows, :], writes=[xB])
            tf = self.tmpf[t % 2]
            tB = B("tmpf%d" % (t % 2))
            self.tt("dve", tf[:], ps[:], self.modsl(v, 2), ALU.mult, [pB, B("mod")], [tB])
            self.tt("dve", xb[:], tf[:], xb[:], ALU.add, [tB, xB], [xB])
            self.dma(self.y[rows, :], xb[:], reads=[xB])

    def peer(self, st):
        B, din = self.B, self.din
        sb = lambda n, s, d=F32: self.sb(st, n, s, d)
        TG = 256
        self.bg_issue(10000)
        scrU, scrV, scrQ = self.scrU, self.scrV, self.scrQ
        G3 = [sb("G3_%d" % i, [128, 128, 128], BF16) for i in range(3)]
        XT = [sb("XT%d" % i, [128, 8, TG], BF16) for i in range(2)]
        x1 = sb("x1_0", [128, 1024])
        xm = sb("xm", [128, 1024], BF16)
        ptmp = sb("ptmp0", [128, 1024])
        keysT = sb("keysT", [128, 16, 128], BF16)
        iota_row = sb("iota_rowp", [128, 128])
        iota16 = sb("iota16p", [128, 16])
        iota_rb = sb("iota_rb", [128, 128], BF16)
        qTc = [sb("qTc%d" % i, [128, TG], BF16) for i in range(2)]
        SscR = [[sb("Ssc%d_%d" % (t, k), [128, 128]) for k in range(3)] for t in range(2)]
        v16 = [sb("v16_%d" % i, [128, 16, 16]) for i in range(2)]
        i16u = [sb("i16u_%d" % i, [128, 16, 16], U32) for i in range(2)]
        i16f = sb("i16f", [128, 16, 16])
        cand = sb("cand", [128, 8, 256])
        oh = cand[:].rearrange("p h (a b) -> p h a b", b=16)
        tops = [sb("top%d" % i, [128, 8, 16]) for i in range(2)]
        pu = sb("pu", [128, 8, 16], U32)
        abu = sb("abu", [128, 2, 8, 16], U32)
        abf = sb("abf", [128, 2, 8, 16])
        wsels = [sb("wsel%d" % i, [128, 3, 128]) for i in range(2)]
        wselb = sb("wselb", [128, 3, 128], BF16)
        zs = sb("zs", [128, 8, 2])
        sT = [sb("sT%d" % i, [128, 3, TG], BF16) for i in range(2)]
        wbf = [sb("pwbf%d" % i, [128, 8, 128], BF16) for i in range(2)]
        ubf = [sb("ubf%d" % i, [128, 1024], BF16) for i in range(3)]
        vbf = [sb("vbf%d" % i, [128, 1024], BF16) for i in range(4)]
        hg = [sb("hg%d" % i, [128, TG], BF16) for i in range(3)]
        AT = [sb("AT%d" % i, [128, TG], BF16) for i in range(3)]
        P1 = [sb("P1_%d" % i, [128, 4, 128], BF16) for i in range(4)]
        P2 = [sb("P2_%d" % i, [128, 4, 128], BF16) for i in range(4)]

        self.dma(iota_row[:], din["iota_row"], writes=[B("iota_rowp")])
        self.dma(iota16[:], din["iota16"], writes=[B("iota16p")])
        self.cp("dve", iota_rb[:], iota_row[:], [B("iota_rowp")], [B("iota_rb")])
        for c4 in range(4):
            kst = ptmp[:, 0:512].rearrange("p (c k) -> p c k", k=128)
            self.dma(kst, din["keysT"][:, c4 * 4:(c4 + 1) * 4, :], writes=[B("ptmp0")])
            self.cp("dve", keysT[:, c4 * 4:(c4 + 1) * 4, :], kst, [B("ptmp0")], [B("keysT")])

        psT = self.psA[0][:].bitcast(BF16).rearrange("p (c n) -> p c n", n=128)
        psG = self.psA[0][:].rearrange("p (t j) -> p t j", j=128)
        psH = [self.psB[:, 0:TG], self.psB[:, 512:512 + TG], self.psA[1][:, 0:TG]]
        psHB = [B("psB0"), B("psB1"), B("psA1")]
        psO = [self.psC, self.psD]
        psOB = [B("psC"), B("psD")]
        ngroups = int(os.environ.get("PGROUPS", "6"))
        nj = int(os.environ.get("PNJ", "128"))

        def prep_a(g):
            v = 1 if g < 4 else 0
            XTg, XB = XT[g % 2], B("XT%d" % (g % 2))
            for tt in range(2):
                rows = slice((2 * g + tt) * 128, (2 * g + tt + 1) * 128)
                xB = B("x1_0")
                self.dma(x1[:], self.y[rows, :], writes=[xB])
                tf = ptmp
                self.memset("dve", self.small[:, 0:1], 0.0, [B("ss")])
                self.act(tf[:], x1[:], AF.Square, [xB], [B("ptmp0"), B("ss")], accum_out=self.small[:, 0:1])
                self.act(self.small[:, 1:2], self.small[:, 0:1], AF.Sqrt, [B("ss")], [B("sd")], scale=1.0 / D, bias=EPS)
                self.recip(self.small[:, 2:3], self.small[:, 1:2], [B("sd")], [B("rstd")])
                self.stt("dve", tf[:], x1[:], self.small[:, 2:3], self.modsl(v, 4), ALU.mult, ALU.mult,
                         [xB, B("rstd"), B("mod")], [B("ptmp0")])
                self.tt("dve", xm[:], tf[:], self.modsl(v, 3), ALU.add, [B("ptmp0"), B("mod")], [B("xm")])
                for kc in range(8):
                    self.tr(psT[:, kc, :], xm[:, kc * 128:(kc + 1) * 128], self.ident_b[:], [B("xm"), B("ident_b")], [B("psA0")])
                self.cp("act", XTg[:, :, tt * 128:(tt + 1) * 128], psT, [B("psA0")], [XB])

        def prep(g):
            XTg, XB = XT[g % 2], B("XT%d" % (g % 2))

            def wload(c):
                i = c % 2
                self.dma(wbf[i][:], scrQ[c].rearrange("p (kc n) -> p kc n", n=128), reads=[B("scrQ%d" % c)], writes=[B("pwbf%d" % i)])

            def scores_mm(c):
                for tt in range(2):
                    self.mm(self.psA[0][:, 256 + tt * 128:256 + (tt + 1) * 128], qTc[c % 2][:, tt * 128:(tt + 1) * 128], keysT[:, c, :], True, True,
                            [B("qTc%d" % (c % 2)), B("keysT")], [B("psA0")])

            def scores_cp(c):
                for tt in range(2):
                    self.cp("dve", SscR[tt][c % 3][:], self.psA[0][:, 256 + tt * 128:256 + (tt + 1) * 128], [B("psA0")], [B("Ssc%d_%d" % (tt, c % 3))])

            def level1(c):
                for tt in range(2):
                    S = SscR[tt][c % 3]
                    SB = B("Ssc%d_%d" % (tt, c % 3))
                    vB, iB = B("v16_%d" % tt), B("i16u_%d" % tt)
                    for half8 in range(2):
                        vs = v16[tt][:, c, half8 * 8:(half8 + 1) * 8]
                        iu = i16u[tt][:, c, half8 * 8:(half8 + 1) * 8]
                        self.S.op("dve", lambda q, vs=vs, S=S: q.max(out=vs, in_=S[:]), [SB], [vB])
                        self.S.op("dve", lambda q, vs=vs, S=S, iu=iu: q.max_index(out=iu, in_max=vs, in_values=S[:]), [SB, vB], [iB])
                        if half8 == 0:
                            self.S.op("dve", lambda q, vs=vs, S=S: q.match_replace(out=S[:], in_to_replace=vs, in_values=S[:], imm_value=NEG),
                                      [SB, vB], [SB])

            wload(0)
            for c in range(17):
                if c + 1 < 16:
                    wload(c + 1)
                if c < 16:
                    i = c % 2
                    for kc in range(8):
                        self.mm(self.psA[0][:, 0:TG], wbf[i][:, kc, :], XTg[:, kc, :], kc == 0, kc == 7, [B("pwbf%d" % i), XB], [B("psA0")])
                if c > 0:
                    scores_mm(c - 1)
                if c < 16:
                    self.cp("dve", qTc[c % 2][:], self.psA[0][:, 0:TG], [B("psA0")], [B("qTc%d" % (c % 2))])
                if c > 0:
                    scores_cp(c - 1)
                    level1(c - 1)
                yield 4.2 if c > 0 else 0.5
            for tt in range(2):
                top, topB = tops[tt], B("top%d" % tt)
                wsel, wselB = wsels[tt], B("wsel%d" % tt)
                vB, iB = B("v16_%d" % tt), B("i16u_%d" % tt)
                self.cp("dve", i16f[:], i16u[tt][:], [iB], [B("i16f")])
                v16r = v16[tt][:].rearrange("p (h f) k -> p h f k", f=2)
                i16r = i16f[:].rearrange("p (h f) k -> p h f k", f=2)
                cand4 = cand[:].rearrange("p h (a b) -> p h a b", b=16)
                self.tt("dve", cand4, v16r[:, :, 0, :].unsqueeze(3).to_broadcast([128, 8, 16, 16]),
                        v16r[:, :, 1, :].unsqueeze(2).to_broadcast([128, 8, 16, 16]), ALU.add, [vB], [B("cand")])
                yield 2.6
                for h in range(8):
                    for half8 in range(2):
                        vs = top[:, h, half8 * 8:(half8 + 1) * 8]
                        self.S.op("dve", lambda q, vs=vs, h=h: q.max(out=vs, in_=cand[:, h, :]), [B("cand")], [topB])
                        self.S.op("dve", lambda q, vs=vs, h=h, half8=half8: q.max_index(out=pu[:, h, half8 * 8:(half8 + 1) * 8], in_max=vs, in_values=cand[:, h, :]),
                                  [B("cand"), topB], [B("pu")])
                        if half8 == 0:
                            self.S.op("dve", lambda q, vs=vs, h=h: q.match_replace(out=cand[:, h, :], in_to_replace=vs, in_values=cand[:, h, :], imm_value=NEG),
                                      [B("cand"), topB], [B("cand")])
                    yield 2.4
                self.S.op("dve", lambda q: q.tensor_single_scalar(out=abu[:, 0], in_=pu[:], scalar=4, op=ALU.logical_shift_right), [B("pu")], [B("abu")])
                self.S.op("dve", lambda q: q.tensor_single_scalar(out=abu[:, 1], in_=pu[:], scalar=15, op=ALU.bitwise_and), [B("pu")], [B("abu")])
                self.cp("dve", abf[:], abu[:], [B("abu")], [B("abf")])
                yield 1.0
                wsel4 = wsel[:].rearrange("p w (h k) -> p w h k", k=16)
                for f in range(2):
                    self.tt("dve", oh[:], abf[:, f].unsqueeze(3).to_broadcast([128, 8, 16, 16]),
                            iota16[:].unsqueeze(1).unsqueeze(1).to_broadcast([128, 8, 16, 16]), ALU.is_equal,
                            [B("abf"), B("iota16p")], [B("cand")])
                    self.tt("dve", oh[:], oh[:], i16r[:, :, f, :].unsqueeze(2).to_broadcast([128, 8, 16, 16]), ALU.mult,
                            [B("cand"), B("i16f")], [B("cand")])
                    self.S.op("dve", lambda q, f=f, wsel4=wsel4: q.tensor_reduce(out=wsel4[:, f], in_=oh[:], axis=AX.X, op=ALU.add), [B("cand")], [wselB])
                    yield 6.6
            yield ("wait", 2.0)
            prep_tail(g)
            yield 3.0

        def prep_tail(g):
            sTg, sTB = sT[g % 2], B("sT%d" % (g % 2))
            for tt in range(2):
                top, topB = tops[tt], B("top%d" % tt)
                wsel, wselB = wsels[tt], B("wsel%d" % tt)
                wsel4 = wsel[:].rearrange("p w (h k) -> p w h k", k=16)
                self.cp("dve", zs[:, :, 0:1], top[:, :, 0:1], [topB], [B("zs")])
                self.tt("dve", top[:], top[:], zs[:, :, 0:1].to_broadcast([128, 8, 16]), ALU.subtract, [topB, B("zs")], [topB])
                self.act(top[:], top[:], AF.Exp, [topB], [topB])
                self.S.op("dve", lambda q, top=top: q.tensor_reduce(out=zs[:, :, 0], in_=top[:], axis=AX.X, op=ALU.add), [topB], [B("zs")])
                self.recip(zs[:, :, 1], zs[:, :, 0], [B("zs")], [B("zs")])
                self.tt("dve", wsel4[:, 2], top[:], zs[:, :, 1:2].to_broadcast([128, 8, 16]), ALU.mult, [topB, B("zs")], [wselB])
                self.cp("dve", wselb[:], wsel[:], [wselB], [B("wselb")])
                for w in range(3):
                    self.tr(psT[:, w, :], wselb[:, w, :], self.ident_b[:], [B("wselb"), B("ident_b")], [B("psA0")])
                self.cp("act", sTg[:, :, tt * 128:(tt + 1) * 128], psT[:, 0:3, :], [B("psA0")], [sTB])

        def gconstruct(g, tt, dst, ev="dve"):
            sTg, sTB = sT[g % 2], B("sT%d" % (g % 2))
            Gd, GB = G3[dst], B("G3_%d" % dst)
            io4 = iota_rb[:].unsqueeze(1).to_broadcast([128, 4, 128])
            nb = 32

            def dve_part(bi):
                r = bi % 4
                n0 = tt * 128 + bi * 4
                bc = lambda w: sTg[:, w, n0:n0 + 4].unsqueeze(2).to_broadcast([128, 4, 128])
                self.tt("dve", P1[r][:], io4, bc(0), ALU.is_equal, [B("iota_rb"), sTB], [B("P1_%d" % r)])
                self.tt("dve", P2[r][:], io4, bc(1), ALU.is_equal, [B("iota_rb"), sTB], [B("P2_%d" % r)])
                self.tt("pool" if ev == "act" else "dve", P1[r][:], P1[r][:], bc(2), ALU.mult, [B("P1_%d" % r), sTB], [B("P1_%d" % r)])

            def pe_part(bi):
                r = bi % 4
                for k in range(4):
                    self.mm(psG[:, k, :], P1[r][:, k, :], P2[r][:, k, :], True, True, [B("P1_%d" % r), B("P2_%d" % r)], [B("psA0")])
                self.cp(ev, Gd[:, bi * 4:bi * 4 + 4, :], psG, [B("psA0")], [GB])

            for u in range(nb // 2 + 1):
                if u > 0:
                    pe_part(2 * u - 2)
                    pe_part(2 * u - 1)
                if u < nb // 2:
                    dve_part(2 * u)
                    dve_part(2 * u + 1)
                yield 4.9 if u < nb // 2 else 1.2

        def run_all(gen):
            for _ in gen:
                pass

        def main(g, Ta, Tb, inter):
            XTg, XB = XT[g % 2], B("XT%d" % (g % 2))
            Gt = [G3[Ta], G3[Tb]]
            GtB = [B("G3_%d" % Ta), B("G3_%d" % Tb)]

            def load(j):
                self.dma(ubf[j % 3][:], scrU[j], reads=[B("scrU%d" % j)], writes=[B("ubf%d" % (j % 3))])
                self.dma(vbf[j % 4][:], scrV[j], reads=[B("scrV%d" % j)], writes=[B("vbf%d" % (j % 4))])

            def Hm(j):
                ib = j % 3
                u3 = ubf[ib][:].rearrange("p (c i) -> p c i", i=128)
                for kc in range(8):
                    self.mm(psH[j % 3], u3[:, kc, :], XTg[:, kc, :], kc == 0, kc == 7, [B("ubf%d" % ib), XB], [psHB[j % 3]])

            def post(j):
                i3 = j % 3
                self.act(hg[i3][:], psH[i3], AF.Gelu_apprx_tanh, [psHB[i3]], [B("hg%d" % i3)])
                for tt in range(2):
                    cs = slice(tt * 128, (tt + 1) * 128)
                    self.tt("pool", AT[i3][:, cs], hg[i3][:, cs], Gt[tt][:, :, j], ALU.mult, [B("hg%d" % i3), GtB[tt]], [B("AT%d" % i3)])

            def outm(j):
                i3, ib = j % 3, j % 4
                for tt in range(2):
                    for half in range(2):
                        cs = slice(half * 512, (half + 1) * 512)
                        self.mm(psO[tt][:, cs], AT[i3][:, tt * 128:(tt + 1) * 128], vbf[ib][:, cs], j == 0, j == nj - 1,
                                [B("AT%d" % i3), B("vbf%d" % ib)], [psOB[tt]])

            W = 0.0
            waiting = None
            for j0 in range(min(3, nj)):
                load(j0)
            Hm(0)
            if nj > 1:
                Hm(1)
            for j in range(nj):
                if j + 3 < nj:
                    load(j + 3)
                if j + 2 < nj:
                    Hm(j + 2)
                post(j)
                outm(j)
                if inter is None:
                    continue
                budget = (j + 1) * 1.8 * 0.9
                emitted = 0
                while inter is not None and emitted < 1:
                    if waiting is not None:
                        if W + waiting > budget:
                            break
                        waiting = None
                    if W > budget:
                        break
                    c = next(inter, "done")
                    if c == "done":
                        inter = None
                    elif isinstance(c, tuple):
                        waiting = c[1]
                    else:
                        W += c
                        emitted += 1
            if inter is not None:
                run_all(inter)

        def epilogue(g):
            v = 1 if g < 4 else 0
            for tt in range(2):
                rows = slice((2 * g + tt) * 128, (2 * g + tt + 1) * 128)
                tf, tB = ptmp, B("ptmp0")
                self.dma(x1[:], self.y[rows, :], writes=[B("x1_0")])
                self.tt("dve", tf[:], psO[tt][:], self.modsl(v, 5), ALU.mult, [psOB[tt], B("mod")], [tB])
                self.tt("dve", x1[:], tf[:], x1[:], ALU.add, [tB, B("x1_0")], [B("x1_0")])
                self.dma(self.y[rows, :], x1[:], reads=[B("x1_0")])

        def chain(*gens):
            for gn in gens:
                for item in gn:
                    yield item

        prep_a(0)
        run_all(prep(0))
        run_all(gconstruct(0, 0, 0, "act"))
        run_all(gconstruct(0, 1, 1, "act"))
        Ta, Tb, Fr = 0, 1, 2
        for g in range(ngroups):
            if g + 1 < ngroups:
                prep_a(g + 1)
                main(g, Ta, Tb, chain(prep(g + 1), gconstruct(g + 1, 0, Fr)))
                epilogue(g)
                run_all(gconstruct(g + 1, 1, Ta, "act"))
                Ta, Tb, Fr = Fr, Ta, Tb
            else:
                main(g, Ta, Tb, None)
                epilogue(g)


def _prep_inputs(inp):
    f = lambda a: np.ascontiguousarray(np.asarray(a, dtype=np.float32))
    consts = _host_constants()
    shared = dict(consts)
    w_in = f(inp["w_in"][0])
    shared["w_ada"] = f(inp["w_ada"][0])
    shared["b_ada"] = f(inp["b_ada"][0]).reshape(1, 6144)
    shared["n1g"] = f(inp["norm1_g"][0]).reshape(1, 1024)
    shared["n2g"] = f(inp["norm2_g"][0]).reshape(1, 1024)
    shared["w_in"] = w_in
    perm = _rope_partner_perm()
    shared["w_sw"] = f(np.concatenate([w_in[:, 0:512][:, perm], w_in[:, 512:1024][:, perm]], axis=1))
    shared["dec"] = f(np.concatenate([inp["ret_decay_f"][0], inp["ret_decay_b"][0]])).reshape(1, 16)
    shared["gn_g"] = f(np.asarray(inp["ret_gn_g"][0]).reshape(4, 128).T)
    qg = np.tile(np.asarray(inp["na_qn_g"][0]), 2)
    kg = np.tile(np.asarray(inp["na_kn_g"][0]), 2)
    shared["qkn_g"] = f(np.stack([qg, kg], axis=1))
    shared["kn_row"] = f(inp["na_kn_g"][0]).reshape(1, 64)
    rpb = np.asarray(inp["na_rpb"][0], dtype=np.float32)
    kc = np.arange(64)[:, None]
    qc = np.arange(64)[None, :]
    dc = np.clip(kc - qc + 15, 0, 30)
    x = np.arange(15)
    tr = rpb[:, (14 - x)[:, None, None], dc[None, :, :]]
    tr = np.transpose(tr, (2, 0, 1, 3))
    shared["rpbT"] = f(np.concatenate([tr, tr], axis=0))
    shared["w_out"] = f(inp["w_out"][0])
    shared["wq"] = f(inp["peer_wq"][0])
    keys = np.asarray(inp["peer_keys"][0], dtype=np.float32)
    shared["keysT"] = f(np.transpose(keys.reshape(16, 128, 128), (2, 0, 1)))
    U = np.asarray(inp["peer_u"][0], dtype=np.float32)
    shared["Ut"] = f(np.transpose(U.reshape(128, 128, 8, 128), (1, 3, 2, 0)))
    V = np.asarray(inp["peer_v"][0], dtype=np.float32)
    shared["Vp"] = f(np.transpose(V.reshape(128, 128, 1024), (1, 0, 2)))
    xp = np.asarray(inp["x_prompt"], dtype=np.float32)
    xs = np.asarray(inp["x_sample"], dtype=np.float32)
    cc = np.asarray(inp["c"], dtype=np.float32)
    cctx = np.asarray(inp["c_ctx"], dtype=np.float32)
    maps = []
    for c in range(NCORES):
        m = dict(shared)
        m["x"] = f(np.concatenate([xs[c], xp[2 * c], xp[2 * c + 1]], axis=0))
        cv = np.stack([cctx, cc[c]], axis=0)
        m["cT"] = f(np.transpose(cv.reshape(2, 8, 128), (2, 1, 0)))
        m["kctxT"] = f(np.asarray(inp["cache_na_k"][c, 0], dtype=np.float32).reshape(256, 512).T)
        m["vctx"] = f(np.asarray(inp["cache_na_v"][c, 0], dtype=np.float32).reshape(256, 512))
        m["s0"] = f(inp["state_ret"][c, 0])
        maps.append(m)
    return maps


_NC_CACHE = {}


def _get_nc(stage=3):
    if stage not in _NC_CACHE:
        _NC_CACHE[stage] = Builder(stage=stage).build()
    return _NC_CACHE[stage]


def kernel(**inputs):
    maps = _prep_inputs(inputs)
    nc = _get_nc()
    res = run_bass_kernel_spmd(nc, maps, core_ids=list(range(NCORES)))
    outs = res.results
    y_p = np.zeros((16, 256, 1024), np.float32)
    y_s = np.zeros((8, 1024, 1024), np.float32)
    nk = np.zeros((16, 1, 256, 8, 64), np.float32)
    nv = np.zeros((16, 1, 256, 8, 64), np.float32)
    st = np.zeros((16, 1, 2, 8, 64, 64), np.float32)
    for c in range(NCORES):
        r = outs[c]
        y = r["y"]
        y_s[c] = y[0:1024]
        y_p[2 * c] = y[1024:1280]
        y_p[2 * c + 1] = y[1280:1536]
        nk[2 * c:2 * c + 2, 0] = r["nk"].reshape(2, 256, 8, 64)
        nv[2 * c:2 * c + 2, 0] = r["nv"].reshape(2, 256, 8, 64)
        st[2 * c:2 * c + 2, 0] = r["st"]
    return (y_p, y_s, nk, nv, st)
```

```python
import math
import os
from contextlib import ExitStack

import numpy as np
import concourse.bass as bass
import concourse.mybir as mybir
from concourse.bass_utils import run_bass_kernel_spmd

F32 = mybir.dt.float32
BF16 = mybir.dt.bfloat16
I32 = mybir.dt.int32
U32 = mybir.dt.uint32
ALU = mybir.AluOpType
AF = mybir.ActivationFunctionType
AX = mybir.AxisListType

NCORES = 8
D = 1024
EPS = 1e-6
NEG = -1e30


class Buf:
    __slots__ = ("name", "w", "r")

    def __init__(self, name):
        self.name = name
        self.w = None
        self.r = []


class Eng:
    def __init__(self, name, sem, same_sync):
        self.name = name
        self.sem = sem
        self.count = 0
        self.waited = {}
        self.same_sync = same_sync
        self.prog = []


class Sched:
    def __init__(self, nc, stack, n_dma_slots=int(os.environ.get("NDMA", "24"))):
        self.nc = nc
        self.sems = {}
        self.engs = {}
        for name, same in (("pe", False), ("act", True), ("dve", True), ("pool", True), ("sp", True)):
            sem = stack.enter_context(nc.semaphore("s_" + name))
            self.sems[name] = sem
            self.engs[name] = Eng(name, sem, same)
        self.dma_slots = []
        for i in range(n_dma_slots):
            key = "dma%d" % i
            self.sems[key] = stack.enter_context(nc.semaphore("s_" + key))
            self.dma_slots.append([key, 0])
        self.dma_i = 0
        self.bg_slots = []
        for i in range(16):
            key = "bg%d" % i
            self.sems[key] = stack.enter_context(nc.semaphore("s_" + key))
            self.bg_slots.append([key, 0])
        self.bg_i = 0
        self.n_inst = 0

    def _wait(self, e, tok):
        if tok is None:
            return
        key, val = tok
        if key == e.name and not e.same_sync:
            return
        if e.waited.get(key, 0) >= val:
            return
        e.waited[key] = val
        sem = self.sems[key]
        e.prog.append(lambda q, sem=sem, val=val: q.wait_ge(sem, val))

    def _deps(self, e, reads, writes):
        for b in reads:
            self._wait(e, b.w)
            if b.name.startswith("ps"):
                for t in b.r:
                    if t[0] != e.name:
                        self._wait(e, t)
        for b in writes:
            self._wait(e, b.w)
            for t in b.r:
                self._wait(e, t)

    @staticmethod
    def _mark(tok, reads, writes):
        for b in reads:
            b.r.append(tok)
            if len(b.r) > 64:
                b.r = b.r[-64:] if False else b.r
        for b in writes:
            b.w = tok
            b.r = []

    def op(self, eng, fn, reads=(), writes=()):
        e = self.engs[eng]
        self._deps(e, reads, writes)
        e.count += 1
        tok = (e.name, e.count)
        sem = e.sem
        e.prog.append(lambda q, fn=fn, sem=sem: fn(q).then_inc(sem, 1))
        self._mark(tok, reads, writes)
        self.n_inst += 1
        return tok

    def dma(self, eng, out, in_, reads=(), writes=(), bg=False, **kw):
        e = self.engs[eng]
        self._deps(e, reads, writes)
        if bg:
            slot = self.bg_slots[self.bg_i % len(self.bg_slots)]
            self.bg_i += 1
        else:
            slot = self.dma_slots[self.dma_i % len(self.dma_slots)]
            self.dma_i += 1
        key = slot[0]
        if slot[1] > 0:
            self._wait(e, (key, slot[1]))
        slot[1] += 16
        tok = (key, slot[1])
        sem = self.sems[key]
        e.prog.append(lambda q, out=out, in_=in_, sem=sem, kw=kw:
                      q.dma_start(out=out, in_=in_, **kw).then_inc(sem, 16))
        self._mark(tok, reads, writes)
        self.n_inst += 1
        return tok

    def barrier(self):
        for e in self.engs.values():
            for key, val in self.dma_slots + self.bg_slots:
                if val > 0:
                    self._wait(e, (key, val))
            for o in self.engs.values():
                if o is not e and o.count > 0:
                    self._wait(e, (o.name, o.count))

    def emit(self):
        nc = self.nc
        progs = {k: v.prog for k, v in self.engs.items()}
        with nc.Block() as block:
            @block.tensor
            def _(q):
                for f in progs["pe"]:
                    f(q)

            @block.scalar
            def _(q):
                for f in progs["act"]:
                    f(q)

            @block.vector
            def _(q):
                for f in progs["dve"]:
                    f(q)

            @block.gpsimd
            def _(q):
                for f in progs["pool"]:
                    f(q)

            @block.sync
            def _(q):
                for f in progs["sp"]:
                    f(q)


def _rope_tables():
    T = 1024
    n = np.arange(T)
    rows = (n // 64).astype(np.float32)
    cols = (n % 64).astype(np.float32)
    freqs = (np.float32(10000.0) ** (-np.arange(16, dtype=np.float32) / np.float32(16))).astype(np.float32)
    C = np.zeros((128, T), np.float32)
    Sg = np.zeros((128, T), np.float32)
    for p in range(128):
        d = p % 64
        pos = rows if d < 32 else cols
        dd = d % 32
        f = dd % 16
        ang = (pos * freqs[f]).astype(np.float32)
        C[p] = np.cos(ang)
        Sg[p] = -np.sin(ang) if dd < 16 else np.sin(ang)
    return C, Sg


def _rope_partner_perm():
    perm = np.zeros(512, np.int64)
    for h in range(8):
        for d in range(64):
            dd = d % 32
            partner = d + 16 if dd < 16 else d - 16
            perm[h * 64 + d] = h * 64 + partner
    return perm


def _na_windows():
    rows, kh = 16, 8
    q_of_k = {}
    for kr in range(rows):
        qs = [qr for qr in range(rows) if min(max(qr - kh // 2, 0), rows - kh) <= kr < min(max(qr - kh // 2, 0), rows - kh) + kh]
        assert qs == list(range(qs[0], qs[-1] + 1))
        q_of_k[kr] = (qs[0], qs[-1])
    return q_of_k


def _host_constants():
    c = {}
    c["ident"] = np.eye(128, dtype=np.float32)
    ob = np.zeros((128, 128), np.float32)
    ob[:64, :64] = 1.0 / 64
    ob[64:, 64:] = 1.0 / 64
    c["onesbd"] = ob
    op = np.zeros((128, 2, 128), np.float32)
    op[:, 0, :64] = 1.0
    op[:, 1, 64:] = 1.0
    c["onespad"] = op
    p = np.arange(128, dtype=np.float32)[:, None]
    j = np.arange(1920, dtype=np.float32)[None, :]
    c["expo"] = (j - 896.0 - p).astype(np.float32)
    c["iota_n1"] = np.broadcast_to(np.arange(1, 1025, dtype=np.float32)[None], (128, 1024)).copy()
    c["iota_rev"] = np.broadcast_to((1024 - np.arange(1024, dtype=np.float32))[None], (128, 1024)).copy()
    tq = np.zeros((128, 2, 2), np.float32)
    for t in range(2):
        tq[:, 0, t] = 255 - (t * 128 + np.arange(128))
        tq[:, 1, t] = t * 128 + np.arange(128)
    c["tq"] = tq
    C, Sg = _rope_tables()
    c["rope_c"] = C
    c["rope_s"] = Sg
    qc = np.arange(64)
    cstart = np.clip(qc - 8, 0, 48)
    kc = np.arange(64)
    inwin = (kc[:, None] >= cstart[None, :]) & (kc[:, None] < cstart[None, :] + 16)
    cm = np.where(inwin, 0.0, NEG).astype(np.float32)
    c["cmask"] = np.concatenate([cm, cm], axis=0)
    c["iota_row"] = np.broadcast_to(np.arange(128, dtype=np.float32)[None], (128, 128)).copy()
    c["iota16"] = np.broadcast_to(np.arange(16, dtype=np.float32)[None], (128, 16)).copy()
    return c


CONST_SHAPES = {
    "ident": [128, 128], "onesbd": [128, 128], "onespad": [128, 2, 128], "expo": [128, 1920],
    "iota_n1": [128, 1024], "iota_rev": [128, 1024], "tq": [128, 2, 2], "rope_c": [128, 1024],
    "rope_s": [128, 1024], "cmask": [128, 64], "iota_row": [128, 128], "iota16": [128, 16],
}

IN_SHAPES = {
    "x": [1536, 1024], "cT": [128, 8, 2], "w_ada": [1024, 6144], "b_ada": [1, 6144],
    "n1g": [1, 1024], "n2g": [1, 1024], "w_in": [1024, 3584], "w_sw": [1024, 1024], "dec": [1, 16],
    "gn_g": [128, 4], "qkn_g": [128, 2], "kn_row": [1, 64], "rpbT": [128, 8, 15, 64],
    "w_out": [1024, 1024], "wq": [1024, 2048], "keysT": [128, 16, 128],
    "Ut": [128, 128, 8, 128], "Vp": [128, 128, 1024],
    "kctxT": [512, 256], "vctx": [256, 512], "s0": [2, 8, 64, 64],
}

SEQS = [(0, 8, 1, True, [(0, 8)]), (8, 4, 0, False, [(0, 2), (2, 2)])]


class Builder:
    def __init__(self, stage=3, debug=False):
        self.stage = stage
        self.debug = debug
        self.nc = bass.Bass("TRN2", target_bir_lowering=False)
        nc = self.nc
        self.din = {}
        for name, shp in list(IN_SHAPES.items()) + list(CONST_SHAPES.items()):
            self.din[name] = nc.dram_tensor(name, shp, F32, kind="ExternalInput").ap()
        self.y = nc.dram_tensor("y", [1536, 1024], F32, kind="ExternalOutput").ap()
        self.nk = nc.dram_tensor("nk", [512, 512], F32, kind="ExternalOutput").ap()
        self.nv = nc.dram_tensor("nv", [512, 512], F32, kind="ExternalOutput").ap()
        self.st = nc.dram_tensor("st", [2, 2, 8, 64, 64], F32, kind="ExternalOutput").ap()
        self.bufs = {}

    def bg_issue(self, n, dep=None):
        for _ in range(n):
            if not self.bg_todo:
                return
            out, in_, name = self.bg_todo.pop(0)
            self.dma(out, in_, reads=[self.B(dep)] if dep else [], writes=[self.B(name)], eng="pool", bg=True)

    def B(self, name):
        b = self.bufs.get(name)
        if b is None:
            b = self.bufs[name] = Buf(name)
        return b

    def sb(self, st, name, shape, dt=F32):
        return st.enter_context(self.nc.sbuf_tensor("sb_" + name, shape, dt))

    def mm(self, out, lhsT, rhs, start, stop, reads, writes, skip=False):
        if skip:
            self.S.op("pe", lambda q: q.matmul(out, lhsT=lhsT, rhs=rhs, start=start, stop=stop, skip_group_check=True), reads, writes)
        else:
            self.S.op("pe", lambda q: q.matmul(out, lhsT=lhsT, rhs=rhs, start=start, stop=stop), reads, writes)

    def tr(self, out, in_, ident, reads, writes):
        self.S.op("pe", lambda q: q.transpose(out=out, in_=in_, identity=ident), reads, writes)

    def act(self, out, in_, func, reads, writes, **kw):
        self.S.op("act", lambda q: q.activation(out=out, in_=in_, func=func, **kw), reads, writes)

    def tt(self, eng, out, in0, in1, op, reads, writes):
        self.S.op(eng, lambda q: q.tensor_tensor(out=out, in0=in0, in1=in1, op=op), reads, writes)

    def ts(self, eng, out, in0, s1, s2, op0, op1, reads, writes):
        if s2 is None:
            self.S.op(eng, lambda q: q.tensor_scalar(out=out, in0=in0, scalar1=s1, scalar2=None, op0=op0), reads, writes)
        else:
            self.S.op(eng, lambda q: q.tensor_scalar(out=out, in0=in0, scalar1=s1, scalar2=s2, op0=op0, op1=op1), reads, writes)

    def stt(self, eng, out, in0, scalar, in1, op0, op1, reads, writes):
        self.S.op(eng, lambda q: q.scalar_tensor_tensor(out=out, in0=in0, scalar=scalar, in1=in1, op0=op0, op1=op1), reads, writes)

    def cp(self, eng, out, in_, reads, writes):
        if eng == "act":
            self.S.op("act", lambda q: q.copy(out=out, in_=in_), reads, writes)
        else:
            self.S.op(eng, lambda q: q.tensor_copy(out=out, in_=in_), reads, writes)

    def memset(self, eng, ap, val, writes):
        self.S.op(eng, lambda q: q.memset(ap, val), (), writes)

    def recip(self, out, in_, reads, writes):
        self.S.op("dve", lambda q: q.reciprocal(out=out, in_=in_), reads, writes)

    def dma(self, out, in_, reads=(), writes=(), eng="sp", bg=False):
        self.S.dma(eng, out, in_, reads, writes, bg=bg)

    def build(self):
        nc = self.nc
        with ExitStack() as top:
            self.S = Sched(nc, top)
            self.psA = [top.enter_context(nc.psum_tensor("psA%d" % i, [128, 512], F32)) for i in range(2)]
            self.psB = top.enter_context(nc.psum_tensor("psB", [128, 1024], F32))
            self.psC = top.enter_context(nc.psum_tensor("psC", [128, 1024], F32))
            self.psD = top.enter_context(nc.psum_tensor("psD", [128, 1024], F32))
            self.scrU = nc.dram_tensor("scrU", [128, 128, 1024], BF16).ap()
            self.scrV = nc.dram_tensor("scrV", [128, 128, 1024], BF16).ap()
            self.scrQ = nc.dram_tensor("scrQ", [16, 128, 1024], BF16).ap()
            self.bg_todo = []
            if self.stage >= 3:
                Ut2 = self.din["Ut"].rearrange("j p c i -> j p (c i)")
                wq4 = self.din["wq"].rearrange("(kc p) (c n) -> c p kc n", p=128, n=128)
                for c in range(16):
                    self.bg_todo.append((self.scrQ[c].rearrange("p (kc n) -> p kc n", n=128), wq4[c], "scrQ%d" % c))
                for j in range(128):
                    self.bg_todo.append((self.scrU[j], Ut2[j], "scrU%d" % j))
                    self.bg_todo.append((self.scrV[j], self.din["Vp"][j], "scrV%d" % j))
            self.mod = self.sb(top, "mod", [128, 2, 6144], BF16)
            self.ident_b = self.sb(top, "ident_b", [128, 128], BF16)
            self.small = self.sb(top, "small", [128, 64], F32)
            with ExitStack() as ph1:
                self.phase1_alloc(ph1)
                self.setup(ph1)
                with ExitStack() as ws:
                    self.phase1_work(ws)
                    if self.stage >= 1:
                        for si, seq in enumerate(SEQS):
                            self.attn_seq(si, *seq)
                    self.S.barrier()
            if self.stage >= 3:
                with ExitStack() as ph2:
                    self.peer(ph2)
                    self.S.barrier()
            self.S.barrier()
            self.S.emit()
        return nc

    def phase1_alloc(self, st):
        sb = lambda n, s, d=F32: self.sb(st, n, s, d)
        self.ones_bd = sb("ones_bd", [128, 128], BF16)
        self.ones_pad = sb("ones_pad", [128, 2, 128], BF16)
        self.strips = sb("strips", [128, 8, 1920], BF16)
        self.lg = sb("lg", [128, 16])
        self.nlgb = sb("nlgb", [128, 8])
        self.lgcol = sb("lgcol", [128, 2, 4])
        self.wst = sb("wst", [128, 2, 2, 8])
        self.gn_g = sb("gn_g", [128, 4])
        self.qkn_g = sb("qkn_g", [128, 2])
        self.kn_bc = sb("kn_bc", [128, 64])
        self.wo = sb("wo", [128, 8, 1024], BF16)
        self.iota_n1 = sb("iota_n1", [128, 1024])
        self.iota_rev = sb("iota_rev", [128, 1024])
        self.rope_c = sb("rope_c", [128, 1024])
        self.rope_s = sb("rope_s", [128, 1024])
        self.trb = sb("trb", [128, 8, 15, 64], BF16)

    def phase1_work(self, st):
        sb = lambda n, s, d=F32: self.sb(st, n, s, d)
        B = self.B
        self.hT = sb("hT", [128, 8, 1024], BF16)
        self.oT = sb("oT", [128, 8, 1024], BF16)
        self.xb = [sb("xb%d" % i, [128, 1024]) for i in range(2)]
        self.tmpf = [sb("tmpf%d" % i, [128, 1024]) for i in range(2)]
        self.hb = sb("hb", [128, 1024], BF16)
        self.wstage = [sb("wstage%d" % i, [128, 8, 128]) for i in range(3)]
        self.wbf = [sb("wbf%d" % i, [128, 8, 128], BF16) for i in range(3)]
        self.qT = sb("qT", [128, 1024], BF16)
        self.kTm = sb("kTm", [128, 2, 1024], BF16)
        self.sg = sb("sg", [128, 1024], BF16)
        self.vpad = sb("vpad", [128, 8, 2, 128], BF16)
        self.kcm = sb("kcm", [128, 2, 256], BF16)
        self.vcp = sb("vcp", [128, 2, 2, 128], BF16)
        self.qf = sb("qf", [128, 2, 1024], BF16)
        self.s0bd = sb("s0bd", [128, 2, 128], BF16)
        self.s0st = sb("s0st", [128, 2, 128])
        self.sc = [sb("sc%d" % i, [128, 768], BF16) for i in range(3)]
        self.sbias = [sb("sbias%d" % i, [128, 768]) for i in range(2)]
        self.ctxst = sb("ctxst", [128, 256])
        self.kw = sb("kw", [128, 2, 128], BF16)
        self.nko = sb("nko", [128, 128])
        self.nvo = sb("nvo", [128, 128])
        self.sto = sb("sto", [64, 2, 2, 2, 64])
        self.memset("pool", self.kTm[:], 0.0, [B("kTm")])
        self.memset("pool", self.vpad[:], 0.0, [B("vpad")])
        self.memset("pool", self.kcm[:], 0.0, [B("kcm")])
        self.memset("pool", self.vcp[:], 0.0, [B("vcp")])
        self.memset("pool", self.s0st[:], 0.0, [B("s0st")])

    def setup(self, st_outer):
        B, din = self.B, self.din
        with ExitStack() as st:
            sb = lambda n, s, d=F32: self.sb(st, n, s, d)
            identf = sb("identf", [128, 128])
            self.dma(identf[:], din["ident"], writes=[B("identf")])
            self.cp("dve", self.ident_b[:], identf[:], [B("identf")], [B("ident_b")])
            tmp128 = sb("tmp128", [128, 2, 128])
            self.dma(tmp128[:, 0, :], din["onesbd"], writes=[B("tmp128")])
            self.cp("dve", self.ones_bd[:], tmp128[:, 0, :], [B("tmp128")], [B("ones_bd")])
            self.dma(tmp128[:], din["onespad"], reads=[], writes=[B("tmp128")])
            self.cp("dve", self.ones_pad[:], tmp128[:], [B("tmp128")], [B("ones_pad")])
            for nm, t in (("iota_n1", self.iota_n1), ("iota_rev", self.iota_rev), ("rope_c", self.rope_c),
                          ("rope_s", self.rope_s), ("gn_g", self.gn_g), ("qkn_g", self.qkn_g)):
                self.dma(t[:], din[nm], writes=[B(nm)])
            self.dma(self.kn_bc[:], din["kn_row"][0].partition_broadcast(128), writes=[B("kn_bc")])
            dec = sb("dec", [128, 16])
            self.dma(dec[:], din["dec"][0].partition_broadcast(128), writes=[B("dec")])
            self.act(dec[:], dec[:], AF.Exp, [B("dec")], [B("dec")], scale=-1.0)
            self.act(dec[:], dec[:], AF.Ln, [B("dec")], [B("dec")], bias=1.0)
            self.ts("dve", self.lg[:], dec[:], -1.0, None, ALU.mult, None, [B("dec")], [B("lg")])
            self.ts("dve", self.nlgb[:], self.lg[:, 8:16], -1.0, None, ALU.mult, None, [B("lg")], [B("nlgb")])
            for r in range(2):
                self.cp("dve", self.lgcol[0:64, r, :], self.lg[0:64, r * 8:r * 8 + 8:2], [B("lg")], [B("lgcol")])
                self.cp("dve", self.lgcol[64:128, r, :], self.lg[64:128, r * 8 + 1:r * 8 + 8:2], [B("lg")], [B("lgcol")])
            tq = sb("tq", [128, 2, 2])
            self.dma(tq[:], din["tq"], writes=[B("tq")])
            for r in range(2):
                self.tt("dve", self.wst[:, r], tq[:, r, :].unsqueeze(2).to_broadcast([128, 2, 8]),
                        self.lg[:, r * 8:(r + 1) * 8].unsqueeze(1).to_broadcast([128, 2, 8]), ALU.mult,
                        [B("tq"), B("lg")], [B("wst")])
            self.act(self.wst[:], self.wst[:], AF.Exp, [B("wst")], [B("wst")])
            self.ts("dve", self.wst[:], self.wst[:], 0.125, None, ALU.mult, None, [B("wst")], [B("wst")])
            expo = sb("expo", [128, 1920])
            t1 = sb("t1", [128, 1920])
            t2 = sb("t2", [128, 1920])
            self.dma(expo[:], din["expo"], writes=[B("expo")])
            for h in range(8):
                self.ts("dve", t1[:], expo[:], self.lg[:, h:h + 1], None, ALU.mult, None, [B("expo"), B("lg")], [B("t1")])
                self.stt("dve", t2[:], expo[:], self.nlgb[:, h:h + 1], t1[:], ALU.mult, ALU.min,
                         [B("expo"), B("nlgb"), B("t1")], [B("t2")])
                self.act(t2[:], t2[:], AF.Exp, [B("t2")], [B("t2")])
                self.ts("dve", self.strips[:, h, :], t2[:], 0.125, None, ALU.mult, None, [B("t2")], [B("strips")])
            cmask = sb("cmask", [128, 64])
            self.dma(cmask[:], din["cmask"], writes=[B("cmask")])
            for h in range(8):
                trs = t1[:, 0:960].rearrange("p (x q) -> p x q", q=64)
                self.dma(trs, din["rpbT"][:, h], writes=[B("t1")])
                self.tt("dve", self.trb[:, h], trs, cmask[:].unsqueeze(1).to_broadcast([128, 15, 64]), ALU.add,
                        [B("t1"), B("cmask")], [B("trb")])
            for kc in range(8):
                wsl = t2[:, 0:1024]
                self.dma(wsl, din["w_out"][kc * 128:(kc + 1) * 128, :], writes=[B("t2")])
                self.cp("act", self.wo[:, kc, :], wsl, [B("t2")], [B("wo")])
        self.S.barrier()
        self.bg_issue(64)
        self.adaln()
        self.S.barrier()

    def adaln(self):
        B, din = self.B, self.din
        with ExitStack() as st:
            sb = lambda n, s, d=F32: self.sb(st, n, s, d)
            cT = sb("cT", [128, 8, 2])
            rep = sb("rep", [128, 8, 2, 128], BF16)
            wst = [sb("awst%d" % i, [128, 8, 512]) for i in range(2)]
            wbf = [sb("awbf%d" % i, [128, 8, 512], BF16) for i in range(2)]
            bbc = [sb("bbc%d" % i, [128, 512]) for i in range(2)]
            ngb = sb("ngb", [128, 2, 1024])
            self.dma(cT[:], din["cT"], writes=[B("cT")])
            self.act(cT[:], cT[:], AF.Silu, [B("cT")], [B("cT")])
            self.cp("dve", rep[:], cT[:].unsqueeze(3).to_broadcast([128, 8, 2, 128]), [B("cT")], [B("rep")])
            self.dma(ngb[:, 0, :], din["n1g"][0].partition_broadcast(128), writes=[B("ngb")])
            self.dma(ngb[:, 1, :], din["n2g"][0].partition_broadcast(128), writes=[B("ngb")])
            w3 = din["w_ada"].rearrange("(kc p) n -> p kc n", p=128)
            for blk in range(12):
                i = blk % 2
                cs = slice(blk * 512, (blk + 1) * 512)
                self.dma(wst[i][:], w3[:, :, cs], writes=[B("awst%d" % i)])
                self.dma(bbc[i][:], din["b_ada"][0, cs].partition_broadcast(128), writes=[B("bbc%d" % i)])
                self.cp("dve" if blk % 2 == 0 else "act", wbf[i][:], wst[i][:], [B("awst%d" % i)], [B("awbf%d" % i)])
                for v in range(2):
                    ps = self.psA[v]
                    for kc in range(8):
                        self.mm(ps[:], rep[:, kc, v, :], wbf[i][:, kc, :], kc == 0, kc == 7,
                                [B("rep"), B("awbf%d" % i)], [B("psA%d" % v)])
                    self.tt("dve", self.mod[:, v, cs], ps[:], bbc[i][:], ALU.add,
                            [B("psA%d" % v), B("bbc%d" % i)], [B("mod")])
            for v in range(2):
                for j, ch in ((0, 1), (1, 4)):
                    sl = self.mod[:, v, ch * 1024:(ch + 1) * 1024]
                    self.stt("dve", sl, sl, 1.0, ngb[:, j, :], ALU.add, ALU.mult, [B("mod"), B("ngb")], [B("mod")])

    def modsl(self, v, ch):
        return self.mod[:, v, ch * 1024:(ch + 1) * 1024]

    def load_w(self, dram_cols, k):
        B = self.B
        i = self.wcount % 3
        j = self.wcount % 3
        self.wcount += 1
        self.bg_issue(3, "wbf%d" % ((j + 2) % 3))
        self.dma(self.wstage[i][:], dram_cols.rearrange("(kc p) n -> p kc n", p=128), writes=[B("wstage%d" % i)])
        self.cp("act", self.wbf[j][:], self.wstage[i][:], [B("wstage%d" % i)], [B("wbf%d" % j)])
        return self.wbf[j], B("wbf%d" % j)

    def proj_fm(self, w, wB, T, ps, psB_, b0, bn):
        for kc in range(8):
            self.mm(ps[:, 0:bn], w[:, kc, :], self.hT[:, kc, b0:b0 + bn], kc == 0, kc == 7,
                    [wB, self.B("hT")], [psB_])

    def proj_tm(self, w, wB, t, ps, psB_):
        for kc in range(8):
            self.mm(ps[:, 0:128], self.hT[:, kc, t * 128:(t + 1) * 128], w[:, kc, :], kc == 0, kc == 7,
                    [wB, self.B("hT")], [psB_])

    def attn_seq(self, si, tile0, NT, v, is_sample, segs):
        B, din = self.B, self.din
        T = NT * 128
        blocks = [(b0, min(512, T - b0)) for b0 in range(0, T, 512)]
        x_rows = lambda t: slice((tile0 + t) * 128, (tile0 + t + 1) * 128)
        w_in = din["w_in"]
        self.wcount = getattr(self, "wcount", 0)
        psT = self.psA[0][:].bitcast(BF16).rearrange("p (c n) -> p c n", n=128)
        for t in range(NT):
            xb = self.xb[t % 2]
            xB = B("xb%d" % (t % 2))
            self.dma(xb[:], din["x"][x_rows(t), :], writes=[xB])
            tf = self.tmpf[0]
            self.memset("dve", self.small[:, 0:1], 0.0, [B("ss")])
            self.act(tf[:], xb[:], AF.Square, [xB], [B("tmpf0"), B("ss")], accum_out=self.small[:, 0:1])
            self.act(self.small[:, 1:2], self.small[:, 0:1], AF.Sqrt, [B("ss")], [B("sd")], scale=1.0 / D, bias=EPS)
            self.recip(self.small[:, 2:3], self.small[:, 1:2], [B("sd")], [B("rstd")])
            self.stt("dve", tf[:], xb[:], self.small[:, 2:3], self.modsl(v, 1), ALU.mult, ALU.mult,
                     [xB, B("rstd"), B("mod")], [B("tmpf0")])
            self.tt("dve", self.hb[:], tf[:], self.modsl(v, 0), ALU.add, [B("tmpf0"), B("mod")], [B("hb")])
            for kc in range(8):
                self.tr(psT[:, kc, :], self.hb[:, kc * 128:(kc + 1) * 128], self.ident_b[:],
                        [B("hb"), B("ident_b")], [B("psA0")])
            self.cp("act", self.hT[:, :, t * 128:(t + 1) * 128], psT, [B("psA0")], [B("hT")])

        ablocks = []
        for (s0, sn) in segs:
            for q0 in range(s0 * 128, (s0 + sn) * 128, 512):
                ablocks.append((q0, min(512, (s0 + sn) * 128 - q0), list(range(s0, s0 + sn))))
        self.ablocks = ablocks
        for a in range(4):
            self.bg_issue(0)
            col = lambda base: w_in[:, base + a * 128: base + (a + 1) * 128]
            wq_, wqB = self.load_w(col(0), 0)
            if is_sample:
                wqs, wqsB = self.load_w(din["w_sw"][:, a * 128:(a + 1) * 128], 0)
            for bi, (b0, bn) in enumerate(blocks):
                self.proj_fm(wq_, wqB, T, self.psA[0], B("psA0"), b0, bn)
                if is_sample:
                    self.proj_fm(wqs, wqsB, T, self.psA[1], B("psA1"), b0, bn)
                    tf = self.tmpf[0]
                    self.tt("dve", tf[:, 0:bn], self.psA[0][:, 0:bn], self.rope_c[:, b0:b0 + bn], ALU.mult,
                            [B("psA0"), B("rope_c")], [B("tmpf0")])
                    tg = self.tmpf[1]
                    self.tt("dve", tg[:, 0:bn], self.psA[1][:, 0:bn], self.rope_s[:, b0:b0 + bn], ALU.mult,
                            [B("psA1"), B("rope_s")], [B("tmpf1")])
                    self.tt("dve", self.qT[:, b0:b0 + bn], tf[:, 0:bn], tg[:, 0:bn], ALU.add,
                            [B("tmpf0"), B("tmpf1")], [B("qT")])
                else:
                    self.cp("act", self.qT[:, b0:b0 + bn], self.psA[0][:, 0:bn], [B("psA0")], [B("qT")])
            wk_, wkB = self.load_w(col(512), 0)
            if is_sample:
                wks, wksB = self.load_w(din["w_sw"][:, 512 + a * 128:512 + (a + 1) * 128], 0)
            for bi, (b0, bn) in enumerate(blocks):
                self.proj_fm(wk_, wkB, T, self.psA[0], B("psA0"), b0, bn)
                if is_sample:
                    self.proj_fm(wks, wksB, T, self.psA[1], B("psA1"), b0, bn)
                    tf = self.tmpf[0]
                    self.tt("dve", tf[:, 0:bn], self.psA[0][:, 0:bn], self.rope_c[:, b0:b0 + bn], ALU.mult,
                            [B("psA0"), B("rope_c")], [B("tmpf0")])
                    tg = self.tmpf[1]
                    self.tt("dve", tg[:, 0:bn], self.psA[1][:, 0:bn], self.rope_s[:, b0:b0 + bn], ALU.mult,
                            [B("psA1"), B("rope_s")], [B("tmpf1")])
                    for hh in range(2):
                        ps_ = slice(hh * 64, (hh + 1) * 64)
                        self.tt("dve", self.kTm[ps_, hh, b0:b0 + bn], tf[ps_, 0:bn], tg[ps_, 0:bn], ALU.add,
                                [B("tmpf0"), B("tmpf1")], [B("kTm")])
                else:
                    for hh in range(2):
                        ps_ = slice(hh * 64, (hh + 1) * 64)
                        self.cp("act", self.kTm[ps_, hh, b0:b0 + bn], self.psA[0][ps_, 0:bn], [B("psA0")], [B("kTm")])
            wg_, wgB = self.load_w(col(1536), 0)
            for bi, (b0, bn) in enumerate(blocks):
                ps, pB = self.psA[bi % 2], B("psA%d" % (bi % 2))
                self.proj_fm(wg_, wgB, T, ps, pB, b0, bn)
                self.act(self.sg[:, b0:b0 + bn], ps[:, 0:bn], AF.Silu, [pB], [B("sg")])
            wv_, wvB = self.load_w(col(1024), 0)
            for t in range(NT):
                ps, pB = self.psA[t % 2], B("psA%d" % (t % 2))
                self.proj_tm(wv_, wvB, t, ps, pB)
                for hh in range(2):
                    cs = slice(hh * 64, (hh + 1) * 64)
                    self.cp("act", self.vpad[:, t, hh, cs], ps[:, cs], [pB], [B("vpad")])
            if not is_sample:
                for si_, (s0, sn) in enumerate(segs):
                    pS = self.psD[0:64, si_ * 512:si_ * 512 + 256].rearrange("p (r h e) -> p r h e", r=2, h=2)
                    for tr in range(sn):
                        t = s0 + tr
                        ps, pB = self.psA[t % 2], B("psA%d" % (t % 2))
                        self.proj_tm(wk_, wkB, t, ps, pB)
                        for r in range(2):
                            self.tt("dve", self.kw[:, r, :].rearrange("p (h d) -> p h d", d=64),
                                    ps[:, 0:128].rearrange("p (h d) -> p h d", d=64),
                                    self.wst[:, r, tr, 2 * a:2 * a + 2].unsqueeze(2).to_broadcast([128, 2, 64]), ALU.mult,
                                    [pB, B("wst")], [B("kw")])
                        for r in range(2):
                            for hh in range(2):
                                cs = slice(hh * 64, (hh + 1) * 64)
                                self.mm(pS[:, r, hh, :], self.kw[:, r, cs], self.vpad[:, t, hh, cs],
                                        (tr == 0 and r == 0 and hh == 0), tr == sn - 1, [B("kw"), B("vpad")], [B("psD")], skip=True)
                    self.cp("dve", self.sto[:, si_], pS, [B("psD")], [B("sto")])
                    for r in range(2):
                        self.dma(self.st[si_][r, 2 * a:2 * a + 2].rearrange("h d e -> d h e"), self.sto[:, si_, r], reads=[B("sto")])
            if is_sample:
                for r in range(2):
                    self.dma(self.s0st[0:64, r, 0:64], din["s0"][r, 2 * a], writes=[B("s0st")])
                    self.dma(self.s0st[64:128, r, 64:128], din["s0"][r, 2 * a + 1], writes=[B("s0st")])
                self.cp("dve", self.s0bd[:], self.s0st[:], [B("s0st")], [B("s0bd")])
                for r in range(2):
                    tf = self.tmpf[r]
                    self.act(tf[:], (self.iota_n1 if r == 0 else self.iota_rev)[:], AF.Exp,
                             [B("iota_n1"), B("iota_rev"), B("lgcol")], [B("tmpf%d" % r)], scale=self.lgcol[:, r, a:a + 1])
                    self.tt("dve", self.qf[:, r, :], self.qT[:], tf[:], ALU.mult, [B("qT"), B("tmpf%d" % r)], [B("qf")])
            for bi, (b0, bn, ktiles) in enumerate(ablocks):
                first = True
                if is_sample:
                    for r in range(2):
                        self.mm(self.psC[:, b0:b0 + bn], self.s0bd[:, r, :], self.qf[:, r, b0:b0 + bn], first, False,
                                [B("s0bd"), B("qf")], [B("psC")])
                        first = False
                items = [(hh, mc) for hh in range(2) for mc in ktiles]

                def r_score(k, b0=b0, bn=bn):
                    hh, mc = items[k]
                    h = 2 * a + hh
                    half = k % 2
                    pb = self.psB[:, half * 512: half * 512 + bn]
                    pbB = B("psB%d" % half)
                    self.mm(pb, self.kTm[:, hh, mc * 128:(mc + 1) * 128], self.qT[:, b0:b0 + bn], True, True,
                            [B("kTm"), B("qT")], [pbB])
                    off = b0 - mc * 128 + 896
                    self.tt("dve", self.sc[k % 3][:, 0:bn], pb, self.strips[:, h, off:off + bn], ALU.mult,
                            [pbB, B("strips")], [B("sc%d" % (k % 3))])

                def r_pv(k, first, b0=b0, bn=bn):
                    hh, mc = items[k]
                    self.mm(self.psC[:, b0:b0 + bn], self.vpad[:, mc, hh, :], self.sc[k % 3][:, 0:bn], first, k == len(items) - 1,
                            [B("vpad"), B("sc%d" % (k % 3))], [B("psC")])

                r_score(0)
                for k in range(len(items)):
                    if k + 1 < len(items):
                        r_score(k + 1)
                    r_pv(k, first)
                    first = False
                sq = self.sc[0]
                self.act(sq[:, 0:bn], self.psC[:, b0:b0 + bn], AF.Square, [B("psC")], [B("sc0")])
                msp = self.psA[0]
                self.mm(msp[:, 0:bn], self.ones_bd[:], sq[:, 0:bn], True, True, [B("ones_bd"), B("sc0")], [B("psA0")])
                tf = self.tmpf[0]
                self.act(tf[:, 0:bn], msp[:, 0:bn], AF.Sqrt, [B("psA0")], [B("tmpf0")], bias=EPS)
                self.recip(tf[:, 0:bn], tf[:, 0:bn], [B("tmpf0")], [B("tmpf0")])
                tg = self.tmpf[1]
                self.tt("dve", tg[:, 0:bn], self.psC[:, b0:b0 + bn], tf[:, 0:bn], ALU.mult, [B("psC"), B("tmpf0")], [B("tmpf1")])
                self.stt("dve", self.oT[:, a, b0:b0 + bn], tg[:, 0:bn], self.gn_g[:, a:a + 1], self.sg[:, b0:b0 + bn],
                         ALU.mult, ALU.mult, [B("tmpf1"), B("gn_g"), B("sg")], [B("oT")])

        opts = os.environ.get("KOPT", "")
        if self.stage >= 2:
            if "nona" not in opts and not ("nonas" in opts and is_sample) and not ("nonap" in opts and not is_sample):
                self.na_seq(si, tile0, NT, v, is_sample, segs)
            if "noout" not in opts:
                self.out_proj(si, tile0, NT, v, is_sample)

    def qk_norm(self, ps, pB, bn, gcol, outs):
        B = self.B
        sq = self.sc[0]
        self.act(sq[:, 0:bn], ps[:, 0:bn], AF.Square, [pB], [B("sc0")])
        msp = self.psB[:, 0:bn]
        self.mm(msp, self.ones_bd[:], sq[:, 0:bn], True, True, [B("ones_bd"), B("sc0")], [B("psB0")])
        tf = self.tmpf[0]
        self.act(tf[:, 0:bn], msp, AF.Sqrt, [B("psB0")], [B("tmpf0")], bias=EPS)
        self.recip(tf[:, 0:bn], tf[:, 0:bn], [B("tmpf0")], [B("tmpf0")])
        for psl, out, oB in outs:
            self.stt("dve", out, ps[psl, 0:bn], self.qkn_g[psl, gcol:gcol + 1], tf[psl, 0:bn], ALU.mult, ALU.mult,
                     [pB, B("qkn_g"), B("tmpf0")], [oB])

    def na_seq(self, si, tile0, NT, v, is_sample, segs):
        B, din = self.B, self.din
        T = NT * 128
        blocks = [(b0, min(512, T - b0)) for b0 in range(0, T, 512)]
        w_in = din["w_in"]
        q_of_k = _na_windows()
        ablocks = self.ablocks
        npairs = int(os.environ.get("NAS" if is_sample else "NAP", "4"))
        for a in range(npairs):
            self.bg_issue(0)
            col = lambda base: w_in[:, base + a * 128: base + (a + 1) * 128]
            wq_, wqB = self.load_w(col(2048), 0)
            for bi, (b0, bn) in enumerate(blocks):
                ps, pB = self.psA[bi % 2], B("psA%d" % (bi % 2))
                self.proj_fm(wq_, wqB, T, ps, pB, b0, bn)
                self.qk_norm(ps, pB, bn, 0, [(slice(0, 128), self.qT[:, b0:b0 + bn], B("qT"))])
            wk_, wkB = self.load_w(col(2560), 0)
            for bi, (b0, bn) in enumerate(blocks):
                ps, pB = self.psA[bi % 2], B("psA%d" % (bi % 2))
                self.proj_fm(wk_, wkB, T, ps, pB, b0, bn)
                self.qk_norm(ps, pB, bn, 1, [(slice(hh * 64, (hh + 1) * 64), self.kTm[hh * 64:(hh + 1) * 64, hh, b0:b0 + bn], B("kTm"))
                                             for hh in range(2)])
            wv_, wvB = self.load_w(col(3072), 0)
            for t in range(NT):
                ps, pB = self.psA[t % 2], B("psA%d" % (t % 2))
                self.proj_tm(wv_, wvB, t, ps, pB)
                for hh in range(2):
                    cs = slice(hh * 64, (hh + 1) * 64)
                    self.cp("act", self.vpad[:, t, hh, cs], ps[:, cs], [pB], [B("vpad")])
                if not is_sample and "nonvout" not in os.environ.get("KOPT", ""):
                    rows = slice(t * 128, (t + 1) * 128)
                    if "nvnocopy" not in os.environ.get("KOPT", ""):
                        self.cp("dve", self.nvo[:], ps[:, 0:128], [pB], [B("nvo")])
                    if "nvnodma" not in os.environ.get("KOPT", ""):
                        self.dma(self.nv[rows, a * 128:(a + 1) * 128], self.nvo[:], reads=[B("nvo")])
            if not is_sample and "nonk" not in os.environ.get("KOPT", ""):
                for t in range(NT):
                    ps, pB = self.psA[t % 2], B("psA%d" % (t % 2))
                    self.proj_tm(wk_, wkB, t, ps, pB)
                    tf = self.tmpf[0]
                    p3 = ps[:, 0:128].rearrange("p (h d) -> p h d", d=64)
                    t3 = tf[:, 0:128].rearrange("p (h d) -> p h d", d=64)
                    self.act(tf[:, 0:128], ps[:, 0:128], AF.Square, [pB], [B("tmpf0")])
                    self.S.op("dve", lambda q, t3=t3: q.tensor_reduce(out=self.small[:, 8:10], in_=t3, axis=AX.X, op=ALU.add),
                              [B("tmpf0")], [B("nkss")])
                    self.act(self.small[:, 10:12], self.small[:, 8:10], AF.Sqrt, [B("nkss")], [B("nksd")], scale=1.0 / 64, bias=EPS)
                    self.recip(self.small[:, 12:14], self.small[:, 10:12], [B("nksd")], [B("nkrs")])
                    self.tt("dve", t3, p3, self.small[:, 12:14].unsqueeze(2).to_broadcast([128, 2, 64]), ALU.mult,
                            [pB, B("nkrs")], [B("tmpf0")])
                    self.tt("dve", self.nko[:].rearrange("p (h d) -> p h d", d=64), t3,
                            self.kn_bc[:].unsqueeze(1).to_broadcast([128, 2, 64]), ALU.mult, [B("tmpf0"), B("kn_bc")], [B("nko")])
                    rows = slice(t * 128, (t + 1) * 128)
                    self.dma(self.nk[rows, a * 128:(a + 1) * 128], self.nko[:], reads=[B("nko")])
            if is_sample:
                self.dma(self.ctxst[:], din["kctxT"][a * 128:(a + 1) * 128, :], writes=[B("ctxst")])
                for hh in range(2):
                    psl = slice(hh * 64, (hh + 1) * 64)
                    self.cp("dve", self.kcm[psl, hh, :], self.ctxst[psl, :], [B("ctxst")], [B("kcm")])
                for kc in range(2):
                    self.dma(self.ctxst[:, 0:128], din["vctx"][kc * 128:(kc + 1) * 128, a * 128:(a + 1) * 128],
                             reads=[], writes=[B("ctxst")])
                    for hh in range(2):
                        cs = slice(hh * 64, (hh + 1) * 64)
                        self.cp("dve", self.vcp[:, kc, hh, cs], self.ctxst[:, cs], [B("ctxst")], [B("vcp")])
            cnt = 0

            def pv(lv, lvB, p, pB_, q0, qn, first, last):
                c0 = q0
                while c0 < q0 + qn:
                    c1 = min((c0 // 512 + 1) * 512, q0 + qn)
                    self.mm(self.psC[:, c0:c1], lv, p[:, c0 - q0:c1 - q0], first, last, [lvB, pB_], [B("psC")])
                    self.mm(self.psD[:, c0:c1], self.ones_pad[:, lv_hh[0], :], p[:, c0 - q0:c1 - q0], first, last,
                            [B("ones_pad"), pB_], [B("psD")])
                    c0 = c1

            lv_hh = [0]

            jobs = []

            def add_dense(hh, kc, first, last, qblocks=None):
                for bi, (b0, bn) in enumerate(qblocks if qblocks is not None else blocks):
                    k = len(jobs)
                    half = k % 2
                    pb = self.psB[:, half * 512: half * 512 + bn]
                    pbB = B("psB%d" % half)
                    sc = self.sc[k % 3]
                    scB = B("sc%d" % (k % 3))

                    def score(hh=hh, kc=kc, b0=b0, bn=bn, pb=pb, pbB=pbB, sc=sc, scB=scB):
                        lk = self.kTm[:, hh, kc * 128:(kc + 1) * 128] if not is_sample else self.kcm[:, hh, kc * 128:(kc + 1) * 128]
                        self.mm(pb, lk, self.qT[:, b0:b0 + bn], True, True, [B("kTm"), B("kcm"), B("qT")], [pbB])
                        self.act(sc[:, 0:bn], pb, AF.Exp, [pbB], [scB], scale=0.125)

                    def pvj(hh=hh, kc=kc, b0=b0, bn=bn, sc=sc, scB=scB, first=first, last=last):
                        lv_hh[0] = hh
                        lv = self.vpad[:, kc, hh, :] if not is_sample else self.vcp[:, kc, hh, :]
                        pv(lv, B("vpad") if not is_sample else B("vcp"), sc, scB, b0, bn, first, last)

                    jobs.append((score, pvj))

            def add_window(hh, c):
                k = len(jobs)
                h = 2 * a + hh
                r0 = [q_of_k[2 * c], q_of_k[2 * c + 1]]
                qlo = min(r0[0][0], r0[1][0])
                qhi = max(r0[0][1], r0[1][1])
                q0, qn = qlo * 64, (qhi - qlo + 1) * 64
                sbt = self.sbias[k % 2]
                sbB = B("sbias%d" % (k % 2))
                sc = self.sc[k % 3]
                scB = B("sc%d" % (k % 3))

                def score():
                    self.mm(self.psB[:, 0:min(qn, 512)], self.kTm[:, hh, c * 128:(c + 1) * 128], self.qT[:, q0:q0 + min(qn, 512)],
                            True, True, [B("kTm"), B("qT")], [B("psB0"), B("psB1")])
                    if qn > 512:
                        self.mm(self.psB[:, 512:qn], self.kTm[:, hh, c * 128:(c + 1) * 128], self.qT[:, q0 + 512:q0 + qn],
                                True, True, [B("kTm"), B("qT")], [B("psB0"), B("psB1")])
                    for krl in range(2):
                        kr = 2 * c + krl
                        psl = slice(krl * 64, (krl + 1) * 64)
                        a0, a1 = r0[krl]
                        lo, hi = (a0 - qlo) * 64, (a1 - qlo + 1) * 64
                        x0 = a0 - kr + 7
                        bias = self.trb[psl, h, x0:x0 + (a1 - a0 + 1), :]
                        self.stt("dve", sbt[psl, lo:hi].rearrange("p (x q) -> p x q", q=64),
                                 self.psB[psl, lo:hi].rearrange("p (x q) -> p x q", q=64), 0.125, bias,
                                 ALU.mult, ALU.add, [B("psB0"), B("psB1"), B("trb")], [sbB])
                        if lo > 0:
                            self.memset("dve", sbt[psl, 0:lo], NEG, [sbB])
                        if hi < qn:
                            self.memset("dve", sbt[psl, hi:qn], NEG, [sbB])
                    self.act(sc[:, 0:qn], sbt[:, 0:qn], AF.Exp, [sbB], [scB])

                def pvj():
                    lv_hh[0] = hh
                    pv(self.vpad[:, c, hh, :], B("vpad"), sc, scB, q0, qn, False, False)

                jobs.append((score, pvj))

            for hh in range(2):
                if "noatt" in os.environ.get("KOPT", ""):
                    continue
                if not is_sample:
                    continue
                else:
                    add_dense(hh, 0, hh == 0, False)
                    for c in range(8):
                        add_window(hh, c)
                    add_dense(hh, 1, False, hh == 1)
            if not is_sample and "noatt" not in os.environ.get("KOPT", ""):
                for (b0, bn, ktiles) in ablocks:
                    for hh in range(2):
                        for kc in ktiles:
                            add_dense(hh, kc, hh == 0 and kc == ktiles[0], hh == 1 and kc == ktiles[-1], [(b0, bn)])
            if jobs:
                jobs[0][0]()
            for k in range(len(jobs)):
                if k + 1 < len(jobs):
                    jobs[k + 1][0]()
                jobs[k][1]()
            for bi, (b0, bn, _kt) in enumerate(ablocks):
                tf = self.tmpf[bi % 2]
                tB = B("tmpf%d" % (bi % 2))
                self.recip(tf[:, 0:bn], self.psD[:, b0:b0 + bn], [B("psD")], [tB])
                self.tt("dve", self.oT[:, 4 + a, b0:b0 + bn], self.psC[:, b0:b0 + bn], tf[:, 0:bn], ALU.mult,
                        [B("psC"), tB], [B("oT")])

    def out_proj(self, si, tile0, NT, v, is_sample):
        B, din = self.B, self.din
        for t in range(NT):
            ps, pB = (self.psC, B("psC")) if t % 2 == 0 else (self.psD, B("psD"))
            for half in range(2):
                cs = slice(half * 512, (half + 1) * 512)
                for c in range(8):
                    self.mm(ps[:, cs], self.oT[:, c, t * 128:(t + 1) * 128], self.wo[:, c, cs], c == 0, c == 7,
                            [B("oT"), B("wo")], [pB])
            xb = self.xb[t % 2]
            xB = B("xb%d" % (t % 2))
            rows = slice((tile0 + t) * 128, (tile0 + t + 1) * 128)
            self.dma(xb[:], din["x"][rows, :], writes=[xB])
            tf = self.tmpf[t % 2]
            tB = B("tmpf%d" % (t % 2))
            self.tt("dve", tf[:], ps[:], self.modsl(v, 2), ALU.mult, [pB, B("mod")], [tB])
            self.tt("dve", xb[:], tf[:], xb[:], ALU.add, [tB, xB], [xB])
            self.dma(self.y[rows, :], xb[:], reads=[xB])

    def peer(self, st):
        B, din = self.B, self.din
        sb = lambda n, s, d=F32: self.sb(st, n, s, d)
        TG = 256
        self.bg_issue(10000)
        scrU, scrV, scrQ = self.scrU, self.scrV, self.scrQ
        G3 = [sb("G3_%d" % i, [128, 128, 128], BF16) for i in range(3)]
        XT = [sb("XT%d" % i, [128, 8, TG], BF16) for i in range(2)]
        x1 = sb("x1_0", [128, 1024])
        xm = sb("xm", [128, 1024], BF16)
        ptmp = sb("ptmp0", [128, 1024])
        keysT = sb("keysT", [128, 16, 128], BF16)
        iota_row = sb("iota_rowp", [128, 128])
        iota16 = sb("iota16p", [128, 16])
        iota_rb = sb("iota_rb", [128, 128], BF16)
        qTc = [sb("qTc%d" % i, [128, TG], BF16) for i in range(2)]
        SscR = [[sb("Ssc%d_%d" % (t, k), [128, 128]) for k in range(3)] for t in range(2)]
        v16 = [sb("v16_%d" % i, [128, 16, 16]) for i in range(2)]
        i16u = [sb("i16u_%d" % i, [128, 16, 16], U32) for i in range(2)]
        i16f = sb("i16f", [128, 16, 16])
        cand = sb("cand", [128, 8, 256])
        oh = cand[:].rearrange("p h (a b) -> p h a b", b=16)
        tops = [sb("top%d" % i, [128, 8, 16]) for i in range(2)]
        pu = sb("pu", [128, 8, 16], U32)
        abu = sb("abu", [128, 2, 8, 16], U32)
        abf = sb("abf", [128, 2, 8, 16])
        wsels = [sb("wsel%d" % i, [128, 3, 128]) for i in range(2)]
        wselb = sb("wselb", [128, 3, 128], BF16)
        zs = sb("zs", [128, 8, 2])
        sT = [sb("sT%d" % i, [128, 3, TG], BF16) for i in range(2)]
        wbf = [sb("pwbf%d" % i, [128, 8, 128], BF16) for i in range(2)]
        ubf = [sb("ubf%d" % i, [128, 1024], BF16) for i in range(3)]
        vbf = [sb("vbf%d" % i, [128, 1024], BF16) for i in range(4)]
        hg = [sb("hg%d" % i, [128, TG], BF16) for i in range(3)]
        AT = [sb("AT%d" % i, [128, TG], BF16) for i in range(3)]
        P1 = [sb("P1_%d" % i, [128, 4, 128], BF16) for i in range(4)]
        P2 = [sb("P2_%d" % i, [128, 4, 128], BF16) for i in range(4)]

        self.dma(iota_row[:], din["iota_row"], writes=[B("iota_rowp")])
        self.dma(iota16[:], din["iota16"], writes=[B("iota16p")])
        self.cp("dve", iota_rb[:], iota_row[:], [B("iota_rowp")], [B("iota_rb")])
        for c4 in range(4):
            kst = ptmp[:, 0:512].rearrange("p (c k) -> p c k", k=128)
            self.dma(kst, din["keysT"][:, c4 * 4:(c4 + 1) * 4, :], writes=[B("ptmp0")])
            self.cp("dve", keysT[:, c4 * 4:(c4 + 1) * 4, :], kst, [B("ptmp0")], [B("keysT")])

        psT = self.psA[0][:].bitcast(BF16).rearrange("p (c n) -> p c n", n=128)
        psG = self.psA[0][:].rearrange("p (t j) -> p t j", j=128)
        psH = [self.psB[:, 0:TG], self.psB[:, 512:512 + TG], self.psA[1][:, 0:TG]]
        psHB = [B("psB0"), B("psB1"), B("psA1")]
        psO = [self.psC, self.psD]
        psOB = [B("psC"), B("psD")]
        ngroups = int(os.environ.get("PGROUPS", "6"))
        nj = int(os.environ.get("PNJ", "128"))

        def prep_a(g):
            v = 1 if g < 4 else 0
            XTg, XB = XT[g % 2], B("XT%d" % (g % 2))
            for tt in range(2):
                rows = slice((2 * g + tt) * 128, (2 * g + tt + 1) * 128)
                xB = B("x1_0")
                self.dma(x1[:], self.y[rows, :], writes=[xB])
                tf = ptmp
                self.memset("dve", self.small[:, 0:1], 0.0, [B("ss")])
                self.act(tf[:], x1[:], AF.Square, [xB], [B("ptmp0"), B("ss")], accum_out=self.small[:, 0:1])
                self.act(self.small[:, 1:2], self.small[:, 0:1], AF.Sqrt, [B("ss")], [B("sd")], scale=1.0 / D, bias=EPS)
                self.recip(self.small[:, 2:3], self.small[:, 1:2], [B("sd")], [B("rstd")])
                self.stt("dve", tf[:], x1[:], self.small[:, 2:3], self.modsl(v, 4), ALU.mult, ALU.mult,
                         [xB, B("rstd"), B("mod")], [B("ptmp0")])
                self.tt("dve", xm[:], tf[:], self.modsl(v, 3), ALU.add, [B("ptmp0"), B("mod")], [B("xm")])
                for kc in range(8):
                    self.tr(psT[:, kc, :], xm[:, kc * 128:(kc + 1) * 128], self.ident_b[:], [B("xm"), B("ident_b")], [B("psA0")])
                self.cp("act", XTg[:, :, tt * 128:(tt + 1) * 128], psT, [B("psA0")], [XB])

        def prep(g, cpe="dve"):
            XTg, XB = XT[g % 2], B("XT%d" % (g % 2))

            def wload(c):
                i = c % 2
                self.dma(wbf[i][:], scrQ[c].rearrange("p (kc n) -> p kc n", n=128), reads=[B("scrQ%d" % c)], writes=[B("pwbf%d" % i)])

            def scores_mm(c):
                for tt in range(2):
                    self.mm(self.psA[0][:, 256 + tt * 128:256 + (tt + 1) * 128], qTc[c % 2][:, tt * 128:(tt + 1) * 128], keysT[:, c, :], True, True,
                            [B("qTc%d" % (c % 2)), B("keysT")], [B("psA0")])

            def scores_cp(c):
                for tt in range(2):
                    self.cp(cpe, SscR[tt][c % 3][:], self.psA[0][:, 256 + tt * 128:256 + (tt + 1) * 128], [B("psA0")], [B("Ssc%d_%d" % (tt, c % 3))])

            def level1(c):
                for tt in range(2):
                    S = SscR[tt][c % 3]
                    SB = B("Ssc%d_%d" % (tt, c % 3))
                    vB, iB = B("v16_%d" % tt), B("i16u_%d" % tt)
                    for half8 in range(2):
                        vs = v16[tt][:, c, half8 * 8:(half8 + 1) * 8]
                        iu = i16u[tt][:, c, half8 * 8:(half8 + 1) * 8]
                        self.S.op("dve", lambda q, vs=vs, S=S: q.max(out=vs, in_=S[:]), [SB], [vB])
                        self.S.op("dve", lambda q, vs=vs, S=S, iu=iu: q.max_index(out=iu, in_max=vs, in_values=S[:]), [SB, vB], [iB])
                        if half8 == 0:
                            self.S.op("dve", lambda q, vs=vs, S=S: q.match_replace(out=S[:], in_to_replace=vs, in_values=S[:], imm_value=NEG),
                                      [SB, vB], [SB])

            wload(0)
            for c in range(17):
                if c + 1 < 16:
                    wload(c + 1)
                if c < 16:
                    i = c % 2
                    for kc in range(8):
                        self.mm(self.psA[0][:, 0:TG], wbf[i][:, kc, :], XTg[:, kc, :], kc == 0, kc == 7, [B("pwbf%d" % i), XB], [B("psA0")])
                if c > 0:
                    scores_mm(c - 1)
                if c < 16:
                    self.cp(cpe, qTc[c % 2][:], self.psA[0][:, 0:TG], [B("psA0")], [B("qTc%d" % (c % 2))])
                if c > 0:
                    scores_cp(c - 1)
                    level1(c - 1)
                yield 4.2 if c > 0 else 0.5
            for tt in range(2):
                top, topB = tops[tt], B("top%d" % tt)
                wsel, wselB = wsels[tt], B("wsel%d" % tt)
                vB, iB = B("v16_%d" % tt), B("i16u_%d" % tt)
                self.cp("dve", i16f[:], i16u[tt][:], [iB], [B("i16f")])
                v16r = v16[tt][:].rearrange("p (h f) k -> p h f k", f=2)
                i16r = i16f[:].rearrange("p (h f) k -> p h f k", f=2)
                cand4 = cand[:].rearrange("p h (a b) -> p h a b", b=16)
                self.tt("dve", cand4, v16r[:, :, 0, :].unsqueeze(3).to_broadcast([128, 8, 16, 16]),
                        v16r[:, :, 1, :].unsqueeze(2).to_broadcast([128, 8, 16, 16]), ALU.add, [vB], [B("cand")])
                yield 2.6
                for h in range(8):
                    for half8 in range(2):
                        vs = top[:, h, half8 * 8:(half8 + 1) * 8]
                        self.S.op("dve", lambda q, vs=vs, h=h: q.max(out=vs, in_=cand[:, h, :]), [B("cand")], [topB])
                        self.S.op("dve", lambda q, vs=vs, h=h, half8=half8: q.max_index(out=pu[:, h, half8 * 8:(half8 + 1) * 8], in_max=vs, in_values=cand[:, h, :]),
                                  [B("cand"), topB], [B("pu")])
                        if half8 == 0:
                            self.S.op("dve", lambda q, vs=vs, h=h: q.match_replace(out=cand[:, h, :], in_to_replace=vs, in_values=cand[:, h, :], imm_value=NEG),
                                      [B("cand"), topB], [B("cand")])
                    yield 2.4
                self.S.op("dve", lambda q: q.tensor_single_scalar(out=abu[:, 0], in_=pu[:], scalar=4, op=ALU.logical_shift_right), [B("pu")], [B("abu")])
                self.S.op("dve", lambda q: q.tensor_single_scalar(out=abu[:, 1], in_=pu[:], scalar=15, op=ALU.bitwise_and), [B("pu")], [B("abu")])
                self.cp("dve", abf[:], abu[:], [B("abu")], [B("abf")])
                yield 1.0
                wsel4 = wsel[:].rearrange("p w (h k) -> p w h k", k=16)
                for f in range(2):
                    self.tt("dve", oh[:], abf[:, f].unsqueeze(3).to_broadcast([128, 8, 16, 16]),
                            iota16[:].unsqueeze(1).unsqueeze(1).to_broadcast([128, 8, 16, 16]), ALU.is_equal,
                            [B("abf"), B("iota16p")], [B("cand")])
                    self.tt("dve", oh[:], oh[:], i16r[:, :, f, :].unsqueeze(2).to_broadcast([128, 8, 16, 16]), ALU.mult,
                            [B("cand"), B("i16f")], [B("cand")])
                    self.S.op("dve", lambda q, f=f, wsel4=wsel4: q.tensor_reduce(out=wsel4[:, f], in_=oh[:], axis=AX.X, op=ALU.add), [B("cand")], [wselB])
                    yield 6.6
            yield ("wait", 2.0)
            prep_tail(g)
            yield 3.0

        def prep_tail(g):
            sTg, sTB = sT[g % 2], B("sT%d" % (g % 2))
            for tt in range(2):
                top, topB = tops[tt], B("top%d" % tt)
                wsel, wselB = wsels[tt], B("wsel%d" % tt)
                wsel4 = wsel[:].rearrange("p w (h k) -> p w h k", k=16)
                self.cp("dve", zs[:, :, 0:1], top[:, :, 0:1], [topB], [B("zs")])
                self.tt("dve", top[:], top[:], zs[:, :, 0:1].to_broadcast([128, 8, 16]), ALU.subtract, [topB, B("zs")], [topB])
                self.act(top[:], top[:], AF.Exp, [topB], [topB])
                self.S.op("dve", lambda q, top=top: q.tensor_reduce(out=zs[:, :, 0], in_=top[:], axis=AX.X, op=ALU.add), [topB], [B("zs")])
                self.recip(zs[:, :, 1], zs[:, :, 0], [B("zs")], [B("zs")])
                self.tt("dve", wsel4[:, 2], top[:], zs[:, :, 1:2].to_broadcast([128, 8, 16]), ALU.mult, [topB, B("zs")], [wselB])
                self.cp("dve", wselb[:], wsel[:], [wselB], [B("wselb")])
                for w in range(3):
                    self.tr(psT[:, w, :], wselb[:, w, :], self.ident_b[:], [B("wselb"), B("ident_b")], [B("psA0")])
                self.cp("act", sTg[:, :, tt * 128:(tt + 1) * 128], psT[:, 0:3, :], [B("psA0")], [sTB])

        def gconstruct(g, tt, dst, ev="dve"):
            sTg, sTB = sT[g % 2], B("sT%d" % (g % 2))
            Gd, GB = G3[dst], B("G3_%d" % dst)
            io4 = iota_rb[:].unsqueeze(1).to_broadcast([128, 4, 128])
            nb = 32

            def dve_part(bi):
                r = bi % 4
                n0 = tt * 128 + bi * 4
                bc = lambda w: sTg[:, w, n0:n0 + 4].unsqueeze(2).to_broadcast([128, 4, 128])
                self.tt("dve", P1[r][:], io4, bc(0), ALU.is_equal, [B("iota_rb"), sTB], [B("P1_%d" % r)])
                self.tt("dve", P2[r][:], io4, bc(1), ALU.is_equal, [B("iota_rb"), sTB], [B("P2_%d" % r)])
                self.tt("dve", P1[r][:], P1[r][:], bc(2), ALU.mult, [B("P1_%d" % r), sTB], [B("P1_%d" % r)])

            def pe_part(bi):
                r = bi % 4
                for k in range(4):
                    self.mm(psG[:, k, :], P1[r][:, k, :], P2[r][:, k, :], True, True, [B("P1_%d" % r), B("P2_%d" % r)], [B("psA0")])
                self.cp(ev, Gd[:, bi * 4:bi * 4 + 4, :], psG, [B("psA0")], [GB])

            for u in range(nb // 2 + 1):
                if u > 0:
                    pe_part(2 * u - 2)
                    pe_part(2 * u - 1)
                if u < nb // 2:
                    dve_part(2 * u)
                    dve_part(2 * u + 1)
                yield 4.9 if u < nb // 2 else 1.2

        def run_all(gen):
            for _ in gen:
                pass

        def main(g, Ta, Tb, inter):
            XTg, XB = XT[g % 2], B("XT%d" % (g % 2))
            Gt = [G3[Ta], G3[Tb]]
            GtB = [B("G3_%d" % Ta), B("G3_%d" % Tb)]

            def load(j):
                self.dma(ubf[j % 3][:], scrU[j], reads=[B("scrU%d" % j)], writes=[B("ubf%d" % (j % 3))])
                self.dma(vbf[j % 4][:], scrV[j], reads=[B("scrV%d" % j)], writes=[B("vbf%d" % (j % 4))])

            def Hm(j):
                ib = j % 3
                u3 = ubf[ib][:].rearrange("p (c i) -> p c i", i=128)
                for kc in range(8):
                    self.mm(psH[j % 3], u3[:, kc, :], XTg[:, kc, :], kc == 0, kc == 7, [B("ubf%d" % ib), XB], [psHB[j % 3]])

            def post(j):
                i3 = j % 3
                self.act(hg[i3][:], psH[i3], AF.Gelu_apprx_tanh, [psHB[i3]], [B("hg%d" % i3)])
                for tt in range(2):
                    cs = slice(tt * 128, (tt + 1) * 128)
                    self.tt("pool", AT[i3][:, cs], hg[i3][:, cs], Gt[tt][:, :, j], ALU.mult, [B("hg%d" % i3), GtB[tt]], [B("AT%d" % i3)])

            def outm(j):
                i3, ib = j % 3, j % 4
                for tt in range(2):
                    for half in range(2):
                        cs = slice(half * 512, (half + 1) * 512)
                        self.mm(psO[tt][:, cs], AT[i3][:, tt * 128:(tt + 1) * 128], vbf[ib][:, cs], j == 0, j == nj - 1,
                                [B("AT%d" % i3), B("vbf%d" % ib)], [psOB[tt]])

            W = 0.0
            waiting = None
            for j0 in range(min(3, nj)):
                load(j0)
            Hm(0)
            if nj > 1:
                Hm(1)
            for j in range(nj):
                if j + 3 < nj:
                    load(j + 3)
                if j + 2 < nj:
                    Hm(j + 2)
                post(j)
                outm(j)
                if inter is None:
                    continue
                budget = (j + 1) * 1.8 * 0.9
                emitted = 0
                while inter is not None and emitted < 1:
                    if waiting is not None:
                        if W + waiting > budget:
                            break
                        waiting = None
                    if W > budget:
                        break
                    c = next(inter, "done")
                    if c == "done":
                        inter = None
                    elif isinstance(c, tuple):
                        waiting = c[1]
                    else:
                        W += c
                        emitted += 1
            if inter is not None:
                run_all(inter)

        def epilogue(g):
            v = 1 if g < 4 else 0
            for tt in range(2):
                rows = slice((2 * g + tt) * 128, (2 * g + tt + 1) * 128)
                tf, tB = ptmp, B("ptmp0")
                self.dma(x1[:], self.y[rows, :], writes=[B("x1_0")])
                self.tt("dve", tf[:], psO[tt][:], self.modsl(v, 5), ALU.mult, [psOB[tt], B("mod")], [tB])
                self.tt("dve", x1[:], tf[:], x1[:], ALU.add, [tB, B("x1_0")], [B("x1_0")])
                self.dma(self.y[rows, :], x1[:], reads=[B("x1_0")])

        def chain(*gens):
            for gn in gens:
                for item in gn:
                    yield item

        prep_a(0)
        run_all(prep(0, "act"))
        run_all(gconstruct(0, 0, 0, "act"))
        run_all(gconstruct(0, 1, 1, "act"))
        Ta, Tb, Fr = 0, 1, 2
        if ngroups > 1:
            prep_a(1)
        for g in range(ngroups):
            if g + 1 < ngroups:
                main(g, Ta, Tb, chain(prep(g + 1), gconstruct(g + 1, 0, Fr)))
                epilogue(g)
                if g + 2 < ngroups:
                    prep_a(g + 2)
                run_all(gconstruct(g + 1, 1, Ta, "act"))
                Ta, Tb, Fr = Fr, Ta, Tb
            else:
                main(g, Ta, Tb, None)
                epilogue(g)


def _prep_inputs(inp):
    f = lambda a: np.ascontiguousarray(np.asarray(a, dtype=np.float32))
    consts = _host_constants()
    shared = dict(consts)
    w_in = f(inp["w_in"][0])
    shared["w_ada"] = f(inp["w_ada"][0])
    shared["b_ada"] = f(inp["b_ada"][0]).reshape(1, 6144)
    shared["n1g"] = f(inp["norm1_g"][0]).reshape(1, 1024)
    shared["n2g"] = f(inp["norm2_g"][0]).reshape(1, 1024)
    shared["w_in"] = w_in
    perm = _rope_partner_perm()
    shared["w_sw"] = f(np.concatenate([w_in[:, 0:512][:, perm], w_in[:, 512:1024][:, perm]], axis=1))
    shared["dec"] = f(np.concatenate([inp["ret_decay_f"][0], inp["ret_decay_b"][0]])).reshape(1, 16)
    shared["gn_g"] = f(np.asarray(inp["ret_gn_g"][0]).reshape(4, 128).T)
    qg = np.tile(np.asarray(inp["na_qn_g"][0]), 2)
    kg = np.tile(np.asarray(inp["na_kn_g"][0]), 2)
    shared["qkn_g"] = f(np.stack([qg, kg], axis=1))
    shared["kn_row"] = f(inp["na_kn_g"][0]).reshape(1, 64)
    rpb = np.asarray(inp["na_rpb"][0], dtype=np.float32)
    kc = np.arange(64)[:, None]
    qc = np.arange(64)[None, :]
    dc = np.clip(kc - qc + 15, 0, 30)
    x = np.arange(15)
    tr = rpb[:, (14 - x)[:, None, None], dc[None, :, :]]
    tr = np.transpose(tr, (2, 0, 1, 3))
    shared["rpbT"] = f(np.concatenate([tr, tr], axis=0))
    shared["w_out"] = f(inp["w_out"][0])
    shared["wq"] = f(inp["peer_wq"][0])
    keys = np.asarray(inp["peer_keys"][0], dtype=np.float32)
    shared["keysT"] = f(np.transpose(keys.reshape(16, 128, 128), (2, 0, 1)))
    U = np.asarray(inp["peer_u"][0], dtype=np.float32)
    shared["Ut"] = f(np.transpose(U.reshape(128, 128, 8, 128), (1, 3, 2, 0)))
    V = np.asarray(inp["peer_v"][0], dtype=np.float32)
    shared["Vp"] = f(np.transpose(V.reshape(128, 128, 1024), (1, 0, 2)))
    xp = np.asarray(inp["x_prompt"], dtype=np.float32)
    xs = np.asarray(inp["x_sample"], dtype=np.float32)
    cc = np.asarray(inp["c"], dtype=np.float32)
    cctx = np.asarray(inp["c_ctx"], dtype=np.float32)
    maps = []
    for c in range(NCORES):
        m = dict(shared)
        m["x"] = f(np.concatenate([xs[c], xp[2 * c], xp[2 * c + 1]], axis=0))
        cv = np.stack([cctx, cc[c]], axis=0)
        m["cT"] = f(np.transpose(cv.reshape(2, 8, 128), (2, 1, 0)))
        m["kctxT"] = f(np.asarray(inp["cache_na_k"][c, 0], dtype=np.float32).reshape(256, 512).T)
        m["vctx"] = f(np.asarray(inp["cache_na_v"][c, 0], dtype=np.float32).reshape(256, 512))
        m["s0"] = f(inp["state_ret"][c, 0])
        maps.append(m)
    return maps


_NC_CACHE = {}


def _get_nc(stage=3):
    if stage not in _NC_CACHE:
        _NC_CACHE[stage] = Builder(stage=stage).build()
    return _NC_CACHE[stage]


def kernel(**inputs):
    maps = _prep_inputs(inputs)
    nc = _get_nc()
    res = run_bass_kernel_spmd(nc, maps, core_ids=list(range(NCORES)))
    outs = res.results
    y_p = np.zeros((16, 256, 1024), np.float32)
    y_s = np.zeros((8, 1024, 1024), np.float32)
    nk = np.zeros((16, 1, 256, 8, 64), np.float32)
    nv = np.zeros((16, 1, 256, 8, 64), np.float32)
    st = np.zeros((16, 1, 2, 8, 64, 64), np.float32)
    for c in range(NCORES):
        r = outs[c]
        y = r["y"]
        y_s[c] = y[0:1024]
        y_p[2 * c] = y[1024:1280]
        y_p[2 * c + 1] = y[1280:1536]
        nk[2 * c:2 * c + 2, 0] = r["nk"].reshape(2, 256, 8, 64)
        nv[2 * c:2 * c + 2, 0] = r["nv"].reshape(2, 256, 8, 64)
        st[2 * c:2 * c + 2, 0] = r["st"]
    return (y_p, y_s, nk, nv, st)
```

```python
import math
import os
from contextlib import ExitStack

import numpy as np
import concourse.bass as bass
import concourse.mybir as mybir
from concourse.bass_utils import run_bass_kernel_spmd

F32 = mybir.dt.float32
BF16 = mybir.dt.bfloat16
I32 = mybir.dt.int32
U32 = mybir.dt.uint32
ALU = mybir.AluOpType
AF = mybir.ActivationFunctionType
AX = mybir.AxisListType

NCORES = 8
D = 1024
EPS = 1e-6
NEG = -1e30


class Buf:
    __slots__ = ("name", "w", "r")

    def __init__(self, name):
        self.name = name
        self.w = None
        self.r = []


class Eng:
    def __init__(self, name, sem, same_sync):
        self.name = name
        self.sem = sem
        self.count = 0
        self.waited = {}
        self.same_sync = same_sync
        self.prog = []


class Sched:
    def __init__(self, nc, stack, n_dma_slots=int(os.environ.get("NDMA", "24"))):
        self.nc = nc
        self.sems = {}
        self.engs = {}
        for name, same in (("pe", False), ("act", True), ("dve", True), ("pool", True), ("sp", True)):
            sem = stack.enter_context(nc.semaphore("s_" + name))
            self.sems[name] = sem
            self.engs[name] = Eng(name, sem, same)
        self.dma_slots = []
        for i in range(n_dma_slots):
            key = "dma%d" % i
            self.sems[key] = stack.enter_context(nc.semaphore("s_" + key))
            self.dma_slots.append([key, 0])
        self.dma_i = 0
        self.bg_slots = []
        for i in range(16):
            key = "bg%d" % i
            self.sems[key] = stack.enter_context(nc.semaphore("s_" + key))
            self.bg_slots.append([key, 0])
        self.bg_i = 0
        self.n_inst = 0

    def _wait(self, e, tok):
        if tok is None:
            return
        key, val = tok
        if key == e.name and not e.same_sync:
            return
        if e.waited.get(key, 0) >= val:
            return
        e.waited[key] = val
        sem = self.sems[key]
        e.prog.append(lambda q, sem=sem, val=val: q.wait_ge(sem, val))

    def _deps(self, e, reads, writes):
        for b in reads:
            self._wait(e, b.w)
            if b.name.startswith("ps"):
                for t in b.r:
                    if t[0] != e.name:
                        self._wait(e, t)
        for b in writes:
            self._wait(e, b.w)
            for t in b.r:
                self._wait(e, t)

    @staticmethod
    def _mark(tok, reads, writes):
        for b in reads:
            b.r.append(tok)
            if len(b.r) > 64:
                b.r = b.r[-64:] if False else b.r
        for b in writes:
            b.w = tok
            b.r = []

    def op(self, eng, fn, reads=(), writes=()):
        e = self.engs[eng]
        self._deps(e, reads, writes)
        e.count += 1
        tok = (e.name, e.count)
        sem = e.sem
        e.prog.append(lambda q, fn=fn, sem=sem: fn(q).then_inc(sem, 1))
        self._mark(tok, reads, writes)
        self.n_inst += 1
        return tok

    def dma(self, eng, out, in_, reads=(), writes=(), bg=False, **kw):
        e = self.engs[eng]
        self._deps(e, reads, writes)
        if bg:
            slot = self.bg_slots[self.bg_i % len(self.bg_slots)]
            self.bg_i += 1
        else:
            slot = self.dma_slots[self.dma_i % len(self.dma_slots)]
            self.dma_i += 1
        key = slot[0]
        if slot[1] > 0:
            self._wait(e, (key, slot[1]))
        slot[1] += 16
        tok = (key, slot[1])
        sem = self.sems[key]
        e.prog.append(lambda q, out=out, in_=in_, sem=sem, kw=kw:
                      q.dma_start(out=out, in_=in_, **kw).then_inc(sem, 16))
        self._mark(tok, reads, writes)
        self.n_inst += 1
        return tok

    def barrier(self):
        for e in self.engs.values():
            for key, val in self.dma_slots + self.bg_slots:
                if val > 0:
                    self._wait(e, (key, val))
            for o in self.engs.values():
                if o is not e and o.count > 0:
                    self._wait(e, (o.name, o.count))

    def emit(self):
        nc = self.nc
        progs = {k: v.prog for k, v in self.engs.items()}
        with nc.Block() as block:
            @block.tensor
            def _(q):
                for f in progs["pe"]:
                    f(q)

            @block.scalar
            def _(q):
                for f in progs["act"]:
                    f(q)

            @block.vector
            def _(q):
                for f in progs["dve"]:
                    f(q)

            @block.gpsimd
            def _(q):
                for f in progs["pool"]:
                    f(q)

            @block.sync
            def _(q):
                for f in progs["sp"]:
                    f(q)


def _rope_tables():
    T = 1024
    n = np.arange(T)
    rows = (n // 64).astype(np.float32)
    cols = (n % 64).astype(np.float32)
    freqs = (np.float32(10000.0) ** (-np.arange(16, dtype=np.float32) / np.float32(16))).astype(np.float32)
    C = np.zeros((128, T), np.float32)
    Sg = np.zeros((128, T), np.float32)
    for p in range(128):
        d = p % 64
        pos = rows if d < 32 else cols
        dd = d % 32
        f = dd % 16
        ang = (pos * freqs[f]).astype(np.float32)
        C[p] = np.cos(ang)
        Sg[p] = -np.sin(ang) if dd < 16 else np.sin(ang)
    return C, Sg


def _rope_partner_perm():
    perm = np.zeros(512, np.int64)
    for h in range(8):
        for d in range(64):
            dd = d % 32
            partner = d + 16 if dd < 16 else d - 16
            perm[h * 64 + d] = h * 64 + partner
    return perm


def _na_windows():
    rows, kh = 16, 8
    q_of_k = {}
    for kr in range(rows):
        qs = [qr for qr in range(rows) if min(max(qr - kh // 2, 0), rows - kh) <= kr < min(max(qr - kh // 2, 0), rows - kh) + kh]
        assert qs == list(range(qs[0], qs[-1] + 1))
        q_of_k[kr] = (qs[0], qs[-1])
    return q_of_k


def _host_constants():
    c = {}
    c["ident"] = np.eye(128, dtype=np.float32)
    ob = np.zeros((128, 128), np.float32)
    ob[:64, :64] = 1.0 / 64
    ob[64:, 64:] = 1.0 / 64
    c["onesbd"] = ob
    op = np.zeros((128, 2, 128), np.float32)
    op[:, 0, :64] = 1.0
    op[:, 1, 64:] = 1.0
    c["onespad"] = op
    p = np.arange(128, dtype=np.float32)[:, None]
    j = np.arange(1920, dtype=np.float32)[None, :]
    c["expo"] = (j - 896.0 - p).astype(np.float32)
    c["iota_n1"] = np.broadcast_to(np.arange(1, 1025, dtype=np.float32)[None], (128, 1024)).copy()
    c["iota_rev"] = np.broadcast_to((1024 - np.arange(1024, dtype=np.float32))[None], (128, 1024)).copy()
    tq = np.zeros((128, 2, 2), np.float32)
    for t in range(2):
        tq[:, 0, t] = 255 - (t * 128 + np.arange(128))
        tq[:, 1, t] = t * 128 + np.arange(128)
    c["tq"] = tq
    C, Sg = _rope_tables()
    c["rope_c"] = C
    c["rope_s"] = Sg
    qc = np.arange(64)
    cstart = np.clip(qc - 8, 0, 48)
    kc = np.arange(64)
    inwin = (kc[:, None] >= cstart[None, :]) & (kc[:, None] < cstart[None, :] + 16)
    cm = np.where(inwin, 0.0, NEG).astype(np.float32)
    c["cmask"] = np.concatenate([cm, cm], axis=0)
    c["iota_row"] = np.broadcast_to(np.arange(128, dtype=np.float32)[None], (128, 128)).copy()
    c["iota16"] = np.broadcast_to(np.arange(16, dtype=np.float32)[None], (128, 16)).copy()
    return c


CONST_SHAPES = {
    "ident": [128, 128], "onesbd": [128, 128], "onespad": [128, 2, 128], "expo": [128, 1920],
    "iota_n1": [128, 1024], "iota_rev": [128, 1024], "tq": [128, 2, 2], "rope_c": [128, 1024],
    "rope_s": [128, 1024], "cmask": [128, 64], "iota_row": [128, 128], "iota16": [128, 16],
}

IN_SHAPES = {
    "x": [1536, 1024], "cT": [128, 8, 2], "w_ada": [1024, 6144], "b_ada": [1, 6144],
    "n1g": [1, 1024], "n2g": [1, 1024], "w_in": [1024, 3584], "w_sw": [1024, 1024], "dec": [1, 16],
    "gn_g": [128, 4], "qkn_g": [128, 2], "kn_row": [1, 64], "rpbT": [128, 8, 15, 64],
    "w_out": [1024, 1024], "wq": [1024, 2048], "keysT": [128, 16, 128],
    "Ut": [128, 128, 8, 128], "Vp": [128, 128, 1024],
    "kctxT": [512, 256], "vctx": [256, 512], "s0": [2, 8, 64, 64],
}

SEQS = [(0, 8, 1, True, [(0, 8)]), (8, 4, 0, False, [(0, 2), (2, 2)])]


class Builder:
    def __init__(self, stage=3, debug=False):
        self.stage = stage
        self.debug = debug
        self.nc = bass.Bass("TRN2", target_bir_lowering=False)
        nc = self.nc
        self.din = {}
        for name, shp in list(IN_SHAPES.items()) + list(CONST_SHAPES.items()):
            self.din[name] = nc.dram_tensor(name, shp, F32, kind="ExternalInput").ap()
        self.y = nc.dram_tensor("y", [1536, 1024], F32, kind="ExternalOutput").ap()
        self.nk = nc.dram_tensor("nk", [512, 512], F32, kind="ExternalOutput").ap()
        self.nv = nc.dram_tensor("nv", [512, 512], F32, kind="ExternalOutput").ap()
        self.st = nc.dram_tensor("st", [2, 2, 8, 64, 64], F32, kind="ExternalOutput").ap()
        self.bufs = {}

    def bg_issue(self, n, dep=None):
        for _ in range(n):
            if not self.bg_todo:
                return
            out, in_, name = self.bg_todo.pop(0)
            self.dma(out, in_, reads=[self.B(dep)] if dep else [], writes=[self.B(name)], eng="pool", bg=True)

    def B(self, name):
        b = self.bufs.get(name)
        if b is None:
            b = self.bufs[name] = Buf(name)
        return b

    def sb(self, st, name, shape, dt=F32):
        return st.enter_context(self.nc.sbuf_tensor("sb_" + name, shape, dt))

    def mm(self, out, lhsT, rhs, start, stop, reads, writes, skip=False):
        if skip:
            self.S.op("pe", lambda q: q.matmul(out, lhsT=lhsT, rhs=rhs, start=start, stop=stop, skip_group_check=True), reads, writes)
        else:
            self.S.op("pe", lambda q: q.matmul(out, lhsT=lhsT, rhs=rhs, start=start, stop=stop), reads, writes)

    def tr(self, out, in_, ident, reads, writes):
        self.S.op("pe", lambda q: q.transpose(out=out, in_=in_, identity=ident), reads, writes)

    def act(self, out, in_, func, reads, writes, **kw):
        self.S.op("act", lambda q: q.activation(out=out, in_=in_, func=func, **kw), reads, writes)

    def tt(self, eng, out, in0, in1, op, reads, writes):
        self.S.op(eng, lambda q: q.tensor_tensor(out=out, in0=in0, in1=in1, op=op), reads, writes)

    def ts(self, eng, out, in0, s1, s2, op0, op1, reads, writes):
        if s2 is None:
            self.S.op(eng, lambda q: q.tensor_scalar(out=out, in0=in0, scalar1=s1, scalar2=None, op0=op0), reads, writes)
        else:
            self.S.op(eng, lambda q: q.tensor_scalar(out=out, in0=in0, scalar1=s1, scalar2=s2, op0=op0, op1=op1), reads, writes)

    def stt(self, eng, out, in0, scalar, in1, op0, op1, reads, writes):
        self.S.op(eng, lambda q: q.scalar_tensor_tensor(out=out, in0=in0, scalar=scalar, in1=in1, op0=op0, op1=op1), reads, writes)

    def cp(self, eng, out, in_, reads, writes):
        if eng == "act":
            self.S.op("act", lambda q: q.copy(out=out, in_=in_), reads, writes)
        else:
            self.S.op(eng, lambda q: q.tensor_copy(out=out, in_=in_), reads, writes)

    def memset(self, eng, ap, val, writes):
        self.S.op(eng, lambda q: q.memset(ap, val), (), writes)

    def recip(self, out, in_, reads, writes):
        self.S.op("dve", lambda q: q.reciprocal(out=out, in_=in_), reads, writes)

    def dma(self, out, in_, reads=(), writes=(), eng="sp", bg=False):
        self.S.dma(eng, out, in_, reads, writes, bg=bg)

    def build(self):
        nc = self.nc
        with ExitStack() as top:
            self.S = Sched(nc, top)
            self.psA = [top.enter_context(nc.psum_tensor("psA%d" % i, [128, 512], F32)) for i in range(2)]
            self.psB = top.enter_context(nc.psum_tensor("psB", [128, 1024], F32))
            self.psC = top.enter_context(nc.psum_tensor("psC", [128, 1024], F32))
            self.psD = top.enter_context(nc.psum_tensor("psD", [128, 1024], F32))
            self.scrU = nc.dram_tensor("scrU", [128, 128, 1024], BF16).ap()
            self.scrV = nc.dram_tensor("scrV", [128, 128, 1024], BF16).ap()
            self.scrQ = nc.dram_tensor("scrQ", [16, 128, 1024], BF16).ap()
            self.bg_todo = []
            if self.stage >= 3:
                Ut2 = self.din["Ut"].rearrange("j p c i -> j p (c i)")
                wq4 = self.din["wq"].rearrange("(kc p) (c n) -> c p kc n", p=128, n=128)
                for c in range(16):
                    self.bg_todo.append((self.scrQ[c].rearrange("p (kc n) -> p kc n", n=128), wq4[c], "scrQ%d" % c))
                for j in range(128):
                    self.bg_todo.append((self.scrU[j], Ut2[j], "scrU%d" % j))
                    self.bg_todo.append((self.scrV[j], self.din["Vp"][j], "scrV%d" % j))
            self.mod = self.sb(top, "mod", [128, 2, 6144], BF16)
            self.ident_b = self.sb(top, "ident_b", [128, 128], BF16)
            self.small = self.sb(top, "small", [128, 64], F32)
            with ExitStack() as ph1:
                self.phase1_alloc(ph1)
                self.setup(ph1)
                with ExitStack() as ws:
                    self.phase1_work(ws)
                    if self.stage >= 1:
                        for si, seq in enumerate(SEQS):
                            self.attn_seq(si, *seq)
                    self.S.barrier()
            if self.stage >= 3:
                with ExitStack() as ph2:
                    self.peer(ph2)
                    self.S.barrier()
            self.S.barrier()
            self.S.emit()
        return nc

    def phase1_alloc(self, st):
        sb = lambda n, s, d=F32: self.sb(st, n, s, d)
        self.ones_bd = sb("ones_bd", [128, 128], BF16)
        self.ones_pad = sb("ones_pad", [128, 2, 128], BF16)
        self.strips = sb("strips", [128, 8, 1920], BF16)
        self.lg = sb("lg", [128, 16])
        self.nlgb = sb("nlgb", [128, 8])
        self.lgcol = sb("lgcol", [128, 2, 4])
        self.wst = sb("wst", [128, 2, 2, 8])
        self.gn_g = sb("gn_g", [128, 4])
        self.qkn_g = sb("qkn_g", [128, 2])
        self.kn_bc = sb("kn_bc", [128, 64])
        self.wo = sb("wo", [128, 8, 1024], BF16)
        self.iota_n1 = sb("iota_n1", [128, 1024])
        self.iota_rev = sb("iota_rev", [128, 1024])
        self.rope_c = sb("rope_c", [128, 1024])
        self.rope_s = sb("rope_s", [128, 1024])
        self.trb = sb("trb", [128, 8, 15, 64], BF16)

    def phase1_work(self, st):
        sb = lambda n, s, d=F32: self.sb(st, n, s, d)
        B = self.B
        self.hT = sb("hT", [128, 8, 1024], BF16)
        self.oT = sb("oT", [128, 8, 1024], BF16)
        self.xb = [sb("xb%d" % i, [128, 1024]) for i in range(2)]
        self.tmpf = [sb("tmpf%d" % i, [128, 1024]) for i in range(2)]
        self.hb = sb("hb", [128, 1024], BF16)
        self.wstage = [sb("wstage%d" % i, [128, 8, 128]) for i in range(3)]
        self.wbf = [sb("wbf%d" % i, [128, 8, 128], BF16) for i in range(3)]
        self.qT = sb("qT", [128, 1024], BF16)
        self.kTm = sb("kTm", [128, 2, 1024], BF16)
        self.sg = sb("sg", [128, 1024], BF16)
        self.vpad = sb("vpad", [128, 8, 2, 128], BF16)
        self.kcm = sb("kcm", [128, 2, 256], BF16)
        self.vcp = sb("vcp", [128, 2, 2, 128], BF16)
        self.qf = sb("qf", [128, 2, 1024], BF16)
        self.s0bd = sb("s0bd", [128, 2, 128], BF16)
        self.s0st = sb("s0st", [128, 2, 128])
        self.sc = [sb("sc%d" % i, [128, 768], BF16) for i in range(3)]
        self.sbias = [sb("sbias%d" % i, [128, 768]) for i in range(2)]
        self.ctxst = sb("ctxst", [128, 256])
        self.kw = sb("kw", [128, 2, 128], BF16)
        self.nko = sb("nko", [128, 128])
        self.nvo = sb("nvo", [128, 128])
        self.sto = sb("sto", [64, 2, 2, 2, 64])
        self.memset("pool", self.kTm[:], 0.0, [B("kTm")])
        self.memset("pool", self.vpad[:], 0.0, [B("vpad")])
        self.memset("pool", self.kcm[:], 0.0, [B("kcm")])
        self.memset("pool", self.vcp[:], 0.0, [B("vcp")])
        self.memset("pool", self.s0st[:], 0.0, [B("s0st")])

    def setup(self, st_outer):
        B, din = self.B, self.din
        with ExitStack() as st:
            sb = lambda n, s, d=F32: self.sb(st, n, s, d)
            identf = sb("identf", [128, 128])
            self.dma(identf[:], din["ident"], writes=[B("identf")])
            self.cp("dve", self.ident_b[:], identf[:], [B("identf")], [B("ident_b")])
            tmp128 = sb("tmp128", [128, 2, 128])
            self.dma(tmp128[:, 0, :], din["onesbd"], writes=[B("tmp128")])
            self.cp("dve", self.ones_bd[:], tmp128[:, 0, :], [B("tmp128")], [B("ones_bd")])
            self.dma(tmp128[:], din["onespad"], reads=[], writes=[B("tmp128")])
            self.cp("dve", self.ones_pad[:], tmp128[:], [B("tmp128")], [B("ones_pad")])
            for nm, t in (("iota_n1", self.iota_n1), ("iota_rev", self.iota_rev), ("rope_c", self.rope_c),
                          ("rope_s", self.rope_s), ("gn_g", self.gn_g), ("qkn_g", self.qkn_g)):
                self.dma(t[:], din[nm], writes=[B(nm)])
            self.dma(self.kn_bc[:], din["kn_row"][0].partition_broadcast(128), writes=[B("kn_bc")])
            dec = sb("dec", [128, 16])
            self.dma(dec[:], din["dec"][0].partition_broadcast(128), writes=[B("dec")])
            self.act(dec[:], dec[:], AF.Exp, [B("dec")], [B("dec")], scale=-1.0)
            self.act(dec[:], dec[:], AF.Ln, [B("dec")], [B("dec")], bias=1.0)
            self.ts("dve", self.lg[:], dec[:], -1.0, None, ALU.mult, None, [B("dec")], [B("lg")])
            self.ts("dve", self.nlgb[:], self.lg[:, 8:16], -1.0, None, ALU.mult, None, [B("lg")], [B("nlgb")])
            for r in range(2):
                self.cp("dve", self.lgcol[0:64, r, :], self.lg[0:64, r * 8:r * 8 + 8:2], [B("lg")], [B("lgcol")])
                self.cp("dve", self.lgcol[64:128, r, :], self.lg[64:128, r * 8 + 1:r * 8 + 8:2], [B("lg")], [B("lgcol")])
            tq = sb("tq", [128, 2, 2])
            self.dma(tq[:], din["tq"], writes=[B("tq")])
            for r in range(2):
                self.tt("dve", self.wst[:, r], tq[:, r, :].unsqueeze(2).to_broadcast([128, 2, 8]),
                        self.lg[:, r * 8:(r + 1) * 8].unsqueeze(1).to_broadcast([128, 2, 8]), ALU.mult,
                        [B("tq"), B("lg")], [B("wst")])
            self.act(self.wst[:], self.wst[:], AF.Exp, [B("wst")], [B("wst")])
            self.ts("dve", self.wst[:], self.wst[:], 0.125, None, ALU.mult, None, [B("wst")], [B("wst")])
            expo = sb("expo", [128, 1920])
            t1 = sb("t1", [128, 1920])
            t2 = sb("t2", [128, 1920])
            self.dma(expo[:], din["expo"], writes=[B("expo")])
            for h in range(8):
                self.ts("dve", t1[:], expo[:], self.lg[:, h:h + 1], None, ALU.mult, None, [B("expo"), B("lg")], [B("t1")])
                self.stt("dve", t2[:], expo[:], self.nlgb[:, h:h + 1], t1[:], ALU.mult, ALU.min,
                         [B("expo"), B("nlgb"), B("t1")], [B("t2")])
                self.act(t2[:], t2[:], AF.Exp, [B("t2")], [B("t2")])
                self.ts("dve", self.strips[:, h, :], t2[:], 0.125, None, ALU.mult, None, [B("t2")], [B("strips")])
            cmask = sb("cmask", [128, 64])
            self.dma(cmask[:], din["cmask"], writes=[B("cmask")])
            for h in range(8):
                trs = t1[:, 0:960].rearrange("p (x q) -> p x q", q=64)
                self.dma(trs, din["rpbT"][:, h], writes=[B("t1")])
                self.tt("dve", self.trb[:, h], trs, cmask[:].unsqueeze(1).to_broadcast([128, 15, 64]), ALU.add,
                        [B("t1"), B("cmask")], [B("trb")])
            for kc in range(8):
                wsl = t2[:, 0:1024]
                self.dma(wsl, din["w_out"][kc * 128:(kc + 1) * 128, :], writes=[B("t2")])
                self.cp("act", self.wo[:, kc, :], wsl, [B("t2")], [B("wo")])
        self.S.barrier()
        self.bg_issue(32)
        self.adaln()
        self.S.barrier()

    def adaln(self):
        B, din = self.B, self.din
        with ExitStack() as st:
            sb = lambda n, s, d=F32: self.sb(st, n, s, d)
            cT = sb("cT", [128, 8, 2])
            rep = sb("rep", [128, 8, 2, 128], BF16)
            wst = [sb("awst%d" % i, [128, 8, 512]) for i in range(2)]
            wbf = [sb("awbf%d" % i, [128, 8, 512], BF16) for i in range(2)]
            bbc = [sb("bbc%d" % i, [128, 512]) for i in range(2)]
            ngb = sb("ngb", [128, 2, 1024])
            self.dma(cT[:], din["cT"], writes=[B("cT")])
            self.act(cT[:], cT[:], AF.Silu, [B("cT")], [B("cT")])
            self.cp("dve", rep[:], cT[:].unsqueeze(3).to_broadcast([128, 8, 2, 128]), [B("cT")], [B("rep")])
            self.dma(ngb[:, 0, :], din["n1g"][0].partition_broadcast(128), writes=[B("ngb")])
            self.dma(ngb[:, 1, :], din["n2g"][0].partition_broadcast(128), writes=[B("ngb")])
            w3 = din["w_ada"].rearrange("(kc p) n -> p kc n", p=128)
            for blk in range(12):
                i = blk % 2
                cs = slice(blk * 512, (blk + 1) * 512)
                self.dma(wst[i][:], w3[:, :, cs], writes=[B("awst%d" % i)])
                self.dma(bbc[i][:], din["b_ada"][0, cs].partition_broadcast(128), writes=[B("bbc%d" % i)])
                self.cp("dve" if blk % 2 == 0 else "act", wbf[i][:], wst[i][:], [B("awst%d" % i)], [B("awbf%d" % i)])
                for v in range(2):
                    ps = self.psA[v]
                    for kc in range(8):
                        self.mm(ps[:], rep[:, kc, v, :], wbf[i][:, kc, :], kc == 0, kc == 7,
                                [B("rep"), B("awbf%d" % i)], [B("psA%d" % v)])
                    self.tt("dve", self.mod[:, v, cs], ps[:], bbc[i][:], ALU.add,
                            [B("psA%d" % v), B("bbc%d" % i)], [B("mod")])
            for v in range(2):
                for j, ch in ((0, 1), (1, 4)):
                    sl = self.mod[:, v, ch * 1024:(ch + 1) * 1024]
                    self.stt("dve", sl, sl, 1.0, ngb[:, j, :], ALU.add, ALU.mult, [B("mod"), B("ngb")], [B("mod")])

    def modsl(self, v, ch):
        return self.mod[:, v, ch * 1024:(ch + 1) * 1024]

    def load_w(self, dram_cols, k):
        B = self.B
        i = self.wcount % 3
        j = self.wcount % 3
        self.wcount += 1
        self.bg_issue(2, "wbf%d" % ((j + 2) % 3))
        self.dma(self.wstage[i][:], dram_cols.rearrange("(kc p) n -> p kc n", p=128), writes=[B("wstage%d" % i)])
        self.cp("act", self.wbf[j][:], self.wstage[i][:], [B("wstage%d" % i)], [B("wbf%d" % j)])
        return self.wbf[j], B("wbf%d" % j)

    def proj_fm(self, w, wB, T, ps, psB_, b0, bn):
        for kc in range(8):
            self.mm(ps[:, 0:bn], w[:, kc, :], self.hT[:, kc, b0:b0 + bn], kc == 0, kc == 7,
                    [wB, self.B("hT")], [psB_])

    def proj_tm(self, w, wB, t, ps, psB_):
        for kc in range(8):
            self.mm(ps[:, 0:128], self.hT[:, kc, t * 128:(t + 1) * 128], w[:, kc, :], kc == 0, kc == 7,
                    [wB, self.B("hT")], [psB_])

    def attn_seq(self, si, tile0, NT, v, is_sample, segs):
        B, din = self.B, self.din
        T = NT * 128
        blocks = [(b0, min(512, T - b0)) for b0 in range(0, T, 512)]
        x_rows = lambda t: slice((tile0 + t) * 128, (tile0 + t + 1) * 128)
        w_in = din["w_in"]
        self.wcount = getattr(self, "wcount", 0)
        psT = self.psA[0][:].bitcast(BF16).rearrange("p (c n) -> p c n", n=128)
        for t in range(NT):
            xb = self.xb[t % 2]
            xB = B("xb%d" % (t % 2))
            self.dma(xb[:], din["x"][x_rows(t), :], writes=[xB])
            tf = self.tmpf[0]
            self.memset("dve", self.small[:, 0:1], 0.0, [B("ss")])
            self.act(tf[:], xb[:], AF.Square, [xB], [B("tmpf0"), B("ss")], accum_out=self.small[:, 0:1])
            self.act(self.small[:, 1:2], self.small[:, 0:1], AF.Sqrt, [B("ss")], [B("sd")], scale=1.0 / D, bias=EPS)
            self.recip(self.small[:, 2:3], self.small[:, 1:2], [B("sd")], [B("rstd")])
            self.stt("dve", tf[:], xb[:], self.small[:, 2:3], self.modsl(v, 1), ALU.mult, ALU.mult,
                     [xB, B("rstd"), B("mod")], [B("tmpf0")])
            self.tt("dve", self.hb[:], tf[:], self.modsl(v, 0), ALU.add, [B("tmpf0"), B("mod")], [B("hb")])
            for kc in range(8):
                self.tr(psT[:, kc, :], self.hb[:, kc * 128:(kc + 1) * 128], self.ident_b[:],
                        [B("hb"), B("ident_b")], [B("psA0")])
            self.cp("act", self.hT[:, :, t * 128:(t + 1) * 128], psT, [B("psA0")], [B("hT")])

        ablocks = []
        for (s0, sn) in segs:
            for q0 in range(s0 * 128, (s0 + sn) * 128, 512):
                ablocks.append((q0, min(512, (s0 + sn) * 128 - q0), list(range(s0, s0 + sn))))
        self.ablocks = ablocks
        for a in range(4):
            self.bg_issue(0)
            col = lambda base: w_in[:, base + a * 128: base + (a + 1) * 128]
            wq_, wqB = self.load_w(col(0), 0)
            if is_sample:
                wqs, wqsB = self.load_w(din["w_sw"][:, a * 128:(a + 1) * 128], 0)
            for bi, (b0, bn) in enumerate(blocks):
                self.proj_fm(wq_, wqB, T, self.psA[0], B("psA0"), b0, bn)
                if is_sample:
                    self.proj_fm(wqs, wqsB, T, self.psA[1], B("psA1"), b0, bn)
                    tf = self.tmpf[0]
                    self.tt("dve", tf[:, 0:bn], self.psA[0][:, 0:bn], self.rope_c[:, b0:b0 + bn], ALU.mult,
                            [B("psA0"), B("rope_c")], [B("tmpf0")])
                    tg = self.tmpf[1]
                    self.tt("dve", tg[:, 0:bn], self.psA[1][:, 0:bn], self.rope_s[:, b0:b0 + bn], ALU.mult,
                            [B("psA1"), B("rope_s")], [B("tmpf1")])
                    self.tt("dve", self.qT[:, b0:b0 + bn], tf[:, 0:bn], tg[:, 0:bn], ALU.add,
                            [B("tmpf0"), B("tmpf1")], [B("qT")])
                else:
                    self.cp("act", self.qT[:, b0:b0 + bn], self.psA[0][:, 0:bn], [B("psA0")], [B("qT")])
            wk_, wkB = self.load_w(col(512), 0)
            if is_sample:
                wks, wksB = self.load_w(din["w_sw"][:, 512 + a * 128:512 + (a + 1) * 128], 0)
            for bi, (b0, bn) in enumerate(blocks):
                self.proj_fm(wk_, wkB, T, self.psA[0], B("psA0"), b0, bn)
                if is_sample:
                    self.proj_fm(wks, wksB, T, self.psA[1], B("psA1"), b0, bn)
                    tf = self.tmpf[0]
                    self.tt("dve", tf[:, 0:bn], self.psA[0][:, 0:bn], self.rope_c[:, b0:b0 + bn], ALU.mult,
                            [B("psA0"), B("rope_c")], [B("tmpf0")])
                    tg = self.tmpf[1]
                    self.tt("dve", tg[:, 0:bn], self.psA[1][:, 0:bn], self.rope_s[:, b0:b0 + bn], ALU.mult,
                            [B("psA1"), B("rope_s")], [B("tmpf1")])
                    for hh in range(2):
                        ps_ = slice(hh * 64, (hh + 1) * 64)
                        self.tt("dve", self.kTm[ps_, hh, b0:b0 + bn], tf[ps_, 0:bn], tg[ps_, 0:bn], ALU.add,
                                [B("tmpf0"), B("tmpf1")], [B("kTm")])
                else:
                    for hh in range(2):
                        ps_ = slice(hh * 64, (hh + 1) * 64)
                        self.cp("act", self.kTm[ps_, hh, b0:b0 + bn], self.psA[0][ps_, 0:bn], [B("psA0")], [B("kTm")])
            wg_, wgB = self.load_w(col(1536), 0)
            for bi, (b0, bn) in enumerate(blocks):
                ps, pB = self.psA[bi % 2], B("psA%d" % (bi % 2))
                self.proj_fm(wg_, wgB, T, ps, pB, b0, bn)
                self.act(self.sg[:, b0:b0 + bn], ps[:, 0:bn], AF.Silu, [pB], [B("sg")])
            wv_, wvB = self.load_w(col(1024), 0)
            for t in range(NT):
                ps, pB = self.psA[t % 2], B("psA%d" % (t % 2))
                self.proj_tm(wv_, wvB, t, ps, pB)
                for hh in range(2):
                    cs = slice(hh * 64, (hh + 1) * 64)
                    self.cp("act", self.vpad[:, t, hh, cs], ps[:, cs], [pB], [B("vpad")])
            if not is_sample:
                for si_, (s0, sn) in enumerate(segs):
                    pS = self.psD[0:64, si_ * 512:si_ * 512 + 256].rearrange("p (r h e) -> p r h e", r=2, h=2)
                    for tr in range(sn):
                        t = s0 + tr
                        ps, pB = self.psA[t % 2], B("psA%d" % (t % 2))
                        self.proj_tm(wk_, wkB, t, ps, pB)
                        for r in range(2):
                            self.tt("dve", self.kw[:, r, :].rearrange("p (h d) -> p h d", d=64),
                                    ps[:, 0:128].rearrange("p (h d) -> p h d", d=64),
                                    self.wst[:, r, tr, 2 * a:2 * a + 2].unsqueeze(2).to_broadcast([128, 2, 64]), ALU.mult,
                                    [pB, B("wst")], [B("kw")])
                        for r in range(2):
                            for hh in range(2):
                                cs = slice(hh * 64, (hh + 1) * 64)
                                self.mm(pS[:, r, hh, :], self.kw[:, r, cs], self.vpad[:, t, hh, cs],
                                        (tr == 0 and r == 0 and hh == 0), tr == sn - 1, [B("kw"), B("vpad")], [B("psD")], skip=True)
                    self.cp("dve", self.sto[:, si_], pS, [B("psD")], [B("sto")])
                    for r in range(2):
                        self.dma(self.st[si_][r, 2 * a:2 * a + 2].rearrange("h d e -> d h e"), self.sto[:, si_, r], reads=[B("sto")])
            if is_sample:
                for r in range(2):
                    self.dma(self.s0st[0:64, r, 0:64], din["s0"][r, 2 * a], writes=[B("s0st")])
                    self.dma(self.s0st[64:128, r, 64:128], din["s0"][r, 2 * a + 1], writes=[B("s0st")])
                self.cp("dve", self.s0bd[:], self.s0st[:], [B("s0st")], [B("s0bd")])
                for r in range(2):
                    tf = self.tmpf[r]
                    self.act(tf[:], (self.iota_n1 if r == 0 else self.iota_rev)[:], AF.Exp,
                             [B("iota_n1"), B("iota_rev"), B("lgcol")], [B("tmpf%d" % r)], scale=self.lgcol[:, r, a:a + 1])
                    self.tt("dve", self.qf[:, r, :], self.qT[:], tf[:], ALU.mult, [B("qT"), B("tmpf%d" % r)], [B("qf")])
            for bi, (b0, bn, ktiles) in enumerate(ablocks):
                first = True
                if is_sample:
                    for r in range(2):
                        self.mm(self.psC[:, b0:b0 + bn], self.s0bd[:, r, :], self.qf[:, r, b0:b0 + bn], first, False,
                                [B("s0bd"), B("qf")], [B("psC")])
                        first = False
                items = [(hh, mc) for hh in range(2) for mc in ktiles]

                def r_score(k, b0=b0, bn=bn):
                    hh, mc = items[k]
                    h = 2 * a + hh
                    half = k % 2
                    pb = self.psB[:, half * 512: half * 512 + bn]
                    pbB = B("psB%d" % half)
                    self.mm(pb, self.kTm[:, hh, mc * 128:(mc + 1) * 128], self.qT[:, b0:b0 + bn], True, True,
                            [B("kTm"), B("qT")], [pbB])
                    off = b0 - mc * 128 + 896
                    self.tt("dve", self.sc[k % 3][:, 0:bn], pb, self.strips[:, h, off:off + bn], ALU.mult,
                            [pbB, B("strips")], [B("sc%d" % (k % 3))])

                def r_pv(k, first, b0=b0, bn=bn):
                    hh, mc = items[k]
                    self.mm(self.psC[:, b0:b0 + bn], self.vpad[:, mc, hh, :], self.sc[k % 3][:, 0:bn], first, k == len(items) - 1,
                            [B("vpad"), B("sc%d" % (k % 3))], [B("psC")])

                r_score(0)
                for k in range(len(items)):
                    if k + 1 < len(items):
                        r_score(k + 1)
                    r_pv(k, first)
                    first = False
                sq = self.sc[0]
                self.act(sq[:, 0:bn], self.psC[:, b0:b0 + bn], AF.Square, [B("psC")], [B("sc0")])
                msp = self.psA[0]
                self.mm(msp[:, 0:bn], self.ones_bd[:], sq[:, 0:bn], True, True, [B("ones_bd"), B("sc0")], [B("psA0")])
                tf = self.tmpf[0]
                self.act(tf[:, 0:bn], msp[:, 0:bn], AF.Sqrt, [B("psA0")], [B("tmpf0")], bias=EPS)
                self.recip(tf[:, 0:bn], tf[:, 0:bn], [B("tmpf0")], [B("tmpf0")])
                tg = self.tmpf[1]
                self.tt("dve", tg[:, 0:bn], self.psC[:, b0:b0 + bn], tf[:, 0:bn], ALU.mult, [B("psC"), B("tmpf0")], [B("tmpf1")])
                self.stt("dve", self.oT[:, a, b0:b0 + bn], tg[:, 0:bn], self.gn_g[:, a:a + 1], self.sg[:, b0:b0 + bn],
                         ALU.mult, ALU.mult, [B("tmpf1"), B("gn_g"), B("sg")], [B("oT")])

        opts = os.environ.get("KOPT", "")
        if self.stage >= 2:
            if "nona" not in opts and not ("nonas" in opts and is_sample) and not ("nonap" in opts and not is_sample):
                self.na_seq(si, tile0, NT, v, is_sample, segs)
            if "noout" not in opts:
                self.out_proj(si, tile0, NT, v, is_sample)

    def qk_norm(self, ps, pB, bn, gcol, outs):
        B = self.B
        sq = self.sc[0]
        self.act(sq[:, 0:bn], ps[:, 0:bn], AF.Square, [pB], [B("sc0")])
        msp = self.psB[:, 0:bn]
        self.mm(msp, self.ones_bd[:], sq[:, 0:bn], True, True, [B("ones_bd"), B("sc0")], [B("psB0")])
        tf = self.tmpf[0]
        self.act(tf[:, 0:bn], msp, AF.Sqrt, [B("psB0")], [B("tmpf0")], bias=EPS)
        self.recip(tf[:, 0:bn], tf[:, 0:bn], [B("tmpf0")], [B("tmpf0")])
        for psl, out, oB in outs:
            self.stt("dve", out, ps[psl, 0:bn], self.qkn_g[psl, gcol:gcol + 1], tf[psl, 0:bn], ALU.mult, ALU.mult,
                     [pB, B("qkn_g"), B("tmpf0")], [oB])

    def na_seq(self, si, tile0, NT, v, is_sample, segs):
        B, din = self.B, self.din
        T = NT * 128
        blocks = [(b0, min(512, T - b0)) for b0 in range(0, T, 512)]
        w_in = din["w_in"]
        q_of_k = _na_windows()
        ablocks = self.ablocks
        npairs = int(os.environ.get("NAS" if is_sample else "NAP", "4"))
        for a in range(npairs):
            self.bg_issue(0)
            col = lambda base: w_in[:, base + a * 128: base + (a + 1) * 128]
            wq_, wqB = self.load_w(col(2048), 0)
            for bi, (b0, bn) in enumerate(blocks):
                ps, pB = self.psA[bi % 2], B("psA%d" % (bi % 2))
                self.proj_fm(wq_, wqB, T, ps, pB, b0, bn)
                self.qk_norm(ps, pB, bn, 0, [(slice(0, 128), self.qT[:, b0:b0 + bn], B("qT"))])
            wk_, wkB = self.load_w(col(2560), 0)
            for bi, (b0, bn) in enumerate(blocks):
                ps, pB = self.psA[bi % 2], B("psA%d" % (bi % 2))
                self.proj_fm(wk_, wkB, T, ps, pB, b0, bn)
                self.qk_norm(ps, pB, bn, 1, [(slice(hh * 64, (hh + 1) * 64), self.kTm[hh * 64:(hh + 1) * 64, hh, b0:b0 + bn], B("kTm"))
                                             for hh in range(2)])
            wv_, wvB = self.load_w(col(3072), 0)
            for t in range(NT):
                ps, pB = self.psA[t % 2], B("psA%d" % (t % 2))
                self.proj_tm(wv_, wvB, t, ps, pB)
                for hh in range(2):
                    cs = slice(hh * 64, (hh + 1) * 64)
                    self.cp("act", self.vpad[:, t, hh, cs], ps[:, cs], [pB], [B("vpad")])
                if not is_sample and "nonvout" not in os.environ.get("KOPT", ""):
                    rows = slice(t * 128, (t + 1) * 128)
                    if "nvnocopy" not in os.environ.get("KOPT", ""):
                        self.cp("dve", self.nvo[:], ps[:, 0:128], [pB], [B("nvo")])
                    if "nvnodma" not in os.environ.get("KOPT", ""):
                        self.dma(self.nv[rows, a * 128:(a + 1) * 128], self.nvo[:], reads=[B("nvo")])
            if not is_sample and "nonk" not in os.environ.get("KOPT", ""):
                for t in range(NT):
                    ps, pB = self.psA[t % 2], B("psA%d" % (t % 2))
                    self.proj_tm(wk_, wkB, t, ps, pB)
                    tf = self.tmpf[0]
                    p3 = ps[:, 0:128].rearrange("p (h d) -> p h d", d=64)
                    t3 = tf[:, 0:128].rearrange("p (h d) -> p h d", d=64)
                    self.act(tf[:, 0:128], ps[:, 0:128], AF.Square, [pB], [B("tmpf0")])
                    self.S.op("dve", lambda q, t3=t3: q.tensor_reduce(out=self.small[:, 8:10], in_=t3, axis=AX.X, op=ALU.add),
                              [B("tmpf0")], [B("nkss")])
                    self.act(self.small[:, 10:12], self.small[:, 8:10], AF.Sqrt, [B("nkss")], [B("nksd")], scale=1.0 / 64, bias=EPS)
                    self.recip(self.small[:, 12:14], self.small[:, 10:12], [B("nksd")], [B("nkrs")])
                    self.tt("dve", t3, p3, self.small[:, 12:14].unsqueeze(2).to_broadcast([128, 2, 64]), ALU.mult,
                            [pB, B("nkrs")], [B("tmpf0")])
                    self.tt("dve", self.nko[:].rearrange("p (h d) -> p h d", d=64), t3,
                            self.kn_bc[:].unsqueeze(1).to_broadcast([128, 2, 64]), ALU.mult, [B("tmpf0"), B("kn_bc")], [B("nko")])
                    rows = slice(t * 128, (t + 1) * 128)
                    self.dma(self.nk[rows, a * 128:(a + 1) * 128], self.nko[:], reads=[B("nko")])
            if is_sample:
                self.dma(self.ctxst[:], din["kctxT"][a * 128:(a + 1) * 128, :], writes=[B("ctxst")])
                for hh in range(2):
                    psl = slice(hh * 64, (hh + 1) * 64)
                    self.cp("dve", self.kcm[psl, hh, :], self.ctxst[psl, :], [B("ctxst")], [B("kcm")])
                for kc in range(2):
                    self.dma(self.ctxst[:, 0:128], din["vctx"][kc * 128:(kc + 1) * 128, a * 128:(a + 1) * 128],
                             reads=[], writes=[B("ctxst")])
                    for hh in range(2):
                        cs = slice(hh * 64, (hh + 1) * 64)
                        self.cp("dve", self.vcp[:, kc, hh, cs], self.ctxst[:, cs], [B("ctxst")], [B("vcp")])
            cnt = 0

            def pv(lv, lvB, p, pB_, q0, qn, first, last):
                c0 = q0
                while c0 < q0 + qn:
                    c1 = min((c0 // 512 + 1) * 512, q0 + qn)
                    self.mm(self.psC[:, c0:c1], lv, p[:, c0 - q0:c1 - q0], first, last, [lvB, pB_], [B("psC")])
                    self.mm(self.psD[:, c0:c1], self.ones_pad[:, lv_hh[0], :], p[:, c0 - q0:c1 - q0], first, last,
                            [B("ones_pad"), pB_], [B("psD")])
                    c0 = c1

            lv_hh = [0]

            jobs = []

            def add_dense(hh, kc, first, last, qblocks=None):
                for bi, (b0, bn) in enumerate(qblocks if qblocks is not None else blocks):
                    k = len(jobs)
                    half = k % 2
                    pb = self.psB[:, half * 512: half * 512 + bn]
                    pbB = B("psB%d" % half)
                    sc = self.sc[k % 3]
                    scB = B("sc%d" % (k % 3))

                    def score(hh=hh, kc=kc, b0=b0, bn=bn, pb=pb, pbB=pbB, sc=sc, scB=scB):
                        lk = self.kTm[:, hh, kc * 128:(kc + 1) * 128] if not is_sample else self.kcm[:, hh, kc * 128:(kc + 1) * 128]
                        self.mm(pb, lk, self.qT[:, b0:b0 + bn], True, True, [B("kTm"), B("kcm"), B("qT")], [pbB])
                        self.act(sc[:, 0:bn], pb, AF.Exp, [pbB], [scB], scale=0.125)

                    def pvj(hh=hh, kc=kc, b0=b0, bn=bn, sc=sc, scB=scB, first=first, last=last):
                        lv_hh[0] = hh
                        lv = self.vpad[:, kc, hh, :] if not is_sample else self.vcp[:, kc, hh, :]
                        pv(lv, B("vpad") if not is_sample else B("vcp"), sc, scB, b0, bn, first, last)

                    jobs.append((score, pvj))

            def add_window(hh, c):
                k = len(jobs)
                h = 2 * a + hh
                r0 = [q_of_k[2 * c], q_of_k[2 * c + 1]]
                qlo = min(r0[0][0], r0[1][0])
                qhi = max(r0[0][1], r0[1][1])
                q0, qn = qlo * 64, (qhi - qlo + 1) * 64
                sbt = self.sbias[k % 2]
                sbB = B("sbias%d" % (k % 2))
                sc = self.sc[k % 3]
                scB = B("sc%d" % (k % 3))

                def score():
                    self.mm(self.psB[:, 0:min(qn, 512)], self.kTm[:, hh, c * 128:(c + 1) * 128], self.qT[:, q0:q0 + min(qn, 512)],
                            True, True, [B("kTm"), B("qT")], [B("psB0"), B("psB1")])
                    if qn > 512:
                        self.mm(self.psB[:, 512:qn], self.kTm[:, hh, c * 128:(c + 1) * 128], self.qT[:, q0 + 512:q0 + qn],
                                True, True, [B("kTm"), B("qT")], [B("psB0"), B("psB1")])
                    for krl in range(2):
                        kr = 2 * c + krl
                        psl = slice(krl * 64, (krl + 1) * 64)
                        a0, a1 = r0[krl]
                        lo, hi = (a0 - qlo) * 64, (a1 - qlo + 1) * 64
                        x0 = a0 - kr + 7
                        bias = self.trb[psl, h, x0:x0 + (a1 - a0 + 1), :]
                        self.stt("dve", sbt[psl, lo:hi].rearrange("p (x q) -> p x q", q=64),
                                 self.psB[psl, lo:hi].rearrange("p (x q) -> p x q", q=64), 0.125, bias,
                                 ALU.mult, ALU.add, [B("psB0"), B("psB1"), B("trb")], [sbB])
                        if lo > 0:
                            self.memset("dve", sbt[psl, 0:lo], NEG, [sbB])
                        if hi < qn:
                            self.memset("dve", sbt[psl, hi:qn], NEG, [sbB])
                    self.act(sc[:, 0:qn], sbt[:, 0:qn], AF.Exp, [sbB], [scB])

                def pvj():
                    lv_hh[0] = hh
                    pv(self.vpad[:, c, hh, :], B("vpad"), sc, scB, q0, qn, False, False)

                jobs.append((score, pvj))

            for hh in range(2):
                if "noatt" in os.environ.get("KOPT", ""):
                    continue
                if not is_sample:
                    continue
                else:
                    add_dense(hh, 0, hh == 0, False)
                    for c in range(8):
                        add_window(hh, c)
                    add_dense(hh, 1, False, hh == 1)
            if not is_sample and "noatt" not in os.environ.get("KOPT", ""):
                for (b0, bn, ktiles) in ablocks:
                    for hh in range(2):
                        for kc in ktiles:
                            add_dense(hh, kc, hh == 0 and kc == ktiles[0], hh == 1 and kc == ktiles[-1], [(b0, bn)])
            if jobs:
                jobs[0][0]()
            for k in range(len(jobs)):
                if k + 1 < len(jobs):
                    jobs[k + 1][0]()
                jobs[k][1]()
            for bi, (b0, bn, _kt) in enumerate(ablocks):
                tf = self.tmpf[bi % 2]
                tB = B("tmpf%d" % (bi % 2))
                self.recip(tf[:, 0:bn], self.psD[:, b0:b0 + bn], [B("psD")], [tB])
                self.tt("dve", self.oT[:, 4 + a, b0:b0 + bn], self.psC[:, b0:b0 + bn], tf[:, 0:bn], ALU.mult,
                        [B("psC"), tB], [B("oT")])

    def out_proj(self, si, tile0, NT, v, is_sample):
        B, din = self.B, self.din
        for t in range(NT):
            ps, pB = (self.psC, B("psC")) if t % 2 == 0 else (self.psD, B("psD"))
            for half in range(2):
                cs = slice(half * 512, (half + 1) * 512)
                for c in range(8):
                    self.mm(ps[:, cs], self.oT[:, c, t * 128:(t + 1) * 128], self.wo[:, c, cs], c == 0, c == 7,
                            [B("oT"), B("wo")], [pB])
            xb = self.xb[t % 2]
            xB = B("xb%d" % (t % 2))
            rows = slice((tile0 + t) * 128, (tile0 + t + 1) * 128)
            self.dma(xb[:], din["x"][rows, :], writes=[xB])
            tf = self.tmpf[t % 2]
            tB = B("tmpf%d" % (t % 2))
            self.tt("dve", tf[:], ps[:], self.modsl(v, 2), ALU.mult, [pB, B("mod")], [tB])
            self.tt("dve", xb[:], tf[:], xb[:], ALU.add, [tB, xB], [xB])
            self.dma(self.y[rows, :], xb[:], reads=[xB])

    def peer(self, st):
        B, din = self.B, self.din
        sb = lambda n, s, d=F32: self.sb(st, n, s, d)
        TG = 256
        self.bg_issue(10000)
        scrU, scrV, scrQ = self.scrU, self.scrV, self.scrQ
        G3 = [sb("G3_%d" % i, [128, 128, 128], BF16) for i in range(3)]
        XT = [sb("XT%d" % i, [128, 8, TG], BF16) for i in range(2)]
        x1 = sb("x1_0", [128, 1024])
        xm = sb("xm", [128, 1024], BF16)
        ptmp = sb("ptmp0", [128, 1024])
        keysT = sb("keysT", [128, 16, 128], BF16)
        iota_row = sb("iota_rowp", [128, 128])
        iota16 = sb("iota16p", [128, 16])
        iota_rb = sb("iota_rb", [128, 128], BF16)
        qTc = [sb("qTc%d" % i, [128, TG], BF16) for i in range(2)]
        SscR = [[sb("Ssc%d_%d" % (t, k), [128, 128]) for k in range(3)] for t in range(2)]
        v16 = [sb("v16_%d" % i, [128, 16, 16]) for i in range(2)]
        i16u = [sb("i16u_%d" % i, [128, 16, 16], U32) for i in range(2)]
        i16f = sb("i16f", [128, 16, 16])
        cand = sb("cand", [128, 8, 256])
        oh = cand[:].rearrange("p h (a b) -> p h a b", b=16)
        tops = [sb("top%d" % i, [128, 8, 16]) for i in range(2)]
        pu = sb("pu", [128, 8, 16], U32)
        abu = sb("abu", [128, 2, 8, 16], U32)
        abf = sb("abf", [128, 2, 8, 16])
        wsels = [sb("wsel%d" % i, [128, 3, 128]) for i in range(2)]
        wselb = sb("wselb", [128, 3, 128], BF16)
        zs = sb("zs", [128, 8, 2])
        sT = [sb("sT%d" % i, [128, 3, TG], BF16) for i in range(2)]
        wbf = [sb("pwbf%d" % i, [128, 8, 128], BF16) for i in range(2)]
        ubf = [sb("ubf%d" % i, [128, 1024], BF16) for i in range(3)]
        vbf = [sb("vbf%d" % i, [128, 1024], BF16) for i in range(4)]
        hg = [sb("hg%d" % i, [128, TG], BF16) for i in range(3)]
        AT = [sb("AT%d" % i, [128, TG], BF16) for i in range(3)]
        P1 = [sb("P1_%d" % i, [128, 4, 128], BF16) for i in range(4)]
        P2 = [sb("P2_%d" % i, [128, 4, 128], BF16) for i in range(4)]

        self.dma(iota_row[:], din["iota_row"], writes=[B("iota_rowp")])
        self.dma(iota16[:], din["iota16"], writes=[B("iota16p")])
        self.cp("dve", iota_rb[:], iota_row[:], [B("iota_rowp")], [B("iota_rb")])
        for c4 in range(4):
            kst = ptmp[:, 0:512].rearrange("p (c k) -> p c k", k=128)
            self.dma(kst, din["keysT"][:, c4 * 4:(c4 + 1) * 4, :], writes=[B("ptmp0")])
            self.cp("dve", keysT[:, c4 * 4:(c4 + 1) * 4, :], kst, [B("ptmp0")], [B("keysT")])

        psT = self.psA[0][:].bitcast(BF16).rearrange("p (c n) -> p c n", n=128)
        psG = self.psA[0][:].rearrange("p (t j) -> p t j", j=128)
        psH = [self.psB[:, 0:TG], self.psB[:, 512:512 + TG], self.psA[1][:, 0:TG]]
        psHB = [B("psB0"), B("psB1"), B("psA1")]
        psO = [self.psC, self.psD]
        psOB = [B("psC"), B("psD")]
        ngroups = int(os.environ.get("PGROUPS", "6"))
        nj = int(os.environ.get("PNJ", "128"))

        def prep_a(g):
            v = 1 if g < 4 else 0
            XTg, XB = XT[g % 2], B("XT%d" % (g % 2))
            for tt in range(2):
                rows = slice((2 * g + tt) * 128, (2 * g + tt + 1) * 128)
                xB = B("x1_0")
                self.dma(x1[:], self.y[rows, :], writes=[xB])
                tf = ptmp
                self.memset("dve", self.small[:, 0:1], 0.0, [B("ss")])
                self.act(tf[:], x1[:], AF.Square, [xB], [B("ptmp0"), B("ss")], accum_out=self.small[:, 0:1])
                self.act(self.small[:, 1:2], self.small[:, 0:1], AF.Sqrt, [B("ss")], [B("sd")], scale=1.0 / D, bias=EPS)
                self.recip(self.small[:, 2:3], self.small[:, 1:2], [B("sd")], [B("rstd")])
                self.stt("dve", tf[:], x1[:], self.small[:, 2:3], self.modsl(v, 4), ALU.mult, ALU.mult,
                         [xB, B("rstd"), B("mod")], [B("ptmp0")])
                self.tt("dve", xm[:], tf[:], self.modsl(v, 3), ALU.add, [B("ptmp0"), B("mod")], [B("xm")])
                for kc in range(8):
                    self.tr(psT[:, kc, :], xm[:, kc * 128:(kc + 1) * 128], self.ident_b[:], [B("xm"), B("ident_b")], [B("psA0")])
                self.cp("act", XTg[:, :, tt * 128:(tt + 1) * 128], psT, [B("psA0")], [XB])

        def prep(g):
            XTg, XB = XT[g % 2], B("XT%d" % (g % 2))

            def wload(c):
                i = c % 2
                self.dma(wbf[i][:], scrQ[c].rearrange("p (kc n) -> p kc n", n=128), reads=[B("scrQ%d" % c)], writes=[B("pwbf%d" % i)])

            def scores_mm(c):
                for tt in range(2):
                    self.mm(self.psA[0][:, 256 + tt * 128:256 + (tt + 1) * 128], qTc[c % 2][:, tt * 128:(tt + 1) * 128], keysT[:, c, :], True, True,
                            [B("qTc%d" % (c % 2)), B("keysT")], [B("psA0")])

            def scores_cp(c):
                for tt in range(2):
                    self.cp("dve", SscR[tt][c % 3][:], self.psA[0][:, 256 + tt * 128:256 + (tt + 1) * 128], [B("psA0")], [B("Ssc%d_%d" % (tt, c % 3))])

            def level1(c):
                for tt in range(2):
                    S = SscR[tt][c % 3]
                    SB = B("Ssc%d_%d" % (tt, c % 3))
                    vB, iB = B("v16_%d" % tt), B("i16u_%d" % tt)
                    for half8 in range(2):
                        vs = v16[tt][:, c, half8 * 8:(half8 + 1) * 8]
                        iu = i16u[tt][:, c, half8 * 8:(half8 + 1) * 8]
                        self.S.op("dve", lambda q, vs=vs, S=S: q.max(out=vs, in_=S[:]), [SB], [vB])
                        self.S.op("dve", lambda q, vs=vs, S=S, iu=iu: q.max_index(out=iu, in_max=vs, in_values=S[:]), [SB, vB], [iB])
                        if half8 == 0:
                            self.S.op("dve", lambda q, vs=vs, S=S: q.match_replace(out=S[:], in_to_replace=vs, in_values=S[:], imm_value=NEG),
                                      [SB, vB], [SB])

            wload(0)
            for c in range(17):
                if c + 1 < 16:
                    wload(c + 1)
                if c < 16:
                    i = c % 2
                    for kc in range(8):
                        self.mm(self.psA[0][:, 0:TG], wbf[i][:, kc, :], XTg[:, kc, :], kc == 0, kc == 7, [B("pwbf%d" % i), XB], [B("psA0")])
                if c > 0:
                    scores_mm(c - 1)
                if c < 16:
                    self.cp("dve", qTc[c % 2][:], self.psA[0][:, 0:TG], [B("psA0")], [B("qTc%d" % (c % 2))])
                if c > 0:
                    scores_cp(c - 1)
                    level1(c - 1)
                yield 4.2 if c > 0 else 0.5
            for tt in range(2):
                top, topB = tops[tt], B("top%d" % tt)
                wsel, wselB = wsels[tt], B("wsel%d" % tt)
                vB, iB = B("v16_%d" % tt), B("i16u_%d" % tt)
                self.cp("dve", i16f[:], i16u[tt][:], [iB], [B("i16f")])
                v16r = v16[tt][:].rearrange("p (h f) k -> p h f k", f=2)
                i16r = i16f[:].rearrange("p (h f) k -> p h f k", f=2)
                cand4 = cand[:].rearrange("p h (a b) -> p h a b", b=16)
                self.tt("dve", cand4, v16r[:, :, 0, :].unsqueeze(3).to_broadcast([128, 8, 16, 16]),
                        v16r[:, :, 1, :].unsqueeze(2).to_broadcast([128, 8, 16, 16]), ALU.add, [vB], [B("cand")])
                yield 2.6
                for h in range(8):
                    for half8 in range(2):
                        vs = top[:, h, half8 * 8:(half8 + 1) * 8]
                        self.S.op("dve", lambda q, vs=vs, h=h: q.max(out=vs, in_=cand[:, h, :]), [B("cand")], [topB])
                        self.S.op("dve", lambda q, vs=vs, h=h, half8=half8: q.max_index(out=pu[:, h, half8 * 8:(half8 + 1) * 8], in_max=vs, in_values=cand[:, h, :]),
                                  [B("cand"), topB], [B("pu")])
                        if half8 == 0:
                            self.S.op("dve", lambda q, vs=vs, h=h: q.match_replace(out=cand[:, h, :], in_to_replace=vs, in_values=cand[:, h, :], imm_value=NEG),
                                      [B("cand"), topB], [B("cand")])
                    yield 2.4
                self.S.op("dve", lambda q: q.tensor_single_scalar(out=abu[:, 0], in_=pu[:], scalar=4, op=ALU.logical_shift_right), [B("pu")], [B("abu")])
                self.S.op("dve", lambda q: q.tensor_single_scalar(out=abu[:, 1], in_=pu[:], scalar=15, op=ALU.bitwise_and), [B("pu")], [B("abu")])
                self.cp("dve", abf[:], abu[:], [B("abu")], [B("abf")])
                yield 1.0
                wsel4 = wsel[:].rearrange("p w (h k) -> p w h k", k=16)
                for f in range(2):
                    self.tt("dve", oh[:], abf[:, f].unsqueeze(3).to_broadcast([128, 8, 16, 16]),
                            iota16[:].unsqueeze(1).unsqueeze(1).to_broadcast([128, 8, 16, 16]), ALU.is_equal,
                            [B("abf"), B("iota16p")], [B("cand")])
                    self.tt("dve", oh[:], oh[:], i16r[:, :, f, :].unsqueeze(2).to_broadcast([128, 8, 16, 16]), ALU.mult,
                            [B("cand"), B("i16f")], [B("cand")])
                    self.S.op("dve", lambda q, f=f, wsel4=wsel4: q.tensor_reduce(out=wsel4[:, f], in_=oh[:], axis=AX.X, op=ALU.add), [B("cand")], [wselB])
                    yield 6.6
            yield ("wait", 2.0)
            prep_tail(g)
            yield 3.0

        def prep_tail(g):
            sTg, sTB = sT[g % 2], B("sT%d" % (g % 2))
            for tt in range(2):
                top, topB = tops[tt], B("top%d" % tt)
                wsel, wselB = wsels[tt], B("wsel%d" % tt)
                wsel4 = wsel[:].rearrange("p w (h k) -> p w h k", k=16)
                self.cp("dve", zs[:, :, 0:1], top[:, :, 0:1], [topB], [B("zs")])
                self.tt("dve", top[:], top[:], zs[:, :, 0:1].to_broadcast([128, 8, 16]), ALU.subtract, [topB, B("zs")], [topB])
                self.act(top[:], top[:], AF.Exp, [topB], [topB])
                self.S.op("dve", lambda q, top=top: q.tensor_reduce(out=zs[:, :, 0], in_=top[:], axis=AX.X, op=ALU.add), [topB], [B("zs")])
                self.recip(zs[:, :, 1], zs[:, :, 0], [B("zs")], [B("zs")])
                self.tt("dve", wsel4[:, 2], top[:], zs[:, :, 1:2].to_broadcast([128, 8, 16]), ALU.mult, [topB, B("zs")], [wselB])
                self.cp("dve", wselb[:], wsel[:], [wselB], [B("wselb")])
                for w in range(3):
                    self.tr(psT[:, w, :], wselb[:, w, :], self.ident_b[:], [B("wselb"), B("ident_b")], [B("psA0")])
                self.cp("act", sTg[:, :, tt * 128:(tt + 1) * 128], psT[:, 0:3, :], [B("psA0")], [sTB])

        def gconstruct(g, tt, dst, ev="dve"):
            sTg, sTB = sT[g % 2], B("sT%d" % (g % 2))
            Gd, GB = G3[dst], B("G3_%d" % dst)
            io4 = iota_rb[:].unsqueeze(1).to_broadcast([128, 4, 128])
            nb = 32

            def dve_part(bi):
                r = bi % 4
                n0 = tt * 128 + bi * 4
                bc = lambda w: sTg[:, w, n0:n0 + 4].unsqueeze(2).to_broadcast([128, 4, 128])
                self.tt("dve", P1[r][:], io4, bc(0), ALU.is_equal, [B("iota_rb"), sTB], [B("P1_%d" % r)])
                self.tt("dve", P2[r][:], io4, bc(1), ALU.is_equal, [B("iota_rb"), sTB], [B("P2_%d" % r)])
                self.tt("dve", P1[r][:], P1[r][:], bc(2), ALU.mult, [B("P1_%d" % r), sTB], [B("P1_%d" % r)])

            def pe_part(bi):
                r = bi % 4
                for k in range(4):
                    self.mm(psG[:, k, :], P1[r][:, k, :], P2[r][:, k, :], True, True, [B("P1_%d" % r), B("P2_%d" % r)], [B("psA0")])
                self.cp(ev, Gd[:, bi * 4:bi * 4 + 4, :], psG, [B("psA0")], [GB])

            for u in range(nb // 2 + 1):
                if u > 0:
                    pe_part(2 * u - 2)
                    pe_part(2 * u - 1)
                if u < nb // 2:
                    dve_part(2 * u)
                    dve_part(2 * u + 1)
                yield 4.9 if u < nb // 2 else 1.2

        def run_all(gen):
            for _ in gen:
                pass

        def main(g, Ta, Tb, inter):
            XTg, XB = XT[g % 2], B("XT%d" % (g % 2))
            Gt = [G3[Ta], G3[Tb]]
            GtB = [B("G3_%d" % Ta), B("G3_%d" % Tb)]

            def load(j):
                self.dma(ubf[j % 3][:], scrU[j], reads=[B("scrU%d" % j)], writes=[B("ubf%d" % (j % 3))])
                self.dma(vbf[j % 4][:], scrV[j], reads=[B("scrV%d" % j)], writes=[B("vbf%d" % (j % 4))])

            def Hm(j):
                ib = j % 3
                u3 = ubf[ib][:].rearrange("p (c i) -> p c i", i=128)
                for kc in range(8):
                    self.mm(psH[j % 3], u3[:, kc, :], XTg[:, kc, :], kc == 0, kc == 7, [B("ubf%d" % ib), XB], [psHB[j % 3]])

            def post(j):
                i3 = j % 3
                self.act(hg[i3][:], psH[i3], AF.Gelu_apprx_tanh, [psHB[i3]], [B("hg%d" % i3)])
                for tt in range(2):
                    cs = slice(tt * 128, (tt + 1) * 128)
                    self.tt("pool", AT[i3][:, cs], hg[i3][:, cs], Gt[tt][:, :, j], ALU.mult, [B("hg%d" % i3), GtB[tt]], [B("AT%d" % i3)])

            def outm(j):
                i3, ib = j % 3, j % 4
                for tt in range(2):
                    for half in range(2):
                        cs = slice(half * 512, (half + 1) * 512)
                        self.mm(psO[tt][:, cs], AT[i3][:, tt * 128:(tt + 1) * 128], vbf[ib][:, cs], j == 0, j == nj - 1,
                                [B("AT%d" % i3), B("vbf%d" % ib)], [psOB[tt]])

            W = 0.0
            waiting = None
            for j0 in range(min(3, nj)):
                load(j0)
            Hm(0)
            if nj > 1:
                Hm(1)
            for j in range(nj):
                if j + 3 < nj:
                    load(j + 3)
                if j + 2 < nj:
                    Hm(j + 2)
                post(j)
                outm(j)
                if inter is None:
                    continue
                budget = (j + 1) * 1.8 * 0.9
                emitted = 0
                while inter is not None and emitted < 1:
                    if waiting is not None:
                        if W + waiting > budget:
                            break
                        waiting = None
                    if W > budget:
                        break
                    c = next(inter, "done")
                    if c == "done":
                        inter = None
                    elif isinstance(c, tuple):
                        waiting = c[1]
                    else:
                        W += c
                        emitted += 1
            if inter is not None:
                run_all(inter)

        def epilogue(g):
            v = 1 if g < 4 else 0
            for tt in range(2):
                rows = slice((2 * g + tt) * 128, (2 * g + tt + 1) * 128)
                tf, tB = ptmp, B("ptmp0")
                self.dma(x1[:], self.y[rows, :], writes=[B("x1_0")])
                self.tt("dve", tf[:], psO[tt][:], self.modsl(v, 5), ALU.mult, [psOB[tt], B("mod")], [tB])
                self.tt("dve", x1[:], tf[:], x1[:], ALU.add, [tB, B("x1_0")], [B("x1_0")])
                self.dma(self.y[rows, :], x1[:], reads=[B("x1_0")])

        def chain(*gens):
            for gn in gens:
                for item in gn:
                    yield item

        prep_a(0)
        run_all(prep(0))
        run_all(gconstruct(0, 0, 0, "act"))
        run_all(gconstruct(0, 1, 1, "act"))
        Ta, Tb, Fr = 0, 1, 2
        for g in range(ngroups):
            if g + 1 < ngroups:
                prep_a(g + 1)
                main(g, Ta, Tb, chain(prep(g + 1), gconstruct(g + 1, 0, Fr)))
                epilogue(g)
                run_all(gconstruct(g + 1, 1, Ta, "act"))
                Ta, Tb, Fr = Fr, Ta, Tb
            else:
                main(g, Ta, Tb, None)
                epilogue(g)


def _prep_inputs(inp):
    f = lambda a: np.ascontiguousarray(np.asarray(a, dtype=np.float32))
    consts = _host_constants()
    shared = dict(consts)
    w_in = f(inp["w_in"][0])
    shared["w_ada"] = f(inp["w_ada"][0])
    shared["b_ada"] = f(inp["b_ada"][0]).reshape(1, 6144)
    shared["n1g"] = f(inp["norm1_g"][0]).reshape(1, 1024)
    shared["n2g"] = f(inp["norm2_g"][0]).reshape(1, 1024)
    shared["w_in"] = w_in
    perm = _rope_partner_perm()
    shared["w_sw"] = f(np.concatenate([w_in[:, 0:512][:, perm], w_in[:, 512:1024][:, perm]], axis=1))
    shared["dec"] = f(np.concatenate([inp["ret_decay_f"][0], inp["ret_decay_b"][0]])).reshape(1, 16)
    shared["gn_g"] = f(np.asarray(inp["ret_gn_g"][0]).reshape(4, 128).T)
    qg = np.tile(np.asarray(inp["na_qn_g"][0]), 2)
    kg = np.tile(np.asarray(inp["na_kn_g"][0]), 2)
    shared["qkn_g"] = f(np.stack([qg, kg], axis=1))
    shared["kn_row"] = f(inp["na_kn_g"][0]).reshape(1, 64)
    rpb = np.asarray(inp["na_rpb"][0], dtype=np.float32)
    kc = np.arange(64)[:, None]
    qc = np.arange(64)[None, :]
    dc = np.clip(kc - qc + 15, 0, 30)
    x = np.arange(15)
    tr = rpb[:, (14 - x)[:, None, None], dc[None, :, :]]
    tr = np.transpose(tr, (2, 0, 1, 3))
    shared["rpbT"] = f(np.concatenate([tr, tr], axis=0))
    shared["w_out"] = f(inp["w_out"][0])
    shared["wq"] = f(inp["peer_wq"][0])
    keys = np.asarray(inp["peer_keys"][0], dtype=np.float32)
    shared["keysT"] = f(np.transpose(keys.reshape(16, 128, 128), (2, 0, 1)))
    U = np.asarray(inp["peer_u"][0], dtype=np.float32)
    shared["Ut"] = f(np.transpose(U.reshape(128, 128, 8, 128), (1, 3, 2, 0)))
    V = np.asarray(inp["peer_v"][0], dtype=np.float32)
    shared["Vp"] = f(np.transpose(V.reshape(128, 128, 1024), (1, 0, 2)))
    xp = np.asarray(inp["x_prompt"], dtype=np.float32)
    xs = np.asarray(inp["x_sample"], dtype=np.float32)
    cc = np.asarray(inp["c"], dtype=np.float32)
    cctx = np.asarray(inp["c_ctx"], dtype=np.float32)
    maps = []
    for c in range(NCORES):
        m = dict(shared)
        m["x"] = f(np.concatenate([xs[c], xp[2 * c], xp[2 * c + 1]], axis=0))
        cv = np.stack([cctx, cc[c]], axis=0)
        m["cT"] = f(np.transpose(cv.reshape(2, 8, 128), (2, 1, 0)))
        m["kctxT"] = f(np.asarray(inp["cache_na_k"][c, 0], dtype=np.float32).reshape(256, 512).T)
        m["vctx"] = f(np.asarray(inp["cache_na_v"][c, 0], dtype=np.float32).reshape(256, 512))
        m["s0"] = f(inp["state_ret"][c, 0])
        maps.append(m)
    return maps


_NC_CACHE = {}


def _get_nc(stage=3):
    if stage not in _NC_CACHE:
        _NC_CACHE[stage] = Builder(stage=stage).build()
    return _NC_CACHE[stage]


def kernel(**inputs):
    maps = _prep_inputs(inputs)
    nc = _get_nc()
    res = run_bass_kernel_spmd(nc, maps, core_ids=list(range(NCORES)))
    outs = res.results
    y_p = np.zeros((16, 256, 1024), np.float32)
    y_s = np.zeros((8, 1024, 1024), np.float32)
    nk = np.zeros((16, 1, 256, 8, 64), np.float32)
    nv = np.zeros((16, 1, 256, 8, 64), np.float32)
    st = np.zeros((16, 1, 2, 8, 64, 64), np.float32)
    for c in range(NCORES):
        r = outs[c]
        y = r["y"]
        y_s[c] = y[0:1024]
        y_p[2 * c] = y[1024:1280]
        y_p[2 * c + 1] = y[1280:1536]
        nk[2 * c:2 * c + 2, 0] = r["nk"].reshape(2, 256, 8, 64)
        nv[2 * c:2 * c + 2, 0] = r["nv"].reshape(2, 256, 8, 64)
        st[2 * c:2 * c + 2, 0] = r["st"]
    return (y_p, y_s, nk, nv, st)
```

```python
import math
import os
from contextlib import ExitStack

import numpy as np
import concourse.bass as bass
import concourse.mybir as mybir
from concourse.bass_utils import run_bass_kernel_spmd

F32 = mybir.dt.float32
BF16 = mybir.dt.bfloat16
I32 = mybir.dt.int32
U32 = mybir.dt.uint32
ALU = mybir.AluOpType
AF = mybir.ActivationFunctionType
AX = mybir.AxisListType

NCORES = 8
D = 1024
EPS = 1e-6
NEG = -1e30


class Buf:
    __slots__ = ("name", "w", "r")

    def __init__(self, name):
        self.name = name
        self.w = None
        self.r = []


class Eng:
    def __init__(self, name, sem, same_sync):
        self.name = name
        self.sem = sem
        self.count = 0
        self.waited = {}
        self.same_sync = same_sync
        self.prog = []


class Sched:
    def __init__(self, nc, stack, n_dma_slots=int(os.environ.get("NDMA", "24"))):
        self.nc = nc
        self.sems = {}
        self.engs = {}
        for name, same in (("pe", False), ("act", True), ("dve", True), ("pool", True), ("sp", True)):
            sem = stack.enter_context(nc.semaphore("s_" + name))
            self.sems[name] = sem
            self.engs[name] = Eng(name, sem, same)
        self.dma_slots = []
        for i in range(n_dma_slots):
            key = "dma%d" % i
            self.sems[key] = stack.enter_context(nc.semaphore("s_" + key))
            self.dma_slots.append([key, 0])
        self.dma_i = 0
        self.bg_slots = []
        for i in range(16):
            key = "bg%d" % i
            self.sems[key] = stack.enter_context(nc.semaphore("s_" + key))
            self.bg_slots.append([key, 0])
        self.bg_i = 0
        self.n_inst = 0

    def _wait(self, e, tok):
        if tok is None:
            return
        key, val = tok
        if key == e.name and not e.same_sync:
            return
        if e.waited.get(key, 0) >= val:
            return
        e.waited[key] = val
        sem = self.sems[key]
        e.prog.append(lambda q, sem=sem, val=val: q.wait_ge(sem, val))

    def _deps(self, e, reads, writes):
        for b in reads:
            self._wait(e, b.w)
            if b.name.startswith("ps"):
                for t in b.r:
                    if t[0] != e.name:
                        self._wait(e, t)
        for b in writes:
            self._wait(e, b.w)
            for t in b.r:
                self._wait(e, t)

    @staticmethod
    def _mark(tok, reads, writes):
        for b in reads:
            b.r.append(tok)
            if len(b.r) > 64:
                b.r = b.r[-64:] if False else b.r
        for b in writes:
            b.w = tok
            b.r = []

    def op(self, eng, fn, reads=(), writes=()):
        e = self.engs[eng]
        self._deps(e, reads, writes)
        e.count += 1
        tok = (e.name, e.count)
        sem = e.sem
        e.prog.append(lambda q, fn=fn, sem=sem: fn(q).then_inc(sem, 1))
        self._mark(tok, reads, writes)
        self.n_inst += 1
        return tok

    def dma(self, eng, out, in_, reads=(), writes=(), bg=False, **kw):
        e = self.engs[eng]
        self._deps(e, reads, writes)
        if bg:
            slot = self.bg_slots[self.bg_i % len(self.bg_slots)]
            self.bg_i += 1
        else:
            slot = self.dma_slots[self.dma_i % len(self.dma_slots)]
            self.dma_i += 1
        key = slot[0]
        if slot[1] > 0:
            self._wait(e, (key, slot[1]))
        slot[1] += 16
        tok = (key, slot[1])
        sem = self.sems[key]
        e.prog.append(lambda q, out=out, in_=in_, sem=sem, kw=kw:
                      q.dma_start(out=out, in_=in_, **kw).then_inc(sem, 16))
        self._mark(tok, reads, writes)
        self.n_inst += 1
        return tok

    def barrier(self):
        for e in self.engs.values():
            for key, val in self.dma_slots + self.bg_slots:
                if val > 0:
                    self._wait(e, (key, val))
            for o in self.engs.values():
                if o is not e and o.count > 0:
                    self._wait(e, (o.name, o.count))

    def emit(self):
        nc = self.nc
        progs = {k: v.prog for k, v in self.engs.items()}
        with nc.Block() as block:
            @block.tensor
            def _(q):
                for f in progs["pe"]:
                    f(q)

            @block.scalar
            def _(q):
                for f in progs["act"]:
                    f(q)

            @block.vector
            def _(q):
                for f in progs["dve"]:
                    f(q)

            @block.gpsimd
            def _(q):
                for f in progs["pool"]:
                    f(q)

            @block.sync
            def _(q):
                for f in progs["sp"]:
                    f(q)


def _rope_tables():
    T = 1024
    n = np.arange(T)
    rows = (n // 64).astype(np.float32)
    cols = (n % 64).astype(np.float32)
    freqs = (np.float32(10000.0) ** (-np.arange(16, dtype=np.float32) / np.float32(16))).astype(np.float32)
    C = np.zeros((128, T), np.float32)
    Sg = np.zeros((128, T), np.float32)
    for p in range(128):
        d = p % 64
        pos = rows if d < 32 else cols
        dd = d % 32
        f = dd % 16
        ang = (pos * freqs[f]).astype(np.float32)
        C[p] = np.cos(ang)
        Sg[p] = -np.sin(ang) if dd < 16 else np.sin(ang)
    return C, Sg


def _rope_partner_perm():
    perm = np.zeros(512, np.int64)
    for h in range(8):
        for d in range(64):
            dd = d % 32
            partner = d + 16 if dd < 16 else d - 16
            perm[h * 64 + d] = h * 64 + partner
    return perm


def _na_windows():
    rows, kh = 16, 8
    q_of_k = {}
    for kr in range(rows):
        qs = [qr for qr in range(rows) if min(max(qr - kh // 2, 0), rows - kh) <= kr < min(max(qr - kh // 2, 0), rows - kh) + kh]
        assert qs == list(range(qs[0], qs[-1] + 1))
        q_of_k[kr] = (qs[0], qs[-1])
    return q_of_k


def _host_constants():
    c = {}
    c["ident"] = np.eye(128, dtype=np.float32)
    ob = np.zeros((128, 128), np.float32)
    ob[:64, :64] = 1.0 / 64
    ob[64:, 64:] = 1.0 / 64
    c["onesbd"] = ob
    op = np.zeros((128, 2, 128), np.float32)
    op[:, 0, :64] = 1.0
    op[:, 1, 64:] = 1.0
    c["onespad"] = op
    p = np.arange(128, dtype=np.float32)[:, None]
    j = np.arange(1920, dtype=np.float32)[None, :]
    c["expo"] = (j - 896.0 - p).astype(np.float32)
    c["iota_n1"] = np.broadcast_to(np.arange(1, 1025, dtype=np.float32)[None], (128, 1024)).copy()
    c["iota_rev"] = np.broadcast_to((1024 - np.arange(1024, dtype=np.float32))[None], (128, 1024)).copy()
    tq = np.zeros((128, 2, 2), np.float32)
    for t in range(2):
        tq[:, 0, t] = 255 - (t * 128 + np.arange(128))
        tq[:, 1, t] = t * 128 + np.arange(128)
    c["tq"] = tq
    C, Sg = _rope_tables()
    c["rope_c"] = C
    c["rope_s"] = Sg
    qc = np.arange(64)
    cstart = np.clip(qc - 8, 0, 48)
    kc = np.arange(64)
    inwin = (kc[:, None] >= cstart[None, :]) & (kc[:, None] < cstart[None, :] + 16)
    cm = np.where(inwin, 0.0, NEG).astype(np.float32)
    c["cmask"] = np.concatenate([cm, cm], axis=0)
    c["iota_row"] = np.broadcast_to(np.arange(128, dtype=np.float32)[None], (128, 128)).copy()
    c["iota16"] = np.broadcast_to(np.arange(16, dtype=np.float32)[None], (128, 16)).copy()
    return c


CONST_SHAPES = {
    "ident": [128, 128], "onesbd": [128, 128], "onespad": [128, 2, 128], "expo": [128, 1920],
    "iota_n1": [128, 1024], "iota_rev": [128, 1024], "tq": [128, 2, 2], "rope_c": [128, 1024],
    "rope_s": [128, 1024], "cmask": [128, 64], "iota_row": [128, 128], "iota16": [128, 16],
}

IN_SHAPES = {
    "x": [1536, 1024], "cT": [128, 8, 2], "w_ada": [1024, 6144], "b_ada": [1, 6144],
    "n1g": [1, 1024], "n2g": [1, 1024], "w_in": [1024, 3584], "w_sw": [1024, 1024], "dec": [1, 16],
    "gn_g": [128, 4], "qkn_g": [128, 2], "kn_row": [1, 64], "rpbT": [128, 8, 15, 64],
    "w_out": [1024, 1024], "wq": [1024, 2048], "keysT": [128, 16, 128],
    "Ut": [128, 128, 8, 128], "Vp": [128, 128, 1024],
    "kctxT": [512, 256], "vctx": [256, 512], "s0": [2, 8, 64, 64],
}

SEQS = [(0, 8, 1, True, [(0, 8)]), (8, 4, 0, False, [(0, 2), (2, 2)])]


class Builder:
    def __init__(self, stage=3, debug=False):
        self.stage = stage
        self.debug = debug
        self.nc = bass.Bass("TRN2", target_bir_lowering=False)
        nc = self.nc
        self.din = {}
        for name, shp in list(IN_SHAPES.items()) + list(CONST_SHAPES.items()):
            self.din[name] = nc.dram_tensor(name, shp, F32, kind="ExternalInput").ap()
        self.y = nc.dram_tensor("y", [1536, 1024], F32, kind="ExternalOutput").ap()
        self.nk = nc.dram_tensor("nk", [512, 512], F32, kind="ExternalOutput").ap()
        self.nv = nc.dram_tensor("nv", [512, 512], F32, kind="ExternalOutput").ap()
        self.st = nc.dram_tensor("st", [2, 2, 8, 64, 64], F32, kind="ExternalOutput").ap()
        self.bufs = {}

    def bg_issue(self, n, dep=None):
        for _ in range(n):
            if not self.bg_todo:
                return
            out, in_, name = self.bg_todo.pop(0)
            self.dma(out, in_, reads=[self.B(dep)] if dep else [], writes=[self.B(name)], eng="pool", bg=True)

    def B(self, name):
        b = self.bufs.get(name)
        if b is None:
            b = self.bufs[name] = Buf(name)
        return b

    def sb(self, st, name, shape, dt=F32):
        return st.enter_context(self.nc.sbuf_tensor("sb_" + name, shape, dt))

    def mm(self, out, lhsT, rhs, start, stop, reads, writes, skip=False):
        if skip:
            self.S.op("pe", lambda q: q.matmul(out, lhsT=lhsT, rhs=rhs, start=start, stop=stop, skip_group_check=True), reads, writes)
        else:
            self.S.op("pe", lambda q: q.matmul(out, lhsT=lhsT, rhs=rhs, start=start, stop=stop), reads, writes)

    def tr(self, out, in_, ident, reads, writes):
        self.S.op("pe", lambda q: q.transpose(out=out, in_=in_, identity=ident), reads, writes)

    def act(self, out, in_, func, reads, writes, **kw):
        self.S.op("act", lambda q: q.activation(out=out, in_=in_, func=func, **kw), reads, writes)

    def tt(self, eng, out, in0, in1, op, reads, writes):
        self.S.op(eng, lambda q: q.tensor_tensor(out=out, in0=in0, in1=in1, op=op), reads, writes)

    def ts(self, eng, out, in0, s1, s2, op0, op1, reads, writes):
        if s2 is None:
            self.S.op(eng, lambda q: q.tensor_scalar(out=out, in0=in0, scalar1=s1, scalar2=None, op0=op0), reads, writes)
        else:
            self.S.op(eng, lambda q: q.tensor_scalar(out=out, in0=in0, scalar1=s1, scalar2=s2, op0=op0, op1=op1), reads, writes)

    def stt(self, eng, out, in0, scalar, in1, op0, op1, reads, writes):
        self.S.op(eng, lambda q: q.scalar_tensor_tensor(out=out, in0=in0, scalar=scalar, in1=in1, op0=op0, op1=op1), reads, writes)

    def cp(self, eng, out, in_, reads, writes):
        if eng == "act":
            self.S.op("act", lambda q: q.copy(out=out, in_=in_), reads, writes)
        else:
            self.S.op(eng, lambda q: q.tensor_copy(out=out, in_=in_), reads, writes)

    def memset(self, eng, ap, val, writes):
        self.S.op(eng, lambda q: q.memset(ap, val), (), writes)

    def recip(self, out, in_, reads, writes):
        self.S.op("dve", lambda q: q.reciprocal(out=out, in_=in_), reads, writes)

    def dma(self, out, in_, reads=(), writes=(), eng="sp", bg=False):
        self.S.dma(eng, out, in_, reads, writes, bg=bg)

    def build(self):
        nc = self.nc
        with ExitStack() as top:
            self.S = Sched(nc, top)
            self.psA = [top.enter_context(nc.psum_tensor("psA%d" % i, [128, 512], F32)) for i in range(2)]
            self.psB = top.enter_context(nc.psum_tensor("psB", [128, 1024], F32))
            self.psC = top.enter_context(nc.psum_tensor("psC", [128, 1024], F32))
            self.psD = top.enter_context(nc.psum_tensor("psD", [128, 1024], F32))
            self.scrU = nc.dram_tensor("scrU", [128, 128, 1024], BF16).ap()
            self.scrV = nc.dram_tensor("scrV", [128, 128, 1024], BF16).ap()
            self.scrQ = nc.dram_tensor("scrQ", [16, 128, 1024], BF16).ap()
            self.bg_todo = []
            if self.stage >= 3:
                Ut2 = self.din["Ut"].rearrange("j p c i -> j p (c i)")
                wq4 = self.din["wq"].rearrange("(kc p) (c n) -> c p kc n", p=128, n=128)
                for c in range(16):
                    self.bg_todo.append((self.scrQ[c].rearrange("p (kc n) -> p kc n", n=128), wq4[c], "scrQ%d" % c))
                for j in range(128):
                    self.bg_todo.append((self.scrU[j], Ut2[j], "scrU%d" % j))
                    self.bg_todo.append((self.scrV[j], self.din["Vp"][j], "scrV%d" % j))
            self.mod = self.sb(top, "mod", [128, 2, 6144], BF16)
            self.ident_b = self.sb(top, "ident_b", [128, 128], BF16)
            self.small = self.sb(top, "small", [128, 64], F32)
            with ExitStack() as ph1:
                self.phase1_alloc(ph1)
                self.setup(ph1)
                with ExitStack() as ws:
                    self.phase1_work(ws)
                    if self.stage >= 1:
                        for si, seq in enumerate(SEQS):
                            self.attn_seq(si, *seq)
                    self.S.barrier()
            if self.stage >= 3:
                with ExitStack() as ph2:
                    self.peer(ph2)
                    self.S.barrier()
            self.S.barrier()
            self.S.emit()
        return nc

    def phase1_alloc(self, st):
        sb = lambda n, s, d=F32: self.sb(st, n, s, d)
        self.ones_bd = sb("ones_bd", [128, 128], BF16)
        self.ones_pad = sb("ones_pad", [128, 2, 128], BF16)
        self.strips = sb("strips", [128, 8, 1920], BF16)
        self.lg = sb("lg", [128, 16])
        self.nlgb = sb("nlgb", [128, 8])
        self.lgcol = sb("lgcol", [128, 2, 4])
        self.wst = sb("wst", [128, 2, 2, 8])
        self.gn_g = sb("gn_g", [128, 4])
        self.qkn_g = sb("qkn_g", [128, 2])
        self.kn_bc = sb("kn_bc", [128, 64])
        self.wo = sb("wo", [128, 8, 1024], BF16)
        self.iota_n1 = sb("iota_n1", [128, 1024])
        self.iota_rev = sb("iota_rev", [128, 1024])
        self.rope_c = sb("rope_c", [128, 1024])
        self.rope_s = sb("rope_s", [128, 1024])
        self.trb = sb("trb", [128, 8, 15, 64], BF16)

    def phase1_work(self, st):
        sb = lambda n, s, d=F32: self.sb(st, n, s, d)
        B = self.B
        self.hT = sb("hT", [128, 8, 1024], BF16)
        self.oT = sb("oT", [128, 8, 1024], BF16)
        self.xb = [sb("xb%d" % i, [128, 1024]) for i in range(2)]
        self.tmpf = [sb("tmpf%d" % i, [128, 1024]) for i in range(2)]
        self.hb = sb("hb", [128, 1024], BF16)
        self.wstage = [sb("wstage%d" % i, [128, 8, 128]) for i in range(3)]
        self.wbf = [sb("wbf%d" % i, [128, 8, 128], BF16) for i in range(3)]
        self.qT = sb("qT", [128, 1024], BF16)
        self.kTm = sb("kTm", [128, 2, 1024], BF16)
        self.sg = sb("sg", [128, 1024], BF16)
        self.vpad = sb("vpad", [128, 8, 2, 128], BF16)
        self.kcm = sb("kcm", [128, 2, 256], BF16)
        self.vcp = sb("vcp", [128, 2, 2, 128], BF16)
        self.qf = sb("qf", [128, 2, 1024], BF16)
        self.s0bd = sb("s0bd", [128, 2, 128], BF16)
        self.s0st = sb("s0st", [128, 2, 128])
        self.sc = [sb("sc%d" % i, [128, 768], BF16) for i in range(3)]
        self.sbias = [sb("sbias%d" % i, [128, 768]) for i in range(2)]
        self.ctxst = sb("ctxst", [128, 256])
        self.kw = sb("kw", [128, 2, 128], BF16)
        self.nko = sb("nko", [128, 128])
        self.nvo = sb("nvo", [128, 128])
        self.sto = sb("sto", [64, 2, 2, 2, 64])
        self.memset("pool", self.kTm[:], 0.0, [B("kTm")])
        self.memset("pool", self.vpad[:], 0.0, [B("vpad")])
        self.memset("pool", self.kcm[:], 0.0, [B("kcm")])
        self.memset("pool", self.vcp[:], 0.0, [B("vcp")])
        self.memset("pool", self.s0st[:], 0.0, [B("s0st")])

    def setup(self, st_outer):
        B, din = self.B, self.din
        with ExitStack() as st:
            sb = lambda n, s, d=F32: self.sb(st, n, s, d)
            identf = sb("identf", [128, 128])
            self.dma(identf[:], din["ident"], writes=[B("identf")])
            self.cp("dve", self.ident_b[:], identf[:], [B("identf")], [B("ident_b")])
            tmp128 = sb("tmp128", [128, 2, 128])
            self.dma(tmp128[:, 0, :], din["onesbd"], writes=[B("tmp128")])
            self.cp("dve", self.ones_bd[:], tmp128[:, 0, :], [B("tmp128")], [B("ones_bd")])
            self.dma(tmp128[:], din["onespad"], reads=[], writes=[B("tmp128")])
            self.cp("dve", self.ones_pad[:], tmp128[:], [B("tmp128")], [B("ones_pad")])
            for nm, t in (("iota_n1", self.iota_n1), ("iota_rev", self.iota_rev), ("rope_c", self.rope_c),
                          ("rope_s", self.rope_s), ("gn_g", self.gn_g), ("qkn_g", self.qkn_g)):
                self.dma(t[:], din[nm], writes=[B(nm)])
            self.dma(self.kn_bc[:], din["kn_row"][0].partition_broadcast(128), writes=[B("kn_bc")])
            dec = sb("dec", [128, 16])
            self.dma(dec[:], din["dec"][0].partition_broadcast(128), writes=[B("dec")])
            self.act(dec[:], dec[:], AF.Exp, [B("dec")], [B("dec")], scale=-1.0)
            self.act(dec[:], dec[:], AF.Ln, [B("dec")], [B("dec")], bias=1.0)
            self.ts("dve", self.lg[:], dec[:], -1.0, None, ALU.mult, None, [B("dec")], [B("lg")])
            self.ts("dve", self.nlgb[:], self.lg[:, 8:16], -1.0, None, ALU.mult, None, [B("lg")], [B("nlgb")])
            for r in range(2):
                self.cp("dve", self.lgcol[0:64, r, :], self.lg[0:64, r * 8:r * 8 + 8:2], [B("lg")], [B("lgcol")])
                self.cp("dve", self.lgcol[64:128, r, :], self.lg[64:128, r * 8 + 1:r * 8 + 8:2], [B("lg")], [B("lgcol")])
            tq = sb("tq", [128, 2, 2])
            self.dma(tq[:], din["tq"], writes=[B("tq")])
            for r in range(2):
                self.tt("dve", self.wst[:, r], tq[:, r, :].unsqueeze(2).to_broadcast([128, 2, 8]),
                        self.lg[:, r * 8:(r + 1) * 8].unsqueeze(1).to_broadcast([128, 2, 8]), ALU.mult,
                        [B("tq"), B("lg")], [B("wst")])
            self.act(self.wst[:], self.wst[:], AF.Exp, [B("wst")], [B("wst")])
            self.ts("dve", self.wst[:], self.wst[:], 0.125, None, ALU.mult, None, [B("wst")], [B("wst")])
            expo = sb("expo", [128, 1920])
            t1 = sb("t1", [128, 1920])
            t2 = sb("t2", [128, 1920])
            self.dma(expo[:], din["expo"], writes=[B("expo")])
            for h in range(8):
                self.ts("dve", t1[:], expo[:], self.lg[:, h:h + 1], None, ALU.mult, None, [B("expo"), B("lg")], [B("t1")])
                self.stt("dve", t2[:], expo[:], self.nlgb[:, h:h + 1], t1[:], ALU.mult, ALU.min,
                         [B("expo"), B("nlgb"), B("t1")], [B("t2")])
                self.act(t2[:], t2[:], AF.Exp, [B("t2")], [B("t2")])
                self.ts("dve", self.strips[:, h, :], t2[:], 0.125, None, ALU.mult, None, [B("t2")], [B("strips")])
            cmask = sb("cmask", [128, 64])
            self.dma(cmask[:], din["cmask"], writes=[B("cmask")])
            for h in range(8):
                trs = t1[:, 0:960].rearrange("p (x q) -> p x q", q=64)
                self.dma(trs, din["rpbT"][:, h], writes=[B("t1")])
                self.tt("dve", self.trb[:, h], trs, cmask[:].unsqueeze(1).to_broadcast([128, 15, 64]), ALU.add,
                        [B("t1"), B("cmask")], [B("trb")])
            for kc in range(8):
                wsl = t2[:, 0:1024]
                self.dma(wsl, din["w_out"][kc * 128:(kc + 1) * 128, :], writes=[B("t2")])
                self.cp("act", self.wo[:, kc, :], wsl, [B("t2")], [B("wo")])
        self.S.barrier()
        self.bg_issue(32)
        self.adaln()
        self.S.barrier()

    def adaln(self):
        B, din = self.B, self.din
        with ExitStack() as st:
            sb = lambda n, s, d=F32: self.sb(st, n, s, d)
            cT = sb("cT", [128, 8, 2])
            rep = sb("rep", [128, 8, 2, 128], BF16)
            wst = [sb("awst%d" % i, [128, 8, 512]) for i in range(2)]
            wbf = [sb("awbf%d" % i, [128, 8, 512], BF16) for i in range(2)]
            bbc = [sb("bbc%d" % i, [128, 512]) for i in range(2)]
            ngb = sb("ngb", [128, 2, 1024])
            self.dma(cT[:], din["cT"], writes=[B("cT")])
            self.act(cT[:], cT[:], AF.Silu, [B("cT")], [B("cT")])
            self.cp("dve", rep[:], cT[:].unsqueeze(3).to_broadcast([128, 8, 2, 128]), [B("cT")], [B("rep")])
            self.dma(ngb[:, 0, :], din["n1g"][0].partition_broadcast(128), writes=[B("ngb")])
            self.dma(ngb[:, 1, :], din["n2g"][0].partition_broadcast(128), writes=[B("ngb")])
            w3 = din["w_ada"].rearrange("(kc p) n -> p kc n", p=128)
            for blk in range(12):
                i = blk % 2
                cs = slice(blk * 512, (blk + 1) * 512)
                self.dma(wst[i][:], w3[:, :, cs], writes=[B("awst%d" % i)])
                self.dma(bbc[i][:], din["b_ada"][0, cs].partition_broadcast(128), writes=[B("bbc%d" % i)])
                self.cp("dve" if blk % 2 == 0 else "act", wbf[i][:], wst[i][:], [B("awst%d" % i)], [B("awbf%d" % i)])
                for v in range(2):
                    ps = self.psA[v]
                    for kc in range(8):
                        self.mm(ps[:], rep[:, kc, v, :], wbf[i][:, kc, :], kc == 0, kc == 7,
                                [B("rep"), B("awbf%d" % i)], [B("psA%d" % v)])
                    self.tt("dve", self.mod[:, v, cs], ps[:], bbc[i][:], ALU.add,
                            [B("psA%d" % v), B("bbc%d" % i)], [B("mod")])
            for v in range(2):
                for j, ch in ((0, 1), (1, 4)):
                    sl = self.mod[:, v, ch * 1024:(ch + 1) * 1024]
                    self.stt("dve", sl, sl, 1.0, ngb[:, j, :], ALU.add, ALU.mult, [B("mod"), B("ngb")], [B("mod")])

    def modsl(self, v, ch):
        return self.mod[:, v, ch * 1024:(ch + 1) * 1024]

    def load_w(self, dram_cols, k):
        B = self.B
        i = self.wcount % 3
        j = self.wcount % 3
        self.wcount += 1
        self.bg_issue(1, "wbf%d" % ((j + 2) % 3))
        self.dma(self.wstage[i][:], dram_cols.rearrange("(kc p) n -> p kc n", p=128), writes=[B("wstage%d" % i)])
        self.cp("act", self.wbf[j][:], self.wstage[i][:], [B("wstage%d" % i)], [B("wbf%d" % j)])
        return self.wbf[j], B("wbf%d" % j)

    def proj_fm(self, w, wB, T, ps, psB_, b0, bn):
        for kc in range(8):
            self.mm(ps[:, 0:bn], w[:, kc, :], self.hT[:, kc, b0:b0 + bn], kc == 0, kc == 7,
                    [wB, self.B("hT")], [psB_])

    def proj_tm(self, w, wB, t, ps, psB_):
        for kc in range(8):
            self.mm(ps[:, 0:128], self.hT[:, kc, t * 128:(t + 1) * 128], w[:, kc, :], kc == 0, kc == 7,
                    [wB, self.B("hT")], [psB_])

    def attn_seq(self, si, tile0, NT, v, is_sample, segs):
        B, din = self.B, self.din
        T = NT * 128
        blocks = [(b0, min(512, T - b0)) for b0 in range(0, T, 512)]
        x_rows = lambda t: slice((tile0 + t) * 128, (tile0 + t + 1) * 128)
        w_in = din["w_in"]
        self.wcount = getattr(self, "wcount", 0)
        psT = self.psA[0][:].bitcast(BF16).rearrange("p (c n) -> p c n", n=128)
        for t in range(NT):
            xb = self.xb[t % 2]
            xB = B("xb%d" % (t % 2))
            self.dma(xb[:], din["x"][x_rows(t), :], writes=[xB])
            tf = self.tmpf[0]
            self.memset("dve", self.small[:, 0:1], 0.0, [B("ss")])
            self.act(tf[:], xb[:], AF.Square, [xB], [B("tmpf0"), B("ss")], accum_out=self.small[:, 0:1])
            self.act(self.small[:, 1:2], self.small[:, 0:1], AF.Sqrt, [B("ss")], [B("sd")], scale=1.0 / D, bias=EPS)
            self.recip(self.small[:, 2:3], self.small[:, 1:2], [B("sd")], [B("rstd")])
            self.stt("dve", tf[:], xb[:], self.small[:, 2:3], self.modsl(v, 1), ALU.mult, ALU.mult,
                     [xB, B("rstd"), B("mod")], [B("tmpf0")])
            self.tt("dve", self.hb[:], tf[:], self.modsl(v, 0), ALU.add, [B("tmpf0"), B("mod")], [B("hb")])
            for kc in range(8):
                self.tr(psT[:, kc, :], self.hb[:, kc * 128:(kc + 1) * 128], self.ident_b[:],
                        [B("hb"), B("ident_b")], [B("psA0")])
            self.cp("act", self.hT[:, :, t * 128:(t + 1) * 128], psT, [B("psA0")], [B("hT")])

        ablocks = []
        for (s0, sn) in segs:
            for q0 in range(s0 * 128, (s0 + sn) * 128, 512):
                ablocks.append((q0, min(512, (s0 + sn) * 128 - q0), list(range(s0, s0 + sn))))
        self.ablocks = ablocks
        for a in range(4):
            self.bg_issue(0)
            col = lambda base: w_in[:, base + a * 128: base + (a + 1) * 128]
            wq_, wqB = self.load_w(col(0), 0)
            if is_sample:
                wqs, wqsB = self.load_w(din["w_sw"][:, a * 128:(a + 1) * 128], 0)
            for bi, (b0, bn) in enumerate(blocks):
                self.proj_fm(wq_, wqB, T, self.psA[0], B("psA0"), b0, bn)
                if is_sample:
                    self.proj_fm(wqs, wqsB, T, self.psA[1], B("psA1"), b0, bn)
                    tf = self.tmpf[0]
                    self.tt("dve", tf[:, 0:bn], self.psA[0][:, 0:bn], self.rope_c[:, b0:b0 + bn], ALU.mult,
                            [B("psA0"), B("rope_c")], [B("tmpf0")])
                    tg = self.tmpf[1]
                    self.tt("dve", tg[:, 0:bn], self.psA[1][:, 0:bn], self.rope_s[:, b0:b0 + bn], ALU.mult,
                            [B("psA1"), B("rope_s")], [B("tmpf1")])
                    self.tt("dve", self.qT[:, b0:b0 + bn], tf[:, 0:bn], tg[:, 0:bn], ALU.add,
                            [B("tmpf0"), B("tmpf1")], [B("qT")])
                else:
                    self.cp("act", self.qT[:, b0:b0 + bn], self.psA[0][:, 0:bn], [B("psA0")], [B("qT")])
            wk_, wkB = self.load_w(col(512), 0)
            if is_sample:
                wks, wksB = self.load_w(din["w_sw"][:, 512 + a * 128:512 + (a + 1) * 128], 0)
            for bi, (b0, bn) in enumerate(blocks):
                self.proj_fm(wk_, wkB, T, self.psA[0], B("psA0"), b0, bn)
                if is_sample:
                    self.proj_fm(wks, wksB, T, self.psA[1], B("psA1"), b0, bn)
                    tf = self.tmpf[0]
                    self.tt("dve", tf[:, 0:bn], self.psA[0][:, 0:bn], self.rope_c[:, b0:b0 + bn], ALU.mult,
                            [B("psA0"), B("rope_c")], [B("tmpf0")])
                    tg = self.tmpf[1]
                    self.tt("dve", tg[:, 0:bn], self.psA[1][:, 0:bn], self.rope_s[:, b0:b0 + bn], ALU.mult,
                            [B("psA1"), B("rope_s")], [B("tmpf1")])
                    for hh in range(2):
                        ps_ = slice(hh * 64, (hh + 1) * 64)
                        self.tt("dve", self.kTm[ps_, hh, b0:b0 + bn], tf[ps_, 0:bn], tg[ps_, 0:bn], ALU.add,
                                [B("tmpf0"), B("tmpf1")], [B("kTm")])
                else:
                    for hh in range(2):
                        ps_ = slice(hh * 64, (hh + 1) * 64)
                        self.cp("act", self.kTm[ps_, hh, b0:b0 + bn], self.psA[0][ps_, 0:bn], [B("psA0")], [B("kTm")])
            wg_, wgB = self.load_w(col(1536), 0)
            for bi, (b0, bn) in enumerate(blocks):
                ps, pB = self.psA[bi % 2], B("psA%d" % (bi % 2))
                self.proj_fm(wg_, wgB, T, ps, pB, b0, bn)
                self.act(self.sg[:, b0:b0 + bn], ps[:, 0:bn], AF.Silu, [pB], [B("sg")])
            wv_, wvB = self.load_w(col(1024), 0)
            for t in range(NT):
                ps, pB = self.psA[t % 2], B("psA%d" % (t % 2))
                self.proj_tm(wv_, wvB, t, ps, pB)
                for hh in range(2):
                    cs = slice(hh * 64, (hh + 1) * 64)
                    self.cp("act", self.vpad[:, t, hh, cs], ps[:, cs], [pB], [B("vpad")])
            if not is_sample:
                for si_, (s0, sn) in enumerate(segs):
                    pS = self.psD[0:64, si_ * 512:si_ * 512 + 256].rearrange("p (r h e) -> p r h e", r=2, h=2)
                    for tr in range(sn):
                        t = s0 + tr
                        ps, pB = self.psA[t % 2], B("psA%d" % (t % 2))
                        self.proj_tm(wk_, wkB, t, ps, pB)
                        for r in range(2):
                            self.tt("dve", self.kw[:, r, :].rearrange("p (h d) -> p h d", d=64),
                                    ps[:, 0:128].rearrange("p (h d) -> p h d", d=64),
                                    self.wst[:, r, tr, 2 * a:2 * a + 2].unsqueeze(2).to_broadcast([128, 2, 64]), ALU.mult,
                                    [pB, B("wst")], [B("kw")])
                        for r in range(2):
                            for hh in range(2):
                                cs = slice(hh * 64, (hh + 1) * 64)
                                self.mm(pS[:, r, hh, :], self.kw[:, r, cs], self.vpad[:, t, hh, cs],
                                        (tr == 0 and r == 0 and hh == 0), tr == sn - 1, [B("kw"), B("vpad")], [B("psD")], skip=True)
                    self.cp("dve", self.sto[:, si_], pS, [B("psD")], [B("sto")])
                    for r in range(2):
                        self.dma(self.st[si_][r, 2 * a:2 * a + 2].rearrange("h d e -> d h e"), self.sto[:, si_, r], reads=[B("sto")])
            if is_sample:
                for r in range(2):
                    self.dma(self.s0st[0:64, r, 0:64], din["s0"][r, 2 * a], writes=[B("s0st")])
                    self.dma(self.s0st[64:128, r, 64:128], din["s0"][r, 2 * a + 1], writes=[B("s0st")])
                self.cp("dve", self.s0bd[:], self.s0st[:], [B("s0st")], [B("s0bd")])
                for r in range(2):
                    tf = self.tmpf[r]
                    self.act(tf[:], (self.iota_n1 if r == 0 else self.iota_rev)[:], AF.Exp,
                             [B("iota_n1"), B("iota_rev"), B("lgcol")], [B("tmpf%d" % r)], scale=self.lgcol[:, r, a:a + 1])
                    self.tt("dve", self.qf[:, r, :], self.qT[:], tf[:], ALU.mult, [B("qT"), B("tmpf%d" % r)], [B("qf")])
            for bi, (b0, bn, ktiles) in enumerate(ablocks):
                first = True
                if is_sample:
                    for r in range(2):
                        self.mm(self.psC[:, b0:b0 + bn], self.s0bd[:, r, :], self.qf[:, r, b0:b0 + bn], first, False,
                                [B("s0bd"), B("qf")], [B("psC")])
                        first = False
                items = [(hh, mc) for hh in range(2) for mc in ktiles]

                def r_score(k, b0=b0, bn=bn):
                    hh, mc = items[k]
                    h = 2 * a + hh
                    half = k % 2
                    pb = self.psB[:, half * 512: half * 512 + bn]
                    pbB = B("psB%d" % half)
                    self.mm(pb, self.kTm[:, hh, mc * 128:(mc + 1) * 128], self.qT[:, b0:b0 + bn], True, True,
                            [B("kTm"), B("qT")], [pbB])
                    off = b0 - mc * 128 + 896
                    self.tt("dve", self.sc[k % 3][:, 0:bn], pb, self.strips[:, h, off:off + bn], ALU.mult,
                            [pbB, B("strips")], [B("sc%d" % (k % 3))])

                def r_pv(k, first, b0=b0, bn=bn):
                    hh, mc = items[k]
                    self.mm(self.psC[:, b0:b0 + bn], self.vpad[:, mc, hh, :], self.sc[k % 3][:, 0:bn], first, k == len(items) - 1,
                            [B("vpad"), B("sc%d" % (k % 3))], [B("psC")])

                r_score(0)
                for k in range(len(items)):
                    if k + 1 < len(items):
                        r_score(k + 1)
                    r_pv(k, first)
                    first = False
                sq = self.sc[0]
                self.act(sq[:, 0:bn], self.psC[:, b0:b0 + bn], AF.Square, [B("psC")], [B("sc0")])
                msp = self.psA[0]
                self.mm(msp[:, 0:bn], self.ones_bd[:], sq[:, 0:bn], True, True, [B("ones_bd"), B("sc0")], [B("psA0")])
                tf = self.tmpf[0]
                self.act(tf[:, 0:bn], msp[:, 0:bn], AF.Sqrt, [B("psA0")], [B("tmpf0")], bias=EPS)
                self.recip(tf[:, 0:bn], tf[:, 0:bn], [B("tmpf0")], [B("tmpf0")])
                tg = self.tmpf[1]
                self.tt("dve", tg[:, 0:bn], self.psC[:, b0:b0 + bn], tf[:, 0:bn], ALU.mult, [B("psC"), B("tmpf0")], [B("tmpf1")])
                self.stt("dve", self.oT[:, a, b0:b0 + bn], tg[:, 0:bn], self.gn_g[:, a:a + 1], self.sg[:, b0:b0 + bn],
                         ALU.mult, ALU.mult, [B("tmpf1"), B("gn_g"), B("sg")], [B("oT")])

        opts = os.environ.get("KOPT", "")
        if self.stage >= 2:
            if "nona" not in opts and not ("nonas" in opts and is_sample) and not ("nonap" in opts and not is_sample):
                self.na_seq(si, tile0, NT, v, is_sample, segs)
            if "noout" not in opts:
                self.out_proj(si, tile0, NT, v, is_sample)

    def qk_norm(self, ps, pB, bn, gcol, outs):
        B = self.B
        sq = self.sc[0]
        self.act(sq[:, 0:bn], ps[:, 0:bn], AF.Square, [pB], [B("sc0")])
        msp = self.psB[:, 0:bn]
        self.mm(msp, self.ones_bd[:], sq[:, 0:bn], True, True, [B("ones_bd"), B("sc0")], [B("psB0")])
        tf = self.tmpf[0]
        self.act(tf[:, 0:bn], msp, AF.Sqrt, [B("psB0")], [B("tmpf0")], bias=EPS)
        self.recip(tf[:, 0:bn], tf[:, 0:bn], [B("tmpf0")], [B("tmpf0")])
        for psl, out, oB in outs:
            self.stt("dve", out, ps[psl, 0:bn], self.qkn_g[psl, gcol:gcol + 1], tf[psl, 0:bn], ALU.mult, ALU.mult,
                     [pB, B("qkn_g"), B("tmpf0")], [oB])

    def na_seq(self, si, tile0, NT, v, is_sample, segs):
        B, din = self.B, self.din
        T = NT * 128
        blocks = [(b0, min(512, T - b0)) for b0 in range(0, T, 512)]
        w_in = din["w_in"]
        q_of_k = _na_windows()
        ablocks = self.ablocks
        npairs = int(os.environ.get("NAS" if is_sample else "NAP", "4"))
        for a in range(npairs):
            self.bg_issue(0)
            col = lambda base: w_in[:, base + a * 128: base + (a + 1) * 128]
            wq_, wqB = self.load_w(col(2048), 0)
            for bi, (b0, bn) in enumerate(blocks):
                ps, pB = self.psA[bi % 2], B("psA%d" % (bi % 2))
                self.proj_fm(wq_, wqB, T, ps, pB, b0, bn)
                self.qk_norm(ps, pB, bn, 0, [(slice(0, 128), self.qT[:, b0:b0 + bn], B("qT"))])
            wk_, wkB = self.load_w(col(2560), 0)
            for bi, (b0, bn) in enumerate(blocks):
                ps, pB = self.psA[bi % 2], B("psA%d" % (bi % 2))
                self.proj_fm(wk_, wkB, T, ps, pB, b0, bn)
                self.qk_norm(ps, pB, bn, 1, [(slice(hh * 64, (hh + 1) * 64), self.kTm[hh * 64:(hh + 1) * 64, hh, b0:b0 + bn], B("kTm"))
                                             for hh in range(2)])
            wv_, wvB = self.load_w(col(3072), 0)
            for t in range(NT):
                ps, pB = self.psA[t % 2], B("psA%d" % (t % 2))
                self.proj_tm(wv_, wvB, t, ps, pB)
                for hh in range(2):
                    cs = slice(hh * 64, (hh + 1) * 64)
                    self.cp("act", self.vpad[:, t, hh, cs], ps[:, cs], [pB], [B("vpad")])
                if not is_sample and "nonvout" not in os.environ.get("KOPT", ""):
                    rows = slice(t * 128, (t + 1) * 128)
                    if "nvnocopy" not in os.environ.get("KOPT", ""):
                        self.cp("dve", self.nvo[:], ps[:, 0:128], [pB], [B("nvo")])
                    if "nvnodma" not in os.environ.get("KOPT", ""):
                        self.dma(self.nv[rows, a * 128:(a + 1) * 128], self.nvo[:], reads=[B("nvo")])
            if not is_sample and "nonk" not in os.environ.get("KOPT", ""):
                for t in range(NT):
                    ps, pB = self.psA[t % 2], B("psA%d" % (t % 2))
                    self.proj_tm(wk_, wkB, t, ps, pB)
                    tf = self.tmpf[0]
                    p3 = ps[:, 0:128].rearrange("p (h d) -> p h d", d=64)
                    t3 = tf[:, 0:128].rearrange("p (h d) -> p h d", d=64)
                    self.act(tf[:, 0:128], ps[:, 0:128], AF.Square, [pB], [B("tmpf0")])
                    self.S.op("dve", lambda q, t3=t3: q.tensor_reduce(out=self.small[:, 8:10], in_=t3, axis=AX.X, op=ALU.add),
                              [B("tmpf0")], [B("nkss")])
                    self.act(self.small[:, 10:12], self.small[:, 8:10], AF.Sqrt, [B("nkss")], [B("nksd")], scale=1.0 / 64, bias=EPS)
                    self.recip(self.small[:, 12:14], self.small[:, 10:12], [B("nksd")], [B("nkrs")])
                    self.tt("dve", t3, p3, self.small[:, 12:14].unsqueeze(2).to_broadcast([128, 2, 64]), ALU.mult,
                            [pB, B("nkrs")], [B("tmpf0")])
                    self.tt("dve", self.nko[:].rearrange("p (h d) -> p h d", d=64), t3,
                            self.kn_bc[:].unsqueeze(1).to_broadcast([128, 2, 64]), ALU.mult, [B("tmpf0"), B("kn_bc")], [B("nko")])
                    rows = slice(t * 128, (t + 1) * 128)
                    self.dma(self.nk[rows, a * 128:(a + 1) * 128], self.nko[:], reads=[B("nko")])
            if is_sample:
                self.dma(self.ctxst[:], din["kctxT"][a * 128:(a + 1) * 128, :], writes=[B("ctxst")])
                for hh in range(2):
                    psl = slice(hh * 64, (hh + 1) * 64)
                    self.cp("dve", self.kcm[psl, hh, :], self.ctxst[psl, :], [B("ctxst")], [B("kcm")])
                for kc in range(2):
                    self.dma(self.ctxst[:, 0:128], din["vctx"][kc * 128:(kc + 1) * 128, a * 128:(a + 1) * 128],
                             reads=[], writes=[B("ctxst")])
                    for hh in range(2):
                        cs = slice(hh * 64, (hh + 1) * 64)
                        self.cp("dve", self.vcp[:, kc, hh, cs], self.ctxst[:, cs], [B("ctxst")], [B("vcp")])
            cnt = 0

            def pv(lv, lvB, p, pB_, q0, qn, first, last):
                c0 = q0
                while c0 < q0 + qn:
                    c1 = min((c0 // 512 + 1) * 512, q0 + qn)
                    self.mm(self.psC[:, c0:c1], lv, p[:, c0 - q0:c1 - q0], first, last, [lvB, pB_], [B("psC")])
                    self.mm(self.psD[:, c0:c1], self.ones_pad[:, lv_hh[0], :], p[:, c0 - q0:c1 - q0], first, last,
                            [B("ones_pad"), pB_], [B("psD")])
                    c0 = c1

            lv_hh = [0]

            jobs = []

            def add_dense(hh, kc, first, last, qblocks=None):
                for bi, (b0, bn) in enumerate(qblocks if qblocks is not None else blocks):
                    k = len(jobs)
                    half = k % 2
                    pb = self.psB[:, half * 512: half * 512 + bn]
                    pbB = B("psB%d" % half)
                    sc = self.sc[k % 3]
                    scB = B("sc%d" % (k % 3))

                    def score(hh=hh, kc=kc, b0=b0, bn=bn, pb=pb, pbB=pbB, sc=sc, scB=scB):
                        lk = self.kTm[:, hh, kc * 128:(kc + 1) * 128] if not is_sample else self.kcm[:, hh, kc * 128:(kc + 1) * 128]
                        self.mm(pb, lk, self.qT[:, b0:b0 + bn], True, True, [B("kTm"), B("kcm"), B("qT")], [pbB])
                        self.act(sc[:, 0:bn], pb, AF.Exp, [pbB], [scB], scale=0.125)

                    def pvj(hh=hh, kc=kc, b0=b0, bn=bn, sc=sc, scB=scB, first=first, last=last):
                        lv_hh[0] = hh
                        lv = self.vpad[:, kc, hh, :] if not is_sample else self.vcp[:, kc, hh, :]
                        pv(lv, B("vpad") if not is_sample else B("vcp"), sc, scB, b0, bn, first, last)

                    jobs.append((score, pvj))

            def add_window(hh, c):
                k = len(jobs)
                h = 2 * a + hh
                r0 = [q_of_k[2 * c], q_of_k[2 * c + 1]]
                qlo = min(r0[0][0], r0[1][0])
                qhi = max(r0[0][1], r0[1][1])
                q0, qn = qlo * 64, (qhi - qlo + 1) * 64
                sbt = self.sbias[k % 2]
                sbB = B("sbias%d" % (k % 2))
                sc = self.sc[k % 3]
                scB = B("sc%d" % (k % 3))

                def score():
                    self.mm(self.psB[:, 0:min(qn, 512)], self.kTm[:, hh, c * 128:(c + 1) * 128], self.qT[:, q0:q0 + min(qn, 512)],
                            True, True, [B("kTm"), B("qT")], [B("psB0"), B("psB1")])
                    if qn > 512:
                        self.mm(self.psB[:, 512:qn], self.kTm[:, hh, c * 128:(c + 1) * 128], self.qT[:, q0 + 512:q0 + qn],
                                True, True, [B("kTm"), B("qT")], [B("psB0"), B("psB1")])
                    for krl in range(2):
                        kr = 2 * c + krl
                        psl = slice(krl * 64, (krl + 1) * 64)
                        a0, a1 = r0[krl]
                        lo, hi = (a0 - qlo) * 64, (a1 - qlo + 1) * 64
                        x0 = a0 - kr + 7
                        bias = self.trb[psl, h, x0:x0 + (a1 - a0 + 1), :]
                        self.stt("dve", sbt[psl, lo:hi].rearrange("p (x q) -> p x q", q=64),
                                 self.psB[psl, lo:hi].rearrange("p (x q) -> p x q", q=64), 0.125, bias,
                                 ALU.mult, ALU.add, [B("psB0"), B("psB1"), B("trb")], [sbB])
                        if lo > 0:
                            self.memset("dve", sbt[psl, 0:lo], NEG, [sbB])
                        if hi < qn:
                            self.memset("dve", sbt[psl, hi:qn], NEG, [sbB])
                    self.act(sc[:, 0:qn], sbt[:, 0:qn], AF.Exp, [sbB], [scB])

                def pvj():
                    lv_hh[0] = hh
                    pv(self.vpad[:, c, hh, :], B("vpad"), sc, scB, q0, qn, False, False)

                jobs.append((score, pvj))

            for hh in range(2):
                if "noatt" in os.environ.get("KOPT", ""):
                    continue
                if not is_sample:
                    continue
                else:
                    add_dense(hh, 0, hh == 0, False)
                    for c in range(8):
                        add_window(hh, c)
                    add_dense(hh, 1, False, hh == 1)
            if not is_sample and "noatt" not in os.environ.get("KOPT", ""):
                for (b0, bn, ktiles) in ablocks:
                    for hh in range(2):
                        for kc in ktiles:
                            add_dense(hh, kc, hh == 0 and kc == ktiles[0], hh == 1 and kc == ktiles[-1], [(b0, bn)])
            if jobs:
                jobs[0][0]()
            for k in range(len(jobs)):
                if k + 1 < len(jobs):
                    jobs[k + 1][0]()
                jobs[k][1]()
            for bi, (b0, bn, _kt) in enumerate(ablocks):
                tf = self.tmpf[bi % 2]
                tB = B("tmpf%d" % (bi % 2))
                self.recip(tf[:, 0:bn], self.psD[:, b0:b0 + bn], [B("psD")], [tB])
                self.tt("dve", self.oT[:, 4 + a, b0:b0 + bn], self.psC[:, b0:b0 + bn], tf[:, 0:bn], ALU.mult,
                        [B("psC"), tB], [B("oT")])

    def out_proj(self, si, tile0, NT, v, is_sample):
        B, din = self.B, self.din
        for t in range(NT):
            ps, pB = (self.psC, B("psC")) if t % 2 == 0 else (self.psD, B("psD"))
            for half in range(2):
                cs = slice(half * 512, (half + 1) * 512)
                for c in range(8):
                    self.mm(ps[:, cs], self.oT[:, c, t * 128:(t + 1) * 128], self.wo[:, c, cs], c == 0, c == 7,
                            [B("oT"), B("wo")], [pB])
            xb = self.xb[t % 2]
            xB = B("xb%d" % (t % 2))
            rows = slice((tile0 + t) * 128, (tile0 + t + 1) * 128)
            self.dma(xb[:], din["x"][rows, :], writes=[xB])
            tf = self.tmpf[t % 2]
            tB = B("tmpf%d" % (t % 2))
            self.tt("dve", tf[:], ps[:], self.modsl(v, 2), ALU.mult, [pB, B("mod")], [tB])
            self.tt("dve", xb[:], tf[:], xb[:], ALU.add, [tB, xB], [xB])
            self.dma(self.y[rows, :], xb[:], reads=[xB])

    def peer(self, st):
        B, din = self.B, self.din
        sb = lambda n, s, d=F32: self.sb(st, n, s, d)
        TG = 256
        self.bg_issue(10000)
        scrU, scrV, scrQ = self.scrU, self.scrV, self.scrQ
        G3 = [sb("G3_%d" % i, [128, 128, 128], BF16) for i in range(3)]
        XT = [sb("XT%d" % i, [128, 8, TG], BF16) for i in range(2)]
        x1 = sb("x1_0", [128, 1024])
        xm = sb("xm", [128, 1024], BF16)
        ptmp = sb("ptmp0", [128, 1024])
        keysT = sb("keysT", [128, 16, 128], BF16)
        iota_row = sb("iota_rowp", [128, 128])
        iota16 = sb("iota16p", [128, 16])
        iota_rb = sb("iota_rb", [128, 128], BF16)
        qTc = [sb("qTc%d" % i, [128, TG], BF16) for i in range(2)]
        SscR = [[sb("Ssc%d_%d" % (t, k), [128, 128]) for k in range(3)] for t in range(2)]
        v16 = [sb("v16_%d" % i, [128, 16, 16]) for i in range(2)]
        i16u = [sb("i16u_%d" % i, [128, 16, 16], U32) for i in range(2)]
        i16f = sb("i16f", [128, 16, 16])
        cand = sb("cand", [128, 8, 256])
        oh = cand[:].rearrange("p h (a b) -> p h a b", b=16)
        tops = [sb("top%d" % i, [128, 8, 16]) for i in range(2)]
        pu = sb("pu", [128, 8, 16], U32)
        abu = sb("abu", [128, 2, 8, 16], U32)
        abf = sb("abf", [128, 2, 8, 16])
        wsels = [sb("wsel%d" % i, [128, 3, 128]) for i in range(2)]
        wselb = sb("wselb", [128, 3, 128], BF16)
        zs = sb("zs", [128, 8, 2])
        sT = [sb("sT%d" % i, [128, 3, TG], BF16) for i in range(2)]
        wbf = [sb("pwbf%d" % i, [128, 8, 128], BF16) for i in range(2)]
        ubf = [sb("ubf%d" % i, [128, 1024], BF16) for i in range(3)]
        vbf = [sb("vbf%d" % i, [128, 1024], BF16) for i in range(4)]
        hg = [sb("hg%d" % i, [128, TG], BF16) for i in range(3)]
        AT = [sb("AT%d" % i, [128, TG], BF16) for i in range(3)]
        P1 = [sb("P1_%d" % i, [128, 4, 128], BF16) for i in range(4)]
        P2 = [sb("P2_%d" % i, [128, 4, 128], BF16) for i in range(4)]

        self.dma(iota_row[:], din["iota_row"], writes=[B("iota_rowp")])
        self.dma(iota16[:], din["iota16"], writes=[B("iota16p")])
        self.cp("dve", iota_rb[:], iota_row[:], [B("iota_rowp")], [B("iota_rb")])
        for c4 in range(4):
            kst = ptmp[:, 0:512].rearrange("p (c k) -> p c k", k=128)
            self.dma(kst, din["keysT"][:, c4 * 4:(c4 + 1) * 4, :], writes=[B("ptmp0")])
            self.cp("dve", keysT[:, c4 * 4:(c4 + 1) * 4, :], kst, [B("ptmp0")], [B("keysT")])

        psT = self.psA[0][:].bitcast(BF16).rearrange("p (c n) -> p c n", n=128)
        psG = self.psA[0][:].rearrange("p (t j) -> p t j", j=128)
        psH = [self.psB[:, 0:TG], self.psB[:, 512:512 + TG], self.psA[1][:, 0:TG]]
        psHB = [B("psB0"), B("psB1"), B("psA1")]
        psO = [self.psC, self.psD]
        psOB = [B("psC"), B("psD")]
        ngroups = int(os.environ.get("PGROUPS", "6"))
        nj = int(os.environ.get("PNJ", "128"))

        def prep_a(g):
            v = 1 if g < 4 else 0
            XTg, XB = XT[g % 2], B("XT%d" % (g % 2))
            for tt in range(2):
                rows = slice((2 * g + tt) * 128, (2 * g + tt + 1) * 128)
                xB = B("x1_0")
                self.dma(x1[:], self.y[rows, :], writes=[xB])
                tf = ptmp
                self.memset("dve", self.small[:, 0:1], 0.0, [B("ss")])
                self.act(tf[:], x1[:], AF.Square, [xB], [B("ptmp0"), B("ss")], accum_out=self.small[:, 0:1])
                self.act(self.small[:, 1:2], self.small[:, 0:1], AF.Sqrt, [B("ss")], [B("sd")], scale=1.0 / D, bias=EPS)
                self.recip(self.small[:, 2:3], self.small[:, 1:2], [B("sd")], [B("rstd")])
                self.stt("dve", tf[:], x1[:], self.small[:, 2:3], self.modsl(v, 4), ALU.mult, ALU.mult,
                         [xB, B("rstd"), B("mod")], [B("ptmp0")])
                self.tt("dve", xm[:], tf[:], self.modsl(v, 3), ALU.add, [B("ptmp0"), B("mod")], [B("xm")])
                for kc in range(8):
                    self.tr(psT[:, kc, :], xm[:, kc * 128:(kc + 1) * 128], self.ident_b[:], [B("xm"), B("ident_b")], [B("psA0")])
                self.cp("act", XTg[:, :, tt * 128:(tt + 1) * 128], psT, [B("psA0")], [XB])

        def prep(g):
            XTg, XB = XT[g % 2], B("XT%d" % (g % 2))

            def wload(c):
                i = c % 2
                self.dma(wbf[i][:], scrQ[c].rearrange("p (kc n) -> p kc n", n=128), reads=[B("scrQ%d" % c)], writes=[B("pwbf%d" % i)])

            def scores_mm(c):
                for tt in range(2):
                    self.mm(self.psA[0][:, 256 + tt * 128:256 + (tt + 1) * 128], qTc[c % 2][:, tt * 128:(tt + 1) * 128], keysT[:, c, :], True, True,
                            [B("qTc%d" % (c % 2)), B("keysT")], [B("psA0")])

            def scores_cp(c):
                for tt in range(2):
                    self.cp("dve", SscR[tt][c % 3][:], self.psA[0][:, 256 + tt * 128:256 + (tt + 1) * 128], [B("psA0")], [B("Ssc%d_%d" % (tt, c % 3))])

            def level1(c):
                for tt in range(2):
                    S = SscR[tt][c % 3]
                    SB = B("Ssc%d_%d" % (tt, c % 3))
                    vB, iB = B("v16_%d" % tt), B("i16u_%d" % tt)
                    for half8 in range(2):
                        vs = v16[tt][:, c, half8 * 8:(half8 + 1) * 8]
                        iu = i16u[tt][:, c, half8 * 8:(half8 + 1) * 8]
                        self.S.op("dve", lambda q, vs=vs, S=S: q.max(out=vs, in_=S[:]), [SB], [vB])
                        self.S.op("dve", lambda q, vs=vs, S=S, iu=iu: q.max_index(out=iu, in_max=vs, in_values=S[:]), [SB, vB], [iB])
                        if half8 == 0:
                            self.S.op("dve", lambda q, vs=vs, S=S: q.match_replace(out=S[:], in_to_replace=vs, in_values=S[:], imm_value=NEG),
                                      [SB, vB], [SB])

            wload(0)
            for c in range(17):
                if c + 1 < 16:
                    wload(c + 1)
                if c < 16:
                    i = c % 2
                    for kc in range(8):
                        self.mm(self.psA[0][:, 0:TG], wbf[i][:, kc, :], XTg[:, kc, :], kc == 0, kc == 7, [B("pwbf%d" % i), XB], [B("psA0")])
                if c > 0:
                    scores_mm(c - 1)
                if c < 16:
                    self.cp("dve", qTc[c % 2][:], self.psA[0][:, 0:TG], [B("psA0")], [B("qTc%d" % (c % 2))])
                if c > 0:
                    scores_cp(c - 1)
                    level1(c - 1)
                yield 4.2 if c > 0 else 0.5
            for tt in range(2):
                top, topB = tops[tt], B("top%d" % tt)
                wsel, wselB = wsels[tt], B("wsel%d" % tt)
                vB, iB = B("v16_%d" % tt), B("i16u_%d" % tt)
                self.cp("dve", i16f[:], i16u[tt][:], [iB], [B("i16f")])
                v16r = v16[tt][:].rearrange("p (h f) k -> p h f k", f=2)
                i16r = i16f[:].rearrange("p (h f) k -> p h f k", f=2)
                cand4 = cand[:].rearrange("p h (a b) -> p h a b", b=16)
                self.tt("dve", cand4, v16r[:, :, 0, :].unsqueeze(3).to_broadcast([128, 8, 16, 16]),
                        v16r[:, :, 1, :].unsqueeze(2).to_broadcast([128, 8, 16, 16]), ALU.add, [vB], [B("cand")])
                yield 2.6
                for h in range(8):
                    for half8 in range(2):
                        vs = top[:, h, half8 * 8:(half8 + 1) * 8]
                        self.S.op("dve", lambda q, vs=vs, h=h: q.max(out=vs, in_=cand[:, h, :]), [B("cand")], [topB])
                        self.S.op("dve", lambda q, vs=vs, h=h, half8=half8: q.max_index(out=pu[:, h, half8 * 8:(half8 + 1) * 8], in_max=vs, in_values=cand[:, h, :]),
                                  [B("cand"), topB], [B("pu")])
                        if half8 == 0:
                            self.S.op("dve", lambda q, vs=vs, h=h: q.match_replace(out=cand[:, h, :], in_to_replace=vs, in_values=cand[:, h, :], imm_value=NEG),
                                      [B("cand"), topB], [B("cand")])
                    yield 2.4
                self.S.op("dve", lambda q: q.tensor_single_scalar(out=abu[:, 0], in_=pu[:], scalar=4, op=ALU.logical_shift_right), [B("pu")], [B("abu")])
                self.S.op("dve", lambda q: q.tensor_single_scalar(out=abu[:, 1], in_=pu[:], scalar=15, op=ALU.bitwise_and), [B("pu")], [B("abu")])
                self.cp("dve", abf[:], abu[:], [B("abu")], [B("abf")])
                yield 1.0
                wsel4 = wsel[:].rearrange("p w (h k) -> p w h k", k=16)
                for f in range(2):
                    self.tt("dve", oh[:], abf[:, f].unsqueeze(3).to_broadcast([128, 8, 16, 16]),
                            iota16[:].unsqueeze(1).unsqueeze(1).to_broadcast([128, 8, 16, 16]), ALU.is_equal,
                            [B("abf"), B("iota16p")], [B("cand")])
                    self.tt("dve", oh[:], oh[:], i16r[:, :, f, :].unsqueeze(2).to_broadcast([128, 8, 16, 16]), ALU.mult,
                            [B("cand"), B("i16f")], [B("cand")])
                    self.S.op("dve", lambda q, f=f, wsel4=wsel4: q.tensor_reduce(out=wsel4[:, f], in_=oh[:], axis=AX.X, op=ALU.add), [B("cand")], [wselB])
                    yield 6.6
            yield ("wait", 2.0)
            prep_tail(g)
            yield 3.0

        def prep_tail(g):
            sTg, sTB = sT[g % 2], B("sT%d" % (g % 2))
            for tt in range(2):
                top, topB = tops[tt], B("top%d" % tt)
                wsel, wselB = wsels[tt], B("wsel%d" % tt)
                wsel4 = wsel[:].rearrange("p w (h k) -> p w h k", k=16)
                self.cp("dve", zs[:, :, 0:1], top[:, :, 0:1], [topB], [B("zs")])
                self.tt("dve", top[:], top[:], zs[:, :, 0:1].to_broadcast([128, 8, 16]), ALU.subtract, [topB, B("zs")], [topB])
                self.act(top[:], top[:], AF.Exp, [topB], [topB])
                self.S.op("dve", lambda q, top=top: q.tensor_reduce(out=zs[:, :, 0], in_=top[:], axis=AX.X, op=ALU.add), [topB], [B("zs")])
                self.recip(zs[:, :, 1], zs[:, :, 0], [B("zs")], [B("zs")])
                self.tt("dve", wsel4[:, 2], top[:], zs[:, :, 1:2].to_broadcast([128, 8, 16]), ALU.mult, [topB, B("zs")], [wselB])
                self.cp("dve", wselb[:], wsel[:], [wselB], [B("wselb")])
                for w in range(3):
                    self.tr(psT[:, w, :], wselb[:, w, :], self.ident_b[:], [B("wselb"), B("ident_b")], [B("psA0")])
                self.cp("act", sTg[:, :, tt * 128:(tt + 1) * 128], psT[:, 0:3, :], [B("psA0")], [sTB])

        def gconstruct(g, tt, dst, ev="dve"):
            sTg, sTB = sT[g % 2], B("sT%d" % (g % 2))
            Gd, GB = G3[dst], B("G3_%d" % dst)
            io4 = iota_rb[:].unsqueeze(1).to_broadcast([128, 4, 128])
            nb = 32

            def dve_part(bi):
                r = bi % 4
                n0 = tt * 128 + bi * 4
                bc = lambda w: sTg[:, w, n0:n0 + 4].unsqueeze(2).to_broadcast([128, 4, 128])
                self.tt("dve", P1[r][:], io4, bc(0), ALU.is_equal, [B("iota_rb"), sTB], [B("P1_%d" % r)])
                self.tt("dve", P2[r][:], io4, bc(1), ALU.is_equal, [B("iota_rb"), sTB], [B("P2_%d" % r)])
                self.tt("dve", P1[r][:], P1[r][:], bc(2), ALU.mult, [B("P1_%d" % r), sTB], [B("P1_%d" % r)])

            def pe_part(bi):
                r = bi % 4
                for k in range(4):
                    self.mm(psG[:, k, :], P1[r][:, k, :], P2[r][:, k, :], True, True, [B("P1_%d" % r), B("P2_%d" % r)], [B("psA0")])
                self.cp(ev, Gd[:, bi * 4:bi * 4 + 4, :], psG, [B("psA0")], [GB])

            for u in range(nb // 2 + 1):
                if u > 0:
                    pe_part(2 * u - 2)
                    pe_part(2 * u - 1)
                if u < nb // 2:
                    dve_part(2 * u)
                    dve_part(2 * u + 1)
                yield 4.9 if u < nb // 2 else 1.2

        def run_all(gen):
            for _ in gen:
                pass

        def main(g, Ta, Tb, inter):
            XTg, XB = XT[g % 2], B("XT%d" % (g % 2))
            Gt = [G3[Ta], G3[Tb]]
            GtB = [B("G3_%d" % Ta), B("G3_%d" % Tb)]

            def load(j):
                self.dma(ubf[j % 3][:], scrU[j], reads=[B("scrU%d" % j)], writes=[B("ubf%d" % (j % 3))])
                self.dma(vbf[j % 4][:], scrV[j], reads=[B("scrV%d" % j)], writes=[B("vbf%d" % (j % 4))])

            def Hm(j):
                ib = j % 3
                u3 = ubf[ib][:].rearrange("p (c i) -> p c i", i=128)
                for kc in range(8):
                    self.mm(psH[j % 3], u3[:, kc, :], XTg[:, kc, :], kc == 0, kc == 7, [B("ubf%d" % ib), XB], [psHB[j % 3]])

            def post(j):
                i3 = j % 3
                self.act(hg[i3][:], psH[i3], AF.Gelu_apprx_tanh, [psHB[i3]], [B("hg%d" % i3)])
                for tt in range(2):
                    cs = slice(tt * 128, (tt + 1) * 128)
                    self.tt("pool", AT[i3][:, cs], hg[i3][:, cs], Gt[tt][:, :, j], ALU.mult, [B("hg%d" % i3), GtB[tt]], [B("AT%d" % i3)])

            def outm(j):
                i3, ib = j % 3, j % 4
                for tt in range(2):
                    for half in range(2):
                        cs = slice(half * 512, (half + 1) * 512)
                        self.mm(psO[tt][:, cs], AT[i3][:, tt * 128:(tt + 1) * 128], vbf[ib][:, cs], j == 0, j == nj - 1,
                                [B("AT%d" % i3), B("vbf%d" % ib)], [psOB[tt]])

            W = 0.0
            waiting = None
            for j0 in range(min(3, nj)):
                load(j0)
            Hm(0)
            if nj > 1:
                Hm(1)
            for j in range(nj):
                if j + 3 < nj:
                    load(j + 3)
                if j + 2 < nj:
                    Hm(j + 2)
                post(j)
                outm(j)
                if inter is None:
                    continue
                budget = (j + 1) * 1.8 * 0.9
                emitted = 0
                while inter is not None and emitted < 1:
                    if waiting is not None:
                        if W + waiting > budget:
                            break
                        waiting = None
                    if W > budget:
                        break
                    c = next(inter, "done")
                    if c == "done":
                        inter = None
                    elif isinstance(c, tuple):
                        waiting = c[1]
                    else:
                        W += c
                        emitted += 1
            if inter is not None:
                run_all(inter)

        def epilogue(g):
            v = 1 if g < 4 else 0
            for tt in range(2):
                rows = slice((2 * g + tt) * 128, (2 * g + tt + 1) * 128)
                tf, tB = ptmp, B("ptmp0")
                self.dma(x1[:], self.y[rows, :], writes=[B("x1_0")])
                self.tt("dve", tf[:], psO[tt][:], self.modsl(v, 5), ALU.mult, [psOB[tt], B("mod")], [tB])
                self.tt("dve", x1[:], tf[:], x1[:], ALU.add, [tB, B("x1_0")], [B("x1_0")])
                self.dma(self.y[rows, :], x1[:], reads=[B("x1_0")])

        def chain(*gens):
            for gn in gens:
                for item in gn:
                    yield item

        prep_a(0)
        run_all(prep(0))
        run_all(gconstruct(0, 0, 0, "act"))
        run_all(gconstruct(0, 1, 1, "act"))
        Ta, Tb, Fr = 0, 1, 2
        for g in range(ngroups):
            if g + 1 < ngroups:
                prep_a(g + 1)
                main(g, Ta, Tb, chain(prep(g + 1), gconstruct(g + 1, 0, Fr)))
                epilogue(g)
                run_all(gconstruct(g + 1, 1, Ta, "act"))
                Ta, Tb, Fr = Fr, Ta, Tb
            else:
                main(g, Ta, Tb, None)
                epilogue(g)


def _prep_inputs(inp):
    f = lambda a: np.ascontiguousarray(np.asarray(a, dtype=np.float32))
    consts = _host_constants()
    shared = dict(consts)
    w_in = f(inp["w_in"][0])
    shared["w_ada"] = f(inp["w_ada"][0])
    shared["b_ada"] = f(inp["b_ada"][0]).reshape(1, 6144)
    shared["n1g"] = f(inp["norm1_g"][0]).reshape(1, 1024)
    shared["n2g"] = f(inp["norm2_g"][0]).reshape(1, 1024)
    shared["w_in"] = w_in
    perm = _rope_partner_perm()
    shared["w_sw"] = f(np.concatenate([w_in[:, 0:512][:, perm], w_in[:, 512:1024][:, perm]], axis=1))
    shared["dec"] = f(np.concatenate([inp["ret_decay_f"][0], inp["ret_decay_b"][0]])).reshape(1, 16)
    shared["gn_g"] = f(np.asarray(inp["ret_gn_g"][0]).reshape(4, 128).T)
    qg = np.tile(np.asarray(inp["na_qn_g"][0]), 2)
    kg = np.tile(np.asarray(inp["na_kn_g"][0]), 2)
    shared["qkn_g"] = f(np.stack([qg, kg], axis=1))
    shared["kn_row"] = f(inp["na_kn_g"][0]).reshape(1, 64)
    rpb = np.asarray(inp["na_rpb"][0], dtype=np.float32)
    kc = np.arange(64)[:, None]
    qc = np.arange(64)[None, :]
    dc = np.clip(kc - qc + 15, 0, 30)
    x = np.arange(15)
    tr = rpb[:, (14 - x)[:, None, None], dc[None, :, :]]
    tr = np.transpose(tr, (2, 0, 1, 3))
    shared["rpbT"] = f(np.concatenate([tr, tr], axis=0))
    shared["w_out"] = f(inp["w_out"][0])
    shared["wq"] = f(inp["peer_wq"][0])
    keys = np.asarray(inp["peer_keys"][0], dtype=np.float32)
    shared["keysT"] = f(np.transpose(keys.reshape(16, 128, 128), (2, 0, 1)))
    U = np.asarray(inp["peer_u"][0], dtype=np.float32)
    shared["Ut"] = f(np.transpose(U.reshape(128, 128, 8, 128), (1, 3, 2, 0)))
    V = np.asarray(inp["peer_v"][0], dtype=np.float32)
    shared["Vp"] = f(np.transpose(V.reshape(128, 128, 1024), (1, 0, 2)))
    xp = np.asarray(inp["x_prompt"], dtype=np.float32)
    xs = np.asarray(inp["x_sample"], dtype=np.float32)
    cc = np.asarray(inp["c"], dtype=np.float32)
    cctx = np.asarray(inp["c_ctx"], dtype=np.float32)
    maps = []
    for c in range(NCORES):
        m = dict(shared)
        m["x"] = f(np.concatenate([xs[c], xp[2 * c], xp[2 * c + 1]], axis=0))
        cv = np.stack([cctx, cc[c]], axis=0)
        m["cT"] = f(np.transpose(cv.reshape(2, 8, 128), (2, 1, 0)))
        m["kctxT"] = f(np.asarray(inp["cache_na_k"][c, 0], dtype=np.float32).reshape(256, 512).T)
        m["vctx"] = f(np.asarray(inp["cache_na_v"][c, 0], dtype=np.float32).reshape(256, 512))
        m["s0"] = f(inp["state_ret"][c, 0])
        maps.append(m)
    return maps


_NC_CACHE = {}


def _get_nc(stage=3):
    if stage not in _NC_CACHE:
        _NC_CACHE[stage] = Builder(stage=stage).build()
    return _NC_CACHE[stage]


def kernel(**inputs):
    maps = _prep_inputs(inputs)
    nc = _get_nc()
    res = run_bass_kernel_spmd(nc, maps, core_ids=list(range(NCORES)))
    outs = res.results
    y_p = np.zeros((16, 256, 1024), np.float32)
    y_s = np.zeros((8, 1024, 1024), np.float32)
    nk = np.zeros((16, 1, 256, 8, 64), np.float32)
    nv = np.zeros((16, 1, 256, 8, 64), np.float32)
    st = np.zeros((16, 1, 2, 8, 64, 64), np.float32)
    for c in range(NCORES):
        r = outs[c]
        y = r["y"]
        y_s[c] = y[0:1024]
        y_p[2 * c] = y[1024:1280]
        y_p[2 * c + 1] = y[1280:1536]
        nk[2 * c:2 * c + 2, 0] = r["nk"].reshape(2, 256, 8, 64)
        nv[2 * c:2 * c + 2, 0] = r["nv"].reshape(2, 256, 8, 64)
        st[2 * c:2 * c + 2, 0] = r["st"]
    return (y_p, y_s, nk, nv, st)
```
